# Optimizing a Trainium2 kernel written in Bass

```python
import math
import jax, jax.numpy as jnp
from jax import lax
import numpy as np

D_MODEL = 2048
BATCH = 4
SEQ = 4096
DEPTH = 4

MIX_WIDTH = D_MODEL // 2
N_BRANCH = 4
Q_BLOCK = 128
DA_QK_DIM = 64
DA_V_DIM = 2 * DA_QK_DIM
DA_HEADS = MIX_WIDTH // DA_V_DIM
HG_DK = 128
HG_DV = 128
HG_HEADS = MIX_WIDTH // HG_DK
HG_CHUNK = 64
RW_HEAD = 64
RW_HEADS = MIX_WIDTH // RW_HEAD
RW_W_RANK = 64
RW_A_RANK = 64
RW_V_RANK = 32
RW_LN_EPS = 64e-5
SB_DIM = 128
SB_HEADS = MIX_WIDTH // SB_DIM
REL_BUCKETS = 32
REL_MAX_DIST = 128
LN_EPS = 1e-5
RMS_EPS = 1e-6
DEEPNORM_ALPHA = (2 * DEPTH) ** 0.25
DEEPNORM_BETA = (8 * DEPTH) ** -0.25

RW_MIX = 3 * MIX_WIDTH + RW_W_RANK + RW_A_RANK
COL_SIZES = (
    DA_HEADS * 2 * DA_QK_DIM, DA_HEADS * 2 * DA_QK_DIM, DA_HEADS * DA_V_DIM, MIX_WIDTH,
    HG_HEADS * HG_DK, HG_HEADS * HG_DK, HG_HEADS * HG_DV, MIX_WIDTH,
    RW_MIX, MIX_WIDTH,
    SB_HEADS * SB_DIM, SB_HEADS * SB_DIM, SB_HEADS * SB_DIM, MIX_WIDTH,
    N_BRANCH * D_MODEL,
)
IN_COLS = sum(COL_SIZES)
SPLIT_POINTS = tuple(int(c) for c in np.cumsum(COL_SIZES)[:-1])
RW_SPLITS = (MIX_WIDTH, 2 * MIX_WIDTH, 3 * MIX_WIDTH, 3 * MIX_WIDTH + RW_W_RANK)
MERGE_SPLITS = tuple(D_MODEL * i for i in range(1, N_BRANCH))

kernel_name = 'hybrid_gated_branch_deepnorm_trunk'


def layer_norm(x, g, b, eps=LN_EPS):
    xf = x.astype(jnp.float32)
    mu = jnp.mean(xf, -1, keepdims=True)
    var = jnp.mean(jnp.square(xf - mu), -1, keepdims=True)
    return ((xf - mu) * lax.rsqrt(var + eps) * g + b).astype(x.dtype)


def rms_norm(x, g, eps=RMS_EPS):
    xf = x.astype(jnp.float32)
    return (xf * lax.rsqrt(jnp.mean(xf * xf, -1, keepdims=True) + eps) * g).astype(x.dtype)


def token_shift(z):
    return jnp.pad(z[:, :-1], ((0, 0), (1, 0), (0, 0)))


def t5_bucket(dist):
    max_exact = REL_BUCKETS // 2
    n = jnp.maximum(dist, 0)
    nf = jnp.maximum(n, 1).astype(jnp.float32)
    large = max_exact + (jnp.log(nf / max_exact) / math.log(REL_MAX_DIST / max_exact)
                         * (REL_BUCKETS - max_exact)).astype(jnp.int32)
    large = jnp.minimum(large, REL_BUCKETS - 1)
    return jnp.where(n < max_exact, n, large)


def diff_attention(q, k, v, rel_table, lam, subln_g, layer):
    B, S, H, _, Dk = q.shape
    Dv = v.shape[-1]
    lam_init = 0.8 - 0.6 * math.exp(-0.3 * layer)
    lamf = lam.astype(jnp.float32)
    lam_full = jnp.exp(jnp.sum(lamf[0] * lamf[1])) - jnp.exp(jnp.sum(lamf[2] * lamf[3])) + lam_init
    nb = S // Q_BLOCK
    qb = q.reshape(B, nb, Q_BLOCK, H, 2, Dk).transpose(1, 0, 2, 3, 4, 5)
    kpos = jnp.arange(S)
    scale = Dk ** -0.5

    def block(args):
        qi, bi = args
        qpos = bi * Q_BLOCK + jnp.arange(Q_BLOCK)
        dist = qpos[:, None] - kpos[None, :]
        bias = rel_table[t5_bucket(dist)].astype(jnp.float32).transpose(2, 0, 1)
        s = jnp.einsum('bqhmd,bkhmd->bhmqk', qi, k).astype(jnp.float32) * scale + bias[None, :, None]
        s = jnp.where((dist >= 0)[None, None, None], s, -jnp.inf)
        p = jax.nn.softmax(s, axis=-1)
        a = p[:, :, 0] - lam_full * p[:, :, 1]
        return jnp.einsum('bhqk,bkhd->bqhd', a.astype(v.dtype), v)

    o = lax.map(block, (qb, jnp.arange(nb)))
    o = o.transpose(1, 0, 2, 3, 4).reshape(B, S, H, Dv)
    o = rms_norm(o, subln_g) * (1.0 - lam_init)
    return o.reshape(B, S, H * Dv)


def hgrn2_chunked(q, log_f, k, i):
    B, S, H, Dk = q.shape
    Dv = i.shape[-1]
    nc = S // HG_CHUNK
    f32 = jnp.float32

    def to_chunks(t):
        return t.astype(f32).reshape(B, nc, HG_CHUNK, H, t.shape[-1]).transpose(1, 0, 3, 2, 4)

    causal = jnp.tril(jnp.ones((HG_CHUNK, HG_CHUNK), dtype=bool))[:, :, None]

    def step(state, xs):
        qc, lfc, kc, ic = xs
        b = jnp.cumsum(lfc, axis=2)
        inter = jnp.einsum('bhcd,bhde->bhce', qc * jnp.exp(b), state)
        diff = b[:, :, :, None, :] - b[:, :, None, :, :]
        decay = jnp.exp(jnp.where(causal, diff, -jnp.inf))
        scores = jnp.sum(qc[:, :, :, None, :] * decay * kc[:, :, None, :, :], axis=-1)
        intra = jnp.einsum('bhts,bhse->bhte', scores, ic)
        b_last = b[:, :, -1:, :]
        state = (jnp.exp(b_last[:, :, 0, :])[..., None] * state
                 + jnp.einsum('bhsd,bhse->bhde', kc * jnp.exp(b_last - b), ic))
        return state, inter + intra

    state0 = jnp.zeros((B, H, Dk, Dv), f32)
    _, o = lax.scan(step, state0, (to_chunks(q), to_chunks(log_f), to_chunks(k), to_chunks(i)))
    return o.transpose(1, 0, 3, 2, 4).reshape(B, S, H, Dv)


def rwkv7_scan(r, w, k, v, a, b):
    B, S, H, N = r.shape

    def step(state, xs):
        rt, wt, kt, vt, at, bt = xs
        sa = jnp.einsum('bhvk,bhk->bhv', state, at)
        state = state * wt[:, :, None, :] + sa[..., None] * bt[:, :, None, :] + vt[..., None] * kt[:, :, None, :]
        return state, jnp.einsum('bhvk,bhk->bhv', state, rt)

    xs = tuple(t.transpose(1, 0, 2, 3) for t in (r, w, k, v, a, b))
    _, o = lax.scan(step, jnp.zeros((B, H, N, N), jnp.float32), xs)
    return o.transpose(1, 0, 2, 3)


def rwkv7_branch(zm, zg, v_gate_logit, v_first, mu, w0, w2, a0, a2, k_k, k_a, r_k, lnx_g, lnx_b):
    B, S, _ = zm.shape
    dt = zm.dtype
    f32 = jnp.float32
    zm = (zm + (token_shift(zm) - zm) * mu).astype(f32)
    r, k, v, wd, ad = jnp.split(zm, RW_SPLITS, axis=-1)
    w_log = -jax.nn.softplus(-(w0 + jnp.tanh(wd) @ w2)) - 0.5
    decay = jnp.exp(-jnp.exp(w_log))
    a = jax.nn.sigmoid(a0 + ad @ a2)
    if v_first is None:
        v_first = v
    else:
        v = v + (v_first - v) * jax.nn.sigmoid(v_gate_logit.astype(f32))

    def hd(t):
        return t.reshape(B, S, RW_HEADS, RW_HEAD)

    kk = hd(k * k_k)
    kk = kk / jnp.maximum(jnp.sqrt(jnp.sum(kk * kk, -1, keepdims=True)), 1e-12)
    k = k * (1.0 + (a - 1.0) * k_a)
    rh, kh, vh, ah = hd(r), hd(k), hd(v), hd(a)
    o = rwkv7_scan(rh, hd(decay), kh, vh, -kk, kk * ah)
    mu_o = jnp.mean(o, -1, keepdims=True)
    var_o = jnp.mean(jnp.square(o - mu_o), -1, keepdims=True)
    g_h = lnx_g.reshape(RW_HEADS, RW_HEAD)
    b_h = lnx_b.reshape(RW_HEADS, RW_HEAD)
    o = (o - mu_o) * lax.rsqrt(var_o + RW_LN_EPS) * g_h + b_h
    o = o + jnp.sum(rh * kh * r_k, -1, keepdims=True) * vh
    y = o.reshape(B, S, MIX_WIDTH).astype(dt) * jax.nn.silu(zg)
    return y, v_first


def stick_breaking_attention(q, k, v):
    B, S, H, D = q.shape
    nb = S // Q_BLOCK
    qb = q.reshape(B, nb, Q_BLOCK, H, D).transpose(1, 0, 2, 3, 4)
    kpos = jnp.arange(S)
    scale = D ** -0.5

    def block(args):
        qi, bi = args
        qpos = bi * Q_BLOCK + jnp.arange(Q_BLOCK)
        strict = (qpos[:, None] > kpos[None, :])[None, None]
        z = jnp.einsum('bqhd,bkhd->bhqk', qi, k).astype(jnp.float32) * scale
        log_beta = jax.nn.log_sigmoid(z)
        log_keep = jnp.where(strict, jax.nn.log_sigmoid(-z), 0.0)
        between = lax.cumsum(log_keep, axis=3, reverse=True) - log_keep
        wgt = jnp.where(strict, jnp.exp(log_beta + between), 0.0)
        return jnp.einsum('bhqk,bkhd->bqhd', wgt.astype(v.dtype), v)

    o = lax.map(block, (qb, jnp.arange(nb)))
    return o.transpose(1, 0, 2, 3, 4).reshape(B, S, H * D)


def setup_inputs(seed: int = 0) -> dict:
    key = jax.random.key(seed)
    ks = jax.random.split(key, 32)
    nrm = jax.random.normal
    f32 = jnp.float32
    L1 = DEPTH - 1
    return {
        'x': nrm(ks[0], (BATCH, SEQ, D_MODEL), f32),
        'w_in': nrm(ks[1], (DEPTH, D_MODEL, IN_COLS), f32) * D_MODEL ** -0.5,
        'rel_bias': nrm(ks[2], (REL_BUCKETS, DA_HEADS), f32) * 0.5,
        'da_lambda': nrm(ks[3], (DEPTH, 4, DA_QK_DIM), f32) * 0.1,
        'da_subln': 1.0 + 0.02 * nrm(ks[4], (DEPTH, DA_V_DIM), f32),
        'hg_lower': nrm(ks[5], (DEPTH, HG_HEADS * HG_DK), f32) * 0.5,
        'hg_norm': 1.0 + 0.02 * nrm(ks[6], (DEPTH, HG_DV), f32),
        'rw_mu': jax.random.uniform(ks[7], (DEPTH, RW_MIX), f32),
        'rw_w0': jax.random.uniform(ks[8], (DEPTH, MIX_WIDTH), f32, -6.0, -1.0),
        'rw_w2': nrm(ks[9], (DEPTH, RW_W_RANK, MIX_WIDTH), f32) * 0.1 * RW_W_RANK ** -0.5,
        'rw_a0': nrm(ks[10], (DEPTH, MIX_WIDTH), f32) * 0.1,
        'rw_a2': nrm(ks[11], (DEPTH, RW_A_RANK, MIX_WIDTH), f32) * 0.5 * RW_A_RANK ** -0.5,
        'rw_v1': nrm(ks[12], (L1, D_MODEL, RW_V_RANK), f32) * D_MODEL ** -0.5,
        'rw_v_mu': jax.random.uniform(ks[13], (L1, RW_V_RANK), f32),
        'rw_v0': 1.0 + 0.1 * nrm(ks[14], (L1, MIX_WIDTH), f32),
        'rw_v2': nrm(ks[15], (L1, RW_V_RANK, MIX_WIDTH), f32) * 0.5 * RW_V_RANK ** -0.5,
        'rw_kk': 0.85 + 0.05 * nrm(ks[16], (DEPTH, MIX_WIDTH), f32),
        'rw_ka': 1.0 + 0.05 * nrm(ks[17], (DEPTH, MIX_WIDTH), f32),
        'rw_rk': nrm(ks[18], (DEPTH, RW_HEADS, RW_HEAD), f32) * 0.1,
        'rw_lnx_g': 1.0 + 0.02 * nrm(ks[19], (DEPTH, MIX_WIDTH), f32),
        'rw_lnx_b': 0.02 * nrm(ks[20], (DEPTH, MIX_WIDTH), f32),
        'w_branch': nrm(ks[21], (DEPTH, N_BRANCH, MIX_WIDTH, D_MODEL), f32) * MIX_WIDTH ** -0.5,
        'w_out': nrm(ks[22], (DEPTH, D_MODEL, D_MODEL), f32) * D_MODEL ** -0.5 * DEEPNORM_BETA,
        'ln_g': 1.0 + 0.02 * nrm(ks[23], (DEPTH, D_MODEL), f32),
        'ln_b': 0.02 * nrm(ks[24], (DEPTH, D_MODEL), f32),
    }


def reference(x, w_in, rel_bias, da_lambda, da_subln, hg_lower, hg_norm, rw_mu, rw_w0, rw_w2, rw_a0,
              rw_a2, rw_v1, rw_v_mu, rw_v0, rw_v2, rw_kk, rw_ka, rw_rk, rw_lnx_g, rw_lnx_b,
              w_branch, w_out, ln_g, ln_b):
    B, S, _ = x.shape
    f32 = jnp.float32
    lbs = jnp.cumsum(jax.nn.softmax(hg_lower.astype(f32), axis=0), axis=0)
    lbs = lbs - lbs[0:1]
    h = x
    v_first = None
    for l in range(DEPTH):
        z = h @ w_in[l]
        (a_q, a_k, a_v, a_g, h_q, h_f, h_i, h_g, r_mix, r_g,
         s_q, s_k, s_v, s_g, m_g) = jnp.split(z, SPLIT_POINTS, axis=-1)

        y_a = diff_attention(a_q.reshape(B, S, DA_HEADS, 2, DA_QK_DIM), a_k.reshape(B, S, DA_HEADS, 2, DA_QK_DIM),
                             a_v.reshape(B, S, DA_HEADS, DA_V_DIM), rel_bias, da_lambda[l], da_subln[l], l)
        y_a = y_a * jax.nn.silu(a_g)

        lb = lbs[l]
        zf = h_f.astype(f32)
        log_f = jnp.logaddexp(jnp.log(lb), jnp.log1p(-lb) + jax.nn.log_sigmoid(zf))
        k_in = (1.0 - lb) * jax.nn.sigmoid(-zf)
        o_h = hgrn2_chunked(h_q.reshape(B, S, HG_HEADS, HG_DK), log_f.reshape(B, S, HG_HEADS, HG_DK),
                            k_in.reshape(B, S, HG_HEADS, HG_DK), h_i.reshape(B, S, HG_HEADS, HG_DV))
        y_b = rms_norm(o_h, hg_norm[l]).reshape(B, S, MIX_WIDTH).astype(h.dtype) * jax.nn.silu(h_g)

        if l == 0:
            v_gate_logit = None
        else:
            vd = h @ rw_v1[l - 1]
            vd = vd + (token_shift(vd) - vd) * rw_v_mu[l - 1]
            v_gate_logit = rw_v0[l - 1] + vd @ rw_v2[l - 1]
        y_c, v_first = rwkv7_branch(r_mix, r_g, v_gate_logit, v_first, rw_mu[l], rw_w0[l], rw_w2[l], rw_a0[l],
                                    rw_a2[l], rw_kk[l], rw_ka[l], rw_rk[l], rw_lnx_g[l], rw_lnx_b[l])

        y_d = stick_breaking_attention(s_q.reshape(B, S, SB_HEADS, SB_DIM), s_k.reshape(B, S, SB_HEADS, SB_DIM),
                                       s_v.reshape(B, S, SB_HEADS, SB_DIM)) * jax.nn.silu(s_g)

        gates = jnp.split(m_g, MERGE_SPLITS, axis=-1)
        branches = (y_a, y_b, y_c, y_d)
        merged = jax.nn.sigmoid(gates[0]) * (branches[0] @ w_branch[l, 0])
        for n in range(1, N_BRANCH):
            merged = merged + jax.nn.sigmoid(gates[n]) * (branches[n] @ w_branch[l, n])
        out = merged @ w_out[l]
        h = layer_norm(DEEPNORM_ALPHA * h + out, ln_g[l], ln_b[l])
    return h
```

```python
import contextlib
import math
import numpy as np
import concourse.bass as bass
import concourse.mybir as mybir
from concourse.bass_utils import run_bass_kernel_spmd

F32 = mybir.dt.float32
BF16 = mybir.dt.bfloat16
AF = mybir.ActivationFunctionType
ALU = mybir.AluOpType
AX = mybir.AxisListType

D_MODEL = 2048
DEPTH = 4
MIXW = 1024
IN_COLS = 24704
KC = D_MODEL // 128
ALPHA = (2 * DEPTH) ** 0.25
LN_EPS = 1e-5
RMS_EPS = 1e-6
RW_LN_EPS = 64e-5
NEG = -30000.0

O_AQ, O_AK, O_AV, O_AG = 0, 1024, 2048, 3072
O_HQ, O_HF, O_HI, O_HG = 4096, 5120, 6144, 7168
O_RM, O_RG = 8192, 11392
O_SQ, O_SK, O_SV, O_SG = 12416, 13440, 14464, 15488
O_MG = 16512


class Prog:
    ENG = ("pe", "act", "dve", "pool", "sp")

    def __init__(self, nc):
        self.nc = nc
        self.E = dict(pe=nc.tensor, act=nc.scalar, dve=nc.vector, pool=nc.gpsimd, sp=nc.sync)
        self.sems = {}
        self.semval = {}
        self.known = {e: {} for e in self.ENG}
        self.res = {}
        self.pend = {e: ([], []) for e in self.ENG}
        self.stack = contextlib.ExitStack()
        self.n_inst = 0
        self.uid = 0

    def sem(self, key):
        if key not in self.sems:
            self.sems[key] = self.stack.enter_context(self.nc.semaphore("s_" + key))
            self.semval[key] = 0
        return self.sems[key]

    def _res(self, k):
        r = self.res.get(k)
        if r is None:
            r = [None, {}]
            self.res[k] = r
        return r

    def _wait(self, eng, tok):
        if tok is None:
            return
        sk, v = tok
        if eng == "pe" and sk == "pe":
            return
        if self.known[eng].get(sk, 0) >= v:
            return
        self.E[eng].wait_ge(self.sems[sk], v)
        self.known[eng][sk] = v
        self.n_inst += 1

    def _deps(self, eng, r, w):
        for k in r:
            self._wait(eng, self._res(k)[0])
        for k in w:
            rr = self._res(k)
            self._wait(eng, rr[0])
            for sk, v in list(rr[1].items()):
                self._wait(eng, (sk, v))

    def op(self, eng, fn, r=(), w=(), sig=True):
        self._deps(eng, r, w)
        inst = fn(self.E[eng])
        self.n_inst += 1
        pr, pw = self.pend[eng]
        pr.extend(r)
        pw.extend(w)
        if not sig:
            return inst
        self.sem(eng)
        self.semval[eng] += 1
        inst.then_inc(self.sems[eng], 1)
        tok = (eng, self.semval[eng])
        for k in pr:
            rr = self._res(k)
            rr[1][eng] = tok[1]
        for k in pw:
            rr = self._res(k)
            rr[0] = tok
            rr[1] = {}
        pr.clear()
        pw.clear()
        return inst

    def dma(self, q, out, in_, r=(), w=(), sem=None):
        assert sem is not None
        self._deps(q, r, w)
        self.sem(sem)
        inst = self.E[q].dma_start(out=out, in_=in_)
        self.n_inst += 1
        self.semval[sem] += 16
        inst.then_inc(self.sems[sem], 16)
        tok = (sem, self.semval[sem])
        for k in r:
            self._res(k)[1][sem] = tok[1]
        for k in w:
            rr = self._res(k)
            rr[0] = tok
            rr[1] = {}
        return inst

    def barrier(self):
        for e in self.ENG:
            assert not self.pend[e][0] and not self.pend[e][1], "pending unsignalled ops at barrier"
        for e in self.ENG:
            for sk, v in self.semval.items():
                if v > 0:
                    self._wait(e, (sk, v))
        self.res = {}

    @contextlib.contextmanager
    def phase(self):
        self.barrier()
        st = contextlib.ExitStack()
        ph = Phase(self, st)
        try:
            yield ph
        finally:
            self.barrier()
            st.close()


class Phase:
    def __init__(self, prog, st):
        self.p = prog
        self.st = st

    def sb(self, name, shape, dt):
        self.p.uid += 1
        return self.st.enter_context(self.p.nc.sbuf_tensor("%s_%d" % (name, self.p.uid), list(shape), dt))

    def ps(self, name, shape, dt=F32):
        self.p.uid += 1
        return self.st.enter_context(self.p.nc.psum_tensor("%s_%d" % (name, self.p.uid), list(shape), dt))


def declare_dram(nc, T, debug_outs=(), debug_ins=()):
    d = {}

    def t(name, shape, dt, kind="Internal"):
        if name in debug_outs:
            kind = "ExternalOutput"
        if name in debug_ins:
            kind = "ExternalInput"
        d[name] = nc.dram_tensor(name, list(shape), dt, kind=kind).ap()

    t("hT", [D_MODEL, T], BF16)
    t("h", [T, D_MODEL], F32)
    t("AqT", [1024, T], BF16)
    t("AkT", [1024, T], BF16)
    t("Av", [T, 1024], BF16)
    t("AgT", [1024, T], BF16)
    t("BqT", [1024, T], BF16)
    t("BfT", [1024, T], F32)
    t("Bi", [T, 1024], BF16)
    t("BgT", [1024, T], BF16)
    t("CmT", [3200, T], F32)
    t("CgT", [1024, T], BF16)
    t("CvdT", [32, T], F32)
    t("Cvf", [1024, T], F32)
    t("DqT", [1024, T], BF16)
    t("DkT", [1024, T], BF16)
    t("Dv", [T, 1024], BF16)
    t("DgT", [1024, T], BF16)
    t("MgT", [8192, T], BF16)
    t("yT", [4 * 1024, T], BF16)
    t("mT", [D_MODEL, T], BF16)
    return d


def phase_gemm(P, T, l, io, d):
    nc = P.nc
    NT = T // 512
    with P.phase() as ph:
        hT = ph.sb("hT", [128, KC, T + 1], BF16)
        wsb = [ph.sb("wsb%d" % i, [128, KC, 512], BF16) for i in range(2)]
        SC = min(T, 2048)
        stg = [ph.sb("stg%d" % i, [128, SC], F32) for i in range(2)]
        stb = [ph.sb("stb%d" % i, [128, max(SC, 1024)], BF16) for i in range(2)]
        tmp = [ph.sb("tmp%d" % i, [128, 512], F32) for i in range(2)]
        mu = ph.sb("mu", [128, 26], F32)
        om = ph.sb("om", [128, 26], F32)
        banks = [ph.ps("bk%d" % i, [128, 512]) for i in range(8)]

        P.op("pool", lambda e: e.memset(hT[:, :, 0:1], 0.0), w=["hT"])
        P.dma("sp", hT[:, :, 1:T + 1], d["hT"].rearrange("(kc p) t -> p kc t", p=128), w=["hT"], sem="d_hT")
        P.dma("sp", mu[:, 0:25], io["rw_mu_t"][l], w=["mu"], sem="d_mu")
        if l > 0:
            P.dma("sp", mu[:, 25:26], io["rw_vmu_t"][l - 1], w=["mu"], sem="d_mu")
        else:
            P.op("pool", lambda e: e.memset(mu[:, 25:26], 0.0), w=["mu"])
        P.op("dve", lambda e: e.tensor_scalar(out=om[:], in0=mu[:], scalar1=-1.0, scalar2=1.0,
                                              op0=ALU.mult, op1=ALU.add), r=["mu"], w=["om"])

        w_l = io["w_in"][l].rearrange("(kc p) c -> p kc c", p=128)
        st = dict(wi=0, bi=0, si=0, ti=0)

        def load_w(src_ap, ncols):
            i = st["wi"] % 2
            st["wi"] += 1
            P.dma("pool", wsb[i][:, :, 0:ncols], src_ap, w=["wsb%d" % i], sem="d_wsb%d" % i)
            return i

        def bank():
            b = st["bi"] % 8
            st["bi"] += 1
            return b

        def mm_F(wi, c0, tt, shift, b):
            off = 0 if shift else 1
            for kc in range(KC):
                P.op("pe", lambda e, kc=kc: e.matmul(banks[b][:, :], lhsT=wsb[wi][:, kc, c0:c0 + 128],
                                                     rhs=hT[:, kc, off + tt * 512: off + tt * 512 + 512],
                                                     start=(kc == 0), stop=(kc == KC - 1)),
                     r=["wsb%d" % wi, "hT"], w=["bk%d" % b], sig=(kc == KC - 1))

        def job_F(col0, ncols, dest, drow0, kind, scale=1.0, mucol0=None, out_dt=BF16, nrows=128):
            for s0 in range(0, ncols, 512):
                sw = min(512, ncols - s0)
                wi = load_w(w_l[:, :, col0 + s0: col0 + s0 + sw], sw)
                for c0 in range(0, sw, 128):
                    for tt in range(NT):
                        if (tt * 512) % SC == 0:
                            si = st["si"] % 2
                            st["si"] += 1
                            so = stb[si] if out_dt == BF16 else stg[si]
                            skey = ("stb%d" if out_dt == BF16 else "stg%d") % si
                        b = bank()
                        mm_F(wi, c0, tt, False, b)
                        lo = (tt * 512) % SC
                        osl = so[:, lo:lo + 512]
                        if kind == "copy":
                            P.op("dve", lambda e: e.tensor_copy(out=osl, in_=banks[b][:, :]),
                                 r=["bk%d" % b], w=[skey])
                        elif kind == "scale":
                            P.op("act", lambda e: e.activation(out=osl, in_=banks[b][:, :], func=AF.Copy, scale=scale),
                                 r=["bk%d" % b], w=[skey])
                        elif kind == "silu":
                            P.op("act", lambda e: e.activation(out=osl, in_=banks[b][:, :], func=AF.Silu),
                                 r=["bk%d" % b], w=[skey])
                        elif kind == "sigmoid":
                            P.op("act", lambda e: e.activation(out=osl, in_=banks[b][:, :], func=AF.Sigmoid),
                                 r=["bk%d" % b], w=[skey])
                        elif kind == "shift":
                            b2 = bank()
                            mm_F(wi, c0, tt, True, b2)
                            mc = mucol0 + (s0 + c0) // 128
                            ti = st["ti"] % 2
                            st["ti"] += 1
                            P.op("dve", lambda e: e.tensor_scalar(out=tmp[ti][:], in0=banks[b2][:, :],
                                                                  scalar1=mu[:, mc:mc + 1], scalar2=None, op0=ALU.mult),
                                 r=["bk%d" % b2, "mu"], w=["tmp%d" % ti])
                            P.op("dve", lambda e: e.scalar_tensor_tensor(out=osl, in0=banks[b][:, :], scalar=om[:, mc:mc + 1],
                                                                         in1=tmp[ti][:], op0=ALU.mult, op1=ALU.add),
                                 r=["bk%d" % b, "om", "tmp%d" % ti], w=[skey])
                        else:
                            raise ValueError(kind)
                        if (tt * 512 + 512) % SC == 0:
                            r0 = drow0 + s0 + c0
                            t_lo = tt * 512 + 512 - SC
                            P.dma("sp", dest[r0:r0 + nrows, t_lo:t_lo + SC], so[0:nrows, 0:SC], r=[skey], w=[], sem="d_" + skey)

        def job_T(col0, dest):
            wis = []
            for s0 in (0, 512):
                wis.append(load_w(w_l[:, :, col0 + s0: col0 + s0 + 512], 512))
            for tk in range(T // 128):
                si = st["si"] % 2
                st["si"] += 1
                for h2 in range(2):
                    b = bank()
                    for kc in range(KC):
                        P.op("pe", lambda e, kc=kc: e.matmul(banks[b][:, :], lhsT=hT[:, kc, 1 + tk * 128: 1 + tk * 128 + 128],
                                                             rhs=wsb[wis[h2]][:, kc, :], start=(kc == 0), stop=(kc == KC - 1)),
                             r=["wsb%d" % wis[h2], "hT"], w=["bk%d" % b], sig=(kc == KC - 1))
                    P.op("dve", lambda e: e.tensor_copy(out=stb[si][:, h2 * 512:(h2 + 1) * 512], in_=banks[b][:, :]),
                         r=["bk%d" % b], w=["stb%d" % si])
                P.dma("sp", dest[tk * 128:(tk + 1) * 128, :], stb[si][:, 0:1024], r=["stb%d" % si], sem="d_stb%d" % si)

        job_F(O_AQ, 1024, d["AqT"], 0, "scale", scale=0.125)
        job_F(O_AK, 1024, d["AkT"], 0, "copy")
        job_T(O_AV, d["Av"])
        job_F(O_AG, 1024, d["AgT"], 0, "silu")
        job_F(O_HQ, 1024, d["BqT"], 0, "copy")
        job_F(O_HF, 1024, d["BfT"], 0, "copy", out_dt=F32)
        job_T(O_HI, d["Bi"])
        job_F(O_HG, 1024, d["BgT"], 0, "silu")
        job_F(O_RM, 3200, d["CmT"], 0, "shift", mucol0=0, out_dt=F32)
        job_F(O_RG, 1024, d["CgT"], 0, "silu")
        job_F(O_SQ, 1024, d["DqT"], 0, "scale", scale=128 ** -0.5)
        job_F(O_SK, 1024, d["DkT"], 0, "copy")
        job_T(O_SV, d["Dv"])
        job_F(O_SG, 1024, d["DgT"], 0, "silu")
        job_F(O_MG, 8192, d["MgT"], 0, "sigmoid")
        if l > 0:
            i = st["wi"] % 2
            st["wi"] += 1
            P.op("pool", lambda e: e.memset(wsb[i][:, :, 0:128], 0.0), w=["wsb%d" % i])
            P.dma("pool", wsb[i][:, :, 0:32], io["rw_v1"][l - 1].rearrange("(kc p) c -> p kc c", p=128),
                  w=["wsb%d" % i], sem="d_wsb%d" % i)
            for tt in range(NT):
                if (tt * 512) % SC == 0:
                    si = st["si"] % 2
                    st["si"] += 1
                b = bank()
                mm_F(i, 0, tt, False, b)
                b2 = bank()
                mm_F(i, 0, tt, True, b2)
                ti = st["ti"] % 2
                st["ti"] += 1
                lo = (tt * 512) % SC
                osl = stg[si][:, lo:lo + 512]
                P.op("dve", lambda e: e.tensor_scalar(out=tmp[ti][:], in0=banks[b2][:, :], scalar1=mu[:, 25:26],
                                                      scalar2=None, op0=ALU.mult), r=["bk%d" % b2, "mu"], w=["tmp%d" % ti])
                P.op("dve", lambda e: e.scalar_tensor_tensor(out=osl, in0=banks[b][:, :], scalar=om[:, 25:26], in1=tmp[ti][:],
                                                             op0=ALU.mult, op1=ALU.add),
                     r=["bk%d" % b, "om", "tmp%d" % ti], w=["stg%d" % si])
                if (tt * 512 + 512) % SC == 0:
                    t_lo = tt * 512 + 512 - SC
                    P.dma("sp", d["CvdT"][:, t_lo:t_lo + SC], stg[si][0:32, 0:SC], r=["stg%d" % si], sem="d_stg%d" % si)


def emit_transpose_tile(P, src, src_key, ident, banks, bank_ctr, hTs, hTs_key, col0):
    for g in range(4):
        b = bank_ctr[0] % len(banks)
        bank_ctr[0] += 1
        for j in range(4):
            kc = g * 4 + j
            P.op("pe", lambda e: e.transpose(out=banks[b][:, j * 128:(j + 1) * 128], in_=src[:, kc * 128:(kc + 1) * 128],
                                             identity=ident[:, :]),
                 r=[src_key, "ident"], w=["tb%d" % b], sig=(j == 3))
        eng = "act" if g % 2 else "dve"
        if eng == "act":
            P.op("act", lambda e: e.activation(out=hTs[:, g * 4:(g + 1) * 4, col0:col0 + 128],
                                               in_=banks[b][:, :].rearrange("p (j t) -> p j t", j=4), func=AF.Copy),
                 r=["tb%d" % b], w=[hTs_key])
        else:
            P.op("dve", lambda e: e.tensor_copy(out=hTs[:, g * 4:(g + 1) * 4, col0:col0 + 128],
                                                in_=banks[b][:, :].rearrange("p (j t) -> p j t", j=4)),
                 r=["tb%d" % b], w=[hTs_key])


def make_ident(P, ph, dt=F32, name="ident"):
    ident = ph.sb(name, [128, 128], dt)
    if dt == F32:
        P.op("pool", lambda e: e.memset(ident[:], 1.0), w=[name])
        P.op("pool", lambda e: e.affine_select(out=ident[:], in_=ident[:], pattern=[[-1, 128]], compare_op=ALU.is_equal,
                                               fill=0.0, base=0, channel_multiplier=1), r=[name], w=[name])
    else:
        tmpi = ph.sb(name + "_f", [128, 128], F32)
        P.op("pool", lambda e: e.memset(tmpi[:], 1.0), w=[name + "_f"])
        P.op("pool", lambda e: e.affine_select(out=tmpi[:], in_=tmpi[:], pattern=[[-1, 128]], compare_op=ALU.is_equal,
                                               fill=0.0, base=0, channel_multiplier=1), r=[name + "_f"], w=[name + "_f"])
        P.op("dve", lambda e: e.tensor_copy(out=ident[:], in_=tmpi[:]), r=[name + "_f"], w=[name])
    return ident


def phase_prep0(P, T, io, d):
    with P.phase() as ph:
        ident = make_ident(P, ph)
        banks = [ph.ps("tb%d" % i, [128, 512]) for i in range(4)]
        xt = [ph.sb("xt%d" % i, [128, D_MODEL], F32) for i in range(2)]
        hTs = [ph.sb("hTs%d" % i, [128, KC, 512], BF16) for i in range(2)]
        ctr = [0]
        for tt in range(T // 512):
            hi = tt % 2
            for q in range(4):
                tk = tt * 4 + q
                xi = tk % 2
                P.dma("sp", xt[xi][:, :], io["x"][tk * 128:(tk + 1) * 128, :], w=["xt%d" % xi], sem="d_xt%d" % xi)
                emit_transpose_tile(P, xt[xi], "xt%d" % xi, ident, banks, ctr, hTs[hi], "hTs%d" % hi, q * 128)
                P.dma("pool", d["h"][tk * 128:(tk + 1) * 128, :], xt[xi][:, :], r=["xt%d" % xi], sem="d_xo%d" % xi)
            P.dma("sp", d["hT"].rearrange("(kc p) t -> p kc t", p=128)[:, :, tt * 512:(tt + 1) * 512], hTs[hi][:, :, :],
                  r=["hTs%d" % hi], sem="d_hTs%d" % hi)


def t5_bucket_np(dist):
    n = np.maximum(dist, 0)
    nf = np.maximum(n, 1).astype(np.float32)
    large = 16 + (np.log(nf / np.float32(16)) / np.float32(math.log(128 / 16)) * np.float32(16)).astype(np.int32)
    large = np.minimum(large, 31)
    return np.where(n < 16, n, large)


def phase_mixA(P, T, l, io, d, NH=8):
    lam_init = 0.8 - 0.6 * math.exp(-0.3 * l)
    NG = T // 512
    NKB = T // 128
    with P.phase() as ph:
        qT = [ph.sb("qT%d" % i, [128, T], BF16) for i in range(2)]
        kT = [ph.sb("kT%d" % i, [128, T], BF16) for i in range(2)]
        V = [ph.sb("V%d" % i, [128, NKB, 128], BF16) for i in range(2)]
        gT = [ph.sb("gT%d" % i, [128, T], BF16) for i in range(2)]
        stf = [ph.sb("stf%d" % i, [128, 640], F32) for i in range(2)]
        shi = [ph.sb("shi%d" % i, [128, 640], BF16) for i in range(2)]
        slo = [ph.sb("slo%d" % i, [128, 640], BF16) for i in range(2)]
        yst = [ph.sb("yst%d" % i, [128, T], BF16) for i in range(2)]
        pT = [ph.sb("pT%d" % i, [128, 512], BF16) for i in range(3)]
        wk = {n: ph.sb(n, [128, 512], F32) for n in ("rl1", "rl2", "a1", "a2", "sq", "t1")}
        identb = make_ident(P, ph, BF16, "identb")
        onesf = ph.sb("onesf", [128, 128], F32)
        onesb = ph.sb("onesb", [128, 128], BF16)
        c31 = ph.sb("c31", [128, 8], F32)
        lam = ph.sb("lam", [128, 256], F32)
        lw = ph.sb("lamw", [128, 128], F32)
        sc = ph.sb("lamsc", [128, 8], F32)
        sub = ph.sb("subln", [128, 1], F32)
        sbk = [ph.ps("sbk%d" % i, [128, 512]) for i in range(3)]
        obk = [ph.ps("obk%d" % i, [128, 512]) for i in range(2)]
        lbk = [ph.ps("lbk%d" % i, [128, 512]) for i in range(2)]

        P.op("pool", lambda e: e.memset(onesf[:], 1.0), w=["onesf"])
        P.op("pool", lambda e: e.memset(onesb[:], 1.0), w=["onesb"])
        P.dma("sp", c31[:, :], io["a_c31"], w=["c31"], sem="d_c31")
        P.dma("sp", lam[:, :], io["da_lambda_b"][l], w=["lam"], sem="d_lam")
        P.dma("sp", sub[:, :], io["da_subln_t"][l], w=["subln"], sem="d_sub")
        P.op("dve", lambda e: e.tensor_tensor(out=lw[:, 0:64], in0=lam[:, 0:64], in1=lam[:, 64:128], op=ALU.mult), r=["lam"], w=["lamw"])
        P.op("dve", lambda e: e.tensor_tensor(out=lw[:, 64:128], in0=lam[:, 128:192], in1=lam[:, 192:256], op=ALU.mult), r=["lam"], w=["lamw"])
        P.op("dve", lambda e: e.reduce_sum(out=sc[:, 0:2], in_=lw[:].rearrange("p (a b) -> p a b", a=2), axis=AX.X), r=["lamw"], w=["lamsc"])
        P.op("act", lambda e: e.activation(out=sc[:, 2:4], in_=sc[:, 0:2], func=AF.Exp), r=["lamsc"], w=["lamsc"])
        P.op("dve", lambda e: e.tensor_tensor(out=sc[:, 4:5], in0=sc[:, 3:4], in1=sc[:, 2:3], op=ALU.subtract), r=["lamsc"], w=["lamsc"])
        P.op("dve", lambda e: e.tensor_scalar(out=sc[:, 5:6], in0=sc[:, 4:5], scalar1=-lam_init, scalar2=None, op0=ALU.add), r=["lamsc"], w=["lamsc"])
        P.op("dve", lambda e: e.tensor_scalar(out=sc[:, 6:7], in0=sub[:, 0:1], scalar1=1.0 - lam_init, scalar2=None, op0=ALU.mult), r=["subln", "lamsc"], w=["lamsc"])
        negl = sc[:, 5:6]
        gsc = sc[:, 6:7]

        def load_head(h):
            i = h % 2
            P.dma("sp", qT[i][:, :], d["AqT"][h * 128:(h + 1) * 128, :], w=["qT%d" % i], sem="d_qT%d" % i)
            P.dma("sp", kT[i][:, :], d["AkT"][h * 128:(h + 1) * 128, :], w=["kT%d" % i], sem="d_kT%d" % i)
            P.dma("sp", V[i][:, :, :], d["Av"].rearrange("(kb p) c -> p kb c", p=128)[:, :, h * 128:(h + 1) * 128],
                  w=["V%d" % i], sem="d_V%d" % i)
            P.dma("sp", gT[i][:, :], d["AgT"][h * 128:(h + 1) * 128, :], w=["gT%d" % i], sem="d_gT%d" % i)
            P.dma("sp", stf[i][:, :], io["a_strip"][h], w=["stf%d" % i], sem="d_stf%d" % i)
            P.op("dve", lambda e: e.tensor_copy(out=shi[i][:], in_=stf[i][:]), r=["stf%d" % i], w=["shi%d" % i])
            P.op("dve", lambda e: e.tensor_tensor(out=stf[i][:], in0=stf[i][:], in1=shi[i][:], op=ALU.subtract),
                 r=["stf%d" % i, "shi%d" % i], w=["stf%d" % i])
            P.op("dve", lambda e: e.tensor_copy(out=slo[i][:], in_=stf[i][:]), r=["stf%d" % i], w=["slo%d" % i])

        cnt = dict(s=0, p=0)
        load_head(0)
        for h in range(NH):
            i = h % 2
            if h + 1 < NH:
                load_head(h + 1)
            for g in range(NG):
                q0 = g * 512
                blocks = [(m, ki) for m in range(2) for ki in range(4 * g + 4)]
                nblk = 4 * g + 4
                info = {}

                def stage1(bd):
                    m, ki = bd
                    pb = slice(m * 64, m * 64 + 64)
                    j = ki - 4 * g
                    near = j >= -1
                    c0 = 128 * j if j >= 1 else 0
                    n = 512 - c0
                    sb_i = cnt["s"] % 3
                    cnt["s"] += 1
                    S = sbk[sb_i]
                    skey = "sbk%d" % sb_i
                    P.op("pe", lambda e: e.matmul(S[:, c0:512], lhsT=kT[i][pb, ki * 128:(ki + 1) * 128],
                                                  rhs=qT[i][pb, q0 + c0:q0 + 512], start=True, stop=not near),
                         r=["kT%d" % i, "qT%d" % i], w=[skey], sig=not near)
                    if near:
                        so = 128 if j == -1 else 0
                        P.op("pe", lambda e: e.matmul(S[:, c0:512], lhsT=identb[:, :], rhs=shi[i][:, so:so + n],
                                                      start=False, stop=False), r=["identb", "shi%d" % i], w=[skey], sig=False)
                        P.op("pe", lambda e: e.matmul(S[:, c0:512], lhsT=identb[:, :], rhs=slo[i][:, so:so + n],
                                                      start=False, stop=True), r=["identb", "slo%d" % i], w=[skey])
                    info[bd] = (S, skey, near, c0)

                def stage2(bd):
                    m, ki = bd
                    S, skey, near, c0 = info.pop(bd)
                    p_i = cnt["p"] % 3
                    cnt["p"] += 1
                    pk = "pT%d" % p_i
                    if near:
                        P.op("act", lambda e: e.activation(out=pT[p_i][:, c0:512], in_=S[:, c0:512], func=AF.Exp),
                             r=[skey], w=[pk])
                    else:
                        P.op("act", lambda e: e.activation(out=pT[p_i][:, c0:512], in_=S[:, c0:512], func=AF.Exp,
                                                           bias=c31[:, h:h + 1]), r=[skey, "c31"], w=[pk])
                    P.op("pe", lambda e: e.matmul(obk[m][:, c0:512], lhsT=V[i][:, ki, :], rhs=pT[p_i][:, c0:512],
                                                  start=(ki == 0), stop=(ki == nblk - 1), skip_group_check=True),
                         r=["V%d" % i, pk], w=["obk%d" % m], sig=False)
                    P.op("pe", lambda e: e.matmul(lbk[m][:, c0:512], lhsT=onesb[:, :], rhs=pT[p_i][:, c0:512],
                                                  start=(ki == 0), stop=(ki == nblk - 1), skip_group_check=True),
                         r=["onesb", pk], w=["lbk%d" % m])

                stage1(blocks[0])
                for bi_, bd in enumerate(blocks):
                    if bi_ + 1 < len(blocks):
                        stage1(blocks[bi_ + 1])
                    stage2(bd)
                P.op("dve", lambda e: e.reciprocal(out=wk["rl1"][:], in_=lbk[0][:, :]), r=["lbk0"], w=["rl1"])
                P.op("dve", lambda e: e.reciprocal(out=wk["rl2"][:], in_=lbk[1][:, :]), r=["lbk1"], w=["rl2"])
                P.op("dve", lambda e: e.tensor_tensor(out=wk["a1"][:], in0=obk[0][:, :], in1=wk["rl1"][:], op=ALU.mult),
                     r=["obk0", "rl1"], w=["a1"])
                P.op("dve", lambda e: e.tensor_tensor(out=wk["a2"][:], in0=obk[1][:, :], in1=wk["rl2"][:], op=ALU.mult),
                     r=["obk1", "rl2"], w=["a2"])
                P.op("dve", lambda e: e.scalar_tensor_tensor(out=wk["a1"][:], in0=wk["a2"][:], scalar=negl, in1=wk["a1"][:],
                                                             op0=ALU.mult, op1=ALU.add), r=["a2", "a1", "lamsc"], w=["a1"])
                P.op("act", lambda e: e.activation(out=wk["sq"][:], in_=wk["a1"][:], func=AF.Square), r=["a1"], w=["sq"])
                sb_i = cnt["s"] % 3
                cnt["s"] += 1
                S = sbk[sb_i]
                skey = "sbk%d" % sb_i
                P.op("pe", lambda e: e.matmul(S[:, :], lhsT=onesf[:, :], rhs=wk["sq"][:], start=True, stop=True),
                     r=["onesf", "sq"], w=[skey])
                P.op("dve", lambda e: e.tensor_scalar(out=wk["t1"][:], in0=S[:, :], scalar1=1.0 / 128, scalar2=RMS_EPS,
                                                      op0=ALU.mult, op1=ALU.add), r=[skey], w=["t1"])
                P.op("act", lambda e: e.activation(out=wk["t1"][:], in_=wk["t1"][:], func=AF.Ln), r=["t1"], w=["t1"])
                P.op("act", lambda e: e.activation(out=wk["t1"][:], in_=wk["t1"][:], func=AF.Exp, scale=-0.5), r=["t1"], w=["t1"])
                P.op("dve", lambda e: e.tensor_tensor(out=wk["a1"][:], in0=wk["a1"][:], in1=wk["t1"][:], op=ALU.mult),
                     r=["a1", "t1"], w=["a1"])
                P.op("dve", lambda e: e.scalar_tensor_tensor(out=yst[i][:, q0:q0 + 512], in0=wk["a1"][:], scalar=gsc,
                                                             in1=gT[i][:, q0:q0 + 512], op0=ALU.mult, op1=ALU.mult),
                     r=["a1", "lamsc", "gT%d" % i], w=["yst%d" % i])
            P.dma("pool", d["yT"][h * 128:(h + 1) * 128, :], yst[i][:, :], r=["yst%d" % i], sem="d_yst%d" % i)


def host_a_strip(rel_bias):
    v = np.arange(640)[None, :]
    s = np.arange(128)[:, None]
    dist = v - s
    bk = t5_bucket_np(dist)
    out = np.empty((8, 128, 640), np.float32)
    for h in range(8):
        out[h] = np.where(dist >= 0, rel_bias[bk, h], np.float32(NEG))
    return out


def phase_mixD(P, T, l, io, d, NH=8):
    NG = T // 512
    NKB = T // 128
    with P.phase() as ph:
        qT = [ph.sb("qT%d" % i, [128, T], BF16) for i in range(2)]
        kT = [ph.sb("kT%d" % i, [128, T], BF16) for i in range(2)]
        V = [ph.sb("V%d" % i, [128, NKB, 128], BF16) for i in range(2)]
        gT = [ph.sb("gT%d" % i, [128, T], BF16) for i in range(2)]
        yst = [ph.sb("yst%d" % i, [128, T], BF16) for i in range(2)]
        ee = [ph.sb("ee%d" % i, [128, 512], F32) for i in range(2)]
        sp = [ph.sb("sp%d" % i, [128, 512], F32) for i in range(2)]
        lk = [ph.sb("lk%d" % i, [128, 512], F32) for i in range(2)]
        aT = [ph.sb("aT%d" % i, [128, 512], BF16) for i in range(2)]
        rsum = ph.sb("rsum", [128, 512], F32)
        identb = make_ident(P, ph, BF16, "identb")
        onesf = ph.sb("onesf", [128, 128], F32)
        lstr = ph.sb("lstr", [128, 128], F32)
        m01 = ph.sb("m01", [128, 640], F32)
        negf = ph.sb("negf", [128, 640], F32)
        negs = ph.sb("negs", [128, 640], BF16)
        zbk = [ph.ps("zbk%d" % i, [128, 512]) for i in range(2)]
        bbk = [ph.ps("bbk%d" % i, [128, 512]) for i in range(2)]
        obk = ph.ps("obk", [128, 512])

        P.op("pool", lambda e: e.memset(onesf[:], 1.0), w=["onesf"])
        P.op("pool", lambda e: e.memset(lstr[:], 1.0), w=["lstr"])
        P.op("pool", lambda e: e.affine_select(out=lstr[:], in_=lstr[:], pattern=[[-1, 128]], compare_op=ALU.is_gt,
                                               fill=0.0, base=0, channel_multiplier=1), r=["lstr"], w=["lstr"])
        P.op("pool", lambda e: e.memset(m01[:], 1.0), w=["m01"])
        P.op("pool", lambda e: e.affine_select(out=m01[:], in_=m01[:], pattern=[[1, 640]], compare_op=ALU.is_gt,
                                               fill=0.0, base=0, channel_multiplier=-1), r=["m01"], w=["m01"])
        P.op("pool", lambda e: e.memset(negf[:], 0.0), w=["negf"])
        P.op("pool", lambda e: e.affine_select(out=negf[:], in_=negf[:], pattern=[[1, 640]], compare_op=ALU.is_gt,
                                               fill=NEG, base=0, channel_multiplier=-1), r=["negf"], w=["negf"])
        P.op("dve", lambda e: e.tensor_copy(out=negs[:], in_=negf[:]), r=["negf"], w=["negs"])

        def load_head(h):
            i = h % 2
            P.dma("sp", qT[i][:, :], d["DqT"][h * 128:(h + 1) * 128, :], w=["qT%d" % i], sem="d_qT%d" % i)
            P.dma("sp", kT[i][:, :], d["DkT"][h * 128:(h + 1) * 128, :], w=["kT%d" % i], sem="d_kT%d" % i)
            P.dma("sp", V[i][:, :, :], d["Dv"].rearrange("(kb p) c -> p kb c", p=128)[:, :, h * 128:(h + 1) * 128],
                  w=["V%d" % i], sem="d_V%d" % i)
            P.dma("sp", gT[i][:, :], d["DgT"][h * 128:(h + 1) * 128, :], w=["gT%d" % i], sem="d_gT%d" % i)

        cnt = dict(z=0, b=0, w=0)
        load_head(0)
        for h in range(NH):
            i = h % 2
            if h + 1 < NH:
                load_head(h + 1)
            for g in range(NG):
                q0 = g * 512
                blocks = list(range(4 * g + 3, -1, -1))
                info = {}
                P.op("pool", lambda e: e.memset(rsum[:], 0.0), w=["rsum"])

                def stage1(ki):
                    j = ki - 4 * g
                    c0 = 128 * j if j >= 1 else 0
                    zi = cnt["z"] % 2
                    cnt["z"] += 1
                    P.op("pe", lambda e: e.matmul(zbk[zi][:, c0:512], lhsT=kT[i][:, ki * 128:(ki + 1) * 128],
                                                  rhs=qT[i][:, q0 + c0:q0 + 512], start=True, stop=True),
                         r=["kT%d" % i, "qT%d" % i], w=["zbk%d" % zi])
                    info[ki] = (zi, c0, j)

                def stage2(ki):
                    zi, c0, j = info.pop(ki)
                    near = j >= -1
                    first = ki == blocks[0]
                    last = ki == 0
                    n = 512 - c0
                    cs = slice(c0, 512)
                    Z = zbk[zi]
                    zk = "zbk%d" % zi
                    wi = cnt["w"] % 2
                    cnt["w"] += 1
                    E_, S_, L_, A_ = ee[wi], sp[wi], lk[wi], aT[wi]
                    ek, sk, lkk, ak = "ee%d" % wi, "sp%d" % wi, "lk%d" % wi, "aT%d" % wi
                    P.op("act", lambda e: e.activation(out=E_[:, cs], in_=Z[:, cs], func=AF.Exp, scale=-1.0), r=[zk], w=[ek])
                    P.op("act", lambda e: e.activation(out=S_[:, cs], in_=E_[:, cs], func=AF.Ln, bias=1.0), r=[ek], w=[sk])
                    P.op("dve", lambda e: e.scalar_tensor_tensor(out=L_[:, cs], in0=S_[:, cs], scalar=-1.0, in1=Z[:, cs],
                                                                 op0=ALU.mult, op1=ALU.subtract), r=[sk, zk], w=[lkk])
                    so = 128 if j == -1 else 0
                    if near:
                        P.op("dve", lambda e: e.tensor_tensor(out=L_[:, cs], in0=L_[:, cs], in1=m01[:, so:so + n], op=ALU.mult),
                             r=[lkk, "m01"], w=[lkk])
                    bi = cnt["b"] % 2
                    cnt["b"] += 1
                    B = bbk[bi]
                    bk = "bbk%d" % bi
                    nmm = 1 + (0 if first else 1)
                    k_ = [0]

                    def fl():
                        k_[0] += 1
                        return dict(start=(k_[0] == 1), stop=(k_[0] == nmm))
                    P.op("pe", lambda e: e.matmul(B[:, cs], lhsT=lstr[:, :], rhs=L_[:, cs], **fl()),
                         r=["lstr", lkk], w=[bk], sig=(nmm == 1))
                    if not first:
                        P.op("pe", lambda e: e.matmul(B[:, cs], lhsT=onesf[:, :], rhs=rsum[:, cs], **fl()),
                             r=["onesf", "rsum"], w=[bk], sig=(k_[0] + 1 == nmm))
                    P.op("dve", lambda e: e.tensor_tensor(out=E_[:, cs], in0=B[:, cs], in1=S_[:, cs], op=ALU.subtract),
                         r=[bk, sk], w=[ek])
                    P.op("act", lambda e: e.activation(out=A_[:, cs], in_=E_[:, cs], func=AF.Exp), r=[ek], w=[ak])
                    if near:
                        P.op("dve", lambda e: e.tensor_tensor(out=A_[:, cs], in0=A_[:, cs], in1=m01[:, so:so + n], op=ALU.mult),
                             r=[ak, "m01"], w=[ak])
                    if not last:
                        P.op("pool", lambda e: e.tensor_tensor(out=rsum[:, cs], in0=rsum[:, cs], in1=L_[:, cs], op=ALU.add),
                             r=["rsum", lkk], w=["rsum"])
                    P.op("pe", lambda e: e.matmul(obk[:, cs], lhsT=V[i][:, ki, :], rhs=A_[:, cs], start=first, stop=last,
                                                  skip_group_check=True), r=["V%d" % i, ak], w=["obk"])

                stage1(blocks[0])
                for bi_, ki in enumerate(blocks):
                    if bi_ + 1 < len(blocks):
                        stage1(blocks[bi_ + 1])
                    stage2(ki)
                P.op("dve", lambda e: e.tensor_tensor(out=yst[i][:, q0:q0 + 512], in0=obk[:, :], in1=gT[i][:, q0:q0 + 512],
                                                      op=ALU.mult), r=["obk", "gT%d" % i], w=["yst%d" % i])
            P.dma("pool", d["yT"][3 * 1024 + h * 128:3 * 1024 + (h + 1) * 128, :], yst[i][:, :], r=["yst%d" % i],
                  sem="d_yst%d" % i)


def phase_mixB(P, T, l, io, d, NH=8):
    NCH = T // 64
    NG = T // 512
    with P.phase() as ph:
        zf = [ph.sb("zf0", [128, T], F32)] * 2
        qb = [ph.sb("qb%d" % i, [128, T], BF16) for i in range(2)]
        gT = [ph.sb("gT%d" % i, [128, T], BF16) for i in range(2)]
        itok = [ph.sb("itok%d" % i, [64, NCH, 128], BF16) for i in range(2)]
        a1 = ph.sb("a1", [128, T], F32)
        a2 = ph.sb("a2", [128, T], F32)
        a3 = ph.sb("a3", [128, T], F32)
        a4 = ph.sb("a4", [128, T], F32)
        qt = ph.sb("qt", [128, T], BF16)
        kh = ph.sb("kh", [128, T], BF16)
        khtok = ph.sb("khtok", [64, NCH, 128], BF16)
        yst = [ph.sb("yst%d" % i, [128, 512], BF16) for i in range(2)]
        rmask = ph.sb("rmask", [128, 512], F32)
        cmask = ph.sb("cmask", [64, 512], F32)
        scb = [ph.sb("scb%d" % i, [64, 512], BF16) for i in range(2)]
        S = ph.sb("S", [128, 128], F32)
        Sb = [ph.sb("Sb%d" % i, [128, 128], BF16) for i in range(2)]
        sq = ph.sb("sq", [128, 512], F32)
        t1 = ph.sb("t1", [128, 512], F32)
        yy = ph.sb("yy", [128, 512], F32)
        identb = make_ident(P, ph, BF16, "identb")
        onesf = ph.sb("onesf", [128, 128], F32)
        hl = ph.sb("hl", [128, 8, 4], F32)
        he = ph.sb("he", [128, 8, 4], F32)
        hs = ph.sb("hs", [128, 8], F32)
        lb = ph.sb("lb", [128, 8], F32)
        oml = ph.sb("oml", [128, 8], F32)
        hgn = ph.sb("hgn", [128, 1], F32)
        sbk = [ph.ps("sbk%d" % i, [128, 512]) for i in range(1)]
        obk = [ph.ps("obk%d" % i, [128, 512]) for i in range(2)]
        spk = [ph.ps("spk%d" % i, [128, 512]) for i in range(2)]
        ssb = ph.ps("ssb", [128, 512])
        tpk = [ph.ps("tpk%d" % i, [64, 8, 128], BF16) for i in range(2)]

        P.op("pool", lambda e: e.memset(onesf[:], 1.0), w=["onesf"])
        P.op("pool", lambda e: e.memset(rmask[:], 1.0), w=["rmask"])
        P.op("pool", lambda e: e.memset(rmask[:].rearrange("p (c k) -> p c k", k=64)[:, :, 0:1], 0.0), w=["rmask"])
        P.op("pool", lambda e: e.memset(cmask[:], 1.0), w=["cmask"])
        P.op("pool", lambda e: e.affine_select(out=cmask[:].rearrange("p (c t) -> p c t", t=64),
                                               in_=cmask[:].rearrange("p (c t) -> p c t", t=64),
                                               pattern=[[0, 8], [1, 64]], compare_op=ALU.is_ge, fill=0.0, base=0,
                                               channel_multiplier=-1), r=["cmask"], w=["cmask"])
        P.dma("sp", hl[:, :, :], io["hg_lower_t"], w=["hl"], sem="d_hl")
        P.dma("sp", hgn[:, :], io["hg_norm_t"][l], w=["hgn"], sem="d_hgn")
        P.op("act", lambda e: e.activation(out=he[:], in_=hl[:], func=AF.Exp), r=["hl"], w=["he"])
        P.op("dve", lambda e: e.reduce_sum(out=hs[:], in_=he[:], axis=AX.X), r=["he"], w=["hs"])
        P.op("dve", lambda e: e.reciprocal(out=hs[:], in_=hs[:]), r=["hs"], w=["hs"])
        P.op("pool", lambda e: e.memset(lb[:], 0.0), w=["lb"])
        for j in range(1, l + 1):
            P.op("dve", lambda e: e.tensor_tensor(out=lb[:], in0=lb[:], in1=he[:, :, j], op=ALU.add), r=["lb", "he"], w=["lb"])
        P.op("dve", lambda e: e.tensor_tensor(out=lb[:], in0=lb[:], in1=hs[:], op=ALU.mult), r=["lb", "hs"], w=["lb"])
        P.op("dve", lambda e: e.tensor_scalar(out=oml[:], in0=lb[:], scalar1=-1.0, scalar2=1.0, op0=ALU.mult, op1=ALU.add),
             r=["lb"], w=["oml"])

        def load_head(h):
            i = h % 2
            P.dma("sp", zf[0][:, :], d["BfT"][h * 128:(h + 1) * 128, :], w=["zf0"], sem="d_zf0")
            P.dma("sp", qb[i][:, :], d["BqT"][h * 128:(h + 1) * 128, :], w=["qb%d" % i], sem="d_qb%d" % i)
            P.dma("sp", gT[i][:, :], d["BgT"][h * 128:(h + 1) * 128, :], w=["gT%d" % i], sem="d_gT%d" % i)
            P.dma("sp", itok[i][:, :, :], d["Bi"].rearrange("(c p) e -> p c e", p=64)[:, :, h * 128:(h + 1) * 128],
                  w=["itok%d" % i], sem="d_itok%d" % i)

        cnt = dict(sb=0)
        load_head(0)
        for h in range(NH):
            i = h % 2
            zk = "zf0"
            P.op("act", lambda e: e.activation(out=a1[:], in_=zf[i][:], func=AF.Sigmoid), r=[zk], w=["a1"])
            P.op("dve", lambda e: e.tensor_scalar(out=a1[:], in0=a1[:], scalar1=oml[:, h:h + 1], scalar2=lb[:, h:h + 1],
                                                  op0=ALU.mult, op1=ALU.add), r=["a1", "oml", "lb"], w=["a1"])
            P.op("act", lambda e: e.activation(out=a2[:], in_=a1[:], func=AF.Ln), r=["a1"], w=["a2"])
            P.op("dve", lambda e: e.tensor_scalar(out=a1[:], in0=a1[:], scalar1=-1.0, scalar2=1.0, op0=ALU.mult, op1=ALU.add),
                 r=["a1"], w=["a1"])
            for g8 in range(NG):
                P.op("dve", lambda e: e.tensor_tensor_scan(out=a3[:, g8 * 512:(g8 + 1) * 512], data0=rmask[:],
                                                           data1=a2[:, g8 * 512:(g8 + 1) * 512], initial=0.0,
                                                           op0=ALU.mult, op1=ALU.add), r=["rmask", "a2"], w=["a3"])
            P.op("act", lambda e: e.activation(out=a4[:], in_=a3[:], func=AF.Exp), r=["a3"], w=["a4"])
            P.op("act", lambda e: e.activation(out=a2[:], in_=a3[:], func=AF.Exp, scale=-1.0), r=["a3"], w=["a2"])
            P.op("dve", lambda e: e.tensor_tensor(out=a1[:], in0=a1[:], in1=a2[:], op=ALU.mult), r=["a1", "a2"], w=["a1"])
            P.op("dve", lambda e: e.tensor_tensor(out=a2[:], in0=qb[i][:], in1=a4[:], op=ALU.mult), r=["qb%d" % i, "a4", "a2"], w=["a2"])
            P.op("act", lambda e: e.activation(out=qt[:], in_=a2[:], func=AF.Copy), r=["a2"], w=["qt"])
            ebl = a4[:].rearrange("p (c k) -> p c k", k=64)[:, :, 63:64]
            P.op("dve", lambda e: e.tensor_tensor(out=kh[:].rearrange("p (c k) -> p c k", k=64),
                                                  in0=a1[:].rearrange("p (c k) -> p c k", k=64),
                                                  in1=ebl.to_broadcast([128, NCH, 64]), op=ALU.mult), r=["a1", "a4"], w=["kh"])
            if h + 1 < NH:
                load_head(h + 1)
            for c8 in range(NCH // 8):
                tp = tpk[c8 % 2]
                tk_ = "tpk%d" % (c8 % 2)
                for cc in range(8):
                    c = c8 * 8 + cc
                    P.op("pe", lambda e: e.transpose(out=tp[:, cc, :], in_=kh[:, c * 64:(c + 1) * 64], identity=identb[:, :]),
                         r=["kh", "identb"], w=[tk_], sig=(cc == 7))
                P.op("act", lambda e: e.activation(out=khtok[:, c8 * 8:(c8 + 1) * 8, :], in_=tp[:, :, :], func=AF.Copy),
                     r=[tk_], w=["khtok"])
            P.op("pool", lambda e: e.memset(S[:], 0.0), w=["S"])
            P.op("pool", lambda e: e.memset(Sb[0][:], 0.0), w=["Sb0"])
            sbi = 0
            for g in range(NG):
                q0 = g * 512
                for cc in range(8):
                    c = g * 8 + cc
                    P.op("pe", lambda e: e.matmul(sbk[0][0:64, cc * 64:(cc + 1) * 64], lhsT=a1[:, c * 64:(c + 1) * 64],
                                                  rhs=a2[:, c * 64:(c + 1) * 64], start=True, stop=True, skip_group_check=True),
                         r=["a1", "a2"], w=["sbk0"], sig=(cc == 7))
                si = g % 2
                P.op("dve", lambda e: e.tensor_tensor(out=scb[si][:], in0=sbk[0][0:64, :], in1=cmask[:], op=ALU.mult),
                     r=["sbk0", "cmask"], w=["scb%d" % si])
                for half in range(2):
                    for c4 in range(4):
                        c = g * 8 + half * 4 + c4
                        P.op("pe", lambda e: e.matmul(spk[half][:, c4 * 128:(c4 + 1) * 128], lhsT=khtok[:, c, :],
                                                      rhs=itok[i][:, c, :], start=True, stop=True, skip_group_check=True),
                             r=["khtok", "itok%d" % i], w=["spk%d" % half], sig=(c4 == 3))
                ob = obk[g % 2]
                ok_ = "obk%d" % (g % 2)
                for cc in range(8):
                    c = g * 8 + cc
                    P.op("pe", lambda e: e.matmul(ob[:, cc * 64:(cc + 1) * 64], lhsT=itok[i][:, c, :],
                                                  rhs=scb[si][:, cc * 64:(cc + 1) * 64], start=True, stop=False,
                                                  skip_group_check=True), r=["itok%d" % i, "scb%d" % si], w=[ok_], sig=False)
                    P.op("pe", lambda e: e.matmul(ob[:, cc * 64:(cc + 1) * 64], lhsT=Sb[sbi][:, :],
                                                  rhs=qt[:, c * 64:(c + 1) * 64], start=False, stop=True,
                                                  skip_group_check=True), r=["Sb%d" % sbi, "qt"], w=[ok_])
                    half, c4 = cc // 4, cc % 4
                    P.op("dve", lambda e: e.scalar_tensor_tensor(out=S[:], in0=S[:], scalar=ebl[:, c, :],
                                                                 in1=spk[half][:, c4 * 128:(c4 + 1) * 128],
                                                                 op0=ALU.mult, op1=ALU.add), r=["S", "a4", "spk%d" % half], w=["S"])
                    sbi = 1 - sbi
                    P.op("act", lambda e: e.activation(out=Sb[sbi][:], in_=S[:], func=AF.Copy), r=["S"], w=["Sb%d" % sbi])
                P.op("act", lambda e: e.activation(out=sq[:], in_=ob[:, :], func=AF.Square), r=[ok_], w=["sq"])
                P.op("pe", lambda e: e.matmul(ssb[:, :], lhsT=onesf[:, :], rhs=sq[:], start=True, stop=True), r=["onesf", "sq"], w=["ssb"])
                P.op("dve", lambda e: e.tensor_scalar(out=t1[:], in0=ssb[:, :], scalar1=1.0 / 128, scalar2=RMS_EPS,
                                                      op0=ALU.mult, op1=ALU.add), r=["ssb"], w=["t1"])
                P.op("act", lambda e: e.activation(out=t1[:], in_=t1[:], func=AF.Ln), r=["t1"], w=["t1"])
                P.op("act", lambda e: e.activation(out=t1[:], in_=t1[:], func=AF.Exp, scale=-0.5), r=["t1"], w=["t1"])
                P.op("dve", lambda e: e.tensor_tensor(out=yy[:], in0=ob[:, :], in1=t1[:], op=ALU.mult), r=[ok_, "t1"], w=["yy"])
                yi = g % 2
                P.op("dve", lambda e: e.scalar_tensor_tensor(out=yst[yi][:, :], in0=yy[:], scalar=hgn[:, 0:1],
                                                             in1=gT[i][:, q0:q0 + 512], op0=ALU.mult, op1=ALU.mult),
                     r=["yy", "hgn", "gT%d" % i], w=["yst%d" % yi])
                P.dma("pool", d["yT"][1024 + h * 128:1024 + (h + 1) * 128, q0:q0 + 512], yst[yi][:, :], r=["yst%d" % yi],
                      sem="d_yst%d" % yi)


WSC = math.exp(-0.5)


def phase_mixC(P, T, l, io, d, NH=16, G=2):
    N = 512
    NST = T // N
    GC = G * 8
    with P.phase() as ph:
        F = lambda name, shape, dt=F32: ph.sb(name, shape, dt)
        cpar = F("cpar", [64, 16, 8])
        omka = F("omka", [64, 16])
        w2s = F("w2s", [64, 1024])
        a2s = F("a2s", [64, 1024])
        v2s = F("v2s", [32, 1024])
        twd = F("twd", [64, N])
        adm = F("adm", [64, N])
        vdm = F("vdm", [32, N])
        Xr = F("Xr", [64, G, N])
        Xk = F("Xk", [64, G, N])
        Xv = F("Xv", [64, G, N])
        Xf = F("Xf", [64, G, N])
        gate = F("gate", [64, G, N], BF16)
        sg = F("sg", [64, G, N])
        aa = F("aa", [64, G, N])
        kk = F("kk", [64, G, N])
        kp = F("kp", [64, G, N])
        cs = F("cs", [64, G, N])
        pp = F("pp", [64, G, N])
        pinv = F("pinv", [64, G, N])
        pprev = F("pprev", [64, G, N])
        tmp = F("tmp", [64, G, N])
        AR = F("AR", [64, G, 8, 2, 64])
        BT = F("BT", [64, G, N])
        KT = F("KT", [64, G, N])
        RK = F("RK", [64, G, N])
        TK = F("TK", [64, G, 8, 5, 64])
        GM = F("GM", [64, GC, 320])
        TT = F("TT", [64, GC, 64])
        XY = [F("XY%d" % i, [64, GC, 128]) for i in range(2)]
        GA = F("GA", [64, GC, 128])
        OL = F("OL", [64, GC, 64])
        RP = F("RP", [64, GC, 64])
        PH = F("PH", [64, GC, 64])
        GP = F("GP", [64, GC, 64])
        OT = F("OT", [64, G, N])
        H = F("H", [64, NH, 64])
        yst = F("yst", [64, G, N], BF16)
        m320 = F("m320", [64, 320])
        rmask = F("rmask", [64, G * N])
        ones64 = F("ones64", [64, 64])
        onesm = F("onesm", [64, 64])
        ident = make_ident(P, ph, F32, "identf")
        idf = ident[0:64, 0:64]
        pbs = [ph.ps("pb%d" % i, [64, 512]) for i in range(8)]
        bctr = [0]

        def nb():
            b = bctr[0] % 8
            bctr[0] += 1
            return pbs[b], "pb%d" % b

        P.op("pool", lambda e: e.memset(ones64[:], 1.0), w=["ones64"])
        P.op("pool", lambda e: e.memset(onesm[:], 1.0 / 64), w=["onesm"])
        P.op("pool", lambda e: e.memset(rmask[:], 1.0), w=["rmask"])
        P.op("pool", lambda e: e.memset(rmask[:].rearrange("p (c k) -> p c k", k=64)[:, :, 0:1], 0.0), w=["rmask"])
        P.op("pool", lambda e: e.memset(H[:], 0.0), w=["H"])
        P.op("pool", lambda e: e.memset(m320[:], 1.0), w=["m320"])
        for blk, op_, cm, pat in ((0, ALU.is_gt, -1, 1), (1, ALU.is_ge, -1, 1), (2, ALU.is_gt, -1, 1), (3, ALU.is_ge, -1, 1),
                                  (4, ALU.is_gt, 1, -1)):
            P.op("pool", lambda e: e.affine_select(out=m320[:, blk * 64:(blk + 1) * 64], in_=m320[:, blk * 64:(blk + 1) * 64],
                                                   pattern=[[pat, 64]], compare_op=op_, fill=0.0, base=0, channel_multiplier=cm),
                 r=["m320"], w=["m320"])
        P.dma("sp", cpar[:, :, :], io["c_par"][l], w=["cpar"], sem="d_cpar")
        P.dma("sp", w2s[:, :], io["rw_w2"][l], w=["w2s"], sem="d_w2s")
        P.dma("sp", a2s[:, :], io["rw_a2"][l], w=["a2s"], sem="d_a2s")
        if l > 0:
            P.dma("sp", v2s[:, :], io["rw_v2"][l - 1], w=["v2s"], sem="d_v2s")
        P.op("dve", lambda e: e.tensor_scalar(out=omka[:], in0=cpar[:, :, 3], scalar1=-1.0, scalar2=1.0, op0=ALU.mult, op1=ALU.add),
             r=["cpar"], w=["omka"])

        def bc(ap2, g0):
            return ap2[:, g0:g0 + G].unsqueeze(2).to_broadcast([64, G, N])

        CM = d["CmT"]
        for st in range(NST):
            t0 = st * N
            P.dma("sp", twd[:, :], CM[3072:3136, t0:t0 + N], w=["twd"], sem="d_twd")
            P.dma("sp", adm[:, :], CM[3136:3200, t0:t0 + N], w=["adm"], sem="d_adm")
            P.op("act", lambda e: e.activation(out=twd[:], in_=twd[:], func=AF.Tanh), r=["twd"], w=["twd"])
            if l > 0:
                P.dma("sp", vdm[:, :], d["CvdT"][:, t0:t0 + N], w=["vdm"], sem="d_vdm")
            for hg in range(NH // G):
                h0 = hg * G
                rows = lambda base: CM[base + h0 * 64: base + (h0 + G) * 64, t0:t0 + N].rearrange("(g k) t -> k g t", k=64)
                P.dma("sp", Xr[:, :, :], rows(0), w=["Xr"], sem="d_Xr")
                P.dma("sp", Xk[:, :, :], rows(1024), w=["Xk"], sem="d_Xk")
                P.dma("sp", Xv[:, :, :], rows(2048), w=["Xv"], sem="d_Xv")
                P.dma("sp", gate[:, :, :], d["CgT"][h0 * 64:(h0 + G) * 64, t0:t0 + N].rearrange("(g k) t -> k g t", k=64),
                      w=["gate"], sem="d_gate")
                if l > 0:
                    P.dma("sp", Xf[:, :, :], d["Cvf"][h0 * 64:(h0 + G) * 64, t0:t0 + N].rearrange("(g k) t -> k g t", k=64),
                          w=["Xf"], sem="d_Xf")
                for g in range(G):
                    h = h0 + g
                    pb, pk = nb()
                    P.op("pe", lambda e: e.matmul(pb[:, :], lhsT=w2s[:, h * 64:(h + 1) * 64], rhs=twd[:, :], start=True, stop=True),
                         r=["w2s", "twd"], w=[pk])
                    P.op("act", lambda e: e.activation(out=sg[:, g, :], in_=pb[:, :], func=AF.Sigmoid, bias=cpar[:, h, 0:1]),
                         r=[pk, "cpar"], w=["sg"])
                    pb, pk = nb()
                    P.op("pe", lambda e: e.matmul(pb[:, :], lhsT=a2s[:, h * 64:(h + 1) * 64], rhs=adm[:, :], start=True, stop=True),
                         r=["a2s", "adm"], w=[pk])
                    P.op("act", lambda e: e.activation(out=aa[:, g, :], in_=pb[:, :], func=AF.Sigmoid, bias=cpar[:, h, 1:2]),
                         r=[pk, "cpar"], w=["aa"])
                    if l > 0:
                        pb, pk = nb()
                        P.op("pe", lambda e: e.matmul(pb[:, :], lhsT=v2s[:, h * 64:(h + 1) * 64], rhs=vdm[:, :], start=True, stop=True),
                             r=["v2s", "vdm"], w=[pk])
                        P.op("act", lambda e: e.activation(out=tmp[:, g, :], in_=pb[:, :], func=AF.Sigmoid, bias=cpar[:, h, 7:8]),
                             r=[pk, "cpar"], w=["tmp"])
                if l > 0:
                    P.op("dve", lambda e: e.tensor_tensor(out=Xf[:], in0=Xf[:], in1=Xv[:], op=ALU.subtract), r=["Xf", "Xv"], w=["Xf"])
                    P.op("dve", lambda e: e.tensor_tensor(out=Xf[:], in0=Xf[:], in1=tmp[:], op=ALU.mult), r=["Xf", "tmp"], w=["Xf"])
                    P.op("dve", lambda e: e.tensor_tensor(out=Xv[:], in0=Xv[:], in1=Xf[:], op=ALU.add), r=["Xf", "Xv"], w=["Xv"])
                else:
                    P.dma("pool", d["Cvf"][h0 * 64:(h0 + G) * 64, t0:t0 + N].rearrange("(g k) t -> k g t", k=64), Xv[:, :, :],
                          r=["Xv"], sem="d_vfo")
                P.op("dve", lambda e: e.tensor_tensor(out=kk[:], in0=Xk[:], in1=bc(cpar[:, :, 2], h0), op=ALU.mult),
                     r=["Xk", "cpar"], w=["kk"])
                P.op("act", lambda e: e.activation(out=tmp[:], in_=kk[:], func=AF.Square), r=["kk"], w=["tmp"])
                for g in range(G):
                    pb, pk = nb()
                    P.op("pe", lambda e: e.matmul(pb[:, :], lhsT=ones64[:, :], rhs=tmp[:, g, :], start=True, stop=True),
                         r=["ones64", "tmp"], w=[pk])
                    P.op("dve", lambda e: e.tensor_scalar(out=kp[:, g, :], in0=pb[:, :], scalar1=1e-24, scalar2=None, op0=ALU.max),
                         r=[pk], w=["kp"])
                P.op("act", lambda e: e.activation(out=kp[:], in_=kp[:], func=AF.Ln), r=["kp"], w=["kp"])
                P.op("act", lambda e: e.activation(out=kp[:], in_=kp[:], func=AF.Exp, scale=-0.5), r=["kp"], w=["kp"])
                P.op("dve", lambda e: e.tensor_tensor(out=kk[:], in0=kk[:], in1=kp[:], op=ALU.mult), r=["kk", "kp"], w=["kk"])
                P.op("dve", lambda e: e.tensor_tensor(out=kp[:], in0=aa[:], in1=bc(cpar[:, :, 3], h0), op=ALU.mult),
                     r=["aa", "cpar"], w=["kp"])
                P.op("dve", lambda e: e.tensor_tensor(out=kp[:], in0=kp[:], in1=bc(omka, h0), op=ALU.add), r=["kp", "omka"], w=["kp"])
                P.op("dve", lambda e: e.tensor_tensor(out=kp[:], in0=kp[:], in1=Xk[:], op=ALU.mult), r=["kp", "Xk"], w=["kp"])
                P.op("dve", lambda e: e.tensor_tensor_scan(out=cs[:].rearrange("p g n -> p (g n)"), data0=rmask[:],
                                                           data1=sg[:].rearrange("p g n -> p (g n)"), initial=0.0,
                                                           op0=ALU.mult, op1=ALU.add), r=["rmask", "sg"], w=["cs"])
                P.op("act", lambda e: e.activation(out=pp[:], in_=cs[:], func=AF.Exp, scale=-WSC), r=["cs"], w=["pp"])
                P.op("act", lambda e: e.activation(out=pinv[:], in_=cs[:], func=AF.Exp, scale=WSC), r=["cs"], w=["pinv"])
                P.op("dve", lambda e: e.tensor_tensor(out=tmp[:], in0=cs[:], in1=sg[:], op=ALU.subtract), r=["cs", "sg"], w=["tmp"])
                P.op("act", lambda e: e.activation(out=pprev[:], in_=tmp[:], func=AF.Exp, scale=-WSC), r=["tmp"], w=["pprev"])
                v4 = lambda t_: t_[:].rearrange("p g (c k) -> p g c k", k=64)
                P.op("dve", lambda e: e.scalar_tensor_tensor(out=AR[:, :, :, 0, :], in0=v4(kk), scalar=-1.0, in1=v4(pprev),
                                                             op0=ALU.mult, op1=ALU.mult), r=["kk", "pprev"], w=["AR"])
                P.op("dve", lambda e: e.tensor_tensor(out=AR[:, :, :, 1, :], in0=v4(Xr), in1=v4(pp), op=ALU.mult),
                     r=["Xr", "pp"], w=["AR"])
                P.op("dve", lambda e: e.tensor_tensor(out=BT[:], in0=kk[:], in1=aa[:], op=ALU.mult), r=["kk", "aa"], w=["BT"])
                P.op("dve", lambda e: e.tensor_tensor(out=BT[:], in0=BT[:], in1=pinv[:], op=ALU.mult), r=["BT", "pinv"], w=["BT"])
                P.op("dve", lambda e: e.tensor_tensor(out=KT[:], in0=kp[:], in1=pinv[:], op=ALU.mult), r=["kp", "pinv"], w=["KT"])
                P.op("dve", lambda e: e.tensor_tensor(out=RK[:], in0=Xr[:], in1=kp[:], op=ALU.mult), r=["Xr", "kp"], w=["RK"])
                P.op("dve", lambda e: e.tensor_tensor(out=RK[:], in0=RK[:], in1=bc(cpar[:, :, 4], h0), op=ALU.mult),
                     r=["RK", "cpar"], w=["RK"])
                for g in range(G):
                    for c2 in range(4):
                        pb, pk = nb()
                        pv = pb[:, :].rearrange("p (c j k) -> p c j k", c=2, j=4)
                        for cc in range(2):
                            c = c2 * 2 + cc
                            csl = slice(c * 64, (c + 1) * 64)
                            srcs = [(BT[:, g, csl], "BT"), (KT[:, g, csl], "KT"), (Xv[:, g, csl], "Xv"), (AR[:, g, c, 0, :], "AR")]
                            for j, (sap, skey) in enumerate(srcs):
                                P.op("pe", lambda e: e.transpose(out=pv[:, cc, j, :], in_=sap, identity=idf),
                                     r=[skey, "identf"], w=[pk], sig=(cc == 1 and j == 3))
                        P.op("act", lambda e: e.activation(out=TK[:, g, c2 * 2:c2 * 2 + 2, 0:3, :], in_=pv[:, :, 0:3, :], func=AF.Copy),
                             r=[pk], w=["TK"])
                        P.op("act", lambda e: e.activation(out=TK[:, g, c2 * 2:c2 * 2 + 2, 4, :], in_=pv[:, :, 3, :], func=AF.Copy),
                             r=[pk], w=["TK"])
                for g in range(G):
                    for c in range(8):
                        gc = g * 8 + c
                        csl = slice(c * 64, (c + 1) * 64)
                        pb, pk = nb()
                        arv = AR[:, g, c, :, :].rearrange("p a k -> p (a k)")
                        P.op("pe", lambda e: e.matmul(pb[:, 0:128], lhsT=BT[:, g, csl], rhs=arv, start=True, stop=True,
                                                      skip_group_check=True), r=["BT", "AR"], w=[pk], sig=False)
                        P.op("pe", lambda e: e.matmul(pb[:, 128:256], lhsT=KT[:, g, csl], rhs=arv, start=True, stop=True,
                                                      skip_group_check=True), r=["KT", "AR"], w=[pk], sig=False)
                        P.op("pe", lambda e: e.matmul(pb[:, 256:320], lhsT=AR[:, g, c, 0, :], rhs=BT[:, g, csl], start=True, stop=True,
                                                      skip_group_check=True), r=["BT", "AR"], w=[pk])
                        P.op("dve", lambda e: e.tensor_tensor(out=GM[:, gc, :], in0=pb[:, 0:320], in1=m320[:], op=ALU.mult),
                             r=[pk, "m320"], w=["GM"])
                P.op("dve", lambda e: e.tensor_tensor(out=TT[:], in0=GM[:, :, 0:64], in1=idf.unsqueeze(1).to_broadcast([64, GC, 64]),
                                                      op=ALU.add), r=["GM", "identf"], w=["TT"])
                for lev in range(5):
                    last = lev == 4
                    src = XY[(lev + 1) % 2]
                    dst = XY[lev % 2]
                    skey, dkey = "XY%d" % ((lev + 1) % 2), "XY%d" % (lev % 2)
                    for b4 in range(GC // 4):
                        pb, pk = nb()
                        pv = pb[:, :].rearrange("p (q k) -> p q k", q=4)
                        for q in range(4):
                            gc = b4 * 4 + q
                            if lev == 0:
                                Xa, Ya, rk_ = GM[:, gc, 256:320], GM[:, gc, 0:64], ["GM"]
                            else:
                                Xa, Ya, rk_ = src[:, gc, 0:64], src[:, gc, 64:128], [skey]
                            P.op("pe", lambda e: e.matmul(pv[:, q, 0:64], lhsT=Ya, rhs=Xa, start=True, stop=True, skip_group_check=True),
                                 r=rk_, w=[pk], sig=(last and q == 3))
                            if not last:
                                P.op("pe", lambda e: e.matmul(pv[:, q, 64:128], lhsT=Xa, rhs=Ya, start=True, stop=True,
                                                              skip_group_check=True), r=rk_, w=[pk], sig=(q == 3))
                        if last:
                            P.op("act", lambda e: e.activation(out=dst[:, b4 * 4:b4 * 4 + 4, 0:64], in_=pv[:, :, 0:64], func=AF.Copy),
                                 r=[pk], w=[dkey])
                        else:
                            P.op("act", lambda e: e.activation(out=dst[:, b4 * 4:b4 * 4 + 4, :], in_=pv[:, :, :], func=AF.Copy),
                                 r=[pk], w=[dkey])
                        pb2, pk2 = nb()
                        pv2 = pb2[:, 0:256].rearrange("p (q k) -> p q k", q=4)
                        for q in range(4):
                            gc = b4 * 4 + q
                            P.op("pe", lambda e: e.matmul(pv2[:, q, :], lhsT=dst[:, gc, 0:64], rhs=TT[:, gc, :], start=True, stop=True,
                                                          skip_group_check=True), r=[dkey, "TT"], w=[pk2], sig=(q == 3))
                        P.op("dve", lambda e: e.tensor_tensor(out=TT[:, b4 * 4:b4 * 4 + 4, :], in0=TT[:, b4 * 4:b4 * 4 + 4, :],
                                                              in1=pv2[:, :, :], op=ALU.add), r=["TT", pk2], w=["TT"])
                for g in range(G):
                    for c4 in range(2):
                        pb, pk = nb()
                        pv = pb[:, 0:256].rearrange("p (q k) -> p q k", q=4)
                        for q in range(4):
                            c = c4 * 4 + q
                            gc = g * 8 + c
                            P.op("pe", lambda e: e.matmul(pv[:, q, :], lhsT=GM[:, gc, 128:192], rhs=TK[:, g, c, 2, :], start=True, stop=True,
                                                          skip_group_check=True), r=["GM", "TK"], w=[pk], sig=(q == 3))
                        P.op("act", lambda e: e.activation(out=TK[:, g, c4 * 4:c4 * 4 + 4, 3, :], in_=pv[:, :, :], func=AF.Copy),
                             r=[pk], w=["TK"])
                for g in range(G):
                    for c4 in range(2):
                        pb, pk = nb()
                        pv = pb[:, :].rearrange("p (q k) -> p q k", q=4)
                        for q in range(4):
                            c = c4 * 4 + q
                            gc = g * 8 + c
                            P.op("pe", lambda e: e.matmul(pv[:, q, :], lhsT=TT[:, gc, :], rhs=TK[:, g, c, 3:5, :].rearrange("p a k -> p (a k)"),
                                                          start=True, stop=True, skip_group_check=True), r=["TT", "TK"], w=[pk], sig=(q == 3))
                        P.op("act", lambda e: e.activation(out=GA[:, g * 8 + c4 * 4:g * 8 + c4 * 4 + 4, :], in_=pv[:, :, :], func=AF.Copy),
                             r=[pk], w=["GA"])
                pC = pp[:].rearrange("p g (c k) -> p g c k", k=64)[:, :, :, 63:64]
                for g in range(G):
                    for c4 in range(2):
                        pbO, pkO = nb()
                        pbR, pkR = nb()
                        pbG, pkG = nb()
                        pvO = pbO[:, 0:256].rearrange("p (q k) -> p q k", q=4)
                        pvR = pbR[:, :].rearrange("p (q a k) -> p q a k", q=4, a=2)
                        pvG = pbG[:, 0:256].rearrange("p (q k) -> p q k", q=4)
                        for q in range(4):
                            c = c4 * 4 + q
                            gc = g * 8 + c
                            P.op("pe", lambda e: e.matmul(pvO[:, q, :], lhsT=GA[:, gc, 0:64], rhs=GM[:, gc, 64:128], start=True, stop=False,
                                                          skip_group_check=True), r=["GA", "GM"], w=[pkO], sig=False)
                            P.op("pe", lambda e: e.matmul(pvO[:, q, :], lhsT=TK[:, g, c, 2, :], rhs=GM[:, gc, 192:256], start=False, stop=True,
                                                          skip_group_check=True), r=["TK", "GM"], w=[pkO], sig=(q == 3))
                            P.op("pe", lambda e: e.matmul(pvR[:, q, 0, :], lhsT=GA[:, gc, 64:128], rhs=GM[:, gc, 64:128], start=True, stop=True,
                                                          skip_group_check=True), r=["GA", "GM"], w=[pkR], sig=False)
                            P.op("pe", lambda e: e.matmul(pvR[:, q, 1, :], lhsT=GA[:, gc, 64:128], rhs=TK[:, g, c, 0, :], start=True, stop=True,
                                                          skip_group_check=True), r=["GA", "TK"], w=[pkR], sig=(q == 3))
                            P.op("pe", lambda e: e.matmul(pvG[:, q, :], lhsT=TK[:, g, c, 0, :], rhs=GA[:, gc, 0:64], start=True, stop=False,
                                                          skip_group_check=True), r=["GA", "TK"], w=[pkG], sig=False)
                            P.op("pe", lambda e: e.matmul(pvG[:, q, :], lhsT=TK[:, g, c, 1, :], rhs=TK[:, g, c, 2, :], start=False, stop=True,
                                                          skip_group_check=True), r=["TK"], w=[pkG], sig=(q == 3))
                        gs = slice(g * 8 + c4 * 4, g * 8 + c4 * 4 + 4)
                        cs4 = slice(c4 * 4, c4 * 4 + 4)
                        P.op("act", lambda e: e.activation(out=OL[:, gs, :], in_=pvO[:, :, :], func=AF.Copy), r=[pkO], w=["OL"])
                        P.op("dve", lambda e: e.tensor_tensor(out=RP[:, gs, :], in0=pvR[:, :, 0, :], in1=AR[:, g, cs4, 1, :], op=ALU.add),
                             r=[pkR, "AR"], w=["RP"])
                        P.op("dve", lambda e: e.tensor_tensor(out=PH[:, gs, :], in0=pvR[:, :, 1, :],
                                                              in1=idf.unsqueeze(1).to_broadcast([64, 4, 64]), op=ALU.add),
                             r=[pkR, "identf"], w=["PH"])
                        P.op("dve", lambda e: e.tensor_tensor(out=GP[:, gs, :], in0=pvG[:, :, :],
                                                              in1=pC[:, g, cs4, :].to_broadcast([64, 4, 64]), op=ALU.mult),
                             r=[pkG, "pp"], w=["GP"])
                for c in range(8):
                    pbH, pkH = nb()
                    pbO, pkO = nb()
                    pvH = pbH[:, 0:G * 64].rearrange("p (g k) -> p g k", g=G)
                    pvO = pbO[:, 0:G * 64].rearrange("p (g k) -> p g k", g=G)
                    for g in range(G):
                        gc = g * 8 + c
                        P.op("pe", lambda e: e.matmul(pvH[:, g, :], lhsT=PH[:, gc, :], rhs=H[:, h0 + g, :], start=True, stop=True,
                                                      skip_group_check=True), r=["PH", "H"], w=[pkH], sig=(g == G - 1))
                    for g in range(G):
                        gc = g * 8 + c
                        P.op("pe", lambda e: e.matmul(pvO[:, g, :], lhsT=H[:, h0 + g, :], rhs=RP[:, gc, :], start=True, stop=True,
                                                      skip_group_check=True), r=["RP", "H"], w=[pkO], sig=(g == G - 1))
                    gcs = GP[:].rearrange("p (g c) k -> p g c k", g=G)[:, :, c, :]
                    ols = OL[:].rearrange("p (g c) k -> p g c k", g=G)[:, :, c, :]
                    P.op("dve", lambda e: e.tensor_tensor(out=OT[:, :, c * 64:(c + 1) * 64], in0=pvO[:, :, :], in1=ols, op=ALU.add),
                         r=[pkO, "OL"], w=["OT"])
                    P.op("dve", lambda e: e.tensor_tensor(out=H[:, h0:h0 + G, :], in0=pvH[:, :, :],
                                                          in1=pC[:, :, c, :].to_broadcast([64, G, 64]), op=ALU.mult),
                         r=[pkH, "pp", "H"], w=["H"])
                    P.op("dve", lambda e: e.tensor_tensor(out=H[:, h0:h0 + G, :], in0=H[:, h0:h0 + G, :], in1=gcs, op=ALU.add),
                         r=["H", "GP"], w=["H"])
                for g in range(G):
                    pb, pk = nb()
                    P.op("pe", lambda e: e.matmul(pb[:, :], lhsT=onesm[:, :], rhs=OT[:, g, :], start=True, stop=True),
                         r=["onesm", "OT"], w=[pk])
                    P.op("dve", lambda e: e.tensor_tensor(out=OT[:, g, :], in0=OT[:, g, :], in1=pb[:, :], op=ALU.subtract),
                         r=[pk, "OT"], w=["OT"])
                P.op("act", lambda e: e.activation(out=tmp[:], in_=OT[:], func=AF.Square), r=["OT"], w=["tmp"])
                for g in range(G):
                    pb, pk = nb()
                    P.op("pe", lambda e: e.matmul(pb[:, :], lhsT=onesm[:, :], rhs=tmp[:, g, :], start=True, stop=True),
                         r=["onesm", "tmp"], w=[pk])
                    P.op("dve", lambda e: e.tensor_scalar(out=cs[:, g, :], in0=pb[:, :], scalar1=RW_LN_EPS, scalar2=None, op0=ALU.add),
                         r=[pk], w=["cs"])
                P.op("act", lambda e: e.activation(out=cs[:], in_=cs[:], func=AF.Ln), r=["cs"], w=["cs"])
                P.op("act", lambda e: e.activation(out=cs[:], in_=cs[:], func=AF.Exp, scale=-0.5), r=["cs"], w=["cs"])
                P.op("dve", lambda e: e.tensor_tensor(out=OT[:], in0=OT[:], in1=cs[:], op=ALU.mult), r=["OT", "cs"], w=["OT"])
                P.op("dve", lambda e: e.tensor_tensor(out=OT[:], in0=OT[:], in1=bc(cpar[:, :, 5], h0), op=ALU.mult),
                     r=["OT", "cpar"], w=["OT"])
                P.op("dve", lambda e: e.tensor_tensor(out=OT[:], in0=OT[:], in1=bc(cpar[:, :, 6], h0), op=ALU.add),
                     r=["OT", "cpar"], w=["OT"])
                for g in range(G):
                    pb, pk = nb()
                    P.op("pe", lambda e: e.matmul(pb[:, :], lhsT=ones64[:, :], rhs=RK[:, g, :], start=True, stop=True),
                         r=["ones64", "RK"], w=[pk])
                    P.op("dve", lambda e: e.tensor_tensor(out=tmp[:, g, :], in0=pb[:, :], in1=Xv[:, g, :], op=ALU.mult),
                         r=[pk, "Xv"], w=["tmp"])
                P.op("dve", lambda e: e.tensor_tensor(out=OT[:], in0=OT[:], in1=tmp[:], op=ALU.add), r=["OT", "tmp"], w=["OT"])
                P.op("dve", lambda e: e.tensor_tensor(out=yst[:], in0=OT[:], in1=gate[:], op=ALU.mult), r=["OT", "gate"], w=["yst"])
                P.dma("pool", d["yT"][2048 + h0 * 64:2048 + (h0 + G) * 64, t0:t0 + N].rearrange("(g k) t -> k g t", k=64), yst[:, :, :],
                      r=["yst"], sem="d_ystc")


def phase_mergeA(P, T, l, io, d):
    NT = T // 512
    with P.phase() as ph:
        W = [ph.sb("W%d" % n, [128, 8, 2048], BF16) for n in range(4)]
        yT = ph.sb("yT", [128, 4, 8, 512], BF16)
        gts = [ph.sb("gts%d" % i, [128, 4, 512], BF16) for i in range(2)]
        acc = ph.sb("acc", [128, 512], F32)
        tmp = ph.sb("tmp", [128, 512], F32)
        mst = [ph.sb("mst%d" % i, [128, 512], BF16) for i in range(2)]
        banks = [ph.ps("bk%d" % i, [128, 512]) for i in range(8)]
        for n in range(4):
            P.dma("pool", W[n][:, :, :], io["w_branch"][l, n].rearrange("(kc p) c -> p kc c", p=128), w=["W%d" % n], sem="d_W%d" % n)
        mg = d["MgT"].rearrange("(n c p) t -> p n c t", n=4, p=128)
        yv = d["yT"].rearrange("(n kc p) t -> p n kc t", n=4, p=128)
        bi = 0
        gi = 0
        for tt in range(NT):
            ts = slice(tt * 512, (tt + 1) * 512)
            for n in range(4):
                P.dma("sp", yT[:, n, :, :], yv[:, n, :, ts], w=["yT"], sem="d_yTm")
            for cc in range(16):
                g_ = gi % 2
                gi += 1
                P.dma("sp", gts[g_][:, :, :], mg[:, :, cc, ts], w=["gts%d" % g_], sem="d_gts%d" % g_)
                for n in range(4):
                    b = bi % 8
                    bi += 1
                    for kc in range(8):
                        P.op("pe", lambda e: e.matmul(banks[b][:, :], lhsT=W[n][:, kc, cc * 128:(cc + 1) * 128], rhs=yT[:, n, kc, :],
                                                      start=(kc == 0), stop=(kc == 7)), r=["W%d" % n, "yT"], w=["bk%d" % b], sig=(kc == 7))
                    if n == 0:
                        P.op("dve", lambda e: e.tensor_tensor(out=acc[:], in0=banks[b][:, :], in1=gts[g_][:, 0, :], op=ALU.mult),
                             r=["bk%d" % b, "gts%d" % g_], w=["acc"])
                    else:
                        P.op("dve", lambda e: e.tensor_tensor(out=tmp[:], in0=banks[b][:, :], in1=gts[g_][:, n, :], op=ALU.mult),
                             r=["bk%d" % b, "gts%d" % g_], w=["tmp"])
                        if n < 3:
                            P.op("pool", lambda e: e.tensor_tensor(out=acc[:], in0=acc[:], in1=tmp[:], op=ALU.add), r=["acc", "tmp"], w=["acc"])
                        else:
                            P.op("pool", lambda e: e.tensor_tensor(out=mst[g_][:], in0=acc[:], in1=tmp[:], op=ALU.add),
                                 r=["acc", "tmp"], w=["mst%d" % g_])
                P.dma("sp", d["mT"][cc * 128:(cc + 1) * 128, ts], mst[g_][:, :], r=["mst%d" % g_], sem="d_mst%d" % g_)


def phase_mergeB(P, T, l, io, d, final_out=None):
    NT = T // 512
    with P.phase() as ph:
        wo = ph.sb("wo", [128, KC, 2048], BF16)
        mT = [ph.sb("mT%d" % i, [128, KC, 512], BF16) for i in range(2)]
        ht = [ph.sb("ht%d" % i, [128, D_MODEL], F32) for i in range(2)]
        xx = [ph.sb("xx%d" % i, [128, D_MODEL], F32) for i in range(2)]
        junk = ph.sb("junk", [128, D_MODEL], F32)
        lg = ph.sb("lg", [128, D_MODEL], F32)
        lbt = ph.sb("lbt", [128, D_MODEL], F32)
        stt = [ph.sb("stt%d" % i, [128, 8], F32) for i in range(2)]
        hTs = [ph.sb("hTs%d" % i, [128, KC, 512], BF16) for i in range(2)]
        ident = make_ident(P, ph)
        banks = [ph.ps("bk%d" % i, [128, 512]) for i in range(4)]
        tbanks = [ph.ps("tb%d" % i, [128, 512]) for i in range(4)]
        ctr = [0]
        P.dma("pool", wo[:, :, :], io["w_out"][l].rearrange("(kc p) c -> p kc c", p=128), w=["wo"], sem="d_wo")
        P.dma("sp", lg[:, :], io["ln_g_b"][l], w=["lg"], sem="d_lg")
        P.dma("sp", lbt[:, :], io["ln_b_b"][l], w=["lbt"], sem="d_lbt")
        mv = d["mT"].rearrange("(kc p) t -> p kc t", p=128)
        for tt in range(NT):
            mi = tt % 2
            P.dma("sp", mT[mi][:, :, :], mv[:, :, tt * 512:(tt + 1) * 512], w=["mT%d" % mi], sem="d_mT%d" % mi)
            for q in range(4):
                tk = tt * 4 + q
                xi = tk % 2
                rows = slice(tk * 128, (tk + 1) * 128)
                P.dma("sp", ht[xi][:, :], d["h"][rows, :], w=["ht%d" % xi], sem="d_ht%d" % xi)
                for nb_ in range(4):
                    for cc in range(KC):
                        P.op("pe", lambda e: e.matmul(banks[nb_][:, :], lhsT=mT[mi][:, cc, q * 128:(q + 1) * 128],
                                                      rhs=wo[:, cc, nb_ * 512:(nb_ + 1) * 512], start=(cc == 0), stop=(cc == KC - 1)),
                             r=["mT%d" % mi, "wo"], w=["bk%d" % nb_], sig=(cc == KC - 1))
                    P.op("dve", lambda e: e.scalar_tensor_tensor(out=xx[xi][:, nb_ * 512:(nb_ + 1) * 512],
                                                                 in0=ht[xi][:, nb_ * 512:(nb_ + 1) * 512], scalar=ALPHA,
                                                                 in1=banks[nb_][:, :], op0=ALU.mult, op1=ALU.add),
                         r=["ht%d" % xi, "bk%d" % nb_], w=["xx%d" % xi])
                s_ = stt[xi]
                sk = "stt%d" % xi
                P.op("act", lambda e: e.activation(out=junk[:], in_=xx[xi][:], func=AF.Copy, accum_out=s_[:, 0:1]), r=["xx%d" % xi], w=["junk", sk])
                P.op("act", lambda e: e.activation(out=junk[:], in_=xx[xi][:], func=AF.Square, accum_out=s_[:, 1:2]), r=["xx%d" % xi], w=["junk", sk])
                P.op("dve", lambda e: e.tensor_scalar(out=s_[:, 2:4], in0=s_[:, 0:2], scalar1=1.0 / D_MODEL, scalar2=None, op0=ALU.mult),
                     r=[sk], w=[sk])
                P.op("dve", lambda e: e.tensor_tensor(out=s_[:, 4:5], in0=s_[:, 2:3], in1=s_[:, 2:3], op=ALU.mult), r=[sk], w=[sk])
                P.op("dve", lambda e: e.tensor_tensor(out=s_[:, 5:6], in0=s_[:, 3:4], in1=s_[:, 4:5], op=ALU.subtract), r=[sk], w=[sk])
                P.op("dve", lambda e: e.tensor_scalar(out=s_[:, 5:6], in0=s_[:, 5:6], scalar1=LN_EPS, scalar2=None, op0=ALU.add), r=[sk], w=[sk])
                P.op("act", lambda e: e.activation(out=s_[:, 6:7], in_=s_[:, 5:6], func=AF.Ln), r=[sk], w=[sk])
                P.op("act", lambda e: e.activation(out=s_[:, 7:8], in_=s_[:, 6:7], func=AF.Exp, scale=-0.5), r=[sk], w=[sk])
                P.op("dve", lambda e: e.tensor_scalar(out=xx[xi][:], in0=xx[xi][:], scalar1=s_[:, 2:3], scalar2=s_[:, 7:8],
                                                      op0=ALU.subtract, op1=ALU.mult), r=["xx%d" % xi, sk], w=["xx%d" % xi])
                P.op("pool", lambda e: e.tensor_tensor(out=xx[xi][:], in0=xx[xi][:], in1=lg[:], op=ALU.mult), r=["xx%d" % xi, "lg"], w=["xx%d" % xi])
                P.op("dve", lambda e: e.tensor_tensor(out=xx[xi][:], in0=xx[xi][:], in1=lbt[:], op=ALU.add), r=["xx%d" % xi, "lbt"], w=["xx%d" % xi])
                dst = final_out if final_out is not None else d["h"]
                P.dma("pool", dst[rows, :], xx[xi][:, :], r=["xx%d" % xi], w=[], sem="d_xo%d" % xi)
                if final_out is None:
                    emit_transpose_tile(P, xx[xi], "xx%d" % xi, ident, tbanks, ctr, hTs[mi], "hTs%d" % mi, q * 128)
            if final_out is None:
                P.dma("sp", d["hT"].rearrange("(kc p) t -> p kc t", p=128)[:, :, tt * 512:(tt + 1) * 512], hTs[mi][:, :, :],
                      r=["hTs%d" % mi], sem="d_hTs%d" % mi)


IO_SPECS = lambda T: {
    "x": ([T, D_MODEL], F32),
    "w_in": ([DEPTH, D_MODEL, IN_COLS], F32),
    "rw_mu_t": ([DEPTH, 128, 25], F32),
    "rw_vmu_t": ([DEPTH - 1, 128, 1], F32),
    "rw_v1": ([DEPTH - 1, D_MODEL, 32], F32),
    "a_strip": ([8, 128, 640], F32),
    "a_c31": ([128, 8], F32),
    "da_lambda_b": ([DEPTH, 128, 256], F32),
    "da_subln_t": ([DEPTH, 128, 1], F32),
    "hg_lower_t": ([128, 8, 4], F32),
    "hg_norm_t": ([DEPTH, 128, 1], F32),
    "c_par": ([DEPTH, 64, 16, 8], F32),
    "rw_w2": ([DEPTH, 64, 1024], F32),
    "rw_a2": ([DEPTH, 64, 1024], F32),
    "rw_v2": ([DEPTH - 1, 32, 1024], F32),
    "w_branch": ([DEPTH, 4, 1024, D_MODEL], F32),
    "w_out": ([DEPTH, D_MODEL, D_MODEL], F32),
    "ln_g_b": ([DEPTH, 128, D_MODEL], F32),
    "ln_b_b": ([DEPTH, 128, D_MODEL], F32),
}


def build_program(T, depth=DEPTH, debug_outs=()):
    nc = bass.Bass("TRN2", target_bir_lowering=False)
    io = {k: nc.dram_tensor(k, list(s), dt, kind="ExternalInput").ap() for k, (s, dt) in IO_SPECS(T).items()}
    out = nc.dram_tensor("out", [T, D_MODEL], F32, kind="ExternalOutput").ap()
    d = declare_dram(nc, T, debug_outs=debug_outs)
    P = Prog(nc)
    phase_prep0(P, T, io, d)
    for l in range(depth):
        phase_gemm(P, T, l, io, d)
        phase_mixA(P, T, l, io, d)
        phase_mixB(P, T, l, io, d)
        phase_mixC(P, T, l, io, d)
        phase_mixD(P, T, l, io, d)
        phase_mergeA(P, T, l, io, d)
        phase_mergeB(P, T, l, io, d, final_out=(out if l == depth - 1 else None))
    P.barrier()
    return nc, P


def host_shared_inputs(inp):
    f = lambda a: np.ascontiguousarray(np.asarray(a, dtype=np.float32))
    L = DEPTH
    sh = {}
    sh["w_in"] = f(inp["w_in"])
    sh["rw_mu_t"] = f(np.asarray(inp["rw_mu"]).reshape(L, 25, 128).transpose(0, 2, 1))
    vmu = np.zeros((L - 1, 128, 1), np.float32)
    vmu[:, :32, 0] = np.asarray(inp["rw_v_mu"])
    sh["rw_vmu_t"] = vmu
    sh["rw_v1"] = f(inp["rw_v1"])
    rel = np.asarray(inp["rel_bias"], dtype=np.float32)
    sh["a_strip"] = host_a_strip(rel)
    sh["a_c31"] = f(np.broadcast_to(rel[31][None, :], (128, 8)))
    sh["da_lambda_b"] = f(np.broadcast_to(np.asarray(inp["da_lambda"]).reshape(L, 1, 256), (L, 128, 256)))
    sh["da_subln_t"] = f(np.asarray(inp["da_subln"]).reshape(L, 128, 1))
    sh["hg_lower_t"] = f(np.asarray(inp["hg_lower"]).reshape(L, 8, 128).transpose(2, 1, 0))
    sh["hg_norm_t"] = f(np.asarray(inp["hg_norm"]).reshape(L, 128, 1))
    hk = lambda a: np.asarray(a).reshape(L, 16, 64).transpose(0, 2, 1)
    v0 = np.zeros((L, 1024), np.float32)
    v0[1:] = np.asarray(inp["rw_v0"])
    cp = np.stack([hk(inp["rw_w0"]), hk(inp["rw_a0"]), hk(inp["rw_kk"]), hk(inp["rw_ka"]),
                   np.asarray(inp["rw_rk"]).transpose(0, 2, 1), hk(inp["rw_lnx_g"]), hk(inp["rw_lnx_b"]), hk(v0)], axis=-1)
    sh["c_par"] = f(cp)
    sh["rw_w2"] = f(inp["rw_w2"])
    sh["rw_a2"] = f(inp["rw_a2"])
    sh["rw_v2"] = f(inp["rw_v2"])
    sh["w_branch"] = f(inp["w_branch"])
    sh["w_out"] = f(inp["w_out"])
    sh["ln_g_b"] = f(np.broadcast_to(np.asarray(inp["ln_g"])[:, None, :], (L, 128, D_MODEL)))
    sh["ln_b_b"] = f(np.broadcast_to(np.asarray(inp["ln_b"])[:, None, :], (L, 128, D_MODEL)))
    return sh


_CACHE = {}


def kernel(**inputs):
    x = np.asarray(inputs["x"], dtype=np.float32)
    B, T, _ = x.shape
    if T not in _CACHE:
        _CACHE[T] = build_program(T)
    nc, _ = _CACHE[T]
    sh = host_shared_inputs(inputs)
    in_maps = []
    for b in range(B):
        m = dict(sh)
        m["x"] = np.ascontiguousarray(x[b])
        in_maps.append(m)
    res = run_bass_kernel_spmd(nc, in_maps, core_ids=list(range(B)))
    return np.stack([np.asarray(res.results[b]["out"], dtype=np.float32) for b in range(B)], axis=0)
```

```python
import contextlib
import math
import numpy as np
import concourse.bass as bass
import concourse.mybir as mybir
from concourse.bass_utils import run_bass_kernel_spmd

F32 = mybir.dt.float32
BF16 = mybir.dt.bfloat16
AF = mybir.ActivationFunctionType
ALU = mybir.AluOpType
AX = mybir.AxisListType

D_MODEL = 2048
DEPTH = 4
MIXW = 1024
IN_COLS = 24704
KC = D_MODEL // 128
ALPHA = (2 * DEPTH) ** 0.25
LN_EPS = 1e-5
RMS_EPS = 1e-6
RW_LN_EPS = 64e-5
NEG = -30000.0

CFG = dict(SPLIT=1, YB=1024, NHA=8, NHC=16, GROUPS=None)


def configure(split, groups=None):
    CFG.update(SPLIT=split, YB=1024 // split, NHA=8 // split, NHC=16 // split, GROUPS=groups)


O_AQ, O_AK, O_AV, O_AG = 0, 1024, 2048, 3072
O_HQ, O_HF, O_HI, O_HG = 4096, 5120, 6144, 7168
O_RM, O_RG = 8192, 11392
O_SQ, O_SK, O_SV, O_SG = 12416, 13440, 14464, 15488
O_MG = 16512


class Prog:
    ENG = ("pe", "act", "dve", "pool", "sp")

    def __init__(self, nc):
        self.nc = nc
        self.E = dict(pe=nc.tensor, act=nc.scalar, dve=nc.vector, pool=nc.gpsimd, sp=nc.sync)
        self.sems = {}
        self.semval = {}
        self.known = {e: {} for e in self.ENG}
        self.res = {}
        self.pend = {e: ([], []) for e in self.ENG}
        self.stack = contextlib.ExitStack()
        self.n_inst = 0
        self.uid = 0
        self.ecount = {e: 0 for e in self.ENG}
        self.marks = []

    def sem(self, key):
        if key not in self.sems:
            self.sems[key] = self.stack.enter_context(self.nc.semaphore("s_" + key))
            self.semval[key] = 0
        return self.sems[key]

    def _res(self, k):
        r = self.res.get(k)
        if r is None:
            r = [None, {}]
            self.res[k] = r
        return r

    def _wait(self, eng, tok):
        if tok is None:
            return
        sk, v = tok
        if eng == "pe" and sk == "pe":
            return
        if self.known[eng].get(sk, 0) >= v:
            return
        self.E[eng].wait_ge(self.sems[sk], v)
        self.known[eng][sk] = v
        self.n_inst += 1

    def _deps(self, eng, r, w):
        for k in r:
            self._wait(eng, self._res(k)[0])
        for k in w:
            rr = self._res(k)
            self._wait(eng, rr[0])
            for sk, v in list(rr[1].items()):
                self._wait(eng, (sk, v))

    def op(self, eng, fn, r=(), w=(), sig=True):
        self._deps(eng, r, w)
        inst = fn(self.E[eng])
        self.n_inst += 1
        self.ecount[eng] += 1
        pr, pw = self.pend[eng]
        pr.extend(r)
        pw.extend(w)
        if not sig:
            return inst
        self.sem(eng)
        self.semval[eng] += 1
        inst.then_inc(self.sems[eng], 1)
        tok = (eng, self.semval[eng])
        for k in pr:
            rr = self._res(k)
            rr[1][eng] = tok[1]
        for k in pw:
            rr = self._res(k)
            rr[0] = tok
            rr[1] = {}
        pr.clear()
        pw.clear()
        return inst

    def dma(self, q, out, in_, r=(), w=(), sem=None):
        assert sem is not None
        self._deps(q, r, w)
        self.sem(sem)
        inst = self.E[q].dma_start(out=out, in_=in_)
        self.n_inst += 1
        self.semval[sem] += 16
        inst.then_inc(self.sems[sem], 16)
        tok = (sem, self.semval[sem])
        for k in r:
            self._res(k)[1][sem] = tok[1]
        for k in w:
            rr = self._res(k)
            rr[0] = tok
            rr[1] = {}
        return inst

    def coll(self, kind, in_ap, out_ap):
        self.sem("cc")
        alu = ALU.add if kind == "ReduceScatter" else ALU.bypass
        inst = self.E["pool"].collective_compute(kind, alu, replica_groups=CFG["GROUPS"], ins=[in_ap.opt()], outs=[out_ap.opt()])
        self.n_inst += 1
        self.semval["cc"] += 1
        inst.then_inc(self.sems["cc"], 1)
        self._wait("pool", ("cc", self.semval["cc"]))

    def barrier(self):
        for e in self.ENG:
            assert not self.pend[e][0] and not self.pend[e][1], "pending unsignalled ops at barrier"
        for e in self.ENG:
            for sk, v in self.semval.items():
                if v > 0:
                    self._wait(e, (sk, v))
        self.res = {}

    @contextlib.contextmanager
    def phase(self, name=""):
        self.barrier()
        self.marks.append((name, dict(self.ecount)))
        st = contextlib.ExitStack()
        ph = Phase(self, st)
        try:
            yield ph
        finally:
            self.barrier()
            st.close()


class Phase:
    def __init__(self, prog, st):
        self.p = prog
        self.st = st

    def sb(self, name, shape, dt):
        self.p.uid += 1
        return self.st.enter_context(self.p.nc.sbuf_tensor("%s_%d" % (name, self.p.uid), list(shape), dt))

    def ps(self, name, shape, dt=F32):
        self.p.uid += 1
        return self.st.enter_context(self.p.nc.psum_tensor("%s_%d" % (name, self.p.uid), list(shape), dt))


def declare_dram(nc, T, debug_outs=(), debug_ins=()):
    d = {}

    def t(name, shape, dt, kind="Internal"):
        if name in debug_outs:
            kind = "ExternalOutput"
        if name in debug_ins:
            kind = "ExternalInput"
        d[name] = nc.dram_tensor(name, list(shape), dt, kind=kind).ap()

    YB = CFG["YB"]
    SP = CFG["SPLIT"]
    TO = T // SP
    t("hT", [D_MODEL, TO], BF16)
    t("h", [TO, D_MODEL], F32)
    if SP > 1:
        NCH = max(1, (D_MODEL * TO * 2) // (2 << 20))
        d["NCH"] = NCH
        t("hTg", [NCH, SP, D_MODEL // NCH, TO], BF16)
        t("mTp", [SP, D_MODEL, TO], F32)
        t("mTs", [D_MODEL, TO], F32)
    t("AqT", [YB, T], BF16)
    t("AkT", [YB, T], BF16)
    t("Av", [T, YB], BF16)
    t("AgT", [YB, T], BF16)
    t("BqT", [YB, T], BF16)
    t("BfT", [YB, T], F32)
    t("Bi", [T, YB], BF16)
    t("BgT", [YB, T], BF16)
    t("CmT", [3 * YB + 128, T], F32)
    t("CgT", [YB, T], BF16)
    t("CvdT", [32, T], F32)
    t("Cvf", [YB, T], F32)
    t("DqT", [YB, T], BF16)
    t("DkT", [YB, T], BF16)
    t("Dv", [T, YB], BF16)
    t("DgT", [YB, T], BF16)
    t("MgT", [8192, T], BF16)
    t("yT", [4 * YB, T], BF16)
    t("mT", [D_MODEL, T], BF16)
    return d


def phase_gemm(P, T, l, io, d):
    nc = P.nc
    NT = T // 512
    with P.phase("gemm") as ph:
        hT = ph.sb("hT", [128, KC, T + 1], BF16)
        wsb = [ph.sb("wsb%d" % i, [128, KC, 512], BF16) for i in range(2)]
        SC = min(T, 2048)
        stg = [ph.sb("stg%d" % i, [128, SC], F32) for i in range(2)]
        stb = [ph.sb("stb%d" % i, [128, max(SC, 1024)], BF16) for i in range(2)]
        tmp = [ph.sb("tmp%d" % i, [128, 512], F32) for i in range(2)]
        mu = ph.sb("mu", [128, 26], F32)
        om = ph.sb("om", [128, 26], F32)
        banks = [ph.ps("bk%d" % i, [128, 512]) for i in range(8)]

        P.op("pool", lambda e: e.memset(hT[:, :, 0:1], 0.0), w=["hT"])
        if CFG["SPLIT"] == 1:
            P.dma("sp", hT[:, :, 1:T + 1], d["hT"].rearrange("(kc p) t -> p kc t", p=128), w=["hT"], sem="d_hT")
        else:
            NCH = d["NCH"]
            TO = T // CFG["SPLIT"]
            JJ = KC // NCH
            for r_ in range(CFG["SPLIT"]):
                for i_ in range(NCH):
                    P.dma("sp", hT[:, i_ * JJ:(i_ + 1) * JJ, 1 + r_ * TO:1 + (r_ + 1) * TO],
                          d["hTg"][i_, r_].rearrange("(j p) t -> p j t", p=128), w=["hT"], sem="d_hT")
        NMU = io["rw_mu_t"].shape[2]
        P.op("pool", lambda e: e.memset(mu[:], 0.0), w=["mu"])
        P.dma("sp", mu[:, 0:NMU], io["rw_mu_t"][l], w=["mu"], sem="d_mu")
        if l > 0:
            P.dma("sp", mu[:, 25:26], io["rw_vmu_t"][l - 1], w=["mu"], sem="d_mu")
        else:
            P.op("pool", lambda e: e.memset(mu[:, 25:26], 0.0), w=["mu"])
        P.op("dve", lambda e: e.tensor_scalar(out=om[:], in0=mu[:], scalar1=-1.0, scalar2=1.0,
                                              op0=ALU.mult, op1=ALU.add), r=["mu"], w=["om"])

        w_l = io["w_in"][l].rearrange("(kc p) c -> p kc c", p=128)
        st = dict(wi=0, bi=0, si=0, ti=0)

        def load_w(src_ap, ncols):
            i = st["wi"] % 2
            st["wi"] += 1
            P.dma("pool", wsb[i][:, :, 0:ncols], src_ap, w=["wsb%d" % i], sem="d_wsb%d" % i)
            return i

        def bank():
            b = st["bi"] % 8
            st["bi"] += 1
            return b

        def mm_F(wi, c0, tt, shift, b):
            off = 0 if shift else 1
            for kc in range(KC):
                P.op("pe", lambda e, kc=kc: e.matmul(banks[b][:, :], lhsT=wsb[wi][:, kc, c0:c0 + 128],
                                                     rhs=hT[:, kc, off + tt * 512: off + tt * 512 + 512],
                                                     start=(kc == 0), stop=(kc == KC - 1)),
                     r=["wsb%d" % wi, "hT"], w=["bk%d" % b], sig=(kc == KC - 1))

        def job_F(col0, ncols, dest, drow0, kind, scale=1.0, mucol0=None, out_dt=BF16, nrows=128):
            for s0 in range(0, ncols, 512):
                sw = min(512, ncols - s0)
                wi = load_w(w_l[:, :, col0 + s0: col0 + s0 + sw], sw)
                for c0 in range(0, sw, 128):
                    for tt in range(NT):
                        if (tt * 512) % SC == 0:
                            si = st["si"] % 2
                            st["si"] += 1
                            so = stb[si] if out_dt == BF16 else stg[si]
                            skey = ("stb%d" if out_dt == BF16 else "stg%d") % si
                        b = bank()
                        mm_F(wi, c0, tt, False, b)
                        lo = (tt * 512) % SC
                        osl = so[:, lo:lo + 512]
                        if kind == "copy":
                            P.op("dve", lambda e: e.tensor_copy(out=osl, in_=banks[b][:, :]),
                                 r=["bk%d" % b], w=[skey])
                        elif kind == "scale":
                            P.op("act", lambda e: e.activation(out=osl, in_=banks[b][:, :], func=AF.Copy, scale=scale),
                                 r=["bk%d" % b], w=[skey])
                        elif kind == "silu":
                            P.op("act", lambda e: e.activation(out=osl, in_=banks[b][:, :], func=AF.Silu),
                                 r=["bk%d" % b], w=[skey])
                        elif kind == "sigmoid":
                            P.op("act", lambda e: e.activation(out=osl, in_=banks[b][:, :], func=AF.Sigmoid),
                                 r=["bk%d" % b], w=[skey])
                        elif kind == "shift":
                            b2 = bank()
                            mm_F(wi, c0, tt, True, b2)
                            mc = mucol0 + (s0 + c0) // 128
                            ti = st["ti"] % 2
                            st["ti"] += 1
                            P.op("dve", lambda e: e.tensor_scalar(out=tmp[ti][:], in0=banks[b2][:, :],
                                                                  scalar1=mu[:, mc:mc + 1], scalar2=None, op0=ALU.mult),
                                 r=["bk%d" % b2, "mu"], w=["tmp%d" % ti])
                            P.op("dve", lambda e: e.scalar_tensor_tensor(out=osl, in0=banks[b][:, :], scalar=om[:, mc:mc + 1],
                                                                         in1=tmp[ti][:], op0=ALU.mult, op1=ALU.add),
                                 r=["bk%d" % b, "om", "tmp%d" % ti], w=[skey])
                        else:
                            raise ValueError(kind)
                        if (tt * 512 + 512) % SC == 0:
                            r0 = drow0 + s0 + c0
                            t_lo = tt * 512 + 512 - SC
                            P.dma("sp", dest[r0:r0 + nrows, t_lo:t_lo + SC], so[0:nrows, 0:SC], r=[skey], w=[], sem="d_" + skey)

        def job_T(col0, dest):
            YBl = CFG["YB"]
            wis = []
            for s0 in range(0, YBl, 512):
                wis.append(load_w(w_l[:, :, col0 + s0: col0 + s0 + 512], 512))
            for tk in range(T // 128):
                si = st["si"] % 2
                st["si"] += 1
                for h2 in range(YBl // 512):
                    b = bank()
                    for kc in range(KC):
                        P.op("pe", lambda e, kc=kc: e.matmul(banks[b][:, :], lhsT=hT[:, kc, 1 + tk * 128: 1 + tk * 128 + 128],
                                                             rhs=wsb[wis[h2]][:, kc, :], start=(kc == 0), stop=(kc == KC - 1)),
                             r=["wsb%d" % wis[h2], "hT"], w=["bk%d" % b], sig=(kc == KC - 1))
                    P.op("dve", lambda e: e.tensor_copy(out=stb[si][:, h2 * 512:(h2 + 1) * 512], in_=banks[b][:, :]),
                         r=["bk%d" % b], w=["stb%d" % si])
                P.dma("sp", dest[tk * 128:(tk + 1) * 128, :], stb[si][:, 0:YBl], r=["stb%d" % si], sem="d_stb%d" % si)

        YB = CFG["YB"]
        if CFG["SPLIT"] == 1:
            o = dict(AQ=O_AQ, AK=O_AK, AV=O_AV, AG=O_AG, HQ=O_HQ, HF=O_HF, HI=O_HI, HG=O_HG, RM=O_RM, RG=O_RG,
                     SQ=O_SQ, SK=O_SK, SV=O_SV, SG=O_SG, MG=O_MG)
        else:
            cb = 4 * YB
            cc_ = 2 * cb
            cd = cc_ + 3 * YB + 128 + YB
            o = dict(AQ=0, AK=YB, AV=2 * YB, AG=3 * YB, HQ=cb, HF=cb + YB, HI=cb + 2 * YB, HG=cb + 3 * YB,
                     RM=cc_, RG=cc_ + 3 * YB + 128, SQ=cd, SK=cd + YB, SV=cd + 2 * YB, SG=cd + 3 * YB, MG=cd + 4 * YB)
        job_F(o["AQ"], YB, d["AqT"], 0, "scale", scale=0.125)
        job_F(o["AK"], YB, d["AkT"], 0, "copy")
        job_T(o["AV"], d["Av"])
        job_F(o["AG"], YB, d["AgT"], 0, "silu")
        job_F(o["HQ"], YB, d["BqT"], 0, "copy")
        job_F(o["HF"], YB, d["BfT"], 0, "copy", out_dt=F32)
        job_T(o["HI"], d["Bi"])
        job_F(o["HG"], YB, d["BgT"], 0, "silu")
        job_F(o["RM"], 3 * YB + 128, d["CmT"], 0, "shift", mucol0=0, out_dt=F32)
        job_F(o["RG"], YB, d["CgT"], 0, "silu")
        job_F(o["SQ"], YB, d["DqT"], 0, "scale", scale=128 ** -0.5)
        job_F(o["SK"], YB, d["DkT"], 0, "copy")
        job_T(o["SV"], d["Dv"])
        job_F(o["SG"], YB, d["DgT"], 0, "silu")
        job_F(o["MG"], 8192, d["MgT"], 0, "sigmoid")
        if l > 0:
            i = st["wi"] % 2
            st["wi"] += 1
            P.op("pool", lambda e: e.memset(wsb[i][:, :, 0:128], 0.0), w=["wsb%d" % i])
            P.dma("pool", wsb[i][:, :, 0:32], io["rw_v1"][l - 1].rearrange("(kc p) c -> p kc c", p=128),
                  w=["wsb%d" % i], sem="d_wsb%d" % i)
            for tt in range(NT):
                if (tt * 512) % SC == 0:
                    si = st["si"] % 2
                    st["si"] += 1
                b = bank()
                mm_F(i, 0, tt, False, b)
                b2 = bank()
                mm_F(i, 0, tt, True, b2)
                ti = st["ti"] % 2
                st["ti"] += 1
                lo = (tt * 512) % SC
                osl = stg[si][:, lo:lo + 512]
                P.op("dve", lambda e: e.tensor_scalar(out=tmp[ti][:], in0=banks[b2][:, :], scalar1=mu[:, 25:26],
                                                      scalar2=None, op0=ALU.mult), r=["bk%d" % b2, "mu"], w=["tmp%d" % ti])
                P.op("dve", lambda e: e.scalar_tensor_tensor(out=osl, in0=banks[b][:, :], scalar=om[:, 25:26], in1=tmp[ti][:],
                                                             op0=ALU.mult, op1=ALU.add),
                     r=["bk%d" % b, "om", "tmp%d" % ti], w=["stg%d" % si])
                if (tt * 512 + 512) % SC == 0:
                    t_lo = tt * 512 + 512 - SC
                    P.dma("sp", d["CvdT"][:, t_lo:t_lo + SC], stg[si][0:32, 0:SC], r=["stg%d" % si], sem="d_stg%d" % si)


def emit_transpose_tile(P, src, src_key, ident, banks, bank_ctr, hTs, hTs_key, col0):
    for g in range(4):
        b = bank_ctr[0] % len(banks)
        bank_ctr[0] += 1
        for j in range(4):
            kc = g * 4 + j
            P.op("pe", lambda e: e.transpose(out=banks[b][:, j * 128:(j + 1) * 128], in_=src[:, kc * 128:(kc + 1) * 128],
                                             identity=ident[:, :]),
                 r=[src_key, "ident"], w=["tb%d" % b], sig=(j == 3))
        eng = "act" if g % 2 else "dve"
        if eng == "act":
            P.op("act", lambda e: e.activation(out=hTs[:, g * 4:(g + 1) * 4, col0:col0 + 128],
                                               in_=banks[b][:, :].rearrange("p (j t) -> p j t", j=4), func=AF.Copy),
                 r=["tb%d" % b], w=[hTs_key])
        else:
            P.op("dve", lambda e: e.tensor_copy(out=hTs[:, g * 4:(g + 1) * 4, col0:col0 + 128],
                                                in_=banks[b][:, :].rearrange("p (j t) -> p j t", j=4)),
                 r=["tb%d" % b], w=[hTs_key])


def make_ident(P, ph, dt=F32, name="ident"):
    ident = ph.sb(name, [128, 128], dt)
    if dt == F32:
        P.op("pool", lambda e: e.memset(ident[:], 1.0), w=[name])
        P.op("pool", lambda e: e.affine_select(out=ident[:], in_=ident[:], pattern=[[-1, 128]], compare_op=ALU.is_equal,
                                               fill=0.0, base=0, channel_multiplier=1), r=[name], w=[name])
    else:
        tmpi = ph.sb(name + "_f", [128, 128], F32)
        P.op("pool", lambda e: e.memset(tmpi[:], 1.0), w=[name + "_f"])
        P.op("pool", lambda e: e.affine_select(out=tmpi[:], in_=tmpi[:], pattern=[[-1, 128]], compare_op=ALU.is_equal,
                                               fill=0.0, base=0, channel_multiplier=1), r=[name + "_f"], w=[name + "_f"])
        P.op("dve", lambda e: e.tensor_copy(out=ident[:], in_=tmpi[:]), r=[name + "_f"], w=[name])
    return ident


def phase_prep0(P, T, io, d):
    with P.phase("prep0") as ph:
        ident = make_ident(P, ph)
        banks = [ph.ps("tb%d" % i, [128, 512]) for i in range(4)]
        xt = [ph.sb("xt%d" % i, [128, D_MODEL], F32) for i in range(2)]
        hTs = [ph.sb("hTs%d" % i, [128, KC, 512], BF16) for i in range(2)]
        ctr = [0]
        for tt in range(T // 512):
            hi = tt % 2
            for q in range(4):
                tk = tt * 4 + q
                xi = tk % 2
                P.dma("sp", xt[xi][:, :], io["x"][tk * 128:(tk + 1) * 128, :], w=["xt%d" % xi], sem="d_xt%d" % xi)
                emit_transpose_tile(P, xt[xi], "xt%d" % xi, ident, banks, ctr, hTs[hi], "hTs%d" % hi, q * 128)
                P.dma("pool", d["h"][tk * 128:(tk + 1) * 128, :], xt[xi][:, :], r=["xt%d" % xi], sem="d_xo%d" % xi)
            P.dma("sp", d["hT"].rearrange("(kc p) t -> p kc t", p=128)[:, :, tt * 512:(tt + 1) * 512], hTs[hi][:, :, :],
                  r=["hTs%d" % hi], sem="d_hTs%d" % hi)


def t5_bucket_np(dist):
    n = np.maximum(dist, 0)
    nf = np.maximum(n, 1).astype(np.float32)
    large = 16 + (np.log(nf / np.float32(16)) / np.float32(math.log(128 / 16)) * np.float32(16)).astype(np.int32)
    large = np.minimum(large, 31)
    return np.where(n < 16, n, large)


def phase_mixA(P, T, l, io, d, NH=None):
    NH = NH or CFG["NHA"]
    lam_init = 0.8 - 0.6 * math.exp(-0.3 * l)
    NG = T // 512
    NKB = T // 128
    with P.phase("mixA") as ph:
        qT = [ph.sb("qT%d" % i, [128, T], BF16) for i in range(2)]
        kT = [ph.sb("kT%d" % i, [128, T], BF16) for i in range(2)]
        V = [ph.sb("V%d" % i, [128, NKB, 128], BF16) for i in range(2)]
        gT = [ph.sb("gT%d" % i, [128, T], BF16) for i in range(2)]
        stf = [ph.sb("stf%d" % i, [128, 640], F32) for i in range(2)]
        shi = [ph.sb("shi%d" % i, [128, 640], BF16) for i in range(2)]
        slo = [ph.sb("slo%d" % i, [128, 640], BF16) for i in range(2)]
        yst = [ph.sb("yst%d" % i, [128, T], BF16) for i in range(2)]
        pT = [ph.sb("pT%d" % i, [128, 512], BF16) for i in range(3)]
        wk = {n: ph.sb(n, [128, 512], F32) for n in ("rl1", "rl2", "a1", "a2", "sq", "t1")}
        identb = make_ident(P, ph, BF16, "identb")
        onesf = ph.sb("onesf", [128, 128], F32)
        onesb = ph.sb("onesb", [128, 128], BF16)
        c31 = ph.sb("c31", [128, CFG["NHA"]], F32)
        lam = ph.sb("lam", [128, 256], F32)
        lw = ph.sb("lamw", [128, 128], F32)
        sc = ph.sb("lamsc", [128, 8], F32)
        sub = ph.sb("subln", [128, 1], F32)
        sbk = [ph.ps("sbk%d" % i, [128, 512]) for i in range(3)]
        obk = [ph.ps("obk%d" % i, [128, 512]) for i in range(2)]
        lbk = [ph.ps("lbk%d" % i, [128, 512]) for i in range(2)]

        P.op("pool", lambda e: e.memset(onesf[:], 1.0), w=["onesf"])
        P.op("pool", lambda e: e.memset(onesb[:], 1.0), w=["onesb"])
        P.dma("sp", c31[:, :], io["a_c31"], w=["c31"], sem="d_c31")
        P.dma("sp", lam[:, :], io["da_lambda_b"][l], w=["lam"], sem="d_lam")
        P.dma("sp", sub[:, :], io["da_subln_t"][l], w=["subln"], sem="d_sub")
        P.op("dve", lambda e: e.tensor_tensor(out=lw[:, 0:64], in0=lam[:, 0:64], in1=lam[:, 64:128], op=ALU.mult), r=["lam"], w=["lamw"])
        P.op("dve", lambda e: e.tensor_tensor(out=lw[:, 64:128], in0=lam[:, 128:192], in1=lam[:, 192:256], op=ALU.mult), r=["lam"], w=["lamw"])
        P.op("dve", lambda e: e.reduce_sum(out=sc[:, 0:2], in_=lw[:].rearrange("p (a b) -> p a b", a=2), axis=AX.X), r=["lamw"], w=["lamsc"])
        P.op("act", lambda e: e.activation(out=sc[:, 2:4], in_=sc[:, 0:2], func=AF.Exp), r=["lamsc"], w=["lamsc"])
        P.op("dve", lambda e: e.tensor_tensor(out=sc[:, 4:5], in0=sc[:, 3:4], in1=sc[:, 2:3], op=ALU.subtract), r=["lamsc"], w=["lamsc"])
        P.op("dve", lambda e: e.tensor_scalar(out=sc[:, 5:6], in0=sc[:, 4:5], scalar1=-lam_init, scalar2=None, op0=ALU.add), r=["lamsc"], w=["lamsc"])
        P.op("dve", lambda e: e.tensor_scalar(out=sc[:, 6:7], in0=sub[:, 0:1], scalar1=1.0 - lam_init, scalar2=None, op0=ALU.mult), r=["subln", "lamsc"], w=["lamsc"])
        negl = sc[:, 5:6]
        gsc = sc[:, 6:7]

        def load_head(h):
            i = h % 2
            P.dma("sp", qT[i][:, :], d["AqT"][h * 128:(h + 1) * 128, :], w=["qT%d" % i], sem="d_qT%d" % i)
            P.dma("sp", kT[i][:, :], d["AkT"][h * 128:(h + 1) * 128, :], w=["kT%d" % i], sem="d_kT%d" % i)
            P.dma("sp", V[i][:, :, :], d["Av"].rearrange("(kb p) c -> p kb c", p=128)[:, :, h * 128:(h + 1) * 128],
                  w=["V%d" % i], sem="d_V%d" % i)
            P.dma("sp", gT[i][:, :], d["AgT"][h * 128:(h + 1) * 128, :], w=["gT%d" % i], sem="d_gT%d" % i)
            P.dma("sp", stf[i][:, :], io["a_strip"][h], w=["stf%d" % i], sem="d_stf%d" % i)
            P.op("dve", lambda e: e.tensor_copy(out=shi[i][:], in_=stf[i][:]), r=["stf%d" % i], w=["shi%d" % i])
            P.op("dve", lambda e: e.tensor_tensor(out=stf[i][:], in0=stf[i][:], in1=shi[i][:], op=ALU.subtract),
                 r=["stf%d" % i, "shi%d" % i], w=["stf%d" % i])
            P.op("dve", lambda e: e.tensor_copy(out=slo[i][:], in_=stf[i][:]), r=["stf%d" % i], w=["slo%d" % i])

        cnt = dict(s=0, p=0)
        load_head(0)
        for h in range(NH):
            i = h % 2
            if h + 1 < NH:
                load_head(h + 1)
            for g in range(NG):
                q0 = g * 512
                blocks = [(m, ki) for m in range(2) for ki in range(4 * g + 4)]
                nblk = 4 * g + 4
                info = {}

                def stage1(bd):
                    m, ki = bd
                    pb = slice(m * 64, m * 64 + 64)
                    j = ki - 4 * g
                    near = j >= -1
                    c0 = 128 * j if j >= 1 else 0
                    n = 512 - c0
                    sb_i = cnt["s"] % 3
                    cnt["s"] += 1
                    S = sbk[sb_i]
                    skey = "sbk%d" % sb_i
                    P.op("pe", lambda e: e.matmul(S[:, c0:512], lhsT=kT[i][pb, ki * 128:(ki + 1) * 128],
                                                  rhs=qT[i][pb, q0 + c0:q0 + 512], start=True, stop=not near),
                         r=["kT%d" % i, "qT%d" % i], w=[skey], sig=not near)
                    if near:
                        so = 128 if j == -1 else 0
                        P.op("pe", lambda e: e.matmul(S[:, c0:512], lhsT=identb[:, :], rhs=shi[i][:, so:so + n],
                                                      start=False, stop=False), r=["identb", "shi%d" % i], w=[skey], sig=False)
                        P.op("pe", lambda e: e.matmul(S[:, c0:512], lhsT=identb[:, :], rhs=slo[i][:, so:so + n],
                                                      start=False, stop=True), r=["identb", "slo%d" % i], w=[skey])
                    info[bd] = (S, skey, near, c0)

                def stage2(bd):
                    m, ki = bd
                    S, skey, near, c0 = info.pop(bd)
                    p_i = cnt["p"] % 3
                    cnt["p"] += 1
                    pk = "pT%d" % p_i
                    if near:
                        P.op("act", lambda e: e.activation(out=pT[p_i][:, c0:512], in_=S[:, c0:512], func=AF.Exp),
                             r=[skey], w=[pk])
                    else:
                        P.op("act", lambda e: e.activation(out=pT[p_i][:, c0:512], in_=S[:, c0:512], func=AF.Exp,
                                                           bias=c31[:, h:h + 1]), r=[skey, "c31"], w=[pk])
                    P.op("pe", lambda e: e.matmul(obk[m][:, c0:512], lhsT=V[i][:, ki, :], rhs=pT[p_i][:, c0:512],
                                                  start=(ki == 0), stop=(ki == nblk - 1), skip_group_check=True),
                         r=["V%d" % i, pk], w=["obk%d" % m], sig=False)
                    P.op("pe", lambda e: e.matmul(lbk[m][:, c0:512], lhsT=onesb[:, :], rhs=pT[p_i][:, c0:512],
                                                  start=(ki == 0), stop=(ki == nblk - 1), skip_group_check=True),
                         r=["onesb", pk], w=["lbk%d" % m])

                stage1(blocks[0])
                for bi_, bd in enumerate(blocks):
                    if bi_ + 1 < len(blocks):
                        stage1(blocks[bi_ + 1])
                    stage2(bd)
                P.op("dve", lambda e: e.reciprocal(out=wk["rl1"][:], in_=lbk[0][:, :]), r=["lbk0"], w=["rl1"])
                P.op("dve", lambda e: e.reciprocal(out=wk["rl2"][:], in_=lbk[1][:, :]), r=["lbk1"], w=["rl2"])
                P.op("dve", lambda e: e.tensor_tensor(out=wk["a1"][:], in0=obk[0][:, :], in1=wk["rl1"][:], op=ALU.mult),
                     r=["obk0", "rl1"], w=["a1"])
                P.op("dve", lambda e: e.tensor_tensor(out=wk["a2"][:], in0=obk[1][:, :], in1=wk["rl2"][:], op=ALU.mult),
                     r=["obk1", "rl2"], w=["a2"])
                P.op("dve", lambda e: e.scalar_tensor_tensor(out=wk["a1"][:], in0=wk["a2"][:], scalar=negl, in1=wk["a1"][:],
                                                             op0=ALU.mult, op1=ALU.add), r=["a2", "a1", "lamsc"], w=["a1"])
                P.op("act", lambda e: e.activation(out=wk["sq"][:], in_=wk["a1"][:], func=AF.Square), r=["a1"], w=["sq"])
                sb_i = cnt["s"] % 3
                cnt["s"] += 1
                S = sbk[sb_i]
                skey = "sbk%d" % sb_i
                P.op("pe", lambda e: e.matmul(S[:, :], lhsT=onesf[:, :], rhs=wk["sq"][:], start=True, stop=True),
                     r=["onesf", "sq"], w=[skey])
                P.op("dve", lambda e: e.tensor_scalar(out=wk["t1"][:], in0=S[:, :], scalar1=1.0 / 128, scalar2=RMS_EPS,
                                                      op0=ALU.mult, op1=ALU.add), r=[skey], w=["t1"])
                P.op("act", lambda e: e.activation(out=wk["t1"][:], in_=wk["t1"][:], func=AF.Ln), r=["t1"], w=["t1"])
                P.op("act", lambda e: e.activation(out=wk["t1"][:], in_=wk["t1"][:], func=AF.Exp, scale=-0.5), r=["t1"], w=["t1"])
                P.op("dve", lambda e: e.tensor_tensor(out=wk["a1"][:], in0=wk["a1"][:], in1=wk["t1"][:], op=ALU.mult),
                     r=["a1", "t1"], w=["a1"])
                P.op("dve", lambda e: e.scalar_tensor_tensor(out=yst[i][:, q0:q0 + 512], in0=wk["a1"][:], scalar=gsc,
                                                             in1=gT[i][:, q0:q0 + 512], op0=ALU.mult, op1=ALU.mult),
                     r=["a1", "lamsc", "gT%d" % i], w=["yst%d" % i])
            P.dma("pool", d["yT"][h * 128:(h + 1) * 128, :], yst[i][:, :], r=["yst%d" % i], sem="d_yst%d" % i)


def host_a_strip(rel_bias):
    v = np.arange(640)[None, :]
    s = np.arange(128)[:, None]
    dist = v - s
    bk = t5_bucket_np(dist)
    out = np.empty((8, 128, 640), np.float32)
    for h in range(8):
        out[h] = np.where(dist >= 0, rel_bias[bk, h], np.float32(NEG))
    return out


def phase_mixD(P, T, l, io, d, NH=None):
    NH = NH or CFG["NHA"]
    NG = T // 512
    NKB = T // 128
    with P.phase("mixD") as ph:
        qT = [ph.sb("qT%d" % i, [128, T], BF16) for i in range(2)]
        kT = [ph.sb("kT%d" % i, [128, T], BF16) for i in range(2)]
        V = [ph.sb("V%d" % i, [128, NKB, 128], BF16) for i in range(2)]
        gT = [ph.sb("gT%d" % i, [128, T], BF16) for i in range(2)]
        yst = [ph.sb("yst%d" % i, [128, T], BF16) for i in range(2)]
        ee = [ph.sb("ee%d" % i, [128, 512], F32) for i in range(2)]
        sp = [ph.sb("sp%d" % i, [128, 512], F32) for i in range(2)]
        lk = [ph.sb("lk%d" % i, [128, 512], F32) for i in range(2)]
        aT = [ph.sb("aT%d" % i, [128, 512], BF16) for i in range(2)]
        rsum = ph.sb("rsum", [128, 512], F32)
        identb = make_ident(P, ph, BF16, "identb")
        onesf = ph.sb("onesf", [128, 128], F32)
        lstr = ph.sb("lstr", [128, 128], F32)
        m01 = ph.sb("m01", [128, 640], F32)
        negf = ph.sb("negf", [128, 640], F32)
        negs = ph.sb("negs", [128, 640], BF16)
        zbk = [ph.ps("zbk%d" % i, [128, 512]) for i in range(2)]
        bbk = [ph.ps("bbk%d" % i, [128, 512]) for i in range(2)]
        obk = ph.ps("obk", [128, 512])

        P.op("pool", lambda e: e.memset(onesf[:], 1.0), w=["onesf"])
        P.op("pool", lambda e: e.memset(lstr[:], 1.0), w=["lstr"])
        P.op("pool", lambda e: e.affine_select(out=lstr[:], in_=lstr[:], pattern=[[-1, 128]], compare_op=ALU.is_gt,
                                               fill=0.0, base=0, channel_multiplier=1), r=["lstr"], w=["lstr"])
        P.op("pool", lambda e: e.memset(m01[:], 1.0), w=["m01"])
        P.op("pool", lambda e: e.affine_select(out=m01[:], in_=m01[:], pattern=[[1, 640]], compare_op=ALU.is_gt,
                                               fill=0.0, base=0, channel_multiplier=-1), r=["m01"], w=["m01"])
        P.op("pool", lambda e: e.memset(negf[:], 0.0), w=["negf"])
        P.op("pool", lambda e: e.affine_select(out=negf[:], in_=negf[:], pattern=[[1, 640]], compare_op=ALU.is_gt,
                                               fill=NEG, base=0, channel_multiplier=-1), r=["negf"], w=["negf"])
        P.op("dve", lambda e: e.tensor_copy(out=negs[:], in_=negf[:]), r=["negf"], w=["negs"])

        def load_head(h):
            i = h % 2
            P.dma("sp", qT[i][:, :], d["DqT"][h * 128:(h + 1) * 128, :], w=["qT%d" % i], sem="d_qT%d" % i)
            P.dma("sp", kT[i][:, :], d["DkT"][h * 128:(h + 1) * 128, :], w=["kT%d" % i], sem="d_kT%d" % i)
            P.dma("sp", V[i][:, :, :], d["Dv"].rearrange("(kb p) c -> p kb c", p=128)[:, :, h * 128:(h + 1) * 128],
                  w=["V%d" % i], sem="d_V%d" % i)
            P.dma("sp", gT[i][:, :], d["DgT"][h * 128:(h + 1) * 128, :], w=["gT%d" % i], sem="d_gT%d" % i)

        cnt = dict(z=0, b=0, w=0)
        load_head(0)
        for h in range(NH):
            i = h % 2
            if h + 1 < NH:
                load_head(h + 1)
            for g in range(NG):
                q0 = g * 512
                blocks = list(range(4 * g + 3, -1, -1))
                info = {}
                P.op("pool", lambda e: e.memset(rsum[:], 0.0), w=["rsum"])

                def stage1(ki):
                    j = ki - 4 * g
                    c0 = 128 * j if j >= 1 else 0
                    zi = cnt["z"] % 2
                    cnt["z"] += 1
                    P.op("pe", lambda e: e.matmul(zbk[zi][:, c0:512], lhsT=kT[i][:, ki * 128:(ki + 1) * 128],
                                                  rhs=qT[i][:, q0 + c0:q0 + 512], start=True, stop=True),
                         r=["kT%d" % i, "qT%d" % i], w=["zbk%d" % zi])
                    info[ki] = (zi, c0, j)

                def stage2(ki):
                    zi, c0, j = info.pop(ki)
                    near = j >= -1
                    first = ki == blocks[0]
                    last = ki == 0
                    n = 512 - c0
                    cs = slice(c0, 512)
                    Z = zbk[zi]
                    zk = "zbk%d" % zi
                    wi = cnt["w"] % 2
                    cnt["w"] += 1
                    E_, S_, L_, A_ = ee[wi], sp[wi], lk[wi], aT[wi]
                    ek, sk, lkk, ak = "ee%d" % wi, "sp%d" % wi, "lk%d" % wi, "aT%d" % wi
                    P.op("act", lambda e: e.activation(out=E_[:, cs], in_=Z[:, cs], func=AF.Exp, scale=-1.0), r=[zk], w=[ek])
                    P.op("act", lambda e: e.activation(out=S_[:, cs], in_=E_[:, cs], func=AF.Ln, bias=1.0), r=[ek], w=[sk])
                    P.op("dve", lambda e: e.scalar_tensor_tensor(out=L_[:, cs], in0=S_[:, cs], scalar=-1.0, in1=Z[:, cs],
                                                                 op0=ALU.mult, op1=ALU.subtract), r=[sk, zk], w=[lkk])
                    so = 128 if j == -1 else 0
                    if near:
                        P.op("dve", lambda e: e.tensor_tensor(out=L_[:, cs], in0=L_[:, cs], in1=m01[:, so:so + n], op=ALU.mult),
                             r=[lkk, "m01"], w=[lkk])
                    bi = cnt["b"] % 2
                    cnt["b"] += 1
                    B = bbk[bi]
                    bk = "bbk%d" % bi
                    nmm = 1 + (0 if first else 1)
                    k_ = [0]

                    def fl():
                        k_[0] += 1
                        return dict(start=(k_[0] == 1), stop=(k_[0] == nmm))
                    P.op("pe", lambda e: e.matmul(B[:, cs], lhsT=lstr[:, :], rhs=L_[:, cs], **fl()),
                         r=["lstr", lkk], w=[bk], sig=(nmm == 1))
                    if not first:
                        P.op("pe", lambda e: e.matmul(B[:, cs], lhsT=onesf[:, :], rhs=rsum[:, cs], **fl()),
                             r=["onesf", "rsum"], w=[bk], sig=(k_[0] + 1 == nmm))
                    P.op("dve", lambda e: e.tensor_tensor(out=E_[:, cs], in0=B[:, cs], in1=S_[:, cs], op=ALU.subtract),
                         r=[bk, sk], w=[ek])
                    P.op("act", lambda e: e.activation(out=A_[:, cs], in_=E_[:, cs], func=AF.Exp), r=[ek], w=[ak])
                    if near:
                        P.op("dve", lambda e: e.tensor_tensor(out=A_[:, cs], in0=A_[:, cs], in1=m01[:, so:so + n], op=ALU.mult),
                             r=[ak, "m01"], w=[ak])
                    if not last:
                        P.op("pool", lambda e: e.tensor_tensor(out=rsum[:, cs], in0=rsum[:, cs], in1=L_[:, cs], op=ALU.add),
                             r=["rsum", lkk], w=["rsum"])
                    P.op("pe", lambda e: e.matmul(obk[:, cs], lhsT=V[i][:, ki, :], rhs=A_[:, cs], start=first, stop=last,
                                                  skip_group_check=True), r=["V%d" % i, ak], w=["obk"])

                stage1(blocks[0])
                for bi_, ki in enumerate(blocks):
                    if bi_ + 1 < len(blocks):
                        stage1(blocks[bi_ + 1])
                    stage2(ki)
                P.op("dve", lambda e: e.tensor_tensor(out=yst[i][:, q0:q0 + 512], in0=obk[:, :], in1=gT[i][:, q0:q0 + 512],
                                                      op=ALU.mult), r=["obk", "gT%d" % i], w=["yst%d" % i])
            P.dma("pool", d["yT"][3 * CFG["YB"] + h * 128:3 * CFG["YB"] + (h + 1) * 128, :], yst[i][:, :], r=["yst%d" % i],
                  sem="d_yst%d" % i)


def phase_mixB(P, T, l, io, d, NH=None):
    NH = NH or CFG["NHA"]
    NHT = CFG["NHA"]
    NCH = T // 64
    NG = T // 512
    with P.phase("mixB") as ph:
        zf = [ph.sb("zf0", [128, T], F32)] * 2
        qb = [ph.sb("qb%d" % i, [128, T], BF16) for i in range(2)]
        gT = [ph.sb("gT%d" % i, [128, T], BF16) for i in range(2)]
        itok = [ph.sb("itok%d" % i, [64, NCH, 128], BF16) for i in range(2)]
        a1 = ph.sb("a1", [128, T], F32)
        a2 = ph.sb("a2", [128, T], F32)
        a3 = ph.sb("a3", [128, T], F32)
        a4 = ph.sb("a4", [128, T], F32)
        qt = ph.sb("qt", [128, T], BF16)
        kh = ph.sb("kh", [128, T], BF16)
        khtok = ph.sb("khtok", [64, NCH, 128], BF16)
        yst = [ph.sb("yst%d" % i, [128, 512], BF16) for i in range(2)]
        rmask = ph.sb("rmask", [128, 512], F32)
        cmask = ph.sb("cmask", [64, 512], F32)
        scb = [ph.sb("scb%d" % i, [64, 512], BF16) for i in range(2)]
        S = ph.sb("S", [128, 128], F32)
        Sb = [ph.sb("Sb%d" % i, [128, 128], BF16) for i in range(2)]
        sq = ph.sb("sq", [128, 512], F32)
        t1 = ph.sb("t1", [128, 512], F32)
        yy = ph.sb("yy", [128, 512], F32)
        identb = make_ident(P, ph, BF16, "identb")
        onesf = ph.sb("onesf", [128, 128], F32)
        hl = ph.sb("hl", [128, NHT, 4], F32)
        he = ph.sb("he", [128, NHT, 4], F32)
        hs = ph.sb("hs", [128, NHT], F32)
        lb = ph.sb("lb", [128, NHT], F32)
        oml = ph.sb("oml", [128, NHT], F32)
        hgn = ph.sb("hgn", [128, 1], F32)
        sbk = [ph.ps("sbk%d" % i, [128, 512]) for i in range(1)]
        obk = [ph.ps("obk%d" % i, [128, 512]) for i in range(2)]
        spk = [ph.ps("spk%d" % i, [128, 512]) for i in range(2)]
        ssb = ph.ps("ssb", [128, 512])
        tpk = [ph.ps("tpk%d" % i, [64, 8, 128], BF16) for i in range(2)]

        P.op("pool", lambda e: e.memset(onesf[:], 1.0), w=["onesf"])
        P.op("pool", lambda e: e.memset(rmask[:], 1.0), w=["rmask"])
        P.op("pool", lambda e: e.memset(rmask[:].rearrange("p (c k) -> p c k", k=64)[:, :, 0:1], 0.0), w=["rmask"])
        P.op("pool", lambda e: e.memset(cmask[:], 1.0), w=["cmask"])
        P.op("pool", lambda e: e.affine_select(out=cmask[:].rearrange("p (c t) -> p c t", t=64),
                                               in_=cmask[:].rearrange("p (c t) -> p c t", t=64),
                                               pattern=[[0, 8], [1, 64]], compare_op=ALU.is_ge, fill=0.0, base=0,
                                               channel_multiplier=-1), r=["cmask"], w=["cmask"])
        P.dma("sp", hl[:, :, :], io["hg_lower_t"], w=["hl"], sem="d_hl")
        P.dma("sp", hgn[:, :], io["hg_norm_t"][l], w=["hgn"], sem="d_hgn")
        P.op("act", lambda e: e.activation(out=he[:], in_=hl[:], func=AF.Exp), r=["hl"], w=["he"])
        P.op("dve", lambda e: e.reduce_sum(out=hs[:], in_=he[:], axis=AX.X), r=["he"], w=["hs"])
        P.op("dve", lambda e: e.reciprocal(out=hs[:], in_=hs[:]), r=["hs"], w=["hs"])
        P.op("pool", lambda e: e.memset(lb[:], 0.0), w=["lb"])
        for j in range(1, l + 1):
            P.op("dve", lambda e: e.tensor_tensor(out=lb[:], in0=lb[:], in1=he[:, :, j], op=ALU.add), r=["lb", "he"], w=["lb"])
        P.op("dve", lambda e: e.tensor_tensor(out=lb[:], in0=lb[:], in1=hs[:], op=ALU.mult), r=["lb", "hs"], w=["lb"])
        P.op("dve", lambda e: e.tensor_scalar(out=oml[:], in0=lb[:], scalar1=-1.0, scalar2=1.0, op0=ALU.mult, op1=ALU.add),
             r=["lb"], w=["oml"])

        def load_head(h):
            i = h % 2
            P.dma("sp", zf[0][:, :], d["BfT"][h * 128:(h + 1) * 128, :], w=["zf0"], sem="d_zf0")
            P.dma("sp", qb[i][:, :], d["BqT"][h * 128:(h + 1) * 128, :], w=["qb%d" % i], sem="d_qb%d" % i)
            P.dma("sp", gT[i][:, :], d["BgT"][h * 128:(h + 1) * 128, :], w=["gT%d" % i], sem="d_gT%d" % i)
            P.dma("sp", itok[i][:, :, :], d["Bi"].rearrange("(c p) e -> p c e", p=64)[:, :, h * 128:(h + 1) * 128],
                  w=["itok%d" % i], sem="d_itok%d" % i)

        cnt = dict(sb=0)
        load_head(0)
        for h in range(NH):
            i = h % 2
            zk = "zf0"
            P.op("act", lambda e: e.activation(out=a1[:], in_=zf[i][:], func=AF.Sigmoid), r=[zk], w=["a1"])
            P.op("dve", lambda e: e.tensor_scalar(out=a1[:], in0=a1[:], scalar1=oml[:, h:h + 1], scalar2=lb[:, h:h + 1],
                                                  op0=ALU.mult, op1=ALU.add), r=["a1", "oml", "lb"], w=["a1"])
            P.op("act", lambda e: e.activation(out=a2[:], in_=a1[:], func=AF.Ln), r=["a1"], w=["a2"])
            P.op("dve", lambda e: e.tensor_scalar(out=a1[:], in0=a1[:], scalar1=-1.0, scalar2=1.0, op0=ALU.mult, op1=ALU.add),
                 r=["a1"], w=["a1"])
            for g8 in range(NG):
                P.op("dve", lambda e: e.tensor_tensor_scan(out=a3[:, g8 * 512:(g8 + 1) * 512], data0=rmask[:],
                                                           data1=a2[:, g8 * 512:(g8 + 1) * 512], initial=0.0,
                                                           op0=ALU.mult, op1=ALU.add), r=["rmask", "a2"], w=["a3"])
            P.op("act", lambda e: e.activation(out=a4[:], in_=a3[:], func=AF.Exp), r=["a3"], w=["a4"])
            P.op("act", lambda e: e.activation(out=a2[:], in_=a3[:], func=AF.Exp, scale=-1.0), r=["a3"], w=["a2"])
            P.op("dve", lambda e: e.tensor_tensor(out=a1[:], in0=a1[:], in1=a2[:], op=ALU.mult), r=["a1", "a2"], w=["a1"])
            P.op("dve", lambda e: e.tensor_tensor(out=a2[:], in0=qb[i][:], in1=a4[:], op=ALU.mult), r=["qb%d" % i, "a4", "a2"], w=["a2"])
            P.op("act", lambda e: e.activation(out=qt[:], in_=a2[:], func=AF.Copy), r=["a2"], w=["qt"])
            ebl = a4[:].rearrange("p (c k) -> p c k", k=64)[:, :, 63:64]
            P.op("dve", lambda e: e.tensor_tensor(out=kh[:].rearrange("p (c k) -> p c k", k=64),
                                                  in0=a1[:].rearrange("p (c k) -> p c k", k=64),
                                                  in1=ebl.to_broadcast([128, NCH, 64]), op=ALU.mult), r=["a1", "a4"], w=["kh"])
            if h + 1 < NH:
                load_head(h + 1)
            for c8 in range(NCH // 8):
                tp = tpk[c8 % 2]
                tk_ = "tpk%d" % (c8 % 2)
                for cc in range(8):
                    c = c8 * 8 + cc
                    P.op("pe", lambda e: e.transpose(out=tp[:, cc, :], in_=kh[:, c * 64:(c + 1) * 64], identity=identb[:, :]),
                         r=["kh", "identb"], w=[tk_], sig=(cc == 7))
                P.op("act", lambda e: e.activation(out=khtok[:, c8 * 8:(c8 + 1) * 8, :], in_=tp[:, :, :], func=AF.Copy),
                     r=[tk_], w=["khtok"])
            P.op("pool", lambda e: e.memset(S[:], 0.0), w=["S"])
            P.op("pool", lambda e: e.memset(Sb[0][:], 0.0), w=["Sb0"])
            sbi = 0
            for g in range(NG):
                q0 = g * 512
                for cc in range(8):
                    c = g * 8 + cc
                    P.op("pe", lambda e: e.matmul(sbk[0][0:64, cc * 64:(cc + 1) * 64], lhsT=a1[:, c * 64:(c + 1) * 64],
                                                  rhs=a2[:, c * 64:(c + 1) * 64], start=True, stop=True, skip_group_check=True),
                         r=["a1", "a2"], w=["sbk0"], sig=(cc == 7))
                si = g % 2
                P.op("dve", lambda e: e.tensor_tensor(out=scb[si][:], in0=sbk[0][0:64, :], in1=cmask[:], op=ALU.mult),
                     r=["sbk0", "cmask"], w=["scb%d" % si])
                for half in range(2):
                    for c4 in range(4):
                        c = g * 8 + half * 4 + c4
                        P.op("pe", lambda e: e.matmul(spk[half][:, c4 * 128:(c4 + 1) * 128], lhsT=khtok[:, c, :],
                                                      rhs=itok[i][:, c, :], start=True, stop=True, skip_group_check=True),
                             r=["khtok", "itok%d" % i], w=["spk%d" % half], sig=(c4 == 3))
                ob = obk[g % 2]
                ok_ = "obk%d" % (g % 2)
                for cc in range(8):
                    c = g * 8 + cc
                    P.op("pe", lambda e: e.matmul(ob[:, cc * 64:(cc + 1) * 64], lhsT=itok[i][:, c, :],
                                                  rhs=scb[si][:, cc * 64:(cc + 1) * 64], start=True, stop=False,
                                                  skip_group_check=True), r=["itok%d" % i, "scb%d" % si], w=[ok_], sig=False)
                    P.op("pe", lambda e: e.matmul(ob[:, cc * 64:(cc + 1) * 64], lhsT=Sb[sbi][:, :],
                                                  rhs=qt[:, c * 64:(c + 1) * 64], start=False, stop=True,
                                                  skip_group_check=True), r=["Sb%d" % sbi, "qt"], w=[ok_])
                    half, c4 = cc // 4, cc % 4
                    P.op("dve", lambda e: e.scalar_tensor_tensor(out=S[:], in0=S[:], scalar=ebl[:, c, :],
                                                                 in1=spk[half][:, c4 * 128:(c4 + 1) * 128],
                                                                 op0=ALU.mult, op1=ALU.add), r=["S", "a4", "spk%d" % half], w=["S"])
                    sbi = 1 - sbi
                    P.op("act", lambda e: e.activation(out=Sb[sbi][:], in_=S[:], func=AF.Copy), r=["S"], w=["Sb%d" % sbi])
                P.op("act", lambda e: e.activation(out=sq[:], in_=ob[:, :], func=AF.Square), r=[ok_], w=["sq"])
                P.op("pe", lambda e: e.matmul(ssb[:, :], lhsT=onesf[:, :], rhs=sq[:], start=True, stop=True), r=["onesf", "sq"], w=["ssb"])
                P.op("dve", lambda e: e.tensor_scalar(out=t1[:], in0=ssb[:, :], scalar1=1.0 / 128, scalar2=RMS_EPS,
                                                      op0=ALU.mult, op1=ALU.add), r=["ssb"], w=["t1"])
                P.op("act", lambda e: e.activation(out=t1[:], in_=t1[:], func=AF.Ln), r=["t1"], w=["t1"])
                P.op("act", lambda e: e.activation(out=t1[:], in_=t1[:], func=AF.Exp, scale=-0.5), r=["t1"], w=["t1"])
                P.op("dve", lambda e: e.tensor_tensor(out=yy[:], in0=ob[:, :], in1=t1[:], op=ALU.mult), r=[ok_, "t1"], w=["yy"])
                yi = g % 2
                P.op("dve", lambda e: e.scalar_tensor_tensor(out=yst[yi][:, :], in0=yy[:], scalar=hgn[:, 0:1],
                                                             in1=gT[i][:, q0:q0 + 512], op0=ALU.mult, op1=ALU.mult),
                     r=["yy", "hgn", "gT%d" % i], w=["yst%d" % yi])
                P.dma("pool", d["yT"][CFG["YB"] + h * 128:CFG["YB"] + (h + 1) * 128, q0:q0 + 512], yst[yi][:, :], r=["yst%d" % yi],
                      sem="d_yst%d" % yi)


WSC = math.exp(-0.5)


def phase_mixC(P, T, l, io, d, NH=None, G=2):
    NH = NH or CFG["NHC"]
    NHT = CFG["NHC"]
    RB = 64 * NHT
    N = 512
    NST = T // N
    GC = G * 8
    with P.phase("mixC") as ph:
        F = lambda name, shape, dt=F32: ph.sb(name, shape, dt)
        cpar = F("cpar", [64, NHT, 8])
        omka = F("omka", [64, NHT])
        w2s = F("w2s", [64, RB])
        a2s = F("a2s", [64, RB])
        v2s = F("v2s", [32, RB])
        twd = F("twd", [64, N])
        adm = F("adm", [64, N])
        vdm = F("vdm", [32, N])
        Xr = F("Xr", [64, G, N])
        Xk = F("Xk", [64, G, N])
        Xv = F("Xv", [64, G, N])
        Xf = F("Xf", [64, G, N])
        gate = F("gate", [64, G, N], BF16)
        sg = F("sg", [64, G, N])
        aa = F("aa", [64, G, N])
        kk = F("kk", [64, G, N])
        kp = F("kp", [64, G, N])
        cs = F("cs", [64, G, N])
        pp = F("pp", [64, G, N])
        pinv = F("pinv", [64, G, N])
        pprev = F("pprev", [64, G, N])
        tmp = F("tmp", [64, G, N])
        AR = F("AR", [64, G, 8, 2, 64])
        BT = F("BT", [64, G, N])
        KT = F("KT", [64, G, N])
        RK = F("RK", [64, G, N])
        TK = F("TK", [64, G, 8, 5, 64])
        GM = F("GM", [64, GC, 320])
        TT = F("TT", [64, GC, 64])
        XY = [F("XY%d" % i, [64, GC, 128]) for i in range(2)]
        GA = F("GA", [64, GC, 128])
        OL = F("OL", [64, GC, 64])
        RP = F("RP", [64, GC, 64])
        PH = F("PH", [64, GC, 64])
        GP = F("GP", [64, GC, 64])
        OT = F("OT", [64, G, N])
        H = F("H", [64, NH, 64])
        yst = F("yst", [64, G, N], BF16)
        m320 = F("m320", [64, 320])
        rmask = F("rmask", [64, G * N])
        ones64 = F("ones64", [64, 64])
        onesm = F("onesm", [64, 64])
        ident = make_ident(P, ph, F32, "identf")
        idf = ident[0:64, 0:64]
        pbs = [ph.ps("pb%d" % i, [64, 512]) for i in range(8)]
        bctr = [0]

        def nb():
            b = bctr[0] % 8
            bctr[0] += 1
            return pbs[b], "pb%d" % b

        P.op("pool", lambda e: e.memset(ones64[:], 1.0), w=["ones64"])
        P.op("pool", lambda e: e.memset(onesm[:], 1.0 / 64), w=["onesm"])
        P.op("pool", lambda e: e.memset(rmask[:], 1.0), w=["rmask"])
        P.op("pool", lambda e: e.memset(rmask[:].rearrange("p (c k) -> p c k", k=64)[:, :, 0:1], 0.0), w=["rmask"])
        P.op("pool", lambda e: e.memset(H[:], 0.0), w=["H"])
        P.op("pool", lambda e: e.memset(m320[:], 1.0), w=["m320"])
        for blk, op_, cm, pat in ((0, ALU.is_gt, -1, 1), (1, ALU.is_ge, -1, 1), (2, ALU.is_gt, -1, 1), (3, ALU.is_ge, -1, 1),
                                  (4, ALU.is_gt, 1, -1)):
            P.op("pool", lambda e: e.affine_select(out=m320[:, blk * 64:(blk + 1) * 64], in_=m320[:, blk * 64:(blk + 1) * 64],
                                                   pattern=[[pat, 64]], compare_op=op_, fill=0.0, base=0, channel_multiplier=cm),
                 r=["m320"], w=["m320"])
        P.dma("sp", cpar[:, :, :], io["c_par"][l], w=["cpar"], sem="d_cpar")
        P.dma("sp", w2s[:, :], io["rw_w2"][l], w=["w2s"], sem="d_w2s")
        P.dma("sp", a2s[:, :], io["rw_a2"][l], w=["a2s"], sem="d_a2s")
        if l > 0:
            P.dma("sp", v2s[:, :], io["rw_v2"][l - 1], w=["v2s"], sem="d_v2s")
        P.op("dve", lambda e: e.tensor_scalar(out=omka[:], in0=cpar[:, :, 3], scalar1=-1.0, scalar2=1.0, op0=ALU.mult, op1=ALU.add),
             r=["cpar"], w=["omka"])

        def bc(ap2, g0):
            return ap2[:, g0:g0 + G].unsqueeze(2).to_broadcast([64, G, N])

        CM = d["CmT"]
        for st in range(NST):
            t0 = st * N
            P.dma("sp", twd[:, :], CM[3 * RB:3 * RB + 64, t0:t0 + N], w=["twd"], sem="d_twd")
            P.dma("sp", adm[:, :], CM[3 * RB + 64:3 * RB + 128, t0:t0 + N], w=["adm"], sem="d_adm")
            P.op("act", lambda e: e.activation(out=twd[:], in_=twd[:], func=AF.Tanh), r=["twd"], w=["twd"])
            if l > 0:
                P.dma("sp", vdm[:, :], d["CvdT"][:, t0:t0 + N], w=["vdm"], sem="d_vdm")
            for hg in range(NH // G):
                h0 = hg * G
                rows = lambda base: CM[base + h0 * 64: base + (h0 + G) * 64, t0:t0 + N].rearrange("(g k) t -> k g t", k=64)
                P.dma("sp", Xr[:, :, :], rows(0), w=["Xr"], sem="d_Xr")
                P.dma("sp", Xk[:, :, :], rows(RB), w=["Xk"], sem="d_Xk")
                P.dma("sp", Xv[:, :, :], rows(2 * RB), w=["Xv"], sem="d_Xv")
                P.dma("sp", gate[:, :, :], d["CgT"][h0 * 64:(h0 + G) * 64, t0:t0 + N].rearrange("(g k) t -> k g t", k=64),
                      w=["gate"], sem="d_gate")
                if l > 0:
                    P.dma("sp", Xf[:, :, :], d["Cvf"][h0 * 64:(h0 + G) * 64, t0:t0 + N].rearrange("(g k) t -> k g t", k=64),
                          w=["Xf"], sem="d_Xf")
                for g in range(G):
                    h = h0 + g
                    pb, pk = nb()
                    P.op("pe", lambda e: e.matmul(pb[:, :], lhsT=w2s[:, h * 64:(h + 1) * 64], rhs=twd[:, :], start=True, stop=True),
                         r=["w2s", "twd"], w=[pk])
                    P.op("act", lambda e: e.activation(out=sg[:, g, :], in_=pb[:, :], func=AF.Sigmoid, bias=cpar[:, h, 0:1]),
                         r=[pk, "cpar"], w=["sg"])
                    pb, pk = nb()
                    P.op("pe", lambda e: e.matmul(pb[:, :], lhsT=a2s[:, h * 64:(h + 1) * 64], rhs=adm[:, :], start=True, stop=True),
                         r=["a2s", "adm"], w=[pk])
                    P.op("act", lambda e: e.activation(out=aa[:, g, :], in_=pb[:, :], func=AF.Sigmoid, bias=cpar[:, h, 1:2]),
                         r=[pk, "cpar"], w=["aa"])
                    if l > 0:
                        pb, pk = nb()
                        P.op("pe", lambda e: e.matmul(pb[:, :], lhsT=v2s[:, h * 64:(h + 1) * 64], rhs=vdm[:, :], start=True, stop=True),
                             r=["v2s", "vdm"], w=[pk])
                        P.op("act", lambda e: e.activation(out=tmp[:, g, :], in_=pb[:, :], func=AF.Sigmoid, bias=cpar[:, h, 7:8]),
                             r=[pk, "cpar"], w=["tmp"])
                if l > 0:
                    P.op("dve", lambda e: e.tensor_tensor(out=Xf[:], in0=Xf[:], in1=Xv[:], op=ALU.subtract), r=["Xf", "Xv"], w=["Xf"])
                    P.op("dve", lambda e: e.tensor_tensor(out=Xf[:], in0=Xf[:], in1=tmp[:], op=ALU.mult), r=["Xf", "tmp"], w=["Xf"])
                    P.op("dve", lambda e: e.tensor_tensor(out=Xv[:], in0=Xv[:], in1=Xf[:], op=ALU.add), r=["Xf", "Xv"], w=["Xv"])
                else:
                    P.dma("pool", d["Cvf"][h0 * 64:(h0 + G) * 64, t0:t0 + N].rearrange("(g k) t -> k g t", k=64), Xv[:, :, :],
                          r=["Xv"], sem="d_vfo")
                P.op("dve", lambda e: e.tensor_tensor(out=kk[:], in0=Xk[:], in1=bc(cpar[:, :, 2], h0), op=ALU.mult),
                     r=["Xk", "cpar"], w=["kk"])
                P.op("act", lambda e: e.activation(out=tmp[:], in_=kk[:], func=AF.Square), r=["kk"], w=["tmp"])
                for g in range(G):
                    pb, pk = nb()
                    P.op("pe", lambda e: e.matmul(pb[:, :], lhsT=ones64[:, :], rhs=tmp[:, g, :], start=True, stop=True),
                         r=["ones64", "tmp"], w=[pk])
                    P.op("dve", lambda e: e.tensor_scalar(out=kp[:, g, :], in0=pb[:, :], scalar1=1e-24, scalar2=None, op0=ALU.max),
                         r=[pk], w=["kp"])
                P.op("act", lambda e: e.activation(out=kp[:], in_=kp[:], func=AF.Ln), r=["kp"], w=["kp"])
                P.op("act", lambda e: e.activation(out=kp[:], in_=kp[:], func=AF.Exp, scale=-0.5), r=["kp"], w=["kp"])
                P.op("dve", lambda e: e.tensor_tensor(out=kk[:], in0=kk[:], in1=kp[:], op=ALU.mult), r=["kk", "kp"], w=["kk"])
                P.op("dve", lambda e: e.tensor_tensor(out=kp[:], in0=aa[:], in1=bc(cpar[:, :, 3], h0), op=ALU.mult),
                     r=["aa", "cpar"], w=["kp"])
                P.op("dve", lambda e: e.tensor_tensor(out=kp[:], in0=kp[:], in1=bc(omka, h0), op=ALU.add), r=["kp", "omka"], w=["kp"])
                P.op("dve", lambda e: e.tensor_tensor(out=kp[:], in0=kp[:], in1=Xk[:], op=ALU.mult), r=["kp", "Xk"], w=["kp"])
                P.op("dve", lambda e: e.tensor_tensor_scan(out=cs[:].rearrange("p g n -> p (g n)"), data0=rmask[:],
                                                           data1=sg[:].rearrange("p g n -> p (g n)"), initial=0.0,
                                                           op0=ALU.mult, op1=ALU.add), r=["rmask", "sg"], w=["cs"])
                P.op("act", lambda e: e.activation(out=pp[:], in_=cs[:], func=AF.Exp, scale=-WSC), r=["cs"], w=["pp"])
                P.op("act", lambda e: e.activation(out=pinv[:], in_=cs[:], func=AF.Exp, scale=WSC), r=["cs"], w=["pinv"])
                P.op("dve", lambda e: e.tensor_tensor(out=tmp[:], in0=cs[:], in1=sg[:], op=ALU.subtract), r=["cs", "sg"], w=["tmp"])
                P.op("act", lambda e: e.activation(out=pprev[:], in_=tmp[:], func=AF.Exp, scale=-WSC), r=["tmp"], w=["pprev"])
                v4 = lambda t_: t_[:].rearrange("p g (c k) -> p g c k", k=64)
                P.op("dve", lambda e: e.scalar_tensor_tensor(out=AR[:, :, :, 0, :], in0=v4(kk), scalar=-1.0, in1=v4(pprev),
                                                             op0=ALU.mult, op1=ALU.mult), r=["kk", "pprev"], w=["AR"])
                P.op("dve", lambda e: e.tensor_tensor(out=AR[:, :, :, 1, :], in0=v4(Xr), in1=v4(pp), op=ALU.mult),
                     r=["Xr", "pp"], w=["AR"])
                P.op("dve", lambda e: e.tensor_tensor(out=BT[:], in0=kk[:], in1=aa[:], op=ALU.mult), r=["kk", "aa"], w=["BT"])
                P.op("dve", lambda e: e.tensor_tensor(out=BT[:], in0=BT[:], in1=pinv[:], op=ALU.mult), r=["BT", "pinv"], w=["BT"])
                P.op("dve", lambda e: e.tensor_tensor(out=KT[:], in0=kp[:], in1=pinv[:], op=ALU.mult), r=["kp", "pinv"], w=["KT"])
                P.op("dve", lambda e: e.tensor_tensor(out=RK[:], in0=Xr[:], in1=kp[:], op=ALU.mult), r=["Xr", "kp"], w=["RK"])
                P.op("dve", lambda e: e.tensor_tensor(out=RK[:], in0=RK[:], in1=bc(cpar[:, :, 4], h0), op=ALU.mult),
                     r=["RK", "cpar"], w=["RK"])
                for g in range(G):
                    for c2 in range(4):
                        pb, pk = nb()
                        pv = pb[:, :].rearrange("p (c j k) -> p c j k", c=2, j=4)
                        for cc in range(2):
                            c = c2 * 2 + cc
                            csl = slice(c * 64, (c + 1) * 64)
                            srcs = [(BT[:, g, csl], "BT"), (KT[:, g, csl], "KT"), (Xv[:, g, csl], "Xv"), (AR[:, g, c, 0, :], "AR")]
                            for j, (sap, skey) in enumerate(srcs):
                                P.op("pe", lambda e: e.transpose(out=pv[:, cc, j, :], in_=sap, identity=idf),
                                     r=[skey, "identf"], w=[pk], sig=(cc == 1 and j == 3))
                        P.op("act", lambda e: e.activation(out=TK[:, g, c2 * 2:c2 * 2 + 2, 0:3, :], in_=pv[:, :, 0:3, :], func=AF.Copy),
                             r=[pk], w=["TK"])
                        P.op("act", lambda e: e.activation(out=TK[:, g, c2 * 2:c2 * 2 + 2, 4, :], in_=pv[:, :, 3, :], func=AF.Copy),
                             r=[pk], w=["TK"])
                for g in range(G):
                    for c in range(8):
                        gc = g * 8 + c
                        csl = slice(c * 64, (c + 1) * 64)
                        pb, pk = nb()
                        arv = AR[:, g, c, :, :].rearrange("p a k -> p (a k)")
                        P.op("pe", lambda e: e.matmul(pb[:, 0:128], lhsT=BT[:, g, csl], rhs=arv, start=True, stop=True,
                                                      skip_group_check=True), r=["BT", "AR"], w=[pk], sig=False)
                        P.op("pe", lambda e: e.matmul(pb[:, 128:256], lhsT=KT[:, g, csl], rhs=arv, start=True, stop=True,
                                                      skip_group_check=True), r=["KT", "AR"], w=[pk], sig=False)
                        P.op("pe", lambda e: e.matmul(pb[:, 256:320], lhsT=AR[:, g, c, 0, :], rhs=BT[:, g, csl], start=True, stop=True,
                                                      skip_group_check=True), r=["BT", "AR"], w=[pk])
                        P.op("dve", lambda e: e.tensor_tensor(out=GM[:, gc, :], in0=pb[:, 0:320], in1=m320[:], op=ALU.mult),
                             r=[pk, "m320"], w=["GM"])
                P.op("dve", lambda e: e.tensor_tensor(out=TT[:], in0=GM[:, :, 0:64], in1=idf.unsqueeze(1).to_broadcast([64, GC, 64]),
                                                      op=ALU.add), r=["GM", "identf"], w=["TT"])
                for lev in range(5):
                    last = lev == 4
                    src = XY[(lev + 1) % 2]
                    dst = XY[lev % 2]
                    skey, dkey = "XY%d" % ((lev + 1) % 2), "XY%d" % (lev % 2)
                    for b4 in range(GC // 4):
                        pb, pk = nb()
                        pv = pb[:, :].rearrange("p (q k) -> p q k", q=4)
                        for q in range(4):
                            gc = b4 * 4 + q
                            if lev == 0:
                                Xa, Ya, rk_ = GM[:, gc, 256:320], GM[:, gc, 0:64], ["GM"]
                            else:
                                Xa, Ya, rk_ = src[:, gc, 0:64], src[:, gc, 64:128], [skey]
                            P.op("pe", lambda e: e.matmul(pv[:, q, 0:64], lhsT=Ya, rhs=Xa, start=True, stop=True, skip_group_check=True),
                                 r=rk_, w=[pk], sig=(last and q == 3))
                            if not last:
                                P.op("pe", lambda e: e.matmul(pv[:, q, 64:128], lhsT=Xa, rhs=Ya, start=True, stop=True,
                                                              skip_group_check=True), r=rk_, w=[pk], sig=(q == 3))
                        if last:
                            P.op("act", lambda e: e.activation(out=dst[:, b4 * 4:b4 * 4 + 4, 0:64], in_=pv[:, :, 0:64], func=AF.Copy),
                                 r=[pk], w=[dkey])
                        else:
                            P.op("act", lambda e: e.activation(out=dst[:, b4 * 4:b4 * 4 + 4, :], in_=pv[:, :, :], func=AF.Copy),
                                 r=[pk], w=[dkey])
                        pb2, pk2 = nb()
                        pv2 = pb2[:, 0:256].rearrange("p (q k) -> p q k", q=4)
                        for q in range(4):
                            gc = b4 * 4 + q
                            P.op("pe", lambda e: e.matmul(pv2[:, q, :], lhsT=dst[:, gc, 0:64], rhs=TT[:, gc, :], start=True, stop=True,
                                                          skip_group_check=True), r=[dkey, "TT"], w=[pk2], sig=(q == 3))
                        P.op("dve", lambda e: e.tensor_tensor(out=TT[:, b4 * 4:b4 * 4 + 4, :], in0=TT[:, b4 * 4:b4 * 4 + 4, :],
                                                              in1=pv2[:, :, :], op=ALU.add), r=["TT", pk2], w=["TT"])
                for g in range(G):
                    for c4 in range(2):
                        pb, pk = nb()
                        pv = pb[:, 0:256].rearrange("p (q k) -> p q k", q=4)
                        for q in range(4):
                            c = c4 * 4 + q
                            gc = g * 8 + c
                            P.op("pe", lambda e: e.matmul(pv[:, q, :], lhsT=GM[:, gc, 128:192], rhs=TK[:, g, c, 2, :], start=True, stop=True,
                                                          skip_group_check=True), r=["GM", "TK"], w=[pk], sig=(q == 3))
                        P.op("act", lambda e: e.activation(out=TK[:, g, c4 * 4:c4 * 4 + 4, 3, :], in_=pv[:, :, :], func=AF.Copy),
                             r=[pk], w=["TK"])
                for g in range(G):
                    for c4 in range(2):
                        pb, pk = nb()
                        pv = pb[:, :].rearrange("p (q k) -> p q k", q=4)
                        for q in range(4):
                            c = c4 * 4 + q
                            gc = g * 8 + c
                            P.op("pe", lambda e: e.matmul(pv[:, q, :], lhsT=TT[:, gc, :], rhs=TK[:, g, c, 3:5, :].rearrange("p a k -> p (a k)"),
                                                          start=True, stop=True, skip_group_check=True), r=["TT", "TK"], w=[pk], sig=(q == 3))
                        P.op("act", lambda e: e.activation(out=GA[:, g * 8 + c4 * 4:g * 8 + c4 * 4 + 4, :], in_=pv[:, :, :], func=AF.Copy),
                             r=[pk], w=["GA"])
                pC = pp[:].rearrange("p g (c k) -> p g c k", k=64)[:, :, :, 63:64]
                for g in range(G):
                    for c4 in range(2):
                        pbO, pkO = nb()
                        pbR, pkR = nb()
                        pbG, pkG = nb()
                        pvO = pbO[:, 0:256].rearrange("p (q k) -> p q k", q=4)
                        pvR = pbR[:, :].rearrange("p (q a k) -> p q a k", q=4, a=2)
                        pvG = pbG[:, 0:256].rearrange("p (q k) -> p q k", q=4)
                        for q in range(4):
                            c = c4 * 4 + q
                            gc = g * 8 + c
                            P.op("pe", lambda e: e.matmul(pvO[:, q, :], lhsT=GA[:, gc, 0:64], rhs=GM[:, gc, 64:128], start=True, stop=False,
                                                          skip_group_check=True), r=["GA", "GM"], w=[pkO], sig=False)
                            P.op("pe", lambda e: e.matmul(pvO[:, q, :], lhsT=TK[:, g, c, 2, :], rhs=GM[:, gc, 192:256], start=False, stop=True,
                                                          skip_group_check=True), r=["TK", "GM"], w=[pkO], sig=(q == 3))
                            P.op("pe", lambda e: e.matmul(pvR[:, q, 0, :], lhsT=GA[:, gc, 64:128], rhs=GM[:, gc, 64:128], start=True, stop=True,
                                                          skip_group_check=True), r=["GA", "GM"], w=[pkR], sig=False)
                            P.op("pe", lambda e: e.matmul(pvR[:, q, 1, :], lhsT=GA[:, gc, 64:128], rhs=TK[:, g, c, 0, :], start=True, stop=True,
                                                          skip_group_check=True), r=["GA", "TK"], w=[pkR], sig=(q == 3))
                            P.op("pe", lambda e: e.matmul(pvG[:, q, :], lhsT=TK[:, g, c, 0, :], rhs=GA[:, gc, 0:64], start=True, stop=False,
                                                          skip_group_check=True), r=["GA", "TK"], w=[pkG], sig=False)
                            P.op("pe", lambda e: e.matmul(pvG[:, q, :], lhsT=TK[:, g, c, 1, :], rhs=TK[:, g, c, 2, :], start=False, stop=True,
                                                          skip_group_check=True), r=["TK"], w=[pkG], sig=(q == 3))
                        gs = slice(g * 8 + c4 * 4, g * 8 + c4 * 4 + 4)
                        cs4 = slice(c4 * 4, c4 * 4 + 4)
                        P.op("act", lambda e: e.activation(out=OL[:, gs, :], in_=pvO[:, :, :], func=AF.Copy), r=[pkO], w=["OL"])
                        P.op("dve", lambda e: e.tensor_tensor(out=RP[:, gs, :], in0=pvR[:, :, 0, :], in1=AR[:, g, cs4, 1, :], op=ALU.add),
                             r=[pkR, "AR"], w=["RP"])
                        P.op("dve", lambda e: e.tensor_tensor(out=PH[:, gs, :], in0=pvR[:, :, 1, :],
                                                              in1=idf.unsqueeze(1).to_broadcast([64, 4, 64]), op=ALU.add),
                             r=[pkR, "identf"], w=["PH"])
                        P.op("dve", lambda e: e.tensor_tensor(out=GP[:, gs, :], in0=pvG[:, :, :],
                                                              in1=pC[:, g, cs4, :].to_broadcast([64, 4, 64]), op=ALU.mult),
                             r=[pkG, "pp"], w=["GP"])
                for c in range(8):
                    pbH, pkH = nb()
                    pbO, pkO = nb()
                    pvH = pbH[:, 0:G * 64].rearrange("p (g k) -> p g k", g=G)
                    pvO = pbO[:, 0:G * 64].rearrange("p (g k) -> p g k", g=G)
                    for g in range(G):
                        gc = g * 8 + c
                        P.op("pe", lambda e: e.matmul(pvH[:, g, :], lhsT=PH[:, gc, :], rhs=H[:, h0 + g, :], start=True, stop=True,
                                                      skip_group_check=True), r=["PH", "H"], w=[pkH], sig=(g == G - 1))
                    for g in range(G):
                        gc = g * 8 + c
                        P.op("pe", lambda e: e.matmul(pvO[:, g, :], lhsT=H[:, h0 + g, :], rhs=RP[:, gc, :], start=True, stop=True,
                                                      skip_group_check=True), r=["RP", "H"], w=[pkO], sig=(g == G - 1))
                    gcs = GP[:].rearrange("p (g c) k -> p g c k", g=G)[:, :, c, :]
                    ols = OL[:].rearrange("p (g c) k -> p g c k", g=G)[:, :, c, :]
                    P.op("dve", lambda e: e.tensor_tensor(out=OT[:, :, c * 64:(c + 1) * 64], in0=pvO[:, :, :], in1=ols, op=ALU.add),
                         r=[pkO, "OL"], w=["OT"])
                    P.op("dve", lambda e: e.tensor_tensor(out=H[:, h0:h0 + G, :], in0=pvH[:, :, :],
                                                          in1=pC[:, :, c, :].to_broadcast([64, G, 64]), op=ALU.mult),
                         r=[pkH, "pp", "H"], w=["H"])
                    P.op("dve", lambda e: e.tensor_tensor(out=H[:, h0:h0 + G, :], in0=H[:, h0:h0 + G, :], in1=gcs, op=ALU.add),
                         r=["H", "GP"], w=["H"])
                for g in range(G):
                    pb, pk = nb()
                    P.op("pe", lambda e: e.matmul(pb[:, :], lhsT=onesm[:, :], rhs=OT[:, g, :], start=True, stop=True),
                         r=["onesm", "OT"], w=[pk])
                    P.op("dve", lambda e: e.tensor_tensor(out=OT[:, g, :], in0=OT[:, g, :], in1=pb[:, :], op=ALU.subtract),
                         r=[pk, "OT"], w=["OT"])
                P.op("act", lambda e: e.activation(out=tmp[:], in_=OT[:], func=AF.Square), r=["OT"], w=["tmp"])
                for g in range(G):
                    pb, pk = nb()
                    P.op("pe", lambda e: e.matmul(pb[:, :], lhsT=onesm[:, :], rhs=tmp[:, g, :], start=True, stop=True),
                         r=["onesm", "tmp"], w=[pk])
                    P.op("dve", lambda e: e.tensor_scalar(out=cs[:, g, :], in0=pb[:, :], scalar1=RW_LN_EPS, scalar2=None, op0=ALU.add),
                         r=[pk], w=["cs"])
                P.op("act", lambda e: e.activation(out=cs[:], in_=cs[:], func=AF.Ln), r=["cs"], w=["cs"])
                P.op("act", lambda e: e.activation(out=cs[:], in_=cs[:], func=AF.Exp, scale=-0.5), r=["cs"], w=["cs"])
                P.op("dve", lambda e: e.tensor_tensor(out=OT[:], in0=OT[:], in1=cs[:], op=ALU.mult), r=["OT", "cs"], w=["OT"])
                P.op("dve", lambda e: e.tensor_tensor(out=OT[:], in0=OT[:], in1=bc(cpar[:, :, 5], h0), op=ALU.mult),
                     r=["OT", "cpar"], w=["OT"])
                P.op("dve", lambda e: e.tensor_tensor(out=OT[:], in0=OT[:], in1=bc(cpar[:, :, 6], h0), op=ALU.add),
                     r=["OT", "cpar"], w=["OT"])
                for g in range(G):
                    pb, pk = nb()
                    P.op("pe", lambda e: e.matmul(pb[:, :], lhsT=ones64[:, :], rhs=RK[:, g, :], start=True, stop=True),
                         r=["ones64", "RK"], w=[pk])
                    P.op("dve", lambda e: e.tensor_tensor(out=tmp[:, g, :], in0=pb[:, :], in1=Xv[:, g, :], op=ALU.mult),
                         r=[pk, "Xv"], w=["tmp"])
                P.op("dve", lambda e: e.tensor_tensor(out=OT[:], in0=OT[:], in1=tmp[:], op=ALU.add), r=["OT", "tmp"], w=["OT"])
                P.op("dve", lambda e: e.tensor_tensor(out=yst[:], in0=OT[:], in1=gate[:], op=ALU.mult), r=["OT", "gate"], w=["yst"])
                P.dma("pool", d["yT"][2 * CFG["YB"] + h0 * 64:2 * CFG["YB"] + (h0 + G) * 64, t0:t0 + N].rearrange("(g k) t -> k g t", k=64), yst[:, :, :],
                      r=["yst"], sem="d_ystc")


def phase_mergeA(P, T, l, io, d):
    NT = T // 512
    SP = CFG["SPLIT"]
    KB = CFG["YB"] // 128
    TO = T // SP
    ODT = BF16 if SP == 1 else F32
    with P.phase("mergeA") as ph:
        W = [ph.sb("W%d" % n, [128, KB, 2048], BF16) for n in range(4)]
        yT = [ph.sb("yT%d" % i, [128, 4, KB, 512], BF16) for i in range(SP)]
        gts = [ph.sb("gts%d" % i, [128, 4, 512], BF16) for i in range(2)]
        acc = ph.sb("acc", [128, 512], F32)
        tmp = ph.sb("tmp", [128, 512], F32)
        mst = [ph.sb("mst%d" % i, [128, 512], ODT) for i in range(2)]
        banks = [ph.ps("bk%d" % i, [128, 512]) for i in range(8)]
        for n in range(4):
            P.dma("pool", W[n][:, :, :], io["w_branch"][l, n].rearrange("(kc p) c -> p kc c", p=128), w=["W%d" % n], sem="d_W%d" % n)
        mg = d["MgT"].rearrange("(n c p) t -> p n c t", n=4, p=128)
        yv = d["yT"].rearrange("(n kc p) t -> p n kc t", n=4, p=128)
        bi = 0
        gi = 0
        for tt in range(NT):
            ts = slice(tt * 512, (tt + 1) * 512)
            yi = tt % len(yT)
            yk = "yT%d" % yi
            for n in range(4):
                P.dma("sp", yT[yi][:, n, :, :], yv[:, n, :, ts], w=[yk], sem="d_" + yk)
            for cc in range(16):
                g_ = gi % 2
                gi += 1
                P.dma("sp", gts[g_][:, :, :], mg[:, :, cc, ts], w=["gts%d" % g_], sem="d_gts%d" % g_)
                for n in range(4):
                    b = bi % 8
                    bi += 1
                    for kc in range(KB):
                        P.op("pe", lambda e: e.matmul(banks[b][:, :], lhsT=W[n][:, kc, cc * 128:(cc + 1) * 128], rhs=yT[yi][:, n, kc, :],
                                                      start=(kc == 0), stop=(kc == KB - 1)), r=["W%d" % n, yk], w=["bk%d" % b], sig=(kc == KB - 1))
                    if n == 0:
                        P.op("dve", lambda e: e.tensor_tensor(out=acc[:], in0=banks[b][:, :], in1=gts[g_][:, 0, :], op=ALU.mult),
                             r=["bk%d" % b, "gts%d" % g_], w=["acc"])
                    else:
                        P.op("dve", lambda e: e.tensor_tensor(out=tmp[:], in0=banks[b][:, :], in1=gts[g_][:, n, :], op=ALU.mult),
                             r=["bk%d" % b, "gts%d" % g_], w=["tmp"])
                        if n < 3:
                            P.op("pool", lambda e: e.tensor_tensor(out=acc[:], in0=acc[:], in1=tmp[:], op=ALU.add), r=["acc", "tmp"], w=["acc"])
                        else:
                            P.op("pool", lambda e: e.tensor_tensor(out=mst[g_][:], in0=acc[:], in1=tmp[:], op=ALU.add),
                                 r=["acc", "tmp"], w=["mst%d" % g_])
                if SP == 1:
                    dst = d["mT"][cc * 128:(cc + 1) * 128, ts]
                else:
                    hf, tl = (tt * 512) // TO, (tt * 512) % TO
                    dst = d["mTp"][hf, cc * 128:(cc + 1) * 128, tl:tl + 512]
                P.dma("sp", dst, mst[g_][:, :], r=["mst%d" % g_], sem="d_mst%d" % g_)


def phase_gather_hT(P, T, d):
    NCH = d["NCH"]
    rows = D_MODEL // NCH
    P.barrier()
    for i in range(NCH):
        P.coll("AllGather", d["hT"][i * rows:(i + 1) * rows, :], d["hTg"][i].rearrange("r f t -> (r f) t"))
    P.barrier()


def phase_scatter_mT(P, T, d):
    P.barrier()
    P.coll("ReduceScatter", d["mTp"].rearrange("r c t -> (r c) t"), d["mTs"][:, :])
    P.barrier()


def phase_mergeB(P, T, l, io, d, final_out=None):
    SP = CFG["SPLIT"]
    T = T // SP
    NT = T // 512
    with P.phase("mergeB") as ph:
        wo = ph.sb("wo", [128, KC, 2048], BF16)
        mT = [ph.sb("mT%d" % i, [128, KC, 512], BF16) for i in range(2)]
        ht = [ph.sb("ht%d" % i, [128, D_MODEL], F32) for i in range(2)]
        xx = [ph.sb("xx%d" % i, [128, D_MODEL], F32) for i in range(2)]
        junk = ph.sb("junk", [128, D_MODEL], F32)
        lg = ph.sb("lg", [128, D_MODEL], F32)
        lbt = ph.sb("lbt", [128, D_MODEL], F32)
        stt = [ph.sb("stt%d" % i, [128, 8], F32) for i in range(2)]
        hTs = [ph.sb("hTs%d" % i, [128, KC, 512], BF16) for i in range(2)]
        ident = make_ident(P, ph)
        banks = [ph.ps("bk%d" % i, [128, 512]) for i in range(4)]
        tbanks = [ph.ps("tb%d" % i, [128, 512]) for i in range(4)]
        ctr = [0]
        P.dma("pool", wo[:, :, :], io["w_out"][l].rearrange("(kc p) c -> p kc c", p=128), w=["wo"], sem="d_wo")
        P.dma("sp", lg[:, :], io["ln_g_b"][l], w=["lg"], sem="d_lg")
        P.dma("sp", lbt[:, :], io["ln_b_b"][l], w=["lbt"], sem="d_lbt")
        mv = (d["mT"] if SP == 1 else d["mTs"]).rearrange("(kc p) t -> p kc t", p=128)
        for tt in range(NT):
            mi = tt % 2
            P.dma("sp" if SP == 1 else "pool", mT[mi][:, :, :], mv[:, :, tt * 512:(tt + 1) * 512], w=["mT%d" % mi], sem="d_mT%d" % mi)
            for q in range(4):
                tk = tt * 4 + q
                xi = tk % 2
                rows = slice(tk * 128, (tk + 1) * 128)
                P.dma("sp", ht[xi][:, :], d["h"][rows, :], w=["ht%d" % xi], sem="d_ht%d" % xi)
                for nb_ in range(4):
                    for cc in range(KC):
                        P.op("pe", lambda e: e.matmul(banks[nb_][:, :], lhsT=mT[mi][:, cc, q * 128:(q + 1) * 128],
                                                      rhs=wo[:, cc, nb_ * 512:(nb_ + 1) * 512], start=(cc == 0), stop=(cc == KC - 1)),
                             r=["mT%d" % mi, "wo"], w=["bk%d" % nb_], sig=(cc == KC - 1))
                    P.op("dve", lambda e: e.scalar_tensor_tensor(out=xx[xi][:, nb_ * 512:(nb_ + 1) * 512],
                                                                 in0=ht[xi][:, nb_ * 512:(nb_ + 1) * 512], scalar=ALPHA,
                                                                 in1=banks[nb_][:, :], op0=ALU.mult, op1=ALU.add),
                         r=["ht%d" % xi, "bk%d" % nb_], w=["xx%d" % xi])
                s_ = stt[xi]
                sk = "stt%d" % xi
                P.op("act", lambda e: e.activation(out=junk[:], in_=xx[xi][:], func=AF.Copy, accum_out=s_[:, 0:1]), r=["xx%d" % xi], w=["junk", sk])
                P.op("act", lambda e: e.activation(out=junk[:], in_=xx[xi][:], func=AF.Square, accum_out=s_[:, 1:2]), r=["xx%d" % xi], w=["junk", sk])
                P.op("dve", lambda e: e.tensor_scalar(out=s_[:, 2:4], in0=s_[:, 0:2], scalar1=1.0 / D_MODEL, scalar2=None, op0=ALU.mult),
                     r=[sk], w=[sk])
                P.op("dve", lambda e: e.tensor_tensor(out=s_[:, 4:5], in0=s_[:, 2:3], in1=s_[:, 2:3], op=ALU.mult), r=[sk], w=[sk])
                P.op("dve", lambda e: e.tensor_tensor(out=s_[:, 5:6], in0=s_[:, 3:4], in1=s_[:, 4:5], op=ALU.subtract), r=[sk], w=[sk])
                P.op("dve", lambda e: e.tensor_scalar(out=s_[:, 5:6], in0=s_[:, 5:6], scalar1=LN_EPS, scalar2=None, op0=ALU.add), r=[sk], w=[sk])
                P.op("act", lambda e: e.activation(out=s_[:, 6:7], in_=s_[:, 5:6], func=AF.Ln), r=[sk], w=[sk])
                P.op("act", lambda e: e.activation(out=s_[:, 7:8], in_=s_[:, 6:7], func=AF.Exp, scale=-0.5), r=[sk], w=[sk])
                P.op("dve", lambda e: e.tensor_scalar(out=xx[xi][:], in0=xx[xi][:], scalar1=s_[:, 2:3], scalar2=s_[:, 7:8],
                                                      op0=ALU.subtract, op1=ALU.mult), r=["xx%d" % xi, sk], w=["xx%d" % xi])
                P.op("pool", lambda e: e.tensor_tensor(out=xx[xi][:], in0=xx[xi][:], in1=lg[:], op=ALU.mult), r=["xx%d" % xi, "lg"], w=["xx%d" % xi])
                P.op("dve", lambda e: e.tensor_tensor(out=xx[xi][:], in0=xx[xi][:], in1=lbt[:], op=ALU.add), r=["xx%d" % xi, "lbt"], w=["xx%d" % xi])
                dst = final_out if final_out is not None else d["h"]
                P.dma("pool", dst[rows, :], xx[xi][:, :], r=["xx%d" % xi], w=[], sem="d_xo%d" % xi)
                if final_out is None:
                    emit_transpose_tile(P, xx[xi], "xx%d" % xi, ident, tbanks, ctr, hTs[mi], "hTs%d" % mi, q * 128)
            if final_out is None:
                P.dma("sp", d["hT"].rearrange("(kc p) t -> p kc t", p=128)[:, :, tt * 512:(tt + 1) * 512], hTs[mi][:, :, :],
                      r=["hTs%d" % mi], sem="d_hTs%d" % mi)


def IO_SPECS(T):
    SP = CFG["SPLIT"]
    YB, NHA, NHC = CFG["YB"], CFG["NHA"], CFG["NHC"]
    ncols = IN_COLS if SP == 1 else (4 * YB) * 3 + (3 * YB + 128 + YB) + 8192
    nmu = (3 * YB + 128) // 128
    return {
        "x": ([T // SP, D_MODEL], F32),
        "w_in": ([DEPTH, D_MODEL, ncols], F32),
        "rw_mu_t": ([DEPTH, 128, nmu], F32),
        "rw_vmu_t": ([DEPTH - 1, 128, 1], F32),
        "rw_v1": ([DEPTH - 1, D_MODEL, 32], F32),
        "a_strip": ([NHA, 128, 640], F32),
        "a_c31": ([128, NHA], F32),
        "da_lambda_b": ([DEPTH, 128, 256], F32),
        "da_subln_t": ([DEPTH, 128, 1], F32),
        "hg_lower_t": ([128, NHA, 4], F32),
        "hg_norm_t": ([DEPTH, 128, 1], F32),
        "c_par": ([DEPTH, 64, NHC, 8], F32),
        "rw_w2": ([DEPTH, 64, 64 * NHC], F32),
        "rw_a2": ([DEPTH, 64, 64 * NHC], F32),
        "rw_v2": ([DEPTH - 1, 32, 64 * NHC], F32),
        "w_branch": ([DEPTH, 4, YB, D_MODEL], F32),
        "w_out": ([DEPTH, D_MODEL, D_MODEL], F32),
        "ln_g_b": ([DEPTH, 128, D_MODEL], F32),
        "ln_b_b": ([DEPTH, 128, D_MODEL], F32),
    }


def build_program(T, depth=DEPTH, debug_outs=()):
    SP = CFG["SPLIT"]
    nc = bass.Bass("TRN2", target_bir_lowering=False)
    io = {k: nc.dram_tensor(k, list(s), dt, kind="ExternalInput").ap() for k, (s, dt) in IO_SPECS(T).items()}
    out = nc.dram_tensor("out", [T // SP, D_MODEL], F32, kind="ExternalOutput").ap()
    d = declare_dram(nc, T, debug_outs=debug_outs)
    P = Prog(nc)
    phase_prep0(P, T // SP, io, d)
    for l in range(depth):
        if SP > 1:
            phase_gather_hT(P, T, d)
        phase_gemm(P, T, l, io, d)
        phase_mixA(P, T, l, io, d)
        phase_mixB(P, T, l, io, d)
        phase_mixC(P, T, l, io, d)
        phase_mixD(P, T, l, io, d)
        phase_mergeA(P, T, l, io, d)
        if SP > 1:
            phase_scatter_mT(P, T, d)
        phase_mergeB(P, T, l, io, d, final_out=(out if l == depth - 1 else None))
    P.barrier()
    return nc, P


def host_shared_inputs(inp):
    f = lambda a: np.ascontiguousarray(np.asarray(a, dtype=np.float32))
    L = DEPTH
    sh = {}
    sh["w_in"] = f(inp["w_in"])
    sh["rw_mu_t"] = f(np.asarray(inp["rw_mu"]).reshape(L, 25, 128).transpose(0, 2, 1))
    vmu = np.zeros((L - 1, 128, 1), np.float32)
    vmu[:, :32, 0] = np.asarray(inp["rw_v_mu"])
    sh["rw_vmu_t"] = vmu
    sh["rw_v1"] = f(inp["rw_v1"])
    rel = np.asarray(inp["rel_bias"], dtype=np.float32)
    sh["a_strip"] = host_a_strip(rel)
    sh["a_c31"] = f(np.broadcast_to(rel[31][None, :], (128, 8)))
    sh["da_lambda_b"] = f(np.broadcast_to(np.asarray(inp["da_lambda"]).reshape(L, 1, 256), (L, 128, 256)))
    sh["da_subln_t"] = f(np.asarray(inp["da_subln"]).reshape(L, 128, 1))
    sh["hg_lower_t"] = f(np.asarray(inp["hg_lower"]).reshape(L, 8, 128).transpose(2, 1, 0))
    sh["hg_norm_t"] = f(np.asarray(inp["hg_norm"]).reshape(L, 128, 1))
    hk = lambda a: np.asarray(a).reshape(L, 16, 64).transpose(0, 2, 1)
    v0 = np.zeros((L, 1024), np.float32)
    v0[1:] = np.asarray(inp["rw_v0"])
    cp = np.stack([hk(inp["rw_w0"]), hk(inp["rw_a0"]), hk(inp["rw_kk"]), hk(inp["rw_ka"]),
                   np.asarray(inp["rw_rk"]).transpose(0, 2, 1), hk(inp["rw_lnx_g"]), hk(inp["rw_lnx_b"]), hk(v0)], axis=-1)
    sh["c_par"] = f(cp)
    sh["rw_w2"] = f(inp["rw_w2"])
    sh["rw_a2"] = f(inp["rw_a2"])
    sh["rw_v2"] = f(inp["rw_v2"])
    sh["w_branch"] = f(inp["w_branch"])
    sh["w_out"] = f(inp["w_out"])
    sh["ln_g_b"] = f(np.broadcast_to(np.asarray(inp["ln_g"])[:, None, :], (L, 128, D_MODEL)))
    sh["ln_b_b"] = f(np.broadcast_to(np.asarray(inp["ln_b"])[:, None, :], (L, 128, D_MODEL)))
    return sh


def host_core_inputs(sh, hh):
    SP = CFG["SPLIT"]
    if SP == 1:
        return dict(sh)
    YB, NHA, NHC = CFG["YB"], CFG["NHA"], CFG["NHC"]
    a = slice(hh * YB, (hh + 1) * YB)
    segs = []
    for base in (O_AQ, O_AK, O_AV, O_AG, O_HQ, O_HF, O_HI, O_HG):
        segs.append(np.arange(base + hh * YB, base + (hh + 1) * YB))
    for j in range(3):
        segs.append(np.arange(O_RM + j * 1024 + hh * YB, O_RM + j * 1024 + (hh + 1) * YB))
    segs.append(np.arange(O_RM + 3072, O_RM + 3200))
    segs.append(np.arange(O_RG + hh * YB, O_RG + (hh + 1) * YB))
    for base in (O_SQ, O_SK, O_SV, O_SG):
        segs.append(np.arange(base + hh * YB, base + (hh + 1) * YB))
    segs.append(np.arange(O_MG, O_MG + 8192))
    cols = np.concatenate(segs)
    m = {}
    m["w_in"] = np.ascontiguousarray(sh["w_in"][:, :, cols])
    mu_full = sh["rw_mu_t"]
    blk = []
    for j in range(3):
        blk += list(range(j * 8 + hh * (YB // 128), j * 8 + (hh + 1) * (YB // 128)))
    blk.append(24)
    m["rw_mu_t"] = np.ascontiguousarray(mu_full[:, :, blk])
    m["rw_vmu_t"] = sh["rw_vmu_t"]
    m["rw_v1"] = sh["rw_v1"]
    ha = slice(hh * NHA, (hh + 1) * NHA)
    hc = slice(hh * NHC, (hh + 1) * NHC)
    m["a_strip"] = np.ascontiguousarray(sh["a_strip"][ha])
    m["a_c31"] = np.ascontiguousarray(sh["a_c31"][:, ha])
    m["da_lambda_b"] = sh["da_lambda_b"]
    m["da_subln_t"] = sh["da_subln_t"]
    m["hg_lower_t"] = np.ascontiguousarray(sh["hg_lower_t"][:, ha, :])
    m["hg_norm_t"] = sh["hg_norm_t"]
    m["c_par"] = np.ascontiguousarray(sh["c_par"][:, :, hc, :])
    m["rw_w2"] = np.ascontiguousarray(sh["rw_w2"][:, :, a])
    m["rw_a2"] = np.ascontiguousarray(sh["rw_a2"][:, :, a])
    m["rw_v2"] = np.ascontiguousarray(sh["rw_v2"][:, :, a])
    m["w_branch"] = np.ascontiguousarray(sh["w_branch"][:, :, a, :])
    m["w_out"] = sh["w_out"]
    m["ln_g_b"] = sh["ln_g_b"]
    m["ln_b_b"] = sh["ln_b_b"]
    return m


_CACHE = {}
SPLIT = 2


def kernel(**inputs):
    x = np.asarray(inputs["x"], dtype=np.float32)
    B, T, _ = x.shape
    ncore = B * SPLIT
    configure(SPLIT, [[2 * i, 2 * i + 1] for i in range(ncore // 2)] if SPLIT == 2 else None)
    if T not in _CACHE:
        _CACHE[T] = build_program(T)
    nc, _ = _CACHE[T]
    sh = host_shared_inputs(inputs)
    per_half = [host_core_inputs(sh, hh) for hh in range(SPLIT)]
    TO = T // SPLIT
    in_maps = []
    for b in range(B):
        for hh in range(SPLIT):
            m = dict(per_half[hh])
            m["x"] = np.ascontiguousarray(x[b, hh * TO:(hh + 1) * TO])
            in_maps.append(m)
    res = run_bass_kernel_spmd(nc, in_maps, core_ids=list(range(ncore)))
    out = np.empty((B, T, D_MODEL), np.float32)
    for b in range(B):
        for hh in range(SPLIT):
            out[b, hh * TO:(hh + 1) * TO] = np.asarray(res.results[b * SPLIT + hh]["out"], dtype=np.float32)
    return out
```

```python
import contextlib
import math
import numpy as np
import concourse.bass as bass
import concourse.mybir as mybir
from concourse.bass_utils import run_bass_kernel_spmd

F32 = mybir.dt.float32
BF16 = mybir.dt.bfloat16
AF = mybir.ActivationFunctionType
ALU = mybir.AluOpType
AX = mybir.AxisListType

D_MODEL = 2048
DEPTH = 4
MIXW = 1024
IN_COLS = 24704
KC = D_MODEL // 128
ALPHA = (2 * DEPTH) ** 0.25
LN_EPS = 1e-5
RMS_EPS = 1e-6
RW_LN_EPS = 64e-5
NEG = -30000.0

CFG = dict(SPLIT=1, YB=1024, NHA=8, NHC=16, GROUPS=None)


def configure(split, groups=None):
    CFG.update(SPLIT=split, YB=1024 // split, NHA=8 // split, NHC=16 // split, GROUPS=groups)


O_AQ, O_AK, O_AV, O_AG = 0, 1024, 2048, 3072
O_HQ, O_HF, O_HI, O_HG = 4096, 5120, 6144, 7168
O_RM, O_RG = 8192, 11392
O_SQ, O_SK, O_SV, O_SG = 12416, 13440, 14464, 15488
O_MG = 16512


class Prog:
    ENG = ("pe", "act", "dve", "pool", "sp")

    def __init__(self, nc):
        self.nc = nc
        self.E = dict(pe=nc.tensor, act=nc.scalar, dve=nc.vector, pool=nc.gpsimd, sp=nc.sync)
        self.sems = {}
        self.semval = {}
        self.known = {e: {} for e in self.ENG}
        self.res = {}
        self.pend = {e: ([], []) for e in self.ENG}
        self.stack = contextlib.ExitStack()
        self.n_inst = 0
        self.uid = 0
        self.ecount = {e: 0 for e in self.ENG}
        self.marks = []

    def sem(self, key):
        if key not in self.sems:
            self.sems[key] = self.stack.enter_context(self.nc.semaphore("s_" + key))
            self.semval[key] = 0
        return self.sems[key]

    def _res(self, k):
        r = self.res.get(k)
        if r is None:
            r = [None, {}]
            self.res[k] = r
        return r

    def _wait(self, eng, tok):
        if tok is None:
            return
        sk, v = tok
        if eng == "pe" and sk == "pe":
            return
        if self.known[eng].get(sk, 0) >= v:
            return
        self.E[eng].wait_ge(self.sems[sk], v)
        self.known[eng][sk] = v
        self.n_inst += 1

    def _deps(self, eng, r, w):
        for k in r:
            self._wait(eng, self._res(k)[0])
        for k in w:
            rr = self._res(k)
            self._wait(eng, rr[0])
            for sk, v in list(rr[1].items()):
                self._wait(eng, (sk, v))

    def op(self, eng, fn, r=(), w=(), sig=True):
        self._deps(eng, r, w)
        inst = fn(self.E[eng])
        self.n_inst += 1
        self.ecount[eng] += 1
        pr, pw = self.pend[eng]
        pr.extend(r)
        pw.extend(w)
        if not sig:
            return inst
        self.sem(eng)
        self.semval[eng] += 1
        inst.then_inc(self.sems[eng], 1)
        tok = (eng, self.semval[eng])
        for k in pr:
            rr = self._res(k)
            rr[1][eng] = tok[1]
        for k in pw:
            rr = self._res(k)
            rr[0] = tok
            rr[1] = {}
        pr.clear()
        pw.clear()
        return inst

    def dma(self, q, out, in_, r=(), w=(), sem=None):
        assert sem is not None
        self._deps(q, r, w)
        self.sem(sem)
        inst = self.E[q].dma_start(out=out, in_=in_)
        self.n_inst += 1
        self.semval[sem] += 16
        inst.then_inc(self.sems[sem], 16)
        tok = (sem, self.semval[sem])
        for k in r:
            self._res(k)[1][sem] = tok[1]
        for k in w:
            rr = self._res(k)
            rr[0] = tok
            rr[1] = {}
        return inst

    def coll(self, kind, in_ap, out_ap):
        self.sem("cc")
        alu = ALU.add if kind == "ReduceScatter" else ALU.bypass
        inst = self.E["pool"].collective_compute(kind, alu, replica_groups=CFG["GROUPS"], ins=[in_ap.opt()], outs=[out_ap.opt()])
        self.n_inst += 1
        self.semval["cc"] += 1
        inst.then_inc(self.sems["cc"], 1)
        self._wait("pool", ("cc", self.semval["cc"]))

    def barrier(self):
        for e in self.ENG:
            assert not self.pend[e][0] and not self.pend[e][1], "pending unsignalled ops at barrier"
        for e in self.ENG:
            for sk, v in self.semval.items():
                if v > 0:
                    self._wait(e, (sk, v))
        self.res = {}

    @contextlib.contextmanager
    def phase(self, name=""):
        self.barrier()
        self.marks.append((name, dict(self.ecount)))
        st = contextlib.ExitStack()
        ph = Phase(self, st)
        try:
            yield ph
        finally:
            self.barrier()
            st.close()


class Phase:
    def __init__(self, prog, st):
        self.p = prog
        self.st = st

    def sb(self, name, shape, dt):
        self.p.uid += 1
        return self.st.enter_context(self.p.nc.sbuf_tensor("%s_%d" % (name, self.p.uid), list(shape), dt))

    def ps(self, name, shape, dt=F32):
        self.p.uid += 1
        return self.st.enter_context(self.p.nc.psum_tensor("%s_%d" % (name, self.p.uid), list(shape), dt))


def declare_dram(nc, T, debug_outs=(), debug_ins=()):
    d = {}

    def t(name, shape, dt, kind="Internal"):
        if name in debug_outs:
            kind = "ExternalOutput"
        if name in debug_ins:
            kind = "ExternalInput"
        d[name] = nc.dram_tensor(name, list(shape), dt, kind=kind).ap()

    YB = CFG["YB"]
    SP = CFG["SPLIT"]
    TO = T // SP
    t("hT", [D_MODEL, TO], BF16)
    t("h", [TO, D_MODEL], F32)
    if SP > 1:
        NCH = max(1, (D_MODEL * TO * 2) // (2 << 20))
        d["NCH"] = NCH
        t("hTg", [NCH, SP, D_MODEL // NCH, TO], BF16)
        t("mTp", [SP, D_MODEL, TO], F32)
        t("mTs", [D_MODEL, TO], F32)
    t("h0g", [SP, D_MODEL], F32)
    t("Bd0", [1, 8], F32)
    t("AqT", [YB, T], BF16)
    t("AkT", [YB, T], BF16)
    t("Av", [T, YB], BF16)
    t("AgT", [YB, T], BF16)
    t("BqT", [YB, T], BF16)
    t("BfT", [YB, T], F32)
    t("Bi", [T, YB], BF16)
    t("BgT", [YB, T], BF16)
    t("CmT", [3 * YB + 128, T], F32)
    t("CgT", [YB, T], BF16)
    t("CvdT", [32, T], F32)
    t("Cvf", [YB, T], F32)
    t("DqT", [YB, T], BF16)
    t("DkT", [YB, T], BF16)
    t("Dv", [T, YB], BF16)
    t("DgT", [YB, T], BF16)
    t("MgT", [8192, T], BF16)
    t("yT", [4 * YB, T], BF16)
    t("mT", [D_MODEL, T], BF16)
    return d


def phase_gemm(P, T, l, io, d):
    nc = P.nc
    NT = T // 512
    with P.phase("gemm") as ph:
        hT = ph.sb("hT", [128, KC, T + 1], BF16)
        wsb = [ph.sb("wsb%d" % i, [128, KC, 512], BF16) for i in range(2)]
        SC = min(T, 2048)
        stg = [ph.sb("stg%d" % i, [128, SC], F32) for i in range(2)]
        stb = [ph.sb("stb%d" % i, [128, max(SC, 1024)], BF16) for i in range(2)]
        tmp = [ph.sb("tmp%d" % i, [128, 512], F32) for i in range(2)]
        mu = ph.sb("mu", [128, 26], F32)
        om = ph.sb("om", [128, 26], F32)
        banks = [ph.ps("bk%d" % i, [128, 512]) for i in range(8)]

        P.op("pool", lambda e: e.memset(hT[:, :, 0:1], 0.0), w=["hT"])
        if CFG["SPLIT"] == 1:
            P.dma("sp", hT[:, :, 1:T + 1], d["hT"].rearrange("(kc p) t -> p kc t", p=128), w=["hT"], sem="d_hT")
        else:
            NCH = d["NCH"]
            TO = T // CFG["SPLIT"]
            JJ = KC // NCH
            for r_ in range(CFG["SPLIT"]):
                for i_ in range(NCH):
                    P.dma("sp", hT[:, i_ * JJ:(i_ + 1) * JJ, 1 + r_ * TO:1 + (r_ + 1) * TO],
                          d["hTg"][i_, r_].rearrange("(j p) t -> p j t", p=128), w=["hT"], sem="d_hT")
        NMU = io["rw_mu_t"].shape[2]
        P.op("pool", lambda e: e.memset(mu[:], 0.0), w=["mu"])
        P.dma("sp", mu[:, 0:NMU], io["rw_mu_t"][l], w=["mu"], sem="d_mu")
        if l > 0:
            P.dma("sp", mu[:, 25:26], io["rw_vmu_t"][l - 1], w=["mu"], sem="d_mu")
        else:
            P.op("pool", lambda e: e.memset(mu[:, 25:26], 0.0), w=["mu"])
        P.op("dve", lambda e: e.tensor_scalar(out=om[:], in0=mu[:], scalar1=-1.0, scalar2=1.0,
                                              op0=ALU.mult, op1=ALU.add), r=["mu"], w=["om"])

        w_l = io["w_in"][l].rearrange("(kc p) c -> p kc c", p=128)
        st = dict(wi=0, bi=0, si=0, ti=0)

        def load_w(src_ap, ncols):
            i = st["wi"] % 2
            st["wi"] += 1
            P.dma("pool", wsb[i][:, :, 0:ncols], src_ap, w=["wsb%d" % i], sem="d_wsb%d" % i)
            return i

        def bank():
            b = st["bi"] % 8
            st["bi"] += 1
            return b

        def mm_F(wi, c0, tt, shift, b):
            off = 0 if shift else 1
            for kc in range(KC):
                P.op("pe", lambda e, kc=kc: e.matmul(banks[b][:, :], lhsT=wsb[wi][:, kc, c0:c0 + 128],
                                                     rhs=hT[:, kc, off + tt * 512: off + tt * 512 + 512],
                                                     start=(kc == 0), stop=(kc == KC - 1)),
                     r=["wsb%d" % wi, "hT"], w=["bk%d" % b], sig=(kc == KC - 1))

        def job_F(col0, ncols, dest, drow0, kind, scale=1.0, mucol0=None, out_dt=BF16, nrows=128):
            for s0 in range(0, ncols, 512):
                sw = min(512, ncols - s0)
                wi = load_w(w_l[:, :, col0 + s0: col0 + s0 + sw], sw)
                for c0 in range(0, sw, 128):
                    for tt in range(NT):
                        if (tt * 512) % SC == 0:
                            si = st["si"] % 2
                            st["si"] += 1
                            so = stb[si] if out_dt == BF16 else stg[si]
                            skey = ("stb%d" if out_dt == BF16 else "stg%d") % si
                        b = bank()
                        mm_F(wi, c0, tt, False, b)
                        lo = (tt * 512) % SC
                        osl = so[:, lo:lo + 512]
                        if kind == "copy":
                            P.op("dve", lambda e: e.tensor_copy(out=osl, in_=banks[b][:, :]),
                                 r=["bk%d" % b], w=[skey])
                        elif kind == "scale":
                            P.op("act", lambda e: e.activation(out=osl, in_=banks[b][:, :], func=AF.Copy, scale=scale),
                                 r=["bk%d" % b], w=[skey])
                        elif kind == "silu":
                            P.op("act", lambda e: e.activation(out=osl, in_=banks[b][:, :], func=AF.Silu),
                                 r=["bk%d" % b], w=[skey])
                        elif kind == "sigmoid":
                            P.op("act", lambda e: e.activation(out=osl, in_=banks[b][:, :], func=AF.Sigmoid),
                                 r=["bk%d" % b], w=[skey])
                        elif kind == "shift":
                            b2 = bank()
                            mm_F(wi, c0, tt, True, b2)
                            mc = mucol0 + (s0 + c0) // 128
                            ti = st["ti"] % 2
                            st["ti"] += 1
                            P.op("dve", lambda e: e.tensor_scalar(out=tmp[ti][:], in0=banks[b2][:, :],
                                                                  scalar1=mu[:, mc:mc + 1], scalar2=None, op0=ALU.mult),
                                 r=["bk%d" % b2, "mu"], w=["tmp%d" % ti])
                            P.op("dve", lambda e: e.scalar_tensor_tensor(out=osl, in0=banks[b][:, :], scalar=om[:, mc:mc + 1],
                                                                         in1=tmp[ti][:], op0=ALU.mult, op1=ALU.add),
                                 r=["bk%d" % b, "om", "tmp%d" % ti], w=[skey])
                        else:
                            raise ValueError(kind)
                        if (tt * 512 + 512) % SC == 0:
                            r0 = drow0 + s0 + c0
                            t_lo = tt * 512 + 512 - SC
                            P.dma("sp", dest[r0:r0 + nrows, t_lo:t_lo + SC], so[0:nrows, 0:SC], r=[skey], w=[], sem="d_" + skey)

        def job_T(col0, dest):
            YBl = CFG["YB"]
            wis = []
            for s0 in range(0, YBl, 512):
                wis.append(load_w(w_l[:, :, col0 + s0: col0 + s0 + 512], 512))
            for tk in range(T // 128):
                si = st["si"] % 2
                st["si"] += 1
                for h2 in range(YBl // 512):
                    b = bank()
                    for kc in range(KC):
                        P.op("pe", lambda e, kc=kc: e.matmul(banks[b][:, :], lhsT=hT[:, kc, 1 + tk * 128: 1 + tk * 128 + 128],
                                                             rhs=wsb[wis[h2]][:, kc, :], start=(kc == 0), stop=(kc == KC - 1)),
                             r=["wsb%d" % wis[h2], "hT"], w=["bk%d" % b], sig=(kc == KC - 1))
                    P.op("dve", lambda e: e.tensor_copy(out=stb[si][:, h2 * 512:(h2 + 1) * 512], in_=banks[b][:, :]),
                         r=["bk%d" % b], w=["stb%d" % si])
                P.dma("sp", dest[tk * 128:(tk + 1) * 128, :], stb[si][:, 0:YBl], r=["stb%d" % si], sem="d_stb%d" % si)

        YB = CFG["YB"]
        if CFG["SPLIT"] == 1:
            o = dict(AQ=O_AQ, AK=O_AK, AV=O_AV, AG=O_AG, HQ=O_HQ, HF=O_HF, HI=O_HI, HG=O_HG, RM=O_RM, RG=O_RG,
                     SQ=O_SQ, SK=O_SK, SV=O_SV, SG=O_SG, MG=O_MG)
        else:
            cb = 4 * YB
            cc_ = 2 * cb
            cd = cc_ + 3 * YB + 128 + YB
            o = dict(AQ=0, AK=YB, AV=2 * YB, AG=3 * YB, HQ=cb, HF=cb + YB, HI=cb + 2 * YB, HG=cb + 3 * YB,
                     RM=cc_, RG=cc_ + 3 * YB + 128, SQ=cd, SK=cd + YB, SV=cd + 2 * YB, SG=cd + 3 * YB, MG=cd + 4 * YB)
        job_F(o["AQ"], YB, d["AqT"], 0, "scale", scale=0.125)
        job_F(o["AK"], YB, d["AkT"], 0, "copy")
        job_T(o["AV"], d["Av"])
        job_F(o["AG"], YB, d["AgT"], 0, "silu")
        job_F(o["HQ"], YB, d["BqT"], 0, "copy")
        job_F(o["HF"], YB, d["BfT"], 0, "copy", out_dt=F32)
        job_T(o["HI"], d["Bi"])
        job_F(o["HG"], YB, d["BgT"], 0, "silu")
        job_F(o["RM"], 3 * YB + 128, d["CmT"], 0, "shift", mucol0=0, out_dt=F32)
        job_F(o["RG"], YB, d["CgT"], 0, "silu")
        job_F(o["SQ"], YB, d["DqT"], 0, "scale", scale=128 ** -0.5)
        job_F(o["SK"], YB, d["DkT"], 0, "copy")
        job_T(o["SV"], d["Dv"])
        job_F(o["SG"], YB, d["DgT"], 0, "silu")
        job_F(o["MG"], 8192, d["MgT"], 0, "sigmoid")
        if l > 0:
            i = st["wi"] % 2
            st["wi"] += 1
            P.op("pool", lambda e: e.memset(wsb[i][:, :, 0:128], 0.0), w=["wsb%d" % i])
            P.dma("pool", wsb[i][:, :, 0:32], io["rw_v1"][l - 1].rearrange("(kc p) c -> p kc c", p=128),
                  w=["wsb%d" % i], sem="d_wsb%d" % i)
            for tt in range(NT):
                if (tt * 512) % SC == 0:
                    si = st["si"] % 2
                    st["si"] += 1
                b = bank()
                mm_F(i, 0, tt, False, b)
                b2 = bank()
                mm_F(i, 0, tt, True, b2)
                ti = st["ti"] % 2
                st["ti"] += 1
                lo = (tt * 512) % SC
                osl = stg[si][:, lo:lo + 512]
                P.op("dve", lambda e: e.tensor_scalar(out=tmp[ti][:], in0=banks[b2][:, :], scalar1=mu[:, 25:26],
                                                      scalar2=None, op0=ALU.mult), r=["bk%d" % b2, "mu"], w=["tmp%d" % ti])
                P.op("dve", lambda e: e.scalar_tensor_tensor(out=osl, in0=banks[b][:, :], scalar=om[:, 25:26], in1=tmp[ti][:],
                                                             op0=ALU.mult, op1=ALU.add),
                     r=["bk%d" % b, "om", "tmp%d" % ti], w=["stg%d" % si])
                if (tt * 512 + 512) % SC == 0:
                    t_lo = tt * 512 + 512 - SC
                    P.dma("sp", d["CvdT"][:, t_lo:t_lo + SC], stg[si][0:32, 0:SC], r=["stg%d" % si], sem="d_stg%d" % si)


def emit_transpose_tile(P, src, src_key, ident, banks, bank_ctr, hTs, hTs_key, col0):
    for g in range(4):
        b = bank_ctr[0] % len(banks)
        bank_ctr[0] += 1
        for j in range(4):
            kc = g * 4 + j
            P.op("pe", lambda e: e.transpose(out=banks[b][:, j * 128:(j + 1) * 128], in_=src[:, kc * 128:(kc + 1) * 128],
                                             identity=ident[:, :]),
                 r=[src_key, "ident"], w=["tb%d" % b], sig=(j == 3))
        eng = "act" if g % 2 else "dve"
        if eng == "act":
            P.op("act", lambda e: e.activation(out=hTs[:, g * 4:(g + 1) * 4, col0:col0 + 128],
                                               in_=banks[b][:, :].rearrange("p (j t) -> p j t", j=4), func=AF.Copy),
                 r=["tb%d" % b], w=[hTs_key])
        else:
            P.op("dve", lambda e: e.tensor_copy(out=hTs[:, g * 4:(g + 1) * 4, col0:col0 + 128],
                                                in_=banks[b][:, :].rearrange("p (j t) -> p j t", j=4)),
                 r=["tb%d" % b], w=[hTs_key])


def make_ident(P, ph, dt=F32, name="ident"):
    ident = ph.sb(name, [128, 128], dt)
    if dt == F32:
        P.op("pool", lambda e: e.memset(ident[:], 1.0), w=[name])
        P.op("pool", lambda e: e.affine_select(out=ident[:], in_=ident[:], pattern=[[-1, 128]], compare_op=ALU.is_equal,
                                               fill=0.0, base=0, channel_multiplier=1), r=[name], w=[name])
    else:
        tmpi = ph.sb(name + "_f", [128, 128], F32)
        P.op("pool", lambda e: e.memset(tmpi[:], 1.0), w=[name + "_f"])
        P.op("pool", lambda e: e.affine_select(out=tmpi[:], in_=tmpi[:], pattern=[[-1, 128]], compare_op=ALU.is_equal,
                                               fill=0.0, base=0, channel_multiplier=1), r=[name + "_f"], w=[name + "_f"])
        P.op("dve", lambda e: e.tensor_copy(out=ident[:], in_=tmpi[:]), r=[name + "_f"], w=[name])
    return ident


def phase_prep0(P, T, io, d):
    with P.phase("prep0") as ph:
        ident = make_ident(P, ph)
        banks = [ph.ps("tb%d" % i, [128, 512]) for i in range(4)]
        xt = [ph.sb("xt%d" % i, [128, D_MODEL], F32) for i in range(2)]
        hTs = [ph.sb("hTs%d" % i, [128, KC, 512], BF16) for i in range(2)]
        ctr = [0]
        for tt in range(T // 512):
            hi = tt % 2
            for q in range(4):
                tk = tt * 4 + q
                xi = tk % 2
                P.dma("sp", xt[xi][:, :], io["x"][tk * 128:(tk + 1) * 128, :], w=["xt%d" % xi], sem="d_xt%d" % xi)
                emit_transpose_tile(P, xt[xi], "xt%d" % xi, ident, banks, ctr, hTs[hi], "hTs%d" % hi, q * 128)
                P.dma("pool", d["h"][tk * 128:(tk + 1) * 128, :], xt[xi][:, :], r=["xt%d" % xi], sem="d_xo%d" % xi)
            P.dma("sp", d["hT"].rearrange("(kc p) t -> p kc t", p=128)[:, :, tt * 512:(tt + 1) * 512], hTs[hi][:, :, :],
                  r=["hTs%d" % hi], sem="d_hTs%d" % hi)


def t5_bucket_np(dist):
    n = np.maximum(dist, 0)
    nf = np.maximum(n, 1).astype(np.float32)
    large = 16 + (np.log(nf / np.float32(16)) / np.float32(math.log(128 / 16)) * np.float32(16)).astype(np.int32)
    large = np.minimum(large, 31)
    return np.where(n < 16, n, large)


def phase_mixA(P, T, l, io, d, NH=None):
    NH = NH or CFG["NHA"]
    lam_init = 0.8 - 0.6 * math.exp(-0.3 * l)
    NG = T // 512
    NKB = T // 128
    with P.phase("mixA") as ph:
        qT = [ph.sb("qT%d" % i, [128, T], BF16) for i in range(2)]
        kT = [ph.sb("kT%d" % i, [128, T], BF16) for i in range(2)]
        V = [ph.sb("V%d" % i, [128, NKB, 128], BF16) for i in range(2)]
        gT = [ph.sb("gT%d" % i, [128, T], BF16) for i in range(2)]
        stf = [ph.sb("stf%d" % i, [128, 640], F32) for i in range(2)]
        shi = [ph.sb("shi%d" % i, [128, 640], BF16) for i in range(2)]
        slo = [ph.sb("slo%d" % i, [128, 640], BF16) for i in range(2)]
        yst = [ph.sb("yst%d" % i, [128, T], BF16) for i in range(2)]
        pT = [ph.sb("pT%d" % i, [128, 512], BF16) for i in range(3)]
        wk = {n: ph.sb(n, [128, 512], F32) for n in ("rl1", "rl2", "a1", "a2", "sq", "t1")}
        identb = make_ident(P, ph, BF16, "identb")
        onesf = ph.sb("onesf", [128, 128], F32)
        onesb = ph.sb("onesb", [128, 128], BF16)
        c31 = ph.sb("c31", [128, CFG["NHA"]], F32)
        lam = ph.sb("lam", [128, 256], F32)
        lw = ph.sb("lamw", [128, 128], F32)
        sc = ph.sb("lamsc", [128, 8], F32)
        sub = ph.sb("subln", [128, 1], F32)
        sbk = [ph.ps("sbk%d" % i, [128, 512]) for i in range(3)]
        obk = [ph.ps("obk%d" % i, [128, 512]) for i in range(2)]
        lbk = [ph.ps("lbk%d" % i, [128, 512]) for i in range(2)]

        P.op("pool", lambda e: e.memset(onesf[:], 1.0), w=["onesf"])
        P.op("pool", lambda e: e.memset(onesb[:], 1.0), w=["onesb"])
        P.dma("sp", c31[:, :], io["a_c31"], w=["c31"], sem="d_c31")
        P.dma("sp", lam[:, :], io["da_lambda_b"][l], w=["lam"], sem="d_lam")
        P.dma("sp", sub[:, :], io["da_subln_t"][l], w=["subln"], sem="d_sub")
        P.op("dve", lambda e: e.tensor_tensor(out=lw[:, 0:64], in0=lam[:, 0:64], in1=lam[:, 64:128], op=ALU.mult), r=["lam"], w=["lamw"])
        P.op("dve", lambda e: e.tensor_tensor(out=lw[:, 64:128], in0=lam[:, 128:192], in1=lam[:, 192:256], op=ALU.mult), r=["lam"], w=["lamw"])
        P.op("dve", lambda e: e.reduce_sum(out=sc[:, 0:2], in_=lw[:].rearrange("p (a b) -> p a b", a=2), axis=AX.X), r=["lamw"], w=["lamsc"])
        P.op("act", lambda e: e.activation(out=sc[:, 2:4], in_=sc[:, 0:2], func=AF.Exp), r=["lamsc"], w=["lamsc"])
        P.op("dve", lambda e: e.tensor_tensor(out=sc[:, 4:5], in0=sc[:, 3:4], in1=sc[:, 2:3], op=ALU.subtract), r=["lamsc"], w=["lamsc"])
        P.op("dve", lambda e: e.tensor_scalar(out=sc[:, 5:6], in0=sc[:, 4:5], scalar1=-lam_init, scalar2=None, op0=ALU.add), r=["lamsc"], w=["lamsc"])
        P.op("dve", lambda e: e.tensor_scalar(out=sc[:, 6:7], in0=sub[:, 0:1], scalar1=1.0 - lam_init, scalar2=None, op0=ALU.mult), r=["subln", "lamsc"], w=["lamsc"])
        negl = sc[:, 5:6]
        gsc = sc[:, 6:7]

        def load_head(h):
            i = h % 2
            P.dma("sp", qT[i][:, :], d["AqT"][h * 128:(h + 1) * 128, :], w=["qT%d" % i], sem="d_qT%d" % i)
            P.dma("sp", kT[i][:, :], d["AkT"][h * 128:(h + 1) * 128, :], w=["kT%d" % i], sem="d_kT%d" % i)
            P.dma("sp", V[i][:, :, :], d["Av"].rearrange("(kb p) c -> p kb c", p=128)[:, :, h * 128:(h + 1) * 128],
                  w=["V%d" % i], sem="d_V%d" % i)
            P.dma("sp", gT[i][:, :], d["AgT"][h * 128:(h + 1) * 128, :], w=["gT%d" % i], sem="d_gT%d" % i)
            P.dma("sp", stf[i][:, :], io["a_strip"][h], w=["stf%d" % i], sem="d_stf%d" % i)
            P.op("dve", lambda e: e.tensor_copy(out=shi[i][:], in_=stf[i][:]), r=["stf%d" % i], w=["shi%d" % i])
            P.op("dve", lambda e: e.tensor_tensor(out=stf[i][:], in0=stf[i][:], in1=shi[i][:], op=ALU.subtract),
                 r=["stf%d" % i, "shi%d" % i], w=["stf%d" % i])
            P.op("dve", lambda e: e.tensor_copy(out=slo[i][:], in_=stf[i][:]), r=["stf%d" % i], w=["slo%d" % i])

        cnt = dict(s=0, p=0)
        load_head(0)
        for h in range(NH):
            i = h % 2
            if h + 1 < NH:
                load_head(h + 1)
            for g in range(NG):
                q0 = g * 512
                blocks = [(m, ki) for m in range(2) for ki in range(4 * g + 4)]
                nblk = 4 * g + 4
                info = {}

                def stage1(bd):
                    m, ki = bd
                    pb = slice(m * 64, m * 64 + 64)
                    j = ki - 4 * g
                    near = j >= -1
                    c0 = 128 * j if j >= 1 else 0
                    n = 512 - c0
                    sb_i = cnt["s"] % 3
                    cnt["s"] += 1
                    S = sbk[sb_i]
                    skey = "sbk%d" % sb_i
                    P.op("pe", lambda e: e.matmul(S[:, c0:512], lhsT=kT[i][pb, ki * 128:(ki + 1) * 128],
                                                  rhs=qT[i][pb, q0 + c0:q0 + 512], start=True, stop=not near),
                         r=["kT%d" % i, "qT%d" % i], w=[skey], sig=not near)
                    if near:
                        so = 128 if j == -1 else 0
                        P.op("pe", lambda e: e.matmul(S[:, c0:512], lhsT=identb[:, :], rhs=shi[i][:, so:so + n],
                                                      start=False, stop=False), r=["identb", "shi%d" % i], w=[skey], sig=False)
                        P.op("pe", lambda e: e.matmul(S[:, c0:512], lhsT=identb[:, :], rhs=slo[i][:, so:so + n],
                                                      start=False, stop=True), r=["identb", "slo%d" % i], w=[skey])
                    info[bd] = (S, skey, near, c0)

                def stage2(bd):
                    m, ki = bd
                    S, skey, near, c0 = info[bd]
                    p_i = cnt["p"] % 3
                    cnt["p"] += 1
                    pk = "pT%d" % p_i
                    if near:
                        P.op("act", lambda e: e.activation(out=pT[p_i][:, c0:512], in_=S[:, c0:512], func=AF.Exp),
                             r=[skey], w=[pk])
                    else:
                        P.op("act", lambda e: e.activation(out=pT[p_i][:, c0:512], in_=S[:, c0:512], func=AF.Exp,
                                                           bias=c31[:, h:h + 1]), r=[skey, "c31"], w=[pk])
                    info[bd] = (p_i, pk, c0)

                def stage3(bd):
                    m, ki = bd
                    p_i, pk, c0 = info.pop(bd)
                    P.op("pe", lambda e: e.matmul(obk[m][:, c0:512], lhsT=V[i][:, ki, :], rhs=pT[p_i][:, c0:512],
                                                  start=(ki == 0), stop=(ki == nblk - 1), skip_group_check=True),
                         r=["V%d" % i, pk], w=["obk%d" % m], sig=False)
                    P.op("pe", lambda e: e.matmul(lbk[m][:, c0:512], lhsT=onesb[:, :], rhs=pT[p_i][:, c0:512],
                                                  start=(ki == 0), stop=(ki == nblk - 1), skip_group_check=True),
                         r=["onesb", pk], w=["lbk%d" % m])

                nb_ = len(blocks)
                for step in range(nb_ + 2):
                    if step < nb_:
                        stage1(blocks[step])
                    if 0 <= step - 1 < nb_:
                        stage2(blocks[step - 1])
                    if 0 <= step - 2 < nb_:
                        stage3(blocks[step - 2])
                P.op("dve", lambda e: e.reciprocal(out=wk["rl1"][:], in_=lbk[0][:, :]), r=["lbk0"], w=["rl1"])
                P.op("dve", lambda e: e.reciprocal(out=wk["rl2"][:], in_=lbk[1][:, :]), r=["lbk1"], w=["rl2"])
                P.op("dve", lambda e: e.tensor_tensor(out=wk["a1"][:], in0=obk[0][:, :], in1=wk["rl1"][:], op=ALU.mult),
                     r=["obk0", "rl1"], w=["a1"])
                P.op("dve", lambda e: e.tensor_tensor(out=wk["a2"][:], in0=obk[1][:, :], in1=wk["rl2"][:], op=ALU.mult),
                     r=["obk1", "rl2"], w=["a2"])
                P.op("dve", lambda e: e.scalar_tensor_tensor(out=wk["a1"][:], in0=wk["a2"][:], scalar=negl, in1=wk["a1"][:],
                                                             op0=ALU.mult, op1=ALU.add), r=["a2", "a1", "lamsc"], w=["a1"])
                P.op("act", lambda e: e.activation(out=wk["sq"][:], in_=wk["a1"][:], func=AF.Square), r=["a1"], w=["sq"])
                sb_i = cnt["s"] % 3
                cnt["s"] += 1
                S = sbk[sb_i]
                skey = "sbk%d" % sb_i
                P.op("pe", lambda e: e.matmul(S[:, :], lhsT=onesf[:, :], rhs=wk["sq"][:], start=True, stop=True),
                     r=["onesf", "sq"], w=[skey])
                P.op("dve", lambda e: e.tensor_scalar(out=wk["t1"][:], in0=S[:, :], scalar1=1.0 / 128, scalar2=RMS_EPS,
                                                      op0=ALU.mult, op1=ALU.add), r=[skey], w=["t1"])
                P.op("act", lambda e: e.activation(out=wk["t1"][:], in_=wk["t1"][:], func=AF.Ln), r=["t1"], w=["t1"])
                P.op("act", lambda e: e.activation(out=wk["t1"][:], in_=wk["t1"][:], func=AF.Exp, scale=-0.5), r=["t1"], w=["t1"])
                P.op("dve", lambda e: e.tensor_tensor(out=wk["a1"][:], in0=wk["a1"][:], in1=wk["t1"][:], op=ALU.mult),
                     r=["a1", "t1"], w=["a1"])
                P.op("dve", lambda e: e.scalar_tensor_tensor(out=yst[i][:, q0:q0 + 512], in0=wk["a1"][:], scalar=gsc,
                                                             in1=gT[i][:, q0:q0 + 512], op0=ALU.mult, op1=ALU.mult),
                     r=["a1", "lamsc", "gT%d" % i], w=["yst%d" % i])
            P.dma("pool", d["yT"][h * 128:(h + 1) * 128, :], yst[i][:, :], r=["yst%d" % i], sem="d_yst%d" % i)


def host_a_strip(rel_bias):
    v = np.arange(640)[None, :]
    s = np.arange(128)[:, None]
    dist = v - s
    bk = t5_bucket_np(dist)
    out = np.empty((8, 128, 640), np.float32)
    for h in range(8):
        out[h] = np.where(dist >= 0, rel_bias[bk, h], np.float32(NEG))
    return out


def phase_mixD(P, T, l, io, d, NH=None):
    NH = NH or CFG["NHA"]
    NG = T // 512
    NKB = T // 128
    with P.phase("mixD") as ph:
        qT = [ph.sb("qT%d" % i, [128, T], BF16) for i in range(2)]
        kT = [ph.sb("kT%d" % i, [128, T], BF16) for i in range(2)]
        V = [ph.sb("V%d" % i, [128, NKB, 128], BF16) for i in range(2)]
        gT = [ph.sb("gT%d" % i, [128, T], BF16) for i in range(2)]
        yst = [ph.sb("yst%d" % i, [128, T], BF16) for i in range(2)]
        ee = [ph.sb("ee%d" % i, [128, 512], F32) for i in range(3)]
        sp = [ph.sb("sp%d" % i, [128, 512], F32) for i in range(3)]
        lk = [ph.sb("lk%d" % i, [128, 512], F32) for i in range(3)]
        aT = [ph.sb("aT%d" % i, [128, 512], BF16) for i in range(3)]
        rsum = [ph.sb("rsum%d" % i, [128, 512], F32) for i in range(2)]
        onesf = ph.sb("onesf", [128, 128], F32)
        lstr = ph.sb("lstr", [128, 128], F32)
        m01 = ph.sb("m01", [128, 640], F32)
        zbk = [ph.ps("zbk%d" % i, [128, 512]) for i in range(3)]
        bbk = [ph.ps("bbk%d" % i, [128, 512]) for i in range(2)]
        obk = [ph.ps("obk%d" % i, [128, 512]) for i in range(2)]

        P.op("pool", lambda e: e.memset(onesf[:], 1.0), w=["onesf"])
        P.op("pool", lambda e: e.memset(lstr[:], 1.0), w=["lstr"])
        P.op("pool", lambda e: e.affine_select(out=lstr[:], in_=lstr[:], pattern=[[-1, 128]], compare_op=ALU.is_gt,
                                               fill=0.0, base=0, channel_multiplier=1), r=["lstr"], w=["lstr"])
        P.op("pool", lambda e: e.memset(m01[:], 1.0), w=["m01"])
        P.op("pool", lambda e: e.affine_select(out=m01[:], in_=m01[:], pattern=[[1, 640]], compare_op=ALU.is_gt,
                                               fill=0.0, base=0, channel_multiplier=-1), r=["m01"], w=["m01"])

        def load_head(h):
            i = h % 2
            P.dma("sp", qT[i][:, :], d["DqT"][h * 128:(h + 1) * 128, :], w=["qT%d" % i], sem="d_qT%d" % i)
            P.dma("sp", kT[i][:, :], d["DkT"][h * 128:(h + 1) * 128, :], w=["kT%d" % i], sem="d_kT%d" % i)
            P.dma("sp", V[i][:, :, :], d["Dv"].rearrange("(kb p) c -> p kb c", p=128)[:, :, h * 128:(h + 1) * 128],
                  w=["V%d" % i], sem="d_V%d" % i)
            P.dma("sp", gT[i][:, :], d["DgT"][h * 128:(h + 1) * 128, :], w=["gT%d" % i], sem="d_gT%d" % i)

        blocks = []
        gidx = 0
        for h in range(NH):
            for g in range(NG):
                kis = list(range(4 * g + 3, -1, -1))
                for ki in kis:
                    j = ki - 4 * g
                    blocks.append(dict(h=h, g=g, ki=ki, j=j, c0=(128 * j if j >= 1 else 0), first=(ki == kis[0]), last=(ki == 0),
                                       gi=gidx, n=len(blocks)))
                gidx += 1

        def S0(b):
            i = b["h"] % 2
            if b["first"] and b["g"] == 0:
                load_head(b["h"])
            zi = b["n"] % 3
            c0, ki, q0 = b["c0"], b["ki"], b["g"] * 512
            P.op("pe", lambda e: e.matmul(zbk[zi][:, c0:512], lhsT=kT[i][:, ki * 128:(ki + 1) * 128],
                                          rhs=qT[i][:, q0 + c0:q0 + 512], start=True, stop=True),
                 r=["kT%d" % i, "qT%d" % i], w=["zbk%d" % zi])

        def S1(b):
            k3 = b["n"] % 3
            cs = slice(b["c0"], 512)
            n = 512 - b["c0"]
            Z, zk = zbk[k3], "zbk%d" % k3
            if b["first"]:
                r_ = b["gi"] % 2
                P.op("pool", lambda e: e.memset(rsum[r_][:], 0.0), w=["rsum%d" % r_])
            P.op("act", lambda e: e.activation(out=ee[k3][:, cs], in_=Z[:, cs], func=AF.Exp, scale=-1.0), r=[zk], w=["ee%d" % k3])
            P.op("act", lambda e: e.activation(out=sp[k3][:, cs], in_=ee[k3][:, cs], func=AF.Ln, bias=1.0), r=["ee%d" % k3], w=["sp%d" % k3])
            P.op("dve", lambda e: e.scalar_tensor_tensor(out=lk[k3][:, cs], in0=sp[k3][:, cs], scalar=-1.0, in1=Z[:, cs],
                                                         op0=ALU.mult, op1=ALU.subtract), r=["sp%d" % k3, zk], w=["lk%d" % k3])
            if b["j"] >= -1:
                so = 128 if b["j"] == -1 else 0
                P.op("dve", lambda e: e.tensor_tensor(out=lk[k3][:, cs], in0=lk[k3][:, cs], in1=m01[:, so:so + n], op=ALU.mult),
                     r=["lk%d" % k3, "m01"], w=["lk%d" % k3])

        def S2(b):
            k3 = b["n"] % 3
            cs = slice(b["c0"], 512)
            r_ = b["gi"] % 2
            bi = b["n"] % 2
            B, bk = bbk[bi], "bbk%d" % bi
            P.op("pe", lambda e: e.matmul(B[:, cs], lhsT=lstr[:, :], rhs=lk[k3][:, cs], start=True, stop=b["first"]),
                 r=["lstr", "lk%d" % k3], w=[bk], sig=b["first"])
            if not b["first"]:
                P.op("pe", lambda e: e.matmul(B[:, cs], lhsT=onesf[:, :], rhs=rsum[r_][:, cs], start=False, stop=True),
                     r=["onesf", "rsum%d" % r_], w=[bk])
            P.op("dve", lambda e: e.tensor_tensor(out=ee[k3][:, cs], in0=B[:, cs], in1=sp[k3][:, cs], op=ALU.subtract),
                 r=[bk, "sp%d" % k3], w=["ee%d" % k3])
            if not b["last"]:
                P.op("pool", lambda e: e.tensor_tensor(out=rsum[r_][:, cs], in0=rsum[r_][:, cs], in1=lk[k3][:, cs], op=ALU.add),
                     r=["rsum%d" % r_, "lk%d" % k3], w=["rsum%d" % r_])

        def S3(b):
            k3 = b["n"] % 3
            cs = slice(b["c0"], 512)
            n = 512 - b["c0"]
            i = b["h"] % 2
            o_ = b["gi"] % 2
            P.op("act", lambda e: e.activation(out=aT[k3][:, cs], in_=ee[k3][:, cs], func=AF.Exp), r=["ee%d" % k3], w=["aT%d" % k3])
            if b["j"] >= -1:
                so = 128 if b["j"] == -1 else 0
                P.op("dve", lambda e: e.tensor_tensor(out=aT[k3][:, cs], in0=aT[k3][:, cs], in1=m01[:, so:so + n], op=ALU.mult),
                     r=["aT%d" % k3, "m01"], w=["aT%d" % k3])
            P.op("pe", lambda e: e.matmul(obk[o_][:, cs], lhsT=V[i][:, b["ki"], :], rhs=aT[k3][:, cs], start=b["first"], stop=b["last"],
                                          skip_group_check=True), r=["V%d" % i, "aT%d" % k3], w=["obk%d" % o_])
            if b["last"]:
                q0 = b["g"] * 512
                P.op("dve", lambda e: e.tensor_tensor(out=yst[i][:, q0:q0 + 512], in0=obk[o_][:, :], in1=gT[i][:, q0:q0 + 512],
                                                      op=ALU.mult), r=["obk%d" % o_, "gT%d" % i], w=["yst%d" % i])
                if b["g"] == NG - 1:
                    h = b["h"]
                    P.dma("pool", d["yT"][3 * CFG["YB"] + h * 128:3 * CFG["YB"] + (h + 1) * 128, :], yst[i][:, :], r=["yst%d" % i],
                          sem="d_yst%d" % i)

        nb_ = len(blocks)
        for step in range(nb_ + 4):
            if step < nb_:
                S0(blocks[step])
            if 0 <= step - 1 < nb_:
                S1(blocks[step - 1])
            if 0 <= step - 2 < nb_:
                S2(blocks[step - 2])
            if 0 <= step - 3 < nb_:
                S3(blocks[step - 3])


def phase_mixB(P, T, l, io, d, NH=None):
    NH = NH or CFG["NHA"]
    NHT = CFG["NHA"]
    NCH = T // 64
    NG = T // 512
    with P.phase("mixB") as ph:
        zf = [ph.sb("zf0", [128, T], F32)] * 2
        qb = [ph.sb("qb%d" % i, [128, T], BF16) for i in range(2)]
        gT = [ph.sb("gT%d" % i, [128, T], BF16) for i in range(2)]
        itok = [ph.sb("itok%d" % i, [64, NCH, 128], BF16) for i in range(2)]
        a1 = ph.sb("a1", [128, T], F32)
        a2 = ph.sb("a2", [128, T], F32)
        a3 = ph.sb("a3", [128, T], F32)
        a4 = ph.sb("a4", [128, T], F32)
        qt = ph.sb("qt", [128, T], BF16)
        kh = ph.sb("kh", [128, T], BF16)
        khtok = ph.sb("khtok", [64, NCH, 128], BF16)
        yst = [ph.sb("yst%d" % i, [128, 512], BF16) for i in range(2)]
        rmask = ph.sb("rmask", [128, 512], F32)
        cmask = ph.sb("cmask", [64, 512], F32)
        scb = [ph.sb("scb%d" % i, [64, 512], BF16) for i in range(2)]
        S = ph.sb("S", [128, 128], F32)
        Sb = [ph.sb("Sb%d" % i, [128, 128], BF16) for i in range(2)]
        sq = ph.sb("sq", [128, 512], F32)
        t1 = ph.sb("t1", [128, 512], F32)
        yy = ph.sb("yy", [128, 512], F32)
        identb = make_ident(P, ph, BF16, "identb")
        onesf = ph.sb("onesf", [128, 128], F32)
        hl = ph.sb("hl", [128, NHT, 4], F32)
        he = ph.sb("he", [128, NHT, 4], F32)
        hs = ph.sb("hs", [128, NHT], F32)
        lb = ph.sb("lb", [128, NHT], F32)
        oml = ph.sb("oml", [128, NHT], F32)
        hgn = ph.sb("hgn", [128, 1], F32)
        sbk = [ph.ps("sbk%d" % i, [128, 512]) for i in range(1)]
        obk = [ph.ps("obk%d" % i, [128, 512]) for i in range(2)]
        spk = [ph.ps("spk%d" % i, [128, 512]) for i in range(2)]
        ssb = ph.ps("ssb", [128, 512])
        tpk = [ph.ps("tpk%d" % i, [64, 8, 128], BF16) for i in range(2)]

        P.op("pool", lambda e: e.memset(onesf[:], 1.0), w=["onesf"])
        P.op("pool", lambda e: e.memset(rmask[:], 1.0), w=["rmask"])
        P.op("pool", lambda e: e.memset(rmask[:].rearrange("p (c k) -> p c k", k=64)[:, :, 0:1], 0.0), w=["rmask"])
        P.op("pool", lambda e: e.memset(cmask[:], 1.0), w=["cmask"])
        P.op("pool", lambda e: e.affine_select(out=cmask[:].rearrange("p (c t) -> p c t", t=64),
                                               in_=cmask[:].rearrange("p (c t) -> p c t", t=64),
                                               pattern=[[0, 8], [1, 64]], compare_op=ALU.is_ge, fill=0.0, base=0,
                                               channel_multiplier=-1), r=["cmask"], w=["cmask"])
        d0 = ph.sb("d0", [1, 8], F32)
        P.dma("sp", d0[:, :], d["Bd0"][:, :], w=["d0"], sem="d_d0")
        P.dma("sp", hl[:, :, :], io["hg_lower_t"], w=["hl"], sem="d_hl")
        P.dma("sp", hgn[:, :], io["hg_norm_t"][l], w=["hgn"], sem="d_hgn")
        P.op("act", lambda e: e.activation(out=he[:], in_=hl[:], func=AF.Exp), r=["hl"], w=["he"])
        P.op("dve", lambda e: e.reduce_sum(out=hs[:], in_=he[:], axis=AX.X), r=["he"], w=["hs"])
        P.op("dve", lambda e: e.reciprocal(out=hs[:], in_=hs[:]), r=["hs"], w=["hs"])
        P.op("pool", lambda e: e.memset(lb[:], 0.0), w=["lb"])
        for j in range(1, l + 1):
            P.op("dve", lambda e: e.tensor_tensor(out=lb[:], in0=lb[:], in1=he[:, :, j], op=ALU.add), r=["lb", "he"], w=["lb"])
        P.op("dve", lambda e: e.tensor_tensor(out=lb[:], in0=lb[:], in1=hs[:], op=ALU.mult), r=["lb", "hs"], w=["lb"])
        P.op("dve", lambda e: e.tensor_scalar(out=oml[:], in0=lb[:], scalar1=-1.0, scalar2=1.0, op0=ALU.mult, op1=ALU.add),
             r=["lb"], w=["oml"])

        def load_head(h):
            i = h % 2
            P.dma("sp", zf[0][:, :], d["BfT"][h * 128:(h + 1) * 128, :], w=["zf0"], sem="d_zf0")
            P.dma("sp", qb[i][:, :], d["BqT"][h * 128:(h + 1) * 128, :], w=["qb%d" % i], sem="d_qb%d" % i)
            P.dma("sp", gT[i][:, :], d["BgT"][h * 128:(h + 1) * 128, :], w=["gT%d" % i], sem="d_gT%d" % i)
            P.dma("sp", itok[i][:, :, :], d["Bi"].rearrange("(c p) e -> p c e", p=64)[:, :, h * 128:(h + 1) * 128],
                  w=["itok%d" % i], sem="d_itok%d" % i)

        cnt = dict(sb=0)
        load_head(0)
        for h in range(NH):
            i = h % 2
            zk = "zf0"
            P.op("act", lambda e: e.activation(out=a1[:], in_=zf[i][:], func=AF.Sigmoid), r=[zk], w=["a1"])
            P.op("dve", lambda e: e.tensor_scalar(out=a1[:], in0=a1[:], scalar1=oml[:, h:h + 1], scalar2=lb[:, h:h + 1],
                                                  op0=ALU.mult, op1=ALU.add), r=["a1", "oml", "lb"], w=["a1"])
            P.op("act", lambda e: e.activation(out=a2[:], in_=a1[:], func=AF.Ln), r=["a1"], w=["a2"])
            P.op("dve", lambda e: e.tensor_scalar(out=a1[:], in0=a1[:], scalar1=-1.0, scalar2=1.0, op0=ALU.mult, op1=ALU.add),
                 r=["a1"], w=["a1"])
            for g8 in range(NG):
                P.op("dve", lambda e: e.tensor_tensor_scan(out=a3[:, g8 * 512:(g8 + 1) * 512], data0=rmask[:],
                                                           data1=a2[:, g8 * 512:(g8 + 1) * 512], initial=0.0,
                                                           op0=ALU.mult, op1=ALU.add), r=["rmask", "a2"], w=["a3"])
            P.op("act", lambda e: e.activation(out=a4[:], in_=a3[:], func=AF.Exp), r=["a3"], w=["a4"])
            P.op("act", lambda e: e.activation(out=a2[:], in_=a3[:], func=AF.Exp, scale=-1.0), r=["a3"], w=["a2"])
            P.op("dve", lambda e: e.tensor_tensor(out=a1[:], in0=a1[:], in1=a2[:], op=ALU.mult), r=["a1", "a2"], w=["a1"])
            P.op("dve", lambda e: e.tensor_tensor(out=a2[:], in0=qb[i][:], in1=a4[:], op=ALU.mult), r=["qb%d" % i, "a4", "a2"], w=["a2"])
            P.op("act", lambda e: e.activation(out=qt[:], in_=a2[:], func=AF.Copy), r=["a2"], w=["qt"])
            ebl = a4[:].rearrange("p (c k) -> p c k", k=64)[:, :, 63:64]
            P.op("dve", lambda e: e.tensor_tensor(out=kh[:].rearrange("p (c k) -> p c k", k=64),
                                                  in0=a1[:].rearrange("p (c k) -> p c k", k=64),
                                                  in1=ebl.to_broadcast([128, NCH, 64]), op=ALU.mult), r=["a1", "a4"], w=["kh"])
            if h + 1 < NH:
                load_head(h + 1)
            for c8 in range(NCH // 8):
                tp = tpk[c8 % 2]
                tk_ = "tpk%d" % (c8 % 2)
                for cc in range(8):
                    c = c8 * 8 + cc
                    P.op("pe", lambda e: e.transpose(out=tp[:, cc, :], in_=kh[:, c * 64:(c + 1) * 64], identity=identb[:, :]),
                         r=["kh", "identb"], w=[tk_], sig=(cc == 7))
                P.op("act", lambda e: e.activation(out=khtok[:, c8 * 8:(c8 + 1) * 8, :], in_=tp[:, :, :], func=AF.Copy),
                     r=[tk_], w=["khtok"])
            P.op("pool", lambda e: e.memset(S[:], 0.0), w=["S"])
            P.op("pool", lambda e: e.memset(Sb[0][:], 0.0), w=["Sb0"])
            sbi = 0
            for g in range(NG):
                q0 = g * 512
                for cc in range(8):
                    c = g * 8 + cc
                    P.op("pe", lambda e: e.matmul(sbk[0][0:64, cc * 64:(cc + 1) * 64], lhsT=a1[:, c * 64:(c + 1) * 64],
                                                  rhs=a2[:, c * 64:(c + 1) * 64], start=True, stop=True, skip_group_check=True),
                         r=["a1", "a2"], w=["sbk0"], sig=(cc == 7))
                si = g % 2
                P.op("dve", lambda e: e.tensor_tensor(out=scb[si][:], in0=sbk[0][0:64, :], in1=cmask[:], op=ALU.mult),
                     r=["sbk0", "cmask"], w=["scb%d" % si])
                if g == 0:
                    P.op("dve", lambda e: e.tensor_copy(out=scb[si][0:1, 0:1], in_=d0[0:1, h:h + 1]), r=["d0", "scb%d" % si],
                         w=["scb%d" % si])
                for half in range(2):
                    for c4 in range(4):
                        c = g * 8 + half * 4 + c4
                        P.op("pe", lambda e: e.matmul(spk[half][:, c4 * 128:(c4 + 1) * 128], lhsT=khtok[:, c, :],
                                                      rhs=itok[i][:, c, :], start=True, stop=True, skip_group_check=True),
                             r=["khtok", "itok%d" % i], w=["spk%d" % half], sig=(c4 == 3))
                ob = obk[g % 2]
                ok_ = "obk%d" % (g % 2)
                for cc in range(8):
                    c = g * 8 + cc
                    P.op("pe", lambda e: e.matmul(ob[:, cc * 64:(cc + 1) * 64], lhsT=itok[i][:, c, :],
                                                  rhs=scb[si][:, cc * 64:(cc + 1) * 64], start=True, stop=False,
                                                  skip_group_check=True), r=["itok%d" % i, "scb%d" % si], w=[ok_], sig=False)
                    P.op("pe", lambda e: e.matmul(ob[:, cc * 64:(cc + 1) * 64], lhsT=Sb[sbi][:, :],
                                                  rhs=qt[:, c * 64:(c + 1) * 64], start=False, stop=True,
                                                  skip_group_check=True), r=["Sb%d" % sbi, "qt"], w=[ok_])
                    half, c4 = cc // 4, cc % 4
                    P.op("dve", lambda e: e.scalar_tensor_tensor(out=S[:], in0=S[:], scalar=ebl[:, c, :],
                                                                 in1=spk[half][:, c4 * 128:(c4 + 1) * 128],
                                                                 op0=ALU.mult, op1=ALU.add), r=["S", "a4", "spk%d" % half], w=["S"])
                    sbi = 1 - sbi
                    P.op("act", lambda e: e.activation(out=Sb[sbi][:], in_=S[:], func=AF.Copy), r=["S"], w=["Sb%d" % sbi])
                P.op("act", lambda e: e.activation(out=sq[:], in_=ob[:, :], func=AF.Square), r=[ok_], w=["sq"])
                P.op("pe", lambda e: e.matmul(ssb[:, :], lhsT=onesf[:, :], rhs=sq[:], start=True, stop=True), r=["onesf", "sq"], w=["ssb"])
                P.op("dve", lambda e: e.tensor_scalar(out=t1[:], in0=ssb[:, :], scalar1=1.0 / 128, scalar2=RMS_EPS,
                                                      op0=ALU.mult, op1=ALU.add), r=["ssb"], w=["t1"])
                P.op("act", lambda e: e.activation(out=t1[:], in_=t1[:], func=AF.Ln), r=["t1"], w=["t1"])
                P.op("act", lambda e: e.activation(out=t1[:], in_=t1[:], func=AF.Exp, scale=-0.5), r=["t1"], w=["t1"])
                P.op("dve", lambda e: e.tensor_tensor(out=yy[:], in0=ob[:, :], in1=t1[:], op=ALU.mult), r=[ok_, "t1"], w=["yy"])
                yi = g % 2
                P.op("dve", lambda e: e.scalar_tensor_tensor(out=yst[yi][:, :], in0=yy[:], scalar=hgn[:, 0:1],
                                                             in1=gT[i][:, q0:q0 + 512], op0=ALU.mult, op1=ALU.mult),
                     r=["yy", "hgn", "gT%d" % i], w=["yst%d" % yi])
                P.dma("pool", d["yT"][CFG["YB"] + h * 128:CFG["YB"] + (h + 1) * 128, q0:q0 + 512], yst[yi][:, :], r=["yst%d" % yi],
                      sem="d_yst%d" % yi)


def phase_b0(P, T, l, io, d):
    SP = CFG["SPLIT"]
    YB = CFG["YB"]
    NHB = CFG["NHA"]
    qcol = O_HQ if SP == 1 else 4 * YB
    fcol = O_HF if SP == 1 else 4 * YB + YB
    P.barrier()
    if SP > 1:
        P.coll("AllGather", d["h"][0:1, :], d["h0g"][:, :])
        P.barrier()
    with P.phase("b0") as ph:
        h0 = ph.sb("h0", [128, KC], F32)
        wb = [ph.sb("wb%d" % i, [128, KC, 256], F32) for i in range(2)]
        row = ph.sb("row", [1, 2 * YB], F32)
        hlr = ph.sb("hlr", [1, 4, YB], F32)
        er = ph.sb("er", [1, 4, YB], F32)
        srow = ph.sb("srow", [1, YB], F32)
        lbr = ph.sb("lbr", [1, YB], F32)
        fr = ph.sb("fr", [1, YB], F32)
        dots = ph.sb("dots", [1, 8], F32)
        pbk = [ph.ps("pbk%d" % i, [1, 512]) for i in range(2)]
        src_h0 = d["h"][0:1, :] if SP == 1 else d["h0g"][0:1, :]
        h0r = ph.sb("h0r", [KC, 128], F32)
        ident = make_ident(P, ph, F32, "identf")
        tps = ph.ps("tps", [128, KC])
        P.dma("sp", h0r[:, :], src_h0.rearrange("o (kc p) -> (o kc) p", p=128), w=["h0r"], sem="d_h0")
        P.op("pe", lambda e: e.transpose(out=tps[:, :], in_=h0r[:, :], identity=ident[0:KC, 0:KC]), r=["h0r", "identf"], w=["tps"])
        P.op("dve", lambda e: e.tensor_copy(out=h0[:, :], in_=tps[:, :]), r=["tps"], w=["h0"])
        P.dma("sp", hlr[:, :, :], io["hg_lower_r"], w=["hlr"], sem="d_hlr")
        w_l = io["w_in"][l].rearrange("(kc p) c -> p kc c", p=128)
        nblk = 2 * YB // 256
        for bi in range(nblk):
            c0 = (qcol if bi < nblk // 2 else fcol) + (bi % (nblk // 2)) * 256
            wi = bi % 2
            P.dma("sp", wb[wi][:, :, :], w_l[:, :, c0:c0 + 256], w=["wb%d" % wi], sem="d_wbf%d" % wi)
            for kc in range(KC):
                P.op("pe", lambda e: e.matmul(pbk[wi][0:1, 0:256], lhsT=h0[:, kc:kc + 1], rhs=wb[wi][:, kc, :],
                                              start=(kc == 0), stop=(kc == KC - 1)), r=["h0", "wb%d" % wi], w=["pbk%d" % wi],
                     sig=(kc == KC - 1))
            P.op("dve", lambda e: e.tensor_copy(out=row[0:1, bi * 256:(bi + 1) * 256], in_=pbk[wi][0:1, 0:256]),
                 r=["pbk%d" % wi], w=["row"])
        P.op("act", lambda e: e.activation(out=er[:], in_=hlr[:], func=AF.Exp), r=["hlr"], w=["er"])
        P.op("dve", lambda e: e.tensor_tensor(out=srow[:], in0=er[:, 0, :], in1=er[:, 1, :], op=ALU.add), r=["er"], w=["srow"])
        P.op("dve", lambda e: e.tensor_tensor(out=srow[:], in0=srow[:], in1=er[:, 2, :], op=ALU.add), r=["er", "srow"], w=["srow"])
        P.op("dve", lambda e: e.tensor_tensor(out=srow[:], in0=srow[:], in1=er[:, 3, :], op=ALU.add), r=["er", "srow"], w=["srow"])
        P.op("dve", lambda e: e.reciprocal(out=srow[:], in_=srow[:]), r=["srow"], w=["srow"])
        P.op("pool", lambda e: e.memset(lbr[:], 0.0), w=["lbr"])
        for j in range(1, l + 1):
            P.op("dve", lambda e: e.tensor_tensor(out=lbr[:], in0=lbr[:], in1=er[:, j, :], op=ALU.add), r=["lbr", "er"], w=["lbr"])
        P.op("dve", lambda e: e.tensor_tensor(out=lbr[:], in0=lbr[:], in1=srow[:], op=ALU.mult), r=["lbr", "srow"], w=["lbr"])
        P.op("act", lambda e: e.activation(out=fr[:], in_=row[0:1, YB:2 * YB], func=AF.Sigmoid), r=["row"], w=["fr"])
        P.op("dve", lambda e: e.tensor_scalar(out=fr[:], in0=fr[:], scalar1=-1.0, scalar2=1.0, op0=ALU.mult, op1=ALU.add),
             r=["fr"], w=["fr"])
        P.op("dve", lambda e: e.tensor_scalar(out=lbr[:], in0=lbr[:], scalar1=-1.0, scalar2=1.0, op0=ALU.mult, op1=ALU.add),
             r=["lbr"], w=["lbr"])
        P.op("dve", lambda e: e.tensor_tensor(out=fr[:], in0=fr[:], in1=lbr[:], op=ALU.mult), r=["fr", "lbr"], w=["fr"])
        P.op("dve", lambda e: e.tensor_tensor(out=fr[:], in0=fr[:], in1=row[0:1, 0:YB], op=ALU.mult), r=["fr", "row"], w=["fr"])
        P.op("pool", lambda e: e.memset(dots[:], 0.0), w=["dots"])
        P.op("dve", lambda e: e.reduce_sum(out=dots[0:1, 0:NHB], in_=fr[:].rearrange("o (h d) -> o h d", d=128), axis=AX.X),
             r=["fr", "dots"], w=["dots"])
        P.dma("sp", d["Bd0"][:, :], dots[:, :], r=["dots"], sem="d_dots")


WSC = math.exp(-0.5)


def phase_mixC(P, T, l, io, d, NH=None, G=2):
    NH = NH or CFG["NHC"]
    NHT = CFG["NHC"]
    RB = 64 * NHT
    N = 512
    NST = T // N
    GC = G * 8
    with P.phase("mixC") as ph:
        F = lambda name, shape, dt=F32: ph.sb(name, shape, dt)
        cpar = F("cpar", [64, NHT, 8])
        omka = F("omka", [64, NHT])
        w2s = F("w2s", [64, RB])
        a2s = F("a2s", [64, RB])
        v2s = F("v2s", [32, RB])
        twd = F("twd", [64, N])
        adm = F("adm", [64, N])
        vdm = F("vdm", [32, N])
        Xr = F("Xr", [64, G, N])
        Xk = F("Xk", [64, G, N])
        Xv = F("Xv", [64, G, N])
        Xf = F("Xf", [64, G, N])
        gate = F("gate", [64, G, N], BF16)
        sg = F("sg", [64, G, N])
        aa = F("aa", [64, G, N])
        kk = F("kk", [64, G, N])
        kp = F("kp", [64, G, N])
        cs = F("cs", [64, G, N])
        pp = F("pp", [64, G, N])
        pinv = F("pinv", [64, G, N])
        pprev = F("pprev", [64, G, N])
        tmp = F("tmp", [64, G, N])
        AR = F("AR", [64, G, 8, 2, 64])
        BT = F("BT", [64, G, N])
        KT = F("KT", [64, G, N])
        RK = F("RK", [64, G, N])
        TK = F("TK", [64, G, 8, 5, 64])
        GM = F("GM", [64, GC, 320])
        TT = F("TT", [64, GC, 64])
        TTb = F("TTb", [64, GC, 64], BF16)
        XY = [F("XY%d" % i, [64, GC, 128], BF16) for i in range(2)]
        tmpb = F("tmpb", [64, G, N], BF16)
        RKb = F("RKb", [64, G, N], BF16)
        ones64b = F("ones64b", [64, 64], BF16)
        onesmb = F("onesmb", [64, 64], BF16)
        GA = F("GA", [64, GC, 128])
        OL = F("OL", [64, GC, 64])
        RP = F("RP", [64, GC, 64])
        PH = F("PH", [64, GC, 64])
        GP = F("GP", [64, GC, 64])
        OT = F("OT", [64, G, N])
        H = F("H", [64, NH, 64])
        yst = F("yst", [64, G, N], BF16)
        m320 = F("m320", [64, 320])
        rmask = F("rmask", [64, G * N])
        ones64 = F("ones64", [64, 64])
        onesm = F("onesm", [64, 64])
        ident = make_ident(P, ph, F32, "identf")
        idf = ident[0:64, 0:64]
        pbs = [ph.ps("pb%d" % i, [64, 512]) for i in range(8)]
        bctr = [0]

        def nb():
            b = bctr[0] % 8
            bctr[0] += 1
            return pbs[b], "pb%d" % b

        P.op("pool", lambda e: e.memset(ones64[:], 1.0), w=["ones64"])
        P.op("pool", lambda e: e.memset(onesm[:], 1.0 / 64), w=["onesm"])
        P.op("pool", lambda e: e.memset(ones64b[:], 1.0), w=["ones64b"])
        P.op("pool", lambda e: e.memset(onesmb[:], 1.0 / 64), w=["onesmb"])
        P.op("pool", lambda e: e.memset(rmask[:], 1.0), w=["rmask"])
        P.op("pool", lambda e: e.memset(rmask[:].rearrange("p (c k) -> p c k", k=64)[:, :, 0:1], 0.0), w=["rmask"])
        P.op("pool", lambda e: e.memset(H[:], 0.0), w=["H"])
        P.op("pool", lambda e: e.memset(m320[:], 1.0), w=["m320"])
        for blk, op_, cm, pat in ((0, ALU.is_gt, -1, 1), (1, ALU.is_ge, -1, 1), (2, ALU.is_gt, -1, 1), (3, ALU.is_ge, -1, 1),
                                  (4, ALU.is_gt, 1, -1)):
            P.op("pool", lambda e: e.affine_select(out=m320[:, blk * 64:(blk + 1) * 64], in_=m320[:, blk * 64:(blk + 1) * 64],
                                                   pattern=[[pat, 64]], compare_op=op_, fill=0.0, base=0, channel_multiplier=cm),
                 r=["m320"], w=["m320"])
        P.dma("sp", cpar[:, :, :], io["c_par"][l], w=["cpar"], sem="d_cpar")
        P.dma("sp", w2s[:, :], io["rw_w2"][l], w=["w2s"], sem="d_w2s")
        P.dma("sp", a2s[:, :], io["rw_a2"][l], w=["a2s"], sem="d_a2s")
        if l > 0:
            P.dma("sp", v2s[:, :], io["rw_v2"][l - 1], w=["v2s"], sem="d_v2s")
        P.op("dve", lambda e: e.tensor_scalar(out=omka[:], in0=cpar[:, :, 3], scalar1=-1.0, scalar2=1.0, op0=ALU.mult, op1=ALU.add),
             r=["cpar"], w=["omka"])

        def bc(ap2, g0):
            return ap2[:, g0:g0 + G].unsqueeze(2).to_broadcast([64, G, N])

        CM = d["CmT"]
        for st in range(NST):
            t0 = st * N
            P.dma("sp", twd[:, :], CM[3 * RB:3 * RB + 64, t0:t0 + N], w=["twd"], sem="d_twd")
            P.dma("sp", adm[:, :], CM[3 * RB + 64:3 * RB + 128, t0:t0 + N], w=["adm"], sem="d_adm")
            P.op("act", lambda e: e.activation(out=twd[:], in_=twd[:], func=AF.Tanh), r=["twd"], w=["twd"])
            if l > 0:
                P.dma("sp", vdm[:, :], d["CvdT"][:, t0:t0 + N], w=["vdm"], sem="d_vdm")
            for hg in range(NH // G):
                h0 = hg * G
                rows = lambda base: CM[base + h0 * 64: base + (h0 + G) * 64, t0:t0 + N].rearrange("(g k) t -> k g t", k=64)
                P.dma("sp", Xr[:, :, :], rows(0), w=["Xr"], sem="d_Xr")
                P.dma("sp", Xk[:, :, :], rows(RB), w=["Xk"], sem="d_Xk")
                P.dma("sp", Xv[:, :, :], rows(2 * RB), w=["Xv"], sem="d_Xv")
                P.dma("sp", gate[:, :, :], d["CgT"][h0 * 64:(h0 + G) * 64, t0:t0 + N].rearrange("(g k) t -> k g t", k=64),
                      w=["gate"], sem="d_gate")
                if l > 0:
                    P.dma("sp", Xf[:, :, :], d["Cvf"][h0 * 64:(h0 + G) * 64, t0:t0 + N].rearrange("(g k) t -> k g t", k=64),
                          w=["Xf"], sem="d_Xf")
                for g in range(G):
                    h = h0 + g
                    pb, pk = nb()
                    P.op("pe", lambda e: e.matmul(pb[:, :], lhsT=w2s[:, h * 64:(h + 1) * 64], rhs=twd[:, :], start=True, stop=True),
                         r=["w2s", "twd"], w=[pk])
                    P.op("act", lambda e: e.activation(out=sg[:, g, :], in_=pb[:, :], func=AF.Sigmoid, bias=cpar[:, h, 0:1]),
                         r=[pk, "cpar"], w=["sg"])
                    pb, pk = nb()
                    P.op("pe", lambda e: e.matmul(pb[:, :], lhsT=a2s[:, h * 64:(h + 1) * 64], rhs=adm[:, :], start=True, stop=True),
                         r=["a2s", "adm"], w=[pk])
                    P.op("act", lambda e: e.activation(out=aa[:, g, :], in_=pb[:, :], func=AF.Sigmoid, bias=cpar[:, h, 1:2]),
                         r=[pk, "cpar"], w=["aa"])
                    if l > 0:
                        pb, pk = nb()
                        P.op("pe", lambda e: e.matmul(pb[:, :], lhsT=v2s[:, h * 64:(h + 1) * 64], rhs=vdm[:, :], start=True, stop=True),
                             r=["v2s", "vdm"], w=[pk])
                        P.op("act", lambda e: e.activation(out=tmp[:, g, :], in_=pb[:, :], func=AF.Sigmoid, bias=cpar[:, h, 7:8]),
                             r=[pk, "cpar"], w=["tmp"])
                if l > 0:
                    P.op("dve", lambda e: e.tensor_tensor(out=Xf[:], in0=Xf[:], in1=Xv[:], op=ALU.subtract), r=["Xf", "Xv"], w=["Xf"])
                    P.op("dve", lambda e: e.tensor_tensor(out=Xf[:], in0=Xf[:], in1=tmp[:], op=ALU.mult), r=["Xf", "tmp"], w=["Xf"])
                    P.op("dve", lambda e: e.tensor_tensor(out=Xv[:], in0=Xv[:], in1=Xf[:], op=ALU.add), r=["Xf", "Xv"], w=["Xv"])
                else:
                    P.dma("pool", d["Cvf"][h0 * 64:(h0 + G) * 64, t0:t0 + N].rearrange("(g k) t -> k g t", k=64), Xv[:, :, :],
                          r=["Xv"], sem="d_vfo")
                P.op("dve", lambda e: e.tensor_tensor(out=kk[:], in0=Xk[:], in1=bc(cpar[:, :, 2], h0), op=ALU.mult),
                     r=["Xk", "cpar"], w=["kk"])
                P.op("act", lambda e: e.activation(out=tmpb[:], in_=kk[:], func=AF.Square), r=["kk"], w=["tmpb"])
                for g in range(G):
                    pb, pk = nb()
                    P.op("pe", lambda e: e.matmul(pb[:, :], lhsT=ones64b[:, :], rhs=tmpb[:, g, :], start=True, stop=True),
                         r=["ones64b", "tmpb"], w=[pk])
                    P.op("dve", lambda e: e.tensor_scalar(out=kp[:, g, :], in0=pb[:, :], scalar1=1e-24, scalar2=None, op0=ALU.max),
                         r=[pk], w=["kp"])
                P.op("act", lambda e: e.activation(out=kp[:], in_=kp[:], func=AF.Ln), r=["kp"], w=["kp"])
                P.op("act", lambda e: e.activation(out=kp[:], in_=kp[:], func=AF.Exp, scale=-0.5), r=["kp"], w=["kp"])
                P.op("dve", lambda e: e.tensor_tensor(out=kk[:], in0=kk[:], in1=kp[:], op=ALU.mult), r=["kk", "kp"], w=["kk"])
                P.op("dve", lambda e: e.tensor_tensor(out=kp[:], in0=aa[:], in1=bc(cpar[:, :, 3], h0), op=ALU.mult),
                     r=["aa", "cpar"], w=["kp"])
                P.op("dve", lambda e: e.tensor_tensor(out=kp[:], in0=kp[:], in1=bc(omka, h0), op=ALU.add), r=["kp", "omka"], w=["kp"])
                P.op("dve", lambda e: e.tensor_tensor(out=kp[:], in0=kp[:], in1=Xk[:], op=ALU.mult), r=["kp", "Xk"], w=["kp"])
                P.op("dve", lambda e: e.tensor_tensor_scan(out=cs[:].rearrange("p g n -> p (g n)"), data0=rmask[:],
                                                           data1=sg[:].rearrange("p g n -> p (g n)"), initial=0.0,
                                                           op0=ALU.mult, op1=ALU.add), r=["rmask", "sg"], w=["cs"])
                P.op("act", lambda e: e.activation(out=pp[:], in_=cs[:], func=AF.Exp, scale=-WSC), r=["cs"], w=["pp"])
                P.op("act", lambda e: e.activation(out=pinv[:], in_=cs[:], func=AF.Exp, scale=WSC), r=["cs"], w=["pinv"])
                P.op("dve", lambda e: e.tensor_tensor(out=tmp[:], in0=cs[:], in1=sg[:], op=ALU.subtract), r=["cs", "sg"], w=["tmp"])
                P.op("act", lambda e: e.activation(out=pprev[:], in_=tmp[:], func=AF.Exp, scale=-WSC), r=["tmp"], w=["pprev"])
                v4 = lambda t_: t_[:].rearrange("p g (c k) -> p g c k", k=64)
                P.op("dve", lambda e: e.scalar_tensor_tensor(out=AR[:, :, :, 0, :], in0=v4(kk), scalar=-1.0, in1=v4(pprev),
                                                             op0=ALU.mult, op1=ALU.mult), r=["kk", "pprev"], w=["AR"])
                P.op("dve", lambda e: e.tensor_tensor(out=AR[:, :, :, 1, :], in0=v4(Xr), in1=v4(pp), op=ALU.mult),
                     r=["Xr", "pp"], w=["AR"])
                P.op("dve", lambda e: e.tensor_tensor(out=BT[:], in0=kk[:], in1=aa[:], op=ALU.mult), r=["kk", "aa"], w=["BT"])
                P.op("dve", lambda e: e.tensor_tensor(out=BT[:], in0=BT[:], in1=pinv[:], op=ALU.mult), r=["BT", "pinv"], w=["BT"])
                P.op("dve", lambda e: e.tensor_tensor(out=KT[:], in0=kp[:], in1=pinv[:], op=ALU.mult), r=["kp", "pinv"], w=["KT"])
                P.op("dve", lambda e: e.tensor_tensor(out=RK[:], in0=Xr[:], in1=kp[:], op=ALU.mult), r=["Xr", "kp"], w=["RK"])
                P.op("dve", lambda e: e.tensor_tensor(out=RKb[:], in0=RK[:], in1=bc(cpar[:, :, 4], h0), op=ALU.mult),
                     r=["RK", "cpar"], w=["RKb"])
                for g in range(G):
                    for c2 in range(4):
                        pb, pk = nb()
                        pv = pb[:, :].rearrange("p (c j k) -> p c j k", c=2, j=4)
                        for cc in range(2):
                            c = c2 * 2 + cc
                            csl = slice(c * 64, (c + 1) * 64)
                            srcs = [(BT[:, g, csl], "BT"), (KT[:, g, csl], "KT"), (Xv[:, g, csl], "Xv"), (AR[:, g, c, 0, :], "AR")]
                            for j, (sap, skey) in enumerate(srcs):
                                P.op("pe", lambda e: e.transpose(out=pv[:, cc, j, :], in_=sap, identity=idf),
                                     r=[skey, "identf"], w=[pk], sig=(cc == 1 and j == 3))
                        P.op("act", lambda e: e.activation(out=TK[:, g, c2 * 2:c2 * 2 + 2, 0:3, :], in_=pv[:, :, 0:3, :], func=AF.Copy),
                             r=[pk], w=["TK"])
                        P.op("act", lambda e: e.activation(out=TK[:, g, c2 * 2:c2 * 2 + 2, 4, :], in_=pv[:, :, 3, :], func=AF.Copy),
                             r=[pk], w=["TK"])
                for g in range(G):
                    for c in range(8):
                        gc = g * 8 + c
                        csl = slice(c * 64, (c + 1) * 64)
                        pb, pk = nb()
                        arv = AR[:, g, c, :, :].rearrange("p a k -> p (a k)")
                        P.op("pe", lambda e: e.matmul(pb[:, 0:128], lhsT=BT[:, g, csl], rhs=arv, start=True, stop=True,
                                                      skip_group_check=True), r=["BT", "AR"], w=[pk], sig=False)
                        P.op("pe", lambda e: e.matmul(pb[:, 128:256], lhsT=KT[:, g, csl], rhs=arv, start=True, stop=True,
                                                      skip_group_check=True), r=["KT", "AR"], w=[pk], sig=False)
                        P.op("pe", lambda e: e.matmul(pb[:, 256:320], lhsT=AR[:, g, c, 0, :], rhs=BT[:, g, csl], start=True, stop=True,
                                                      skip_group_check=True), r=["BT", "AR"], w=[pk])
                        P.op("dve", lambda e: e.tensor_tensor(out=GM[:, gc, :], in0=pb[:, 0:320], in1=m320[:], op=ALU.mult),
                             r=[pk, "m320"], w=["GM"])
                P.op("dve", lambda e: e.tensor_tensor(out=TTb[:], in0=GM[:, :, 0:64], in1=idf.unsqueeze(1).to_broadcast([64, GC, 64]),
                                                      op=ALU.add), r=["GM", "identf"], w=["TTb"])
                for lev in range(5):
                    last = lev == 4
                    src = XY[(lev + 1) % 2]
                    dst = XY[lev % 2]
                    skey, dkey = "XY%d" % ((lev + 1) % 2), "XY%d" % (lev % 2)
                    for b4 in range(GC // 4):
                        pb, pk = nb()
                        pv = pb[:, :].rearrange("p (q k) -> p q k", q=4)
                        for q in range(4):
                            gc = b4 * 4 + q
                            if lev == 0:
                                Xa, Ya, rk_ = GM[:, gc, 256:320], GM[:, gc, 0:64], ["GM"]
                            else:
                                Xa, Ya, rk_ = src[:, gc, 0:64], src[:, gc, 64:128], [skey]
                            P.op("pe", lambda e: e.matmul(pv[:, q, 0:64], lhsT=Ya, rhs=Xa, start=True, stop=True, skip_group_check=True),
                                 r=rk_, w=[pk], sig=(last and q == 3))
                            if not last:
                                P.op("pe", lambda e: e.matmul(pv[:, q, 64:128], lhsT=Xa, rhs=Ya, start=True, stop=True,
                                                              skip_group_check=True), r=rk_, w=[pk], sig=(q == 3))
                        if last:
                            P.op("act", lambda e: e.activation(out=dst[:, b4 * 4:b4 * 4 + 4, 0:64], in_=pv[:, :, 0:64], func=AF.Copy),
                                 r=[pk], w=[dkey])
                        else:
                            P.op("act", lambda e: e.activation(out=dst[:, b4 * 4:b4 * 4 + 4, :], in_=pv[:, :, :], func=AF.Copy),
                                 r=[pk], w=[dkey])
                    for b4 in range(GC // 4):
                        pb2, pk2 = nb()
                        pv2 = pb2[:, 0:256].rearrange("p (q k) -> p q k", q=4)
                        for q in range(4):
                            gc = b4 * 4 + q
                            P.op("pe", lambda e: e.matmul(pv2[:, q, :], lhsT=dst[:, gc, 0:64], rhs=TTb[:, gc, :], start=True, stop=True,
                                                          skip_group_check=True), r=[dkey, "TTb"], w=[pk2], sig=(q == 3))
                        if last:
                            P.op("dve", lambda e: e.tensor_tensor(out=TT[:, b4 * 4:b4 * 4 + 4, :], in0=TTb[:, b4 * 4:b4 * 4 + 4, :],
                                                                  in1=pv2[:, :, :], op=ALU.add), r=["TTb", pk2], w=["TT"])
                        else:
                            P.op("dve", lambda e: e.tensor_tensor(out=TTb[:, b4 * 4:b4 * 4 + 4, :], in0=TTb[:, b4 * 4:b4 * 4 + 4, :],
                                                                  in1=pv2[:, :, :], op=ALU.add), r=["TTb", pk2], w=["TTb"])
                for g in range(G):
                    for c4 in range(2):
                        pb, pk = nb()
                        pv = pb[:, 0:256].rearrange("p (q k) -> p q k", q=4)
                        for q in range(4):
                            c = c4 * 4 + q
                            gc = g * 8 + c
                            P.op("pe", lambda e: e.matmul(pv[:, q, :], lhsT=GM[:, gc, 128:192], rhs=TK[:, g, c, 2, :], start=True, stop=True,
                                                          skip_group_check=True), r=["GM", "TK"], w=[pk], sig=(q == 3))
                        P.op("act", lambda e: e.activation(out=TK[:, g, c4 * 4:c4 * 4 + 4, 3, :], in_=pv[:, :, :], func=AF.Copy),
                             r=[pk], w=["TK"])
                for g in range(G):
                    for c4 in range(2):
                        pb, pk = nb()
                        pv = pb[:, :].rearrange("p (q k) -> p q k", q=4)
                        for q in range(4):
                            c = c4 * 4 + q
                            gc = g * 8 + c
                            P.op("pe", lambda e: e.matmul(pv[:, q, :], lhsT=TT[:, gc, :], rhs=TK[:, g, c, 3:5, :].rearrange("p a k -> p (a k)"),
                                                          start=True, stop=True, skip_group_check=True), r=["TT", "TK"], w=[pk], sig=(q == 3))
                        P.op("act", lambda e: e.activation(out=GA[:, g * 8 + c4 * 4:g * 8 + c4 * 4 + 4, :], in_=pv[:, :, :], func=AF.Copy),
                             r=[pk], w=["GA"])
                pC = pp[:].rearrange("p g (c k) -> p g c k", k=64)[:, :, :, 63:64]
                for g in range(G):
                    for c4 in range(2):
                        pbO, pkO = nb()
                        pbR, pkR = nb()
                        pbG, pkG = nb()
                        pvO = pbO[:, 0:256].rearrange("p (q k) -> p q k", q=4)
                        pvR = pbR[:, :].rearrange("p (q a k) -> p q a k", q=4, a=2)
                        pvG = pbG[:, 0:256].rearrange("p (q k) -> p q k", q=4)
                        for q in range(4):
                            c = c4 * 4 + q
                            gc = g * 8 + c
                            P.op("pe", lambda e: e.matmul(pvO[:, q, :], lhsT=GA[:, gc, 0:64], rhs=GM[:, gc, 64:128], start=True, stop=False,
                                                          skip_group_check=True), r=["GA", "GM"], w=[pkO], sig=False)
                            P.op("pe", lambda e: e.matmul(pvO[:, q, :], lhsT=TK[:, g, c, 2, :], rhs=GM[:, gc, 192:256], start=False, stop=True,
                                                          skip_group_check=True), r=["TK", "GM"], w=[pkO], sig=(q == 3))
                            P.op("pe", lambda e: e.matmul(pvR[:, q, 0, :], lhsT=GA[:, gc, 64:128], rhs=GM[:, gc, 64:128], start=True, stop=True,
                                                          skip_group_check=True), r=["GA", "GM"], w=[pkR], sig=False)
                            P.op("pe", lambda e: e.matmul(pvR[:, q, 1, :], lhsT=GA[:, gc, 64:128], rhs=TK[:, g, c, 0, :], start=True, stop=True,
                                                          skip_group_check=True), r=["GA", "TK"], w=[pkR], sig=(q == 3))
                            P.op("pe", lambda e: e.matmul(pvG[:, q, :], lhsT=TK[:, g, c, 0, :], rhs=GA[:, gc, 0:64], start=True, stop=False,
                                                          skip_group_check=True), r=["GA", "TK"], w=[pkG], sig=False)
                            P.op("pe", lambda e: e.matmul(pvG[:, q, :], lhsT=TK[:, g, c, 1, :], rhs=TK[:, g, c, 2, :], start=False, stop=True,
                                                          skip_group_check=True), r=["TK"], w=[pkG], sig=(q == 3))
                        gs = slice(g * 8 + c4 * 4, g * 8 + c4 * 4 + 4)
                        cs4 = slice(c4 * 4, c4 * 4 + 4)
                        P.op("act", lambda e: e.activation(out=OL[:, gs, :], in_=pvO[:, :, :], func=AF.Copy), r=[pkO], w=["OL"])
                        P.op("dve", lambda e: e.tensor_tensor(out=RP[:, gs, :], in0=pvR[:, :, 0, :], in1=AR[:, g, cs4, 1, :], op=ALU.add),
                             r=[pkR, "AR"], w=["RP"])
                        P.op("dve", lambda e: e.tensor_tensor(out=PH[:, gs, :], in0=pvR[:, :, 1, :],
                                                              in1=idf.unsqueeze(1).to_broadcast([64, 4, 64]), op=ALU.add),
                             r=[pkR, "identf"], w=["PH"])
                        P.op("dve", lambda e: e.tensor_tensor(out=GP[:, gs, :], in0=pvG[:, :, :],
                                                              in1=pC[:, g, cs4, :].to_broadcast([64, 4, 64]), op=ALU.mult),
                             r=[pkG, "pp"], w=["GP"])
                for c in range(8):
                    pbH, pkH = nb()
                    pbO, pkO = nb()
                    pvH = pbH[:, 0:G * 64].rearrange("p (g k) -> p g k", g=G)
                    pvO = pbO[:, 0:G * 64].rearrange("p (g k) -> p g k", g=G)
                    for g in range(G):
                        gc = g * 8 + c
                        P.op("pe", lambda e: e.matmul(pvH[:, g, :], lhsT=PH[:, gc, :], rhs=H[:, h0 + g, :], start=True, stop=True,
                                                      skip_group_check=True), r=["PH", "H"], w=[pkH], sig=(g == G - 1))
                    for g in range(G):
                        gc = g * 8 + c
                        P.op("pe", lambda e: e.matmul(pvO[:, g, :], lhsT=H[:, h0 + g, :], rhs=RP[:, gc, :], start=True, stop=True,
                                                      skip_group_check=True), r=["RP", "H"], w=[pkO], sig=(g == G - 1))
                    gcs = GP[:].rearrange("p (g c) k -> p g c k", g=G)[:, :, c, :]
                    ols = OL[:].rearrange("p (g c) k -> p g c k", g=G)[:, :, c, :]
                    P.op("dve", lambda e: e.tensor_tensor(out=OT[:, :, c * 64:(c + 1) * 64], in0=pvO[:, :, :], in1=ols, op=ALU.add),
                         r=[pkO, "OL"], w=["OT"])
                    P.op("dve", lambda e: e.tensor_tensor(out=H[:, h0:h0 + G, :], in0=pvH[:, :, :],
                                                          in1=pC[:, :, c, :].to_broadcast([64, G, 64]), op=ALU.mult),
                         r=[pkH, "pp", "H"], w=["H"])
                    P.op("dve", lambda e: e.tensor_tensor(out=H[:, h0:h0 + G, :], in0=H[:, h0:h0 + G, :], in1=gcs, op=ALU.add),
                         r=["H", "GP"], w=["H"])
                P.op("act", lambda e: e.activation(out=tmpb[:], in_=OT[:], func=AF.Copy), r=["OT"], w=["tmpb"])
                for g in range(G):
                    pb, pk = nb()
                    P.op("pe", lambda e: e.matmul(pb[:, :], lhsT=onesmb[:, :], rhs=tmpb[:, g, :], start=True, stop=True),
                         r=["onesmb", "tmpb"], w=[pk])
                    P.op("dve", lambda e: e.tensor_tensor(out=OT[:, g, :], in0=OT[:, g, :], in1=pb[:, :], op=ALU.subtract),
                         r=[pk, "OT"], w=["OT"])
                P.op("act", lambda e: e.activation(out=tmpb[:], in_=OT[:], func=AF.Square), r=["OT"], w=["tmpb"])
                for g in range(G):
                    pb, pk = nb()
                    P.op("pe", lambda e: e.matmul(pb[:, :], lhsT=onesmb[:, :], rhs=tmpb[:, g, :], start=True, stop=True),
                         r=["onesmb", "tmpb"], w=[pk])
                    P.op("dve", lambda e: e.tensor_scalar(out=cs[:, g, :], in0=pb[:, :], scalar1=RW_LN_EPS, scalar2=None, op0=ALU.add),
                         r=[pk], w=["cs"])
                P.op("act", lambda e: e.activation(out=cs[:], in_=cs[:], func=AF.Ln), r=["cs"], w=["cs"])
                P.op("act", lambda e: e.activation(out=cs[:], in_=cs[:], func=AF.Exp, scale=-0.5), r=["cs"], w=["cs"])
                P.op("dve", lambda e: e.tensor_tensor(out=OT[:], in0=OT[:], in1=cs[:], op=ALU.mult), r=["OT", "cs"], w=["OT"])
                P.op("dve", lambda e: e.tensor_tensor(out=OT[:], in0=OT[:], in1=bc(cpar[:, :, 5], h0), op=ALU.mult),
                     r=["OT", "cpar"], w=["OT"])
                P.op("dve", lambda e: e.tensor_tensor(out=OT[:], in0=OT[:], in1=bc(cpar[:, :, 6], h0), op=ALU.add),
                     r=["OT", "cpar"], w=["OT"])
                for g in range(G):
                    pb, pk = nb()
                    P.op("pe", lambda e: e.matmul(pb[:, :], lhsT=ones64b[:, :], rhs=RKb[:, g, :], start=True, stop=True),
                         r=["ones64b", "RKb"], w=[pk])
                    P.op("dve", lambda e: e.tensor_tensor(out=tmp[:, g, :], in0=pb[:, :], in1=Xv[:, g, :], op=ALU.mult),
                         r=[pk, "Xv"], w=["tmp"])
                P.op("dve", lambda e: e.tensor_tensor(out=OT[:], in0=OT[:], in1=tmp[:], op=ALU.add), r=["OT", "tmp"], w=["OT"])
                P.op("dve", lambda e: e.tensor_tensor(out=yst[:], in0=OT[:], in1=gate[:], op=ALU.mult), r=["OT", "gate"], w=["yst"])
                P.dma("pool", d["yT"][2 * CFG["YB"] + h0 * 64:2 * CFG["YB"] + (h0 + G) * 64, t0:t0 + N].rearrange("(g k) t -> k g t", k=64), yst[:, :, :],
                      r=["yst"], sem="d_ystc")


def phase_mergeA(P, T, l, io, d):
    NT = T // 512
    SP = CFG["SPLIT"]
    KB = CFG["YB"] // 128
    TO = T // SP
    ODT = BF16 if SP == 1 else F32
    with P.phase("mergeA") as ph:
        W = [ph.sb("W%d" % n, [128, KB, 2048], BF16) for n in range(4)]
        yT = [ph.sb("yT%d" % i, [128, 4, KB, 512], BF16) for i in range(SP)]
        gts = [ph.sb("gts%d" % i, [128, 4, 512], BF16) for i in range(2)]
        acc = ph.sb("acc", [128, 512], F32)
        tmp = ph.sb("tmp", [128, 512], F32)
        mst = [ph.sb("mst%d" % i, [128, 512], ODT) for i in range(2)]
        banks = [ph.ps("bk%d" % i, [128, 512]) for i in range(8)]
        for n in range(4):
            P.dma("pool", W[n][:, :, :], io["w_branch"][l, n].rearrange("(kc p) c -> p kc c", p=128), w=["W%d" % n], sem="d_W%d" % n)
        mg = d["MgT"].rearrange("(n c p) t -> p n c t", n=4, p=128)
        yv = d["yT"].rearrange("(n kc p) t -> p n kc t", n=4, p=128)
        bi = 0
        gi = 0
        for tt in range(NT):
            ts = slice(tt * 512, (tt + 1) * 512)
            yi = tt % len(yT)
            yk = "yT%d" % yi
            for n in range(4):
                P.dma("sp", yT[yi][:, n, :, :], yv[:, n, :, ts], w=[yk], sem="d_" + yk)
            for cc in range(16):
                g_ = gi % 2
                gi += 1
                P.dma("sp", gts[g_][:, :, :], mg[:, :, cc, ts], w=["gts%d" % g_], sem="d_gts%d" % g_)
                for n in range(4):
                    b = bi % 8
                    bi += 1
                    for kc in range(KB):
                        P.op("pe", lambda e: e.matmul(banks[b][:, :], lhsT=W[n][:, kc, cc * 128:(cc + 1) * 128], rhs=yT[yi][:, n, kc, :],
                                                      start=(kc == 0), stop=(kc == KB - 1)), r=["W%d" % n, yk], w=["bk%d" % b], sig=(kc == KB - 1))
                    if n == 0:
                        P.op("dve", lambda e: e.tensor_tensor(out=acc[:], in0=banks[b][:, :], in1=gts[g_][:, 0, :], op=ALU.mult),
                             r=["bk%d" % b, "gts%d" % g_], w=["acc"])
                    else:
                        P.op("dve", lambda e: e.tensor_tensor(out=tmp[:], in0=banks[b][:, :], in1=gts[g_][:, n, :], op=ALU.mult),
                             r=["bk%d" % b, "gts%d" % g_], w=["tmp"])
                        if n < 3:
                            P.op("pool", lambda e: e.tensor_tensor(out=acc[:], in0=acc[:], in1=tmp[:], op=ALU.add), r=["acc", "tmp"], w=["acc"])
                        else:
                            P.op("pool", lambda e: e.tensor_tensor(out=mst[g_][:], in0=acc[:], in1=tmp[:], op=ALU.add),
                                 r=["acc", "tmp"], w=["mst%d" % g_])
                if SP == 1:
                    dst = d["mT"][cc * 128:(cc + 1) * 128, ts]
                else:
                    hf, tl = (tt * 512) // TO, (tt * 512) % TO
                    dst = d["mTp"][hf, cc * 128:(cc + 1) * 128, tl:tl + 512]
                P.dma("sp", dst, mst[g_][:, :], r=["mst%d" % g_], sem="d_mst%d" % g_)


def phase_gather_hT(P, T, d):
    NCH = d["NCH"]
    rows = D_MODEL // NCH
    P.barrier()
    for i in range(NCH):
        P.coll("AllGather", d["hT"][i * rows:(i + 1) * rows, :], d["hTg"][i].rearrange("r f t -> (r f) t"))
    P.barrier()


def phase_scatter_mT(P, T, d):
    P.barrier()
    P.coll("ReduceScatter", d["mTp"].rearrange("r c t -> (r c) t"), d["mTs"][:, :])
    P.barrier()


def phase_mergeB(P, T, l, io, d, final_out=None):
    SP = CFG["SPLIT"]
    T = T // SP
    NT = T // 512
    with P.phase("mergeB") as ph:
        wo = ph.sb("wo", [128, KC, 2048], BF16)
        mT = [ph.sb("mT%d" % i, [128, KC, 512], BF16) for i in range(2)]
        ht = [ph.sb("ht%d" % i, [128, D_MODEL], F32) for i in range(2)]
        xx = [ph.sb("xx%d" % i, [128, D_MODEL], F32) for i in range(2)]
        junk = ph.sb("junk", [128, D_MODEL], F32)
        lg = ph.sb("lg", [128, D_MODEL], F32)
        lbt = ph.sb("lbt", [128, D_MODEL], F32)
        stt = [ph.sb("stt%d" % i, [128, 8], F32) for i in range(2)]
        hTs = [ph.sb("hTs%d" % i, [128, KC, 512], BF16) for i in range(2)]
        ident = make_ident(P, ph)
        banks = [ph.ps("bk%d" % i, [128, 512]) for i in range(4)]
        tbanks = [ph.ps("tb%d" % i, [128, 512]) for i in range(4)]
        ctr = [0]
        P.dma("pool", wo[:, :, :], io["w_out"][l].rearrange("(kc p) c -> p kc c", p=128), w=["wo"], sem="d_wo")
        P.dma("sp", lg[:, :], io["ln_g_b"][l], w=["lg"], sem="d_lg")
        P.dma("sp", lbt[:, :], io["ln_b_b"][l], w=["lbt"], sem="d_lbt")
        mv = (d["mT"] if SP == 1 else d["mTs"]).rearrange("(kc p) t -> p kc t", p=128)
        for tt in range(NT):
            mi = tt % 2
            P.dma("sp" if SP == 1 else "pool", mT[mi][:, :, :], mv[:, :, tt * 512:(tt + 1) * 512], w=["mT%d" % mi], sem="d_mT%d" % mi)
            for q in range(4):
                tk = tt * 4 + q
                xi = tk % 2
                rows = slice(tk * 128, (tk + 1) * 128)
                P.dma("sp", ht[xi][:, :], d["h"][rows, :], w=["ht%d" % xi], sem="d_ht%d" % xi)
                for nb_ in range(4):
                    for cc in range(KC):
                        P.op("pe", lambda e: e.matmul(banks[nb_][:, :], lhsT=mT[mi][:, cc, q * 128:(q + 1) * 128],
                                                      rhs=wo[:, cc, nb_ * 512:(nb_ + 1) * 512], start=(cc == 0), stop=(cc == KC - 1)),
                             r=["mT%d" % mi, "wo"], w=["bk%d" % nb_], sig=(cc == KC - 1))
                    P.op("dve", lambda e: e.scalar_tensor_tensor(out=xx[xi][:, nb_ * 512:(nb_ + 1) * 512],
                                                                 in0=ht[xi][:, nb_ * 512:(nb_ + 1) * 512], scalar=ALPHA,
                                                                 in1=banks[nb_][:, :], op0=ALU.mult, op1=ALU.add),
                         r=["ht%d" % xi, "bk%d" % nb_], w=["xx%d" % xi])
                s_ = stt[xi]
                sk = "stt%d" % xi
                P.op("act", lambda e: e.activation(out=junk[:], in_=xx[xi][:], func=AF.Copy, accum_out=s_[:, 0:1]), r=["xx%d" % xi], w=["junk", sk])
                P.op("act", lambda e: e.activation(out=junk[:], in_=xx[xi][:], func=AF.Square, accum_out=s_[:, 1:2]), r=["xx%d" % xi], w=["junk", sk])
                P.op("dve", lambda e: e.tensor_scalar(out=s_[:, 2:4], in0=s_[:, 0:2], scalar1=1.0 / D_MODEL, scalar2=None, op0=ALU.mult),
                     r=[sk], w=[sk])
                P.op("dve", lambda e: e.tensor_tensor(out=s_[:, 4:5], in0=s_[:, 2:3], in1=s_[:, 2:3], op=ALU.mult), r=[sk], w=[sk])
                P.op("dve", lambda e: e.tensor_tensor(out=s_[:, 5:6], in0=s_[:, 3:4], in1=s_[:, 4:5], op=ALU.subtract), r=[sk], w=[sk])
                P.op("dve", lambda e: e.tensor_scalar(out=s_[:, 5:6], in0=s_[:, 5:6], scalar1=LN_EPS, scalar2=None, op0=ALU.add), r=[sk], w=[sk])
                P.op("act", lambda e: e.activation(out=s_[:, 6:7], in_=s_[:, 5:6], func=AF.Ln), r=[sk], w=[sk])
                P.op("act", lambda e: e.activation(out=s_[:, 7:8], in_=s_[:, 6:7], func=AF.Exp, scale=-0.5), r=[sk], w=[sk])
                P.op("dve", lambda e: e.tensor_scalar(out=xx[xi][:], in0=xx[xi][:], scalar1=s_[:, 2:3], scalar2=s_[:, 7:8],
                                                      op0=ALU.subtract, op1=ALU.mult), r=["xx%d" % xi, sk], w=["xx%d" % xi])
                P.op("pool", lambda e: e.tensor_tensor(out=xx[xi][:], in0=xx[xi][:], in1=lg[:], op=ALU.mult), r=["xx%d" % xi, "lg"], w=["xx%d" % xi])
                P.op("dve", lambda e: e.tensor_tensor(out=xx[xi][:], in0=xx[xi][:], in1=lbt[:], op=ALU.add), r=["xx%d" % xi, "lbt"], w=["xx%d" % xi])
                dst = final_out if final_out is not None else d["h"]
                P.dma("pool", dst[rows, :], xx[xi][:, :], r=["xx%d" % xi], w=[], sem="d_xo%d" % xi)
                if final_out is None:
                    emit_transpose_tile(P, xx[xi], "xx%d" % xi, ident, tbanks, ctr, hTs[mi], "hTs%d" % mi, q * 128)
            if final_out is None:
                P.dma("sp", d["hT"].rearrange("(kc p) t -> p kc t", p=128)[:, :, tt * 512:(tt + 1) * 512], hTs[mi][:, :, :],
                      r=["hTs%d" % mi], sem="d_hTs%d" % mi)


def IO_SPECS(T):
    SP = CFG["SPLIT"]
    YB, NHA, NHC = CFG["YB"], CFG["NHA"], CFG["NHC"]
    ncols = IN_COLS if SP == 1 else (4 * YB) * 3 + (3 * YB + 128 + YB) + 8192
    nmu = (3 * YB + 128) // 128
    return {
        "x": ([T // SP, D_MODEL], F32),
        "w_in": ([DEPTH, D_MODEL, ncols], F32),
        "rw_mu_t": ([DEPTH, 128, nmu], F32),
        "rw_vmu_t": ([DEPTH - 1, 128, 1], F32),
        "rw_v1": ([DEPTH - 1, D_MODEL, 32], F32),
        "a_strip": ([NHA, 128, 640], F32),
        "a_c31": ([128, NHA], F32),
        "da_lambda_b": ([DEPTH, 128, 256], F32),
        "da_subln_t": ([DEPTH, 128, 1], F32),
        "hg_lower_t": ([128, NHA, 4], F32),
        "hg_lower_r": ([1, 4, 128 * NHA], F32),
        "hg_norm_t": ([DEPTH, 128, 1], F32),
        "c_par": ([DEPTH, 64, NHC, 8], F32),
        "rw_w2": ([DEPTH, 64, 64 * NHC], F32),
        "rw_a2": ([DEPTH, 64, 64 * NHC], F32),
        "rw_v2": ([DEPTH - 1, 32, 64 * NHC], F32),
        "w_branch": ([DEPTH, 4, YB, D_MODEL], F32),
        "w_out": ([DEPTH, D_MODEL, D_MODEL], F32),
        "ln_g_b": ([DEPTH, 128, D_MODEL], F32),
        "ln_b_b": ([DEPTH, 128, D_MODEL], F32),
    }


def build_program(T, depth=DEPTH, debug_outs=()):
    SP = CFG["SPLIT"]
    nc = bass.Bass("TRN2", target_bir_lowering=False)
    io = {k: nc.dram_tensor(k, list(s), dt, kind="ExternalInput").ap() for k, (s, dt) in IO_SPECS(T).items()}
    out = nc.dram_tensor("out", [T // SP, D_MODEL], F32, kind="ExternalOutput").ap()
    d = declare_dram(nc, T, debug_outs=debug_outs)
    P = Prog(nc)
    phase_prep0(P, T // SP, io, d)
    for l in range(depth):
        if SP > 1:
            phase_gather_hT(P, T, d)
        phase_gemm(P, T, l, io, d)
        phase_mixA(P, T, l, io, d)
        phase_b0(P, T, l, io, d)
        phase_mixB(P, T, l, io, d)
        phase_mixC(P, T, l, io, d)
        phase_mixD(P, T, l, io, d)
        phase_mergeA(P, T, l, io, d)
        if SP > 1:
            phase_scatter_mT(P, T, d)
        phase_mergeB(P, T, l, io, d, final_out=(out if l == depth - 1 else None))
    P.barrier()
    return nc, P


def host_shared_inputs(inp):
    f = lambda a: np.ascontiguousarray(np.asarray(a, dtype=np.float32))
    L = DEPTH
    sh = {}
    sh["w_in"] = f(inp["w_in"])
    sh["rw_mu_t"] = f(np.asarray(inp["rw_mu"]).reshape(L, 25, 128).transpose(0, 2, 1))
    vmu = np.zeros((L - 1, 128, 1), np.float32)
    vmu[:, :32, 0] = np.asarray(inp["rw_v_mu"])
    sh["rw_vmu_t"] = vmu
    sh["rw_v1"] = f(inp["rw_v1"])
    rel = np.asarray(inp["rel_bias"], dtype=np.float32)
    sh["a_strip"] = host_a_strip(rel)
    sh["a_c31"] = f(np.broadcast_to(rel[31][None, :], (128, 8)))
    sh["da_lambda_b"] = f(np.broadcast_to(np.asarray(inp["da_lambda"]).reshape(L, 1, 256), (L, 128, 256)))
    sh["da_subln_t"] = f(np.asarray(inp["da_subln"]).reshape(L, 128, 1))
    sh["hg_lower_t"] = f(np.asarray(inp["hg_lower"]).reshape(L, 8, 128).transpose(2, 1, 0))
    sh["hg_lower_r"] = f(np.asarray(inp["hg_lower"]).reshape(1, L, 1024))
    sh["hg_norm_t"] = f(np.asarray(inp["hg_norm"]).reshape(L, 128, 1))
    hk = lambda a: np.asarray(a).reshape(L, 16, 64).transpose(0, 2, 1)
    v0 = np.zeros((L, 1024), np.float32)
    v0[1:] = np.asarray(inp["rw_v0"])
    cp = np.stack([hk(inp["rw_w0"]), hk(inp["rw_a0"]), hk(inp["rw_kk"]), hk(inp["rw_ka"]),
                   np.asarray(inp["rw_rk"]).transpose(0, 2, 1), hk(inp["rw_lnx_g"]), hk(inp["rw_lnx_b"]), hk(v0)], axis=-1)
    sh["c_par"] = f(cp)
    sh["rw_w2"] = f(inp["rw_w2"])
    sh["rw_a2"] = f(inp["rw_a2"])
    sh["rw_v2"] = f(inp["rw_v2"])
    sh["w_branch"] = f(inp["w_branch"])
    sh["w_out"] = f(inp["w_out"])
    sh["ln_g_b"] = f(np.broadcast_to(np.asarray(inp["ln_g"])[:, None, :], (L, 128, D_MODEL)))
    sh["ln_b_b"] = f(np.broadcast_to(np.asarray(inp["ln_b"])[:, None, :], (L, 128, D_MODEL)))
    return sh


def host_core_inputs(sh, hh):
    SP = CFG["SPLIT"]
    if SP == 1:
        return dict(sh)
    YB, NHA, NHC = CFG["YB"], CFG["NHA"], CFG["NHC"]
    a = slice(hh * YB, (hh + 1) * YB)
    segs = []
    for base in (O_AQ, O_AK, O_AV, O_AG, O_HQ, O_HF, O_HI, O_HG):
        segs.append(np.arange(base + hh * YB, base + (hh + 1) * YB))
    for j in range(3):
        segs.append(np.arange(O_RM + j * 1024 + hh * YB, O_RM + j * 1024 + (hh + 1) * YB))
    segs.append(np.arange(O_RM + 3072, O_RM + 3200))
    segs.append(np.arange(O_RG + hh * YB, O_RG + (hh + 1) * YB))
    for base in (O_SQ, O_SK, O_SV, O_SG):
        segs.append(np.arange(base + hh * YB, base + (hh + 1) * YB))
    segs.append(np.arange(O_MG, O_MG + 8192))
    cols = np.concatenate(segs)
    m = {}
    m["w_in"] = np.ascontiguousarray(sh["w_in"][:, :, cols])
    mu_full = sh["rw_mu_t"]
    blk = []
    for j in range(3):
        blk += list(range(j * 8 + hh * (YB // 128), j * 8 + (hh + 1) * (YB // 128)))
    blk.append(24)
    m["rw_mu_t"] = np.ascontiguousarray(mu_full[:, :, blk])
    m["rw_vmu_t"] = sh["rw_vmu_t"]
    m["rw_v1"] = sh["rw_v1"]
    ha = slice(hh * NHA, (hh + 1) * NHA)
    hc = slice(hh * NHC, (hh + 1) * NHC)
    m["a_strip"] = np.ascontiguousarray(sh["a_strip"][ha])
    m["a_c31"] = np.ascontiguousarray(sh["a_c31"][:, ha])
    m["da_lambda_b"] = sh["da_lambda_b"]
    m["da_subln_t"] = sh["da_subln_t"]
    m["hg_lower_t"] = np.ascontiguousarray(sh["hg_lower_t"][:, ha, :])
    m["hg_lower_r"] = np.ascontiguousarray(sh["hg_lower_r"][:, :, a])
    m["hg_norm_t"] = sh["hg_norm_t"]
    m["c_par"] = np.ascontiguousarray(sh["c_par"][:, :, hc, :])
    m["rw_w2"] = np.ascontiguousarray(sh["rw_w2"][:, :, a])
    m["rw_a2"] = np.ascontiguousarray(sh["rw_a2"][:, :, a])
    m["rw_v2"] = np.ascontiguousarray(sh["rw_v2"][:, :, a])
    m["w_branch"] = np.ascontiguousarray(sh["w_branch"][:, :, a, :])
    m["w_out"] = sh["w_out"]
    m["ln_g_b"] = sh["ln_g_b"]
    m["ln_b_b"] = sh["ln_b_b"]
    return m


_CACHE = {}
SPLIT = 2


def kernel(**inputs):
    x = np.asarray(inputs["x"], dtype=np.float32)
    B, T, _ = x.shape
    ncore = B * SPLIT
    configure(SPLIT, [[2 * i, 2 * i + 1] for i in range(ncore // 2)] if SPLIT == 2 else None)
    if T not in _CACHE:
        _CACHE[T] = build_program(T)
    nc, _ = _CACHE[T]
    sh = host_shared_inputs(inputs)
    per_half = [host_core_inputs(sh, hh) for hh in range(SPLIT)]
    TO = T // SPLIT
    in_maps = []
    for b in range(B):
        for hh in range(SPLIT):
            m = dict(per_half[hh])
            m["x"] = np.ascontiguousarray(x[b, hh * TO:(hh + 1) * TO])
            in_maps.append(m)
    res = run_bass_kernel_spmd(nc, in_maps, core_ids=list(range(ncore)))
    out = np.empty((B, T, D_MODEL), np.float32)
    for b in range(B):
        for hh in range(SPLIT):
            out[b, hh * TO:(hh + 1) * TO] = np.asarray(res.results[b * SPLIT + hh]["out"], dtype=np.float32)
    return out
```

```python
import contextlib
import math
import numpy as np
import concourse.bass as bass
import concourse.mybir as mybir
from concourse.bass_utils import run_bass_kernel_spmd

F32 = mybir.dt.float32
BF16 = mybir.dt.bfloat16
AF = mybir.ActivationFunctionType
ALU = mybir.AluOpType
AX = mybir.AxisListType

D_MODEL = 2048
DEPTH = 4
MIXW = 1024
IN_COLS = 24704
KC = D_MODEL // 128
ALPHA = (2 * DEPTH) ** 0.25
LN_EPS = 1e-5
RMS_EPS = 1e-6
RW_LN_EPS = 64e-5
NEG = -30000.0

CFG = dict(SPLIT=1, YB=1024, NHA=8, NHC=16, GROUPS=None)


def configure(split, groups=None):
    CFG.update(SPLIT=split, YB=1024 // split, NHA=8 // split, NHC=16 // split, GROUPS=groups)


O_AQ, O_AK, O_AV, O_AG = 0, 1024, 2048, 3072
O_HQ, O_HF, O_HI, O_HG = 4096, 5120, 6144, 7168
O_RM, O_RG = 8192, 11392
O_SQ, O_SK, O_SV, O_SG = 12416, 13440, 14464, 15488
O_MG = 16512


class Prog:
    ENG = ("pe", "act", "dve", "pool", "sp")

    def __init__(self, nc):
        self.nc = nc
        self.E = dict(pe=nc.tensor, act=nc.scalar, dve=nc.vector, pool=nc.gpsimd, sp=nc.sync)
        self.sems = {}
        self.semval = {}
        self.known = {e: {} for e in self.ENG}
        self.res = {}
        self.pend = {e: ([], []) for e in self.ENG}
        self.stack = contextlib.ExitStack()
        self.n_inst = 0
        self.uid = 0
        self.ecount = {e: 0 for e in self.ENG}
        self.marks = []

    def sem(self, key):
        if key not in self.sems:
            self.sems[key] = self.stack.enter_context(self.nc.semaphore("s_" + key))
            self.semval[key] = 0
        return self.sems[key]

    def _res(self, k):
        r = self.res.get(k)
        if r is None:
            r = [None, {}]
            self.res[k] = r
        return r

    def _wait(self, eng, tok):
        if tok is None:
            return
        sk, v = tok
        if eng == "pe" and sk == "pe":
            return
        if self.known[eng].get(sk, 0) >= v:
            return
        self.E[eng].wait_ge(self.sems[sk], v)
        self.known[eng][sk] = v
        self.n_inst += 1

    def _deps(self, eng, r, w):
        for k in r:
            self._wait(eng, self._res(k)[0])
        for k in w:
            rr = self._res(k)
            self._wait(eng, rr[0])
            for sk, v in list(rr[1].items()):
                self._wait(eng, (sk, v))

    def op(self, eng, fn, r=(), w=(), sig=True):
        self._deps(eng, r, w)
        inst = fn(self.E[eng])
        self.n_inst += 1
        self.ecount[eng] += 1
        pr, pw = self.pend[eng]
        pr.extend(r)
        pw.extend(w)
        if not sig:
            return inst
        self.sem(eng)
        self.semval[eng] += 1
        inst.then_inc(self.sems[eng], 1)
        tok = (eng, self.semval[eng])
        for k in pr:
            rr = self._res(k)
            rr[1][eng] = tok[1]
        for k in pw:
            rr = self._res(k)
            rr[0] = tok
            rr[1] = {}
        pr.clear()
        pw.clear()
        return inst

    def dma(self, q, out, in_, r=(), w=(), sem=None):
        assert sem is not None
        self._deps(q, r, w)
        self.sem(sem)
        inst = self.E[q].dma_start(out=out, in_=in_)
        self.n_inst += 1
        self.semval[sem] += 16
        inst.then_inc(self.sems[sem], 16)
        tok = (sem, self.semval[sem])
        for k in r:
            self._res(k)[1][sem] = tok[1]
        for k in w:
            rr = self._res(k)
            rr[0] = tok
            rr[1] = {}
        return inst

    def coll(self, kind, in_ap, out_ap):
        self.sem("cc")
        alu = ALU.add if kind == "ReduceScatter" else ALU.bypass
        inst = self.E["pool"].collective_compute(kind, alu, replica_groups=CFG["GROUPS"], ins=[in_ap.opt()], outs=[out_ap.opt()])
        self.n_inst += 1
        self.semval["cc"] += 1
        inst.then_inc(self.sems["cc"], 1)
        self._wait("pool", ("cc", self.semval["cc"]))

    def barrier(self):
        for e in self.ENG:
            assert not self.pend[e][0] and not self.pend[e][1], "pending unsignalled ops at barrier"
        for e in self.ENG:
            for sk, v in self.semval.items():
                if v > 0:
                    self._wait(e, (sk, v))
        self.res = {}

    @contextlib.contextmanager
    def phase(self, name=""):
        self.barrier()
        self.marks.append((name, dict(self.ecount)))
        st = contextlib.ExitStack()
        ph = Phase(self, st)
        try:
            yield ph
        finally:
            self.barrier()
            st.close()


class Phase:
    def __init__(self, prog, st):
        self.p = prog
        self.st = st

    def sb(self, name, shape, dt):
        self.p.uid += 1
        return self.st.enter_context(self.p.nc.sbuf_tensor("%s_%d" % (name, self.p.uid), list(shape), dt))

    def ps(self, name, shape, dt=F32):
        self.p.uid += 1
        return self.st.enter_context(self.p.nc.psum_tensor("%s_%d" % (name, self.p.uid), list(shape), dt))


def declare_dram(nc, T, debug_outs=(), debug_ins=()):
    d = {}

    def t(name, shape, dt, kind="Internal"):
        if name in debug_outs:
            kind = "ExternalOutput"
        if name in debug_ins:
            kind = "ExternalInput"
        d[name] = nc.dram_tensor(name, list(shape), dt, kind=kind).ap()

    YB = CFG["YB"]
    SP = CFG["SPLIT"]
    TO = T // SP
    t("hT", [D_MODEL, TO], BF16)
    t("h", [TO, D_MODEL], F32)
    if SP > 1:
        NCH = max(1, (D_MODEL * TO * 2) // (2 << 20))
        d["NCH"] = NCH
        t("hTg", [NCH, SP, D_MODEL // NCH, TO], BF16)
        t("mTp", [SP, D_MODEL, TO], F32)
        t("mTs", [D_MODEL, TO], F32)
    t("h0g", [SP, D_MODEL], F32)
    t("Bd0", [1, 8], F32)
    t("AqT", [YB, T], BF16)
    t("AkT", [YB, T], BF16)
    t("Av", [T, YB], BF16)
    t("AgT", [YB, T], BF16)
    t("BqT", [YB, T], BF16)
    t("BfT", [YB, T], F32)
    t("Bi", [T, YB], BF16)
    t("BgT", [YB, T], BF16)
    t("CmT", [3 * YB + 128, T], F32)
    t("CgT", [YB, T], BF16)
    t("CvdT", [32, T], F32)
    t("Cvf", [YB, T], F32)
    t("DqT", [YB, T], BF16)
    t("DkT", [YB, T], BF16)
    t("Dv", [T, YB], BF16)
    t("DgT", [YB, T], BF16)
    t("MgT", [8192, T], BF16)
    t("yT", [4 * YB, T], BF16)
    t("mT", [D_MODEL, T], BF16)
    return d


def phase_gemm(P, T, l, io, d):
    nc = P.nc
    NT = T // 512
    with P.phase("gemm") as ph:
        hT = ph.sb("hT", [128, KC, T + 1], BF16)
        wsb = [ph.sb("wsb%d" % i, [128, KC, 512], BF16) for i in range(2)]
        SC = min(T, 2048)
        stg = [ph.sb("stg%d" % i, [128, SC], F32) for i in range(2)]
        stb = [ph.sb("stb%d" % i, [128, max(SC, 1024)], BF16) for i in range(2)]
        tmp = [ph.sb("tmp%d" % i, [128, 512], F32) for i in range(2)]
        mu = ph.sb("mu", [128, 26], F32)
        om = ph.sb("om", [128, 26], F32)
        banks = [ph.ps("bk%d" % i, [128, 512]) for i in range(8)]

        P.op("pool", lambda e: e.memset(hT[:, :, 0:1], 0.0), w=["hT"])
        if CFG["SPLIT"] == 1:
            P.dma("sp", hT[:, :, 1:T + 1], d["hT"].rearrange("(kc p) t -> p kc t", p=128), w=["hT"], sem="d_hT")
        else:
            NCH = d["NCH"]
            TO = T // CFG["SPLIT"]
            JJ = KC // NCH
            for r_ in range(CFG["SPLIT"]):
                for i_ in range(NCH):
                    P.dma("sp", hT[:, i_ * JJ:(i_ + 1) * JJ, 1 + r_ * TO:1 + (r_ + 1) * TO],
                          d["hTg"][i_, r_].rearrange("(j p) t -> p j t", p=128), w=["hT"], sem="d_hT")
        NMU = io["rw_mu_t"].shape[2]
        P.op("pool", lambda e: e.memset(mu[:], 0.0), w=["mu"])
        P.dma("sp", mu[:, 0:NMU], io["rw_mu_t"][l], w=["mu"], sem="d_mu")
        if l > 0:
            P.dma("sp", mu[:, 25:26], io["rw_vmu_t"][l - 1], w=["mu"], sem="d_mu")
        else:
            P.op("pool", lambda e: e.memset(mu[:, 25:26], 0.0), w=["mu"])
        P.op("dve", lambda e: e.tensor_scalar(out=om[:], in0=mu[:], scalar1=-1.0, scalar2=1.0,
                                              op0=ALU.mult, op1=ALU.add), r=["mu"], w=["om"])

        w_l = io["w_in"][l].rearrange("(kc p) c -> p kc c", p=128)
        st = dict(wi=0, bi=0, si=0, ti=0)

        def load_w(src_ap, ncols):
            i = st["wi"] % 2
            st["wi"] += 1
            P.dma("pool", wsb[i][:, :, 0:ncols], src_ap, w=["wsb%d" % i], sem="d_wsb%d" % i)
            return i

        def bank():
            b = st["bi"] % 8
            st["bi"] += 1
            return b

        def mm_F(wi, c0, tt, shift, b):
            off = 0 if shift else 1
            for kc in range(KC):
                P.op("pe", lambda e, kc=kc: e.matmul(banks[b][:, :], lhsT=wsb[wi][:, kc, c0:c0 + 128],
                                                     rhs=hT[:, kc, off + tt * 512: off + tt * 512 + 512],
                                                     start=(kc == 0), stop=(kc == KC - 1)),
                     r=["wsb%d" % wi, "hT"], w=["bk%d" % b], sig=(kc == KC - 1))

        def job_F(col0, ncols, dest, drow0, kind, scale=1.0, mucol0=None, out_dt=BF16, nrows=128):
            for s0 in range(0, ncols, 512):
                sw = min(512, ncols - s0)
                wi = load_w(w_l[:, :, col0 + s0: col0 + s0 + sw], sw)
                for c0 in range(0, sw, 128):
                    for tt in range(NT):
                        if (tt * 512) % SC == 0:
                            si = st["si"] % 2
                            st["si"] += 1
                            so = stb[si] if out_dt == BF16 else stg[si]
                            skey = ("stb%d" if out_dt == BF16 else "stg%d") % si
                        b = bank()
                        mm_F(wi, c0, tt, False, b)
                        lo = (tt * 512) % SC
                        osl = so[:, lo:lo + 512]
                        if kind == "copy":
                            P.op("dve", lambda e: e.tensor_copy(out=osl, in_=banks[b][:, :]),
                                 r=["bk%d" % b], w=[skey])
                        elif kind == "scale":
                            P.op("act", lambda e: e.activation(out=osl, in_=banks[b][:, :], func=AF.Copy, scale=scale),
                                 r=["bk%d" % b], w=[skey])
                        elif kind == "silu":
                            P.op("act", lambda e: e.activation(out=osl, in_=banks[b][:, :], func=AF.Silu),
                                 r=["bk%d" % b], w=[skey])
                        elif kind == "sigmoid":
                            P.op("act", lambda e: e.activation(out=osl, in_=banks[b][:, :], func=AF.Sigmoid),
                                 r=["bk%d" % b], w=[skey])
                        elif kind == "shift":
                            b2 = bank()
                            mm_F(wi, c0, tt, True, b2)
                            mc = mucol0 + (s0 + c0) // 128
                            ti = st["ti"] % 2
                            st["ti"] += 1
                            P.op("dve", lambda e: e.tensor_scalar(out=tmp[ti][:], in0=banks[b2][:, :],
                                                                  scalar1=mu[:, mc:mc + 1], scalar2=None, op0=ALU.mult),
                                 r=["bk%d" % b2, "mu"], w=["tmp%d" % ti])
                            P.op("dve", lambda e: e.scalar_tensor_tensor(out=osl, in0=banks[b][:, :], scalar=om[:, mc:mc + 1],
                                                                         in1=tmp[ti][:], op0=ALU.mult, op1=ALU.add),
                                 r=["bk%d" % b, "om", "tmp%d" % ti], w=[skey])
                        else:
                            raise ValueError(kind)
                        if (tt * 512 + 512) % SC == 0:
                            r0 = drow0 + s0 + c0
                            t_lo = tt * 512 + 512 - SC
                            P.dma("sp", dest[r0:r0 + nrows, t_lo:t_lo + SC], so[0:nrows, 0:SC], r=[skey], w=[], sem="d_" + skey)

        def job_T(col0, dest):
            YBl = CFG["YB"]
            wis = []
            for s0 in range(0, YBl, 512):
                wis.append(load_w(w_l[:, :, col0 + s0: col0 + s0 + 512], 512))
            for tk in range(T // 128):
                si = st["si"] % 2
                st["si"] += 1
                for h2 in range(YBl // 512):
                    b = bank()
                    for kc in range(KC):
                        P.op("pe", lambda e, kc=kc: e.matmul(banks[b][:, :], lhsT=hT[:, kc, 1 + tk * 128: 1 + tk * 128 + 128],
                                                             rhs=wsb[wis[h2]][:, kc, :], start=(kc == 0), stop=(kc == KC - 1)),
                             r=["wsb%d" % wis[h2], "hT"], w=["bk%d" % b], sig=(kc == KC - 1))
                    P.op("dve", lambda e: e.tensor_copy(out=stb[si][:, h2 * 512:(h2 + 1) * 512], in_=banks[b][:, :]),
                         r=["bk%d" % b], w=["stb%d" % si])
                P.dma("sp", dest[tk * 128:(tk + 1) * 128, :], stb[si][:, 0:YBl], r=["stb%d" % si], sem="d_stb%d" % si)

        YB = CFG["YB"]
        if CFG["SPLIT"] == 1:
            o = dict(AQ=O_AQ, AK=O_AK, AV=O_AV, AG=O_AG, HQ=O_HQ, HF=O_HF, HI=O_HI, HG=O_HG, RM=O_RM, RG=O_RG,
                     SQ=O_SQ, SK=O_SK, SV=O_SV, SG=O_SG, MG=O_MG)
        else:
            cb = 4 * YB
            cc_ = 2 * cb
            cd = cc_ + 3 * YB + 128 + YB
            o = dict(AQ=0, AK=YB, AV=2 * YB, AG=3 * YB, HQ=cb, HF=cb + YB, HI=cb + 2 * YB, HG=cb + 3 * YB,
                     RM=cc_, RG=cc_ + 3 * YB + 128, SQ=cd, SK=cd + YB, SV=cd + 2 * YB, SG=cd + 3 * YB, MG=cd + 4 * YB)
        job_F(o["AQ"], YB, d["AqT"], 0, "scale", scale=0.125)
        job_F(o["AK"], YB, d["AkT"], 0, "copy")
        job_T(o["AV"], d["Av"])
        job_F(o["AG"], YB, d["AgT"], 0, "silu")
        job_F(o["HQ"], YB, d["BqT"], 0, "copy")
        job_F(o["HF"], YB, d["BfT"], 0, "copy", out_dt=F32)
        job_T(o["HI"], d["Bi"])
        job_F(o["HG"], YB, d["BgT"], 0, "silu")
        job_F(o["RM"], 3 * YB + 128, d["CmT"], 0, "shift", mucol0=0, out_dt=F32)
        job_F(o["RG"], YB, d["CgT"], 0, "silu")
        job_F(o["SQ"], YB, d["DqT"], 0, "scale", scale=128 ** -0.5)
        job_F(o["SK"], YB, d["DkT"], 0, "copy")
        job_T(o["SV"], d["Dv"])
        job_F(o["SG"], YB, d["DgT"], 0, "silu")
        job_F(o["MG"], 8192, d["MgT"], 0, "sigmoid")
        if l > 0:
            i = st["wi"] % 2
            st["wi"] += 1
            P.op("pool", lambda e: e.memset(wsb[i][:, :, 0:128], 0.0), w=["wsb%d" % i])
            P.dma("pool", wsb[i][:, :, 0:32], io["rw_v1"][l - 1].rearrange("(kc p) c -> p kc c", p=128),
                  w=["wsb%d" % i], sem="d_wsb%d" % i)
            for tt in range(NT):
                if (tt * 512) % SC == 0:
                    si = st["si"] % 2
                    st["si"] += 1
                b = bank()
                mm_F(i, 0, tt, False, b)
                b2 = bank()
                mm_F(i, 0, tt, True, b2)
                ti = st["ti"] % 2
                st["ti"] += 1
                lo = (tt * 512) % SC
                osl = stg[si][:, lo:lo + 512]
                P.op("dve", lambda e: e.tensor_scalar(out=tmp[ti][:], in0=banks[b2][:, :], scalar1=mu[:, 25:26],
                                                      scalar2=None, op0=ALU.mult), r=["bk%d" % b2, "mu"], w=["tmp%d" % ti])
                P.op("dve", lambda e: e.scalar_tensor_tensor(out=osl, in0=banks[b][:, :], scalar=om[:, 25:26], in1=tmp[ti][:],
                                                             op0=ALU.mult, op1=ALU.add),
                     r=["bk%d" % b, "om", "tmp%d" % ti], w=["stg%d" % si])
                if (tt * 512 + 512) % SC == 0:
                    t_lo = tt * 512 + 512 - SC
                    P.dma("sp", d["CvdT"][:, t_lo:t_lo + SC], stg[si][0:32, 0:SC], r=["stg%d" % si], sem="d_stg%d" % si)


def emit_transpose_tile(P, src, src_key, ident, banks, bank_ctr, hTs, hTs_key, col0):
    for g in range(4):
        b = bank_ctr[0] % len(banks)
        bank_ctr[0] += 1
        for j in range(4):
            kc = g * 4 + j
            P.op("pe", lambda e: e.transpose(out=banks[b][:, j * 128:(j + 1) * 128], in_=src[:, kc * 128:(kc + 1) * 128],
                                             identity=ident[:, :]),
                 r=[src_key, "ident"], w=["tb%d" % b], sig=(j == 3))
        eng = "act" if g % 2 else "dve"
        if eng == "act":
            P.op("act", lambda e: e.activation(out=hTs[:, g * 4:(g + 1) * 4, col0:col0 + 128],
                                               in_=banks[b][:, :].rearrange("p (j t) -> p j t", j=4), func=AF.Copy),
                 r=["tb%d" % b], w=[hTs_key])
        else:
            P.op("dve", lambda e: e.tensor_copy(out=hTs[:, g * 4:(g + 1) * 4, col0:col0 + 128],
                                                in_=banks[b][:, :].rearrange("p (j t) -> p j t", j=4)),
                 r=["tb%d" % b], w=[hTs_key])


def make_ident(P, ph, dt=F32, name="ident"):
    ident = ph.sb(name, [128, 128], dt)
    if dt == F32:
        P.op("pool", lambda e: e.memset(ident[:], 1.0), w=[name])
        P.op("pool", lambda e: e.affine_select(out=ident[:], in_=ident[:], pattern=[[-1, 128]], compare_op=ALU.is_equal,
                                               fill=0.0, base=0, channel_multiplier=1), r=[name], w=[name])
    else:
        tmpi = ph.sb(name + "_f", [128, 128], F32)
        P.op("pool", lambda e: e.memset(tmpi[:], 1.0), w=[name + "_f"])
        P.op("pool", lambda e: e.affine_select(out=tmpi[:], in_=tmpi[:], pattern=[[-1, 128]], compare_op=ALU.is_equal,
                                               fill=0.0, base=0, channel_multiplier=1), r=[name + "_f"], w=[name + "_f"])
        P.op("dve", lambda e: e.tensor_copy(out=ident[:], in_=tmpi[:]), r=[name + "_f"], w=[name])
    return ident


def phase_prep0(P, T, io, d):
    with P.phase("prep0") as ph:
        ident = make_ident(P, ph)
        banks = [ph.ps("tb%d" % i, [128, 512]) for i in range(4)]
        xt = [ph.sb("xt%d" % i, [128, D_MODEL], F32) for i in range(2)]
        hTs = [ph.sb("hTs%d" % i, [128, KC, 512], BF16) for i in range(2)]
        ctr = [0]
        for tt in range(T // 512):
            hi = tt % 2
            for q in range(4):
                tk = tt * 4 + q
                xi = tk % 2
                P.dma("sp", xt[xi][:, :], io["x"][tk * 128:(tk + 1) * 128, :], w=["xt%d" % xi], sem="d_xt%d" % xi)
                emit_transpose_tile(P, xt[xi], "xt%d" % xi, ident, banks, ctr, hTs[hi], "hTs%d" % hi, q * 128)
                P.dma("pool", d["h"][tk * 128:(tk + 1) * 128, :], xt[xi][:, :], r=["xt%d" % xi], sem="d_xo%d" % xi)
            P.dma("sp", d["hT"].rearrange("(kc p) t -> p kc t", p=128)[:, :, tt * 512:(tt + 1) * 512], hTs[hi][:, :, :],
                  r=["hTs%d" % hi], sem="d_hTs%d" % hi)


def t5_bucket_np(dist):
    n = np.maximum(dist, 0)
    nf = np.maximum(n, 1).astype(np.float32)
    large = 16 + (np.log(nf / np.float32(16)) / np.float32(math.log(128 / 16)) * np.float32(16)).astype(np.int32)
    large = np.minimum(large, 31)
    return np.where(n < 16, n, large)


def phase_mixA(P, T, l, io, d, NH=None):
    NH = NH or CFG["NHA"]
    lam_init = 0.8 - 0.6 * math.exp(-0.3 * l)
    NG = T // 512
    NKB = T // 128
    with P.phase("mixA") as ph:
        qT = [ph.sb("qT%d" % i, [128, T], BF16) for i in range(2)]
        kT = [ph.sb("kT%d" % i, [128, T], BF16) for i in range(2)]
        V = [ph.sb("V%d" % i, [128, NKB, 128], BF16) for i in range(2)]
        gT = [ph.sb("gT%d" % i, [128, T], BF16) for i in range(2)]
        stf = [ph.sb("stf%d" % i, [128, 640], F32) for i in range(2)]
        shi = [ph.sb("shi%d" % i, [128, 640], BF16) for i in range(2)]
        slo = [ph.sb("slo%d" % i, [128, 640], BF16) for i in range(2)]
        yst = [ph.sb("yst%d" % i, [128, T], BF16) for i in range(2)]
        pT = [ph.sb("pT%d" % i, [128, 512], BF16) for i in range(3)]
        wk = {n: ph.sb(n, [128, 512], F32) for n in ("rl1", "rl2", "a1", "a2", "sq", "t1")}
        identb = make_ident(P, ph, BF16, "identb")
        onesf = ph.sb("onesf", [128, 128], F32)
        onesb = ph.sb("onesb", [128, 128], BF16)
        c31 = ph.sb("c31", [128, CFG["NHA"]], F32)
        lam = ph.sb("lam", [128, 256], F32)
        lw = ph.sb("lamw", [128, 128], F32)
        sc = ph.sb("lamsc", [128, 8], F32)
        sub = ph.sb("subln", [128, 1], F32)
        sbk = [ph.ps("sbk%d" % i, [128, 512]) for i in range(3)]
        obk = [ph.ps("obk%d" % i, [128, 512]) for i in range(2)]
        lbk = [ph.ps("lbk%d" % i, [128, 512]) for i in range(2)]

        P.op("pool", lambda e: e.memset(onesf[:], 1.0), w=["onesf"])
        P.op("pool", lambda e: e.memset(onesb[:], 1.0), w=["onesb"])
        P.dma("sp", c31[:, :], io["a_c31"], w=["c31"], sem="d_c31")
        P.dma("sp", lam[:, :], io["da_lambda_b"][l], w=["lam"], sem="d_lam")
        P.dma("sp", sub[:, :], io["da_subln_t"][l], w=["subln"], sem="d_sub")
        P.op("dve", lambda e: e.tensor_tensor(out=lw[:, 0:64], in0=lam[:, 0:64], in1=lam[:, 64:128], op=ALU.mult), r=["lam"], w=["lamw"])
        P.op("dve", lambda e: e.tensor_tensor(out=lw[:, 64:128], in0=lam[:, 128:192], in1=lam[:, 192:256], op=ALU.mult), r=["lam"], w=["lamw"])
        P.op("dve", lambda e: e.reduce_sum(out=sc[:, 0:2], in_=lw[:].rearrange("p (a b) -> p a b", a=2), axis=AX.X), r=["lamw"], w=["lamsc"])
        P.op("act", lambda e: e.activation(out=sc[:, 2:4], in_=sc[:, 0:2], func=AF.Exp), r=["lamsc"], w=["lamsc"])
        P.op("dve", lambda e: e.tensor_tensor(out=sc[:, 4:5], in0=sc[:, 3:4], in1=sc[:, 2:3], op=ALU.subtract), r=["lamsc"], w=["lamsc"])
        P.op("dve", lambda e: e.tensor_scalar(out=sc[:, 5:6], in0=sc[:, 4:5], scalar1=-lam_init, scalar2=None, op0=ALU.add), r=["lamsc"], w=["lamsc"])
        P.op("dve", lambda e: e.tensor_scalar(out=sc[:, 6:7], in0=sub[:, 0:1], scalar1=1.0 - lam_init, scalar2=None, op0=ALU.mult), r=["subln", "lamsc"], w=["lamsc"])
        negl = sc[:, 5:6]
        gsc = sc[:, 6:7]

        def load_head(h):
            i = h % 2
            P.dma("sp", qT[i][:, :], d["AqT"][h * 128:(h + 1) * 128, :], w=["qT%d" % i], sem="d_qT%d" % i)
            P.dma("sp", kT[i][:, :], d["AkT"][h * 128:(h + 1) * 128, :], w=["kT%d" % i], sem="d_kT%d" % i)
            P.dma("sp", V[i][:, :, :], d["Av"].rearrange("(kb p) c -> p kb c", p=128)[:, :, h * 128:(h + 1) * 128],
                  w=["V%d" % i], sem="d_V%d" % i)
            P.dma("sp", gT[i][:, :], d["AgT"][h * 128:(h + 1) * 128, :], w=["gT%d" % i], sem="d_gT%d" % i)
            P.dma("sp", stf[i][:, :], io["a_strip"][h], w=["stf%d" % i], sem="d_stf%d" % i)
            P.op("dve", lambda e: e.tensor_copy(out=shi[i][:], in_=stf[i][:]), r=["stf%d" % i], w=["shi%d" % i])
            P.op("dve", lambda e: e.tensor_tensor(out=stf[i][:], in0=stf[i][:], in1=shi[i][:], op=ALU.subtract),
                 r=["stf%d" % i, "shi%d" % i], w=["stf%d" % i])
            P.op("dve", lambda e: e.tensor_copy(out=slo[i][:], in_=stf[i][:]), r=["stf%d" % i], w=["slo%d" % i])

        cnt = dict(s=0, p=0)
        load_head(0)
        for h in range(NH):
            i = h % 2
            if h + 1 < NH:
                load_head(h + 1)
            for g in range(NG):
                q0 = g * 512
                blocks = [(m, ki) for m in range(2) for ki in range(4 * g + 4)]
                nblk = 4 * g + 4
                info = {}

                def stage1(bd):
                    m, ki = bd
                    pb = slice(m * 64, m * 64 + 64)
                    j = ki - 4 * g
                    near = j >= -1
                    c0 = 128 * j if j >= 1 else 0
                    n = 512 - c0
                    sb_i = cnt["s"] % 3
                    cnt["s"] += 1
                    S = sbk[sb_i]
                    skey = "sbk%d" % sb_i
                    P.op("pe", lambda e: e.matmul(S[:, c0:512], lhsT=kT[i][pb, ki * 128:(ki + 1) * 128],
                                                  rhs=qT[i][pb, q0 + c0:q0 + 512], start=True, stop=not near),
                         r=["kT%d" % i, "qT%d" % i], w=[skey], sig=not near)
                    if near:
                        so = 128 if j == -1 else 0
                        P.op("pe", lambda e: e.matmul(S[:, c0:512], lhsT=identb[:, :], rhs=shi[i][:, so:so + n],
                                                      start=False, stop=False), r=["identb", "shi%d" % i], w=[skey], sig=False)
                        P.op("pe", lambda e: e.matmul(S[:, c0:512], lhsT=identb[:, :], rhs=slo[i][:, so:so + n],
                                                      start=False, stop=True), r=["identb", "slo%d" % i], w=[skey])
                    info[bd] = (S, skey, near, c0)

                def stage2(bd):
                    m, ki = bd
                    S, skey, near, c0 = info[bd]
                    p_i = cnt["p"] % 3
                    cnt["p"] += 1
                    pk = "pT%d" % p_i
                    if near:
                        P.op("act", lambda e: e.activation(out=pT[p_i][:, c0:512], in_=S[:, c0:512], func=AF.Exp),
                             r=[skey], w=[pk])
                    else:
                        P.op("act", lambda e: e.activation(out=pT[p_i][:, c0:512], in_=S[:, c0:512], func=AF.Exp,
                                                           bias=c31[:, h:h + 1]), r=[skey, "c31"], w=[pk])
                    info[bd] = (p_i, pk, c0)

                def stage3(bd):
                    m, ki = bd
                    p_i, pk, c0 = info.pop(bd)
                    P.op("pe", lambda e: e.matmul(obk[m][:, c0:512], lhsT=V[i][:, ki, :], rhs=pT[p_i][:, c0:512],
                                                  start=(ki == 0), stop=(ki == nblk - 1), skip_group_check=True),
                         r=["V%d" % i, pk], w=["obk%d" % m], sig=False)
                    P.op("pe", lambda e: e.matmul(lbk[m][:, c0:512], lhsT=onesb[:, :], rhs=pT[p_i][:, c0:512],
                                                  start=(ki == 0), stop=(ki == nblk - 1), skip_group_check=True),
                         r=["onesb", pk], w=["lbk%d" % m])

                nb_ = len(blocks)
                for step in range(nb_ + 2):
                    if step < nb_:
                        stage1(blocks[step])
                    if 0 <= step - 1 < nb_:
                        stage2(blocks[step - 1])
                    if 0 <= step - 2 < nb_:
                        stage3(blocks[step - 2])
                P.op("dve", lambda e: e.reciprocal(out=wk["rl1"][:], in_=lbk[0][:, :]), r=["lbk0"], w=["rl1"])
                P.op("dve", lambda e: e.reciprocal(out=wk["rl2"][:], in_=lbk[1][:, :]), r=["lbk1"], w=["rl2"])
                P.op("dve", lambda e: e.tensor_tensor(out=wk["a1"][:], in0=obk[0][:, :], in1=wk["rl1"][:], op=ALU.mult),
                     r=["obk0", "rl1"], w=["a1"])
                P.op("dve", lambda e: e.tensor_tensor(out=wk["a2"][:], in0=obk[1][:, :], in1=wk["rl2"][:], op=ALU.mult),
                     r=["obk1", "rl2"], w=["a2"])
                P.op("dve", lambda e: e.scalar_tensor_tensor(out=wk["a1"][:], in0=wk["a2"][:], scalar=negl, in1=wk["a1"][:],
                                                             op0=ALU.mult, op1=ALU.add), r=["a2", "a1", "lamsc"], w=["a1"])
                P.op("act", lambda e: e.activation(out=wk["sq"][:], in_=wk["a1"][:], func=AF.Square), r=["a1"], w=["sq"])
                sb_i = cnt["s"] % 3
                cnt["s"] += 1
                S = sbk[sb_i]
                skey = "sbk%d" % sb_i
                P.op("pe", lambda e: e.matmul(S[:, :], lhsT=onesf[:, :], rhs=wk["sq"][:], start=True, stop=True),
                     r=["onesf", "sq"], w=[skey])
                P.op("dve", lambda e: e.tensor_scalar(out=wk["t1"][:], in0=S[:, :], scalar1=1.0 / 128, scalar2=RMS_EPS,
                                                      op0=ALU.mult, op1=ALU.add), r=[skey], w=["t1"])
                P.op("act", lambda e: e.activation(out=wk["t1"][:], in_=wk["t1"][:], func=AF.Ln), r=["t1"], w=["t1"])
                P.op("act", lambda e: e.activation(out=wk["t1"][:], in_=wk["t1"][:], func=AF.Exp, scale=-0.5), r=["t1"], w=["t1"])
                P.op("dve", lambda e: e.tensor_tensor(out=wk["a1"][:], in0=wk["a1"][:], in1=wk["t1"][:], op=ALU.mult),
                     r=["a1", "t1"], w=["a1"])
                P.op("dve", lambda e: e.scalar_tensor_tensor(out=yst[i][:, q0:q0 + 512], in0=wk["a1"][:], scalar=gsc,
                                                             in1=gT[i][:, q0:q0 + 512], op0=ALU.mult, op1=ALU.mult),
                     r=["a1", "lamsc", "gT%d" % i], w=["yst%d" % i])
            P.dma("pool", d["yT"][h * 128:(h + 1) * 128, :], yst[i][:, :], r=["yst%d" % i], sem="d_yst%d" % i)


def host_a_strip(rel_bias):
    v = np.arange(640)[None, :]
    s = np.arange(128)[:, None]
    dist = v - s
    bk = t5_bucket_np(dist)
    out = np.empty((8, 128, 640), np.float32)
    for h in range(8):
        out[h] = np.where(dist >= 0, rel_bias[bk, h], np.float32(NEG))
    return out


def phase_mixD(P, T, l, io, d, NH=None):
    NH = NH or CFG["NHA"]
    NG = T // 512
    NKB = T // 128
    with P.phase("mixD") as ph:
        qT = [ph.sb("qT%d" % i, [128, T], BF16) for i in range(2)]
        kT = [ph.sb("kT%d" % i, [128, T], BF16) for i in range(2)]
        V = [ph.sb("V%d" % i, [128, NKB, 128], BF16) for i in range(2)]
        gT = [ph.sb("gT%d" % i, [128, T], BF16) for i in range(2)]
        yst = [ph.sb("yst%d" % i, [128, T], BF16) for i in range(2)]
        ee = [ph.sb("ee%d" % i, [128, 512], F32) for i in range(3)]
        sp = [ph.sb("sp%d" % i, [128, 512], F32) for i in range(3)]
        lk = [ph.sb("lk%d" % i, [128, 512], F32) for i in range(3)]
        aT = [ph.sb("aT%d" % i, [128, 512], BF16) for i in range(3)]
        rsum = [ph.sb("rsum%d" % i, [128, 512], F32) for i in range(2)]
        onesf = ph.sb("onesf", [128, 128], F32)
        lstr = ph.sb("lstr", [128, 128], F32)
        m01 = ph.sb("m01", [128, 640], F32)
        zbk = [ph.ps("zbk%d" % i, [128, 512]) for i in range(3)]
        bbk = [ph.ps("bbk%d" % i, [128, 512]) for i in range(2)]
        obk = [ph.ps("obk%d" % i, [128, 512]) for i in range(2)]

        P.op("pool", lambda e: e.memset(onesf[:], 1.0), w=["onesf"])
        P.op("pool", lambda e: e.memset(lstr[:], 1.0), w=["lstr"])
        P.op("pool", lambda e: e.affine_select(out=lstr[:], in_=lstr[:], pattern=[[-1, 128]], compare_op=ALU.is_gt,
                                               fill=0.0, base=0, channel_multiplier=1), r=["lstr"], w=["lstr"])
        P.op("pool", lambda e: e.memset(m01[:], 1.0), w=["m01"])
        P.op("pool", lambda e: e.affine_select(out=m01[:], in_=m01[:], pattern=[[1, 640]], compare_op=ALU.is_gt,
                                               fill=0.0, base=0, channel_multiplier=-1), r=["m01"], w=["m01"])

        def load_head(h):
            i = h % 2
            P.dma("sp", qT[i][:, :], d["DqT"][h * 128:(h + 1) * 128, :], w=["qT%d" % i], sem="d_qT%d" % i)
            P.dma("sp", kT[i][:, :], d["DkT"][h * 128:(h + 1) * 128, :], w=["kT%d" % i], sem="d_kT%d" % i)
            P.dma("sp", V[i][:, :, :], d["Dv"].rearrange("(kb p) c -> p kb c", p=128)[:, :, h * 128:(h + 1) * 128],
                  w=["V%d" % i], sem="d_V%d" % i)
            P.dma("sp", gT[i][:, :], d["DgT"][h * 128:(h + 1) * 128, :], w=["gT%d" % i], sem="d_gT%d" % i)

        blocks = []
        gidx = 0
        for h in range(NH):
            for g in range(NG):
                kis = list(range(4 * g + 3, -1, -1))
                for ki in kis:
                    j = ki - 4 * g
                    blocks.append(dict(h=h, g=g, ki=ki, j=j, c0=(128 * j if j >= 1 else 0), first=(ki == kis[0]), last=(ki == 0),
                                       gi=gidx, n=len(blocks)))
                gidx += 1

        def S0(b):
            i = b["h"] % 2
            if b["first"] and b["g"] == 0:
                load_head(b["h"])
            zi = b["n"] % 3
            c0, ki, q0 = b["c0"], b["ki"], b["g"] * 512
            P.op("pe", lambda e: e.matmul(zbk[zi][:, c0:512], lhsT=kT[i][:, ki * 128:(ki + 1) * 128],
                                          rhs=qT[i][:, q0 + c0:q0 + 512], start=True, stop=True),
                 r=["kT%d" % i, "qT%d" % i], w=["zbk%d" % zi])

        def S1(b):
            k3 = b["n"] % 3
            cs = slice(b["c0"], 512)
            n = 512 - b["c0"]
            Z, zk = zbk[k3], "zbk%d" % k3
            if b["first"]:
                r_ = b["gi"] % 2
                P.op("pool", lambda e: e.memset(rsum[r_][:], 0.0), w=["rsum%d" % r_])
            P.op("act", lambda e: e.activation(out=ee[k3][:, cs], in_=Z[:, cs], func=AF.Exp, scale=-1.0), r=[zk], w=["ee%d" % k3])
            P.op("act", lambda e: e.activation(out=sp[k3][:, cs], in_=ee[k3][:, cs], func=AF.Ln, bias=1.0), r=["ee%d" % k3], w=["sp%d" % k3])
            P.op("dve", lambda e: e.scalar_tensor_tensor(out=lk[k3][:, cs], in0=sp[k3][:, cs], scalar=-1.0, in1=Z[:, cs],
                                                         op0=ALU.mult, op1=ALU.subtract), r=["sp%d" % k3, zk], w=["lk%d" % k3])
            if b["j"] >= -1:
                so = 128 if b["j"] == -1 else 0
                P.op("dve", lambda e: e.tensor_tensor(out=lk[k3][:, cs], in0=lk[k3][:, cs], in1=m01[:, so:so + n], op=ALU.mult),
                     r=["lk%d" % k3, "m01"], w=["lk%d" % k3])

        def S2(b):
            k3 = b["n"] % 3
            cs = slice(b["c0"], 512)
            r_ = b["gi"] % 2
            bi = b["n"] % 2
            B, bk = bbk[bi], "bbk%d" % bi
            P.op("pe", lambda e: e.matmul(B[:, cs], lhsT=lstr[:, :], rhs=lk[k3][:, cs], start=True, stop=b["first"]),
                 r=["lstr", "lk%d" % k3], w=[bk], sig=b["first"])
            if not b["first"]:
                P.op("pe", lambda e: e.matmul(B[:, cs], lhsT=onesf[:, :], rhs=rsum[r_][:, cs], start=False, stop=True),
                     r=["onesf", "rsum%d" % r_], w=[bk])
            P.op("dve", lambda e: e.tensor_tensor(out=ee[k3][:, cs], in0=B[:, cs], in1=sp[k3][:, cs], op=ALU.subtract),
                 r=[bk, "sp%d" % k3], w=["ee%d" % k3])
            if not b["last"]:
                P.op("pool", lambda e: e.tensor_tensor(out=rsum[r_][:, cs], in0=rsum[r_][:, cs], in1=lk[k3][:, cs], op=ALU.add),
                     r=["rsum%d" % r_, "lk%d" % k3], w=["rsum%d" % r_])

        def S3(b):
            k3 = b["n"] % 3
            cs = slice(b["c0"], 512)
            n = 512 - b["c0"]
            i = b["h"] % 2
            o_ = b["gi"] % 2
            P.op("act", lambda e: e.activation(out=aT[k3][:, cs], in_=ee[k3][:, cs], func=AF.Exp), r=["ee%d" % k3], w=["aT%d" % k3])
            if b["j"] >= -1:
                so = 128 if b["j"] == -1 else 0
                P.op("dve", lambda e: e.tensor_tensor(out=aT[k3][:, cs], in0=aT[k3][:, cs], in1=m01[:, so:so + n], op=ALU.mult),
                     r=["aT%d" % k3, "m01"], w=["aT%d" % k3])
            P.op("pe", lambda e: e.matmul(obk[o_][:, cs], lhsT=V[i][:, b["ki"], :], rhs=aT[k3][:, cs], start=b["first"], stop=b["last"],
                                          skip_group_check=True), r=["V%d" % i, "aT%d" % k3], w=["obk%d" % o_])
            if b["last"]:
                q0 = b["g"] * 512
                P.op("dve", lambda e: e.tensor_tensor(out=yst[i][:, q0:q0 + 512], in0=obk[o_][:, :], in1=gT[i][:, q0:q0 + 512],
                                                      op=ALU.mult), r=["obk%d" % o_, "gT%d" % i], w=["yst%d" % i])
                if b["g"] == NG - 1:
                    h = b["h"]
                    P.dma("pool", d["yT"][3 * CFG["YB"] + h * 128:3 * CFG["YB"] + (h + 1) * 128, :], yst[i][:, :], r=["yst%d" % i],
                          sem="d_yst%d" % i)

        nb_ = len(blocks)
        for step in range(nb_ + 4):
            if step < nb_:
                S0(blocks[step])
            if 0 <= step - 1 < nb_:
                S1(blocks[step - 1])
            if 0 <= step - 2 < nb_:
                S2(blocks[step - 2])
            if 0 <= step - 3 < nb_:
                S3(blocks[step - 3])


def phase_mixB(P, T, l, io, d, NH=None):
    NH = NH or CFG["NHA"]
    NHT = CFG["NHA"]
    NCH = T // 64
    NG = T // 512
    with P.phase("mixB") as ph:
        zf = [ph.sb("zf0", [128, T], F32)] * 2
        qb = [ph.sb("qb%d" % i, [128, T], BF16) for i in range(2)]
        gT = [ph.sb("gT%d" % i, [128, T], BF16) for i in range(2)]
        itok = [ph.sb("itok%d" % i, [64, NCH, 128], BF16) for i in range(2)]
        a1 = ph.sb("a1", [128, T], F32)
        a2 = ph.sb("a2", [128, T], F32)
        a3 = ph.sb("a3", [128, T], F32)
        a4 = ph.sb("a4", [128, T], F32)
        qt = ph.sb("qt", [128, T], BF16)
        kh = ph.sb("kh", [128, T], BF16)
        khtok = ph.sb("khtok", [64, NCH, 128], BF16)
        yst = [ph.sb("yst%d" % i, [128, 512], BF16) for i in range(2)]
        rmask = ph.sb("rmask", [128, 512], F32)
        cmask = ph.sb("cmask", [64, 512], F32)
        scb = [ph.sb("scb%d" % i, [64, 512], BF16) for i in range(2)]
        S = ph.sb("S", [128, 128], F32)
        Sb = [ph.sb("Sb%d" % i, [128, 128], BF16) for i in range(2)]
        sq = ph.sb("sq", [128, 512], F32)
        t1 = ph.sb("t1", [128, 512], F32)
        yy = ph.sb("yy", [128, 512], F32)
        identb = make_ident(P, ph, BF16, "identb")
        onesf = ph.sb("onesf", [128, 128], F32)
        hl = ph.sb("hl", [128, NHT, 4], F32)
        he = ph.sb("he", [128, NHT, 4], F32)
        hs = ph.sb("hs", [128, NHT], F32)
        lb = ph.sb("lb", [128, NHT], F32)
        oml = ph.sb("oml", [128, NHT], F32)
        hgn = ph.sb("hgn", [128, 1], F32)
        sbk = [ph.ps("sbk%d" % i, [128, 512]) for i in range(1)]
        obk = [ph.ps("obk%d" % i, [128, 512]) for i in range(2)]
        spk = [ph.ps("spk%d" % i, [128, 512]) for i in range(2)]
        ssb = ph.ps("ssb", [128, 512])
        tpk = [ph.ps("tpk%d" % i, [64, 8, 128], BF16) for i in range(2)]

        P.op("pool", lambda e: e.memset(onesf[:], 1.0), w=["onesf"])
        P.op("pool", lambda e: e.memset(rmask[:], 1.0), w=["rmask"])
        P.op("pool", lambda e: e.memset(rmask[:].rearrange("p (c k) -> p c k", k=64)[:, :, 0:1], 0.0), w=["rmask"])
        P.op("pool", lambda e: e.memset(cmask[:], 1.0), w=["cmask"])
        P.op("pool", lambda e: e.affine_select(out=cmask[:].rearrange("p (c t) -> p c t", t=64),
                                               in_=cmask[:].rearrange("p (c t) -> p c t", t=64),
                                               pattern=[[0, 8], [1, 64]], compare_op=ALU.is_ge, fill=0.0, base=0,
                                               channel_multiplier=-1), r=["cmask"], w=["cmask"])
        d0 = ph.sb("d0", [1, 8], F32)
        P.dma("sp", d0[:, :], d["Bd0"][:, :], w=["d0"], sem="d_d0")
        P.dma("sp", hl[:, :, :], io["hg_lower_t"], w=["hl"], sem="d_hl")
        P.dma("sp", hgn[:, :], io["hg_norm_t"][l], w=["hgn"], sem="d_hgn")
        P.op("act", lambda e: e.activation(out=he[:], in_=hl[:], func=AF.Exp), r=["hl"], w=["he"])
        P.op("dve", lambda e: e.reduce_sum(out=hs[:], in_=he[:], axis=AX.X), r=["he"], w=["hs"])
        P.op("dve", lambda e: e.reciprocal(out=hs[:], in_=hs[:]), r=["hs"], w=["hs"])
        P.op("pool", lambda e: e.memset(lb[:], 0.0), w=["lb"])
        for j in range(1, l + 1):
            P.op("dve", lambda e: e.tensor_tensor(out=lb[:], in0=lb[:], in1=he[:, :, j], op=ALU.add), r=["lb", "he"], w=["lb"])
        P.op("dve", lambda e: e.tensor_tensor(out=lb[:], in0=lb[:], in1=hs[:], op=ALU.mult), r=["lb", "hs"], w=["lb"])
        P.op("dve", lambda e: e.tensor_scalar(out=oml[:], in0=lb[:], scalar1=-1.0, scalar2=1.0, op0=ALU.mult, op1=ALU.add),
             r=["lb"], w=["oml"])

        def load_head(h):
            i = h % 2
            P.dma("sp", zf[0][:, :], d["BfT"][h * 128:(h + 1) * 128, :], w=["zf0"], sem="d_zf0")
            P.dma("sp", qb[i][:, :], d["BqT"][h * 128:(h + 1) * 128, :], w=["qb%d" % i], sem="d_qb%d" % i)
            P.dma("sp", gT[i][:, :], d["BgT"][h * 128:(h + 1) * 128, :], w=["gT%d" % i], sem="d_gT%d" % i)
            P.dma("sp", itok[i][:, :, :], d["Bi"].rearrange("(c p) e -> p c e", p=64)[:, :, h * 128:(h + 1) * 128],
                  w=["itok%d" % i], sem="d_itok%d" % i)

        cnt = dict(sb=0)
        load_head(0)
        for h in range(NH):
            i = h % 2
            zk = "zf0"
            P.op("act", lambda e: e.activation(out=a1[:], in_=zf[i][:], func=AF.Sigmoid), r=[zk], w=["a1"])
            P.op("dve", lambda e: e.tensor_scalar(out=a1[:], in0=a1[:], scalar1=oml[:, h:h + 1], scalar2=lb[:, h:h + 1],
                                                  op0=ALU.mult, op1=ALU.add), r=["a1", "oml", "lb"], w=["a1"])
            P.op("act", lambda e: e.activation(out=a2[:], in_=a1[:], func=AF.Ln), r=["a1"], w=["a2"])
            P.op("dve", lambda e: e.tensor_scalar(out=a1[:], in0=a1[:], scalar1=-1.0, scalar2=1.0, op0=ALU.mult, op1=ALU.add),
                 r=["a1"], w=["a1"])
            for g8 in range(NG):
                P.op("dve", lambda e: e.tensor_tensor_scan(out=a3[:, g8 * 512:(g8 + 1) * 512], data0=rmask[:],
                                                           data1=a2[:, g8 * 512:(g8 + 1) * 512], initial=0.0,
                                                           op0=ALU.mult, op1=ALU.add), r=["rmask", "a2"], w=["a3"])
            P.op("act", lambda e: e.activation(out=a4[:], in_=a3[:], func=AF.Exp), r=["a3"], w=["a4"])
            P.op("act", lambda e: e.activation(out=a2[:], in_=a3[:], func=AF.Exp, scale=-1.0), r=["a3"], w=["a2"])
            P.op("dve", lambda e: e.tensor_tensor(out=a1[:], in0=a1[:], in1=a2[:], op=ALU.mult), r=["a1", "a2"], w=["a1"])
            P.op("dve", lambda e: e.tensor_tensor(out=a2[:], in0=qb[i][:], in1=a4[:], op=ALU.mult), r=["qb%d" % i, "a4", "a2"], w=["a2"])
            P.op("act", lambda e: e.activation(out=qt[:], in_=a2[:], func=AF.Copy), r=["a2"], w=["qt"])
            ebl = a4[:].rearrange("p (c k) -> p c k", k=64)[:, :, 63:64]
            P.op("dve", lambda e: e.tensor_tensor(out=kh[:].rearrange("p (c k) -> p c k", k=64),
                                                  in0=a1[:].rearrange("p (c k) -> p c k", k=64),
                                                  in1=ebl.to_broadcast([128, NCH, 64]), op=ALU.mult), r=["a1", "a4"], w=["kh"])
            if h + 1 < NH:
                load_head(h + 1)
            for c8 in range(NCH // 8):
                tp = tpk[c8 % 2]
                tk_ = "tpk%d" % (c8 % 2)
                for cc in range(8):
                    c = c8 * 8 + cc
                    P.op("pe", lambda e: e.transpose(out=tp[:, cc, :], in_=kh[:, c * 64:(c + 1) * 64], identity=identb[:, :]),
                         r=["kh", "identb"], w=[tk_], sig=(cc == 7))
                P.op("act", lambda e: e.activation(out=khtok[:, c8 * 8:(c8 + 1) * 8, :], in_=tp[:, :, :], func=AF.Copy),
                     r=[tk_], w=["khtok"])
            P.op("pool", lambda e: e.memset(S[:], 0.0), w=["S"])
            P.op("pool", lambda e: e.memset(Sb[0][:], 0.0), w=["Sb0"])
            sbi = 0
            for g in range(NG):
                q0 = g * 512
                for cc in range(8):
                    c = g * 8 + cc
                    P.op("pe", lambda e: e.matmul(sbk[0][0:64, cc * 64:(cc + 1) * 64], lhsT=a1[:, c * 64:(c + 1) * 64],
                                                  rhs=a2[:, c * 64:(c + 1) * 64], start=True, stop=True, skip_group_check=True),
                         r=["a1", "a2"], w=["sbk0"], sig=(cc == 7))
                si = g % 2
                P.op("dve", lambda e: e.tensor_tensor(out=scb[si][:], in0=sbk[0][0:64, :], in1=cmask[:], op=ALU.mult),
                     r=["sbk0", "cmask"], w=["scb%d" % si])
                if g == 0:
                    P.op("dve", lambda e: e.tensor_copy(out=scb[si][0:1, 0:1], in_=d0[0:1, h:h + 1]), r=["d0", "scb%d" % si],
                         w=["scb%d" % si])
                for half in range(2):
                    for c4 in range(4):
                        c = g * 8 + half * 4 + c4
                        P.op("pe", lambda e: e.matmul(spk[half][:, c4 * 128:(c4 + 1) * 128], lhsT=khtok[:, c, :],
                                                      rhs=itok[i][:, c, :], start=True, stop=True, skip_group_check=True),
                             r=["khtok", "itok%d" % i], w=["spk%d" % half], sig=(c4 == 3))
                ob = obk[g % 2]
                ok_ = "obk%d" % (g % 2)
                for cc in range(8):
                    c = g * 8 + cc
                    P.op("pe", lambda e: e.matmul(ob[:, cc * 64:(cc + 1) * 64], lhsT=itok[i][:, c, :],
                                                  rhs=scb[si][:, cc * 64:(cc + 1) * 64], start=True, stop=False,
                                                  skip_group_check=True), r=["itok%d" % i, "scb%d" % si], w=[ok_], sig=False)
                    P.op("pe", lambda e: e.matmul(ob[:, cc * 64:(cc + 1) * 64], lhsT=Sb[sbi][:, :],
                                                  rhs=qt[:, c * 64:(c + 1) * 64], start=False, stop=True,
                                                  skip_group_check=True), r=["Sb%d" % sbi, "qt"], w=[ok_])
                    half, c4 = cc // 4, cc % 4
                    P.op("dve", lambda e: e.scalar_tensor_tensor(out=S[:], in0=S[:], scalar=ebl[:, c, :],
                                                                 in1=spk[half][:, c4 * 128:(c4 + 1) * 128],
                                                                 op0=ALU.mult, op1=ALU.add), r=["S", "a4", "spk%d" % half], w=["S"])
                    sbi = 1 - sbi
                    P.op("act", lambda e: e.activation(out=Sb[sbi][:], in_=S[:], func=AF.Copy), r=["S"], w=["Sb%d" % sbi])
                P.op("act", lambda e: e.activation(out=sq[:], in_=ob[:, :], func=AF.Square), r=[ok_], w=["sq"])
                P.op("pe", lambda e: e.matmul(ssb[:, :], lhsT=onesf[:, :], rhs=sq[:], start=True, stop=True), r=["onesf", "sq"], w=["ssb"])
                P.op("dve", lambda e: e.tensor_scalar(out=t1[:], in0=ssb[:, :], scalar1=1.0 / 128, scalar2=RMS_EPS,
                                                      op0=ALU.mult, op1=ALU.add), r=["ssb"], w=["t1"])
                P.op("act", lambda e: e.activation(out=t1[:], in_=t1[:], func=AF.Ln), r=["t1"], w=["t1"])
                P.op("act", lambda e: e.activation(out=t1[:], in_=t1[:], func=AF.Exp, scale=-0.5), r=["t1"], w=["t1"])
                P.op("dve", lambda e: e.tensor_tensor(out=yy[:], in0=ob[:, :], in1=t1[:], op=ALU.mult), r=[ok_, "t1"], w=["yy"])
                yi = g % 2
                P.op("dve", lambda e: e.scalar_tensor_tensor(out=yst[yi][:, :], in0=yy[:], scalar=hgn[:, 0:1],
                                                             in1=gT[i][:, q0:q0 + 512], op0=ALU.mult, op1=ALU.mult),
                     r=["yy", "hgn", "gT%d" % i], w=["yst%d" % yi])
                P.dma("pool", d["yT"][CFG["YB"] + h * 128:CFG["YB"] + (h + 1) * 128, q0:q0 + 512], yst[yi][:, :], r=["yst%d" % yi],
                      sem="d_yst%d" % yi)


def phase_b0(P, T, l, io, d):
    SP = CFG["SPLIT"]
    YB = CFG["YB"]
    NHB = CFG["NHA"]
    qcol = O_HQ if SP == 1 else 4 * YB
    fcol = O_HF if SP == 1 else 4 * YB + YB
    P.barrier()
    if SP > 1:
        P.coll("AllGather", d["h"][0:1, :], d["h0g"][:, :])
        P.barrier()
    with P.phase("b0") as ph:
        h0 = ph.sb("h0", [128, KC], F32)
        wb = [ph.sb("wb%d" % i, [128, KC, 256], F32) for i in range(2)]
        row = ph.sb("row", [1, 2 * YB], F32)
        hlr = ph.sb("hlr", [1, 4, YB], F32)
        er = ph.sb("er", [1, 4, YB], F32)
        srow = ph.sb("srow", [1, YB], F32)
        lbr = ph.sb("lbr", [1, YB], F32)
        fr = ph.sb("fr", [1, YB], F32)
        dots = ph.sb("dots", [1, 8], F32)
        pbk = [ph.ps("pbk%d" % i, [1, 512]) for i in range(2)]
        src_h0 = d["h"][0:1, :] if SP == 1 else d["h0g"][0:1, :]
        h0r = ph.sb("h0r", [KC, 128], F32)
        ident = make_ident(P, ph, F32, "identf")
        tps = ph.ps("tps", [128, KC])
        P.dma("sp", h0r[:, :], src_h0.rearrange("o (kc p) -> (o kc) p", p=128), w=["h0r"], sem="d_h0")
        P.op("pe", lambda e: e.transpose(out=tps[:, :], in_=h0r[:, :], identity=ident[0:KC, 0:KC]), r=["h0r", "identf"], w=["tps"])
        P.op("dve", lambda e: e.tensor_copy(out=h0[:, :], in_=tps[:, :]), r=["tps"], w=["h0"])
        P.dma("sp", hlr[:, :, :], io["hg_lower_r"], w=["hlr"], sem="d_hlr")
        w_l = io["w_in"][l].rearrange("(kc p) c -> p kc c", p=128)
        nblk = 2 * YB // 256
        for bi in range(nblk):
            c0 = (qcol if bi < nblk // 2 else fcol) + (bi % (nblk // 2)) * 256
            wi = bi % 2
            P.dma("sp", wb[wi][:, :, :], w_l[:, :, c0:c0 + 256], w=["wb%d" % wi], sem="d_wbf%d" % wi)
            for kc in range(KC):
                P.op("pe", lambda e: e.matmul(pbk[wi][0:1, 0:256], lhsT=h0[:, kc:kc + 1], rhs=wb[wi][:, kc, :],
                                              start=(kc == 0), stop=(kc == KC - 1)), r=["h0", "wb%d" % wi], w=["pbk%d" % wi],
                     sig=(kc == KC - 1))
            P.op("dve", lambda e: e.tensor_copy(out=row[0:1, bi * 256:(bi + 1) * 256], in_=pbk[wi][0:1, 0:256]),
                 r=["pbk%d" % wi], w=["row"])
        P.op("act", lambda e: e.activation(out=er[:], in_=hlr[:], func=AF.Exp), r=["hlr"], w=["er"])
        P.op("dve", lambda e: e.tensor_tensor(out=srow[:], in0=er[:, 0, :], in1=er[:, 1, :], op=ALU.add), r=["er"], w=["srow"])
        P.op("dve", lambda e: e.tensor_tensor(out=srow[:], in0=srow[:], in1=er[:, 2, :], op=ALU.add), r=["er", "srow"], w=["srow"])
        P.op("dve", lambda e: e.tensor_tensor(out=srow[:], in0=srow[:], in1=er[:, 3, :], op=ALU.add), r=["er", "srow"], w=["srow"])
        P.op("dve", lambda e: e.reciprocal(out=srow[:], in_=srow[:]), r=["srow"], w=["srow"])
        P.op("pool", lambda e: e.memset(lbr[:], 0.0), w=["lbr"])
        for j in range(1, l + 1):
            P.op("dve", lambda e: e.tensor_tensor(out=lbr[:], in0=lbr[:], in1=er[:, j, :], op=ALU.add), r=["lbr", "er"], w=["lbr"])
        P.op("dve", lambda e: e.tensor_tensor(out=lbr[:], in0=lbr[:], in1=srow[:], op=ALU.mult), r=["lbr", "srow"], w=["lbr"])
        P.op("act", lambda e: e.activation(out=fr[:], in_=row[0:1, YB:2 * YB], func=AF.Sigmoid), r=["row"], w=["fr"])
        P.op("dve", lambda e: e.tensor_scalar(out=fr[:], in0=fr[:], scalar1=-1.0, scalar2=1.0, op0=ALU.mult, op1=ALU.add),
             r=["fr"], w=["fr"])
        P.op("dve", lambda e: e.tensor_scalar(out=lbr[:], in0=lbr[:], scalar1=-1.0, scalar2=1.0, op0=ALU.mult, op1=ALU.add),
             r=["lbr"], w=["lbr"])
        P.op("dve", lambda e: e.tensor_tensor(out=fr[:], in0=fr[:], in1=lbr[:], op=ALU.mult), r=["fr", "lbr"], w=["fr"])
        P.op("dve", lambda e: e.tensor_tensor(out=fr[:], in0=fr[:], in1=row[0:1, 0:YB], op=ALU.mult), r=["fr", "row"], w=["fr"])
        P.op("pool", lambda e: e.memset(dots[:], 0.0), w=["dots"])
        P.op("dve", lambda e: e.reduce_sum(out=dots[0:1, 0:NHB], in_=fr[:].rearrange("o (h d) -> o h d", d=128), axis=AX.X),
             r=["fr", "dots"], w=["dots"])
        P.dma("sp", d["Bd0"][:, :], dots[:, :], r=["dots"], sem="d_dots")


WSC = math.exp(-0.5)


def phase_mixC(P, T, l, io, d, NH=None, G=2):
    NH = NH or CFG["NHC"]
    NHT = CFG["NHC"]
    RB = 64 * NHT
    N = 512
    NST = T // N
    GC = G * 8
    with P.phase("mixC") as ph:
        F = lambda name, shape, dt=F32: ph.sb(name, shape, dt)
        cpar = F("cpar", [64, NHT, 8])
        omka = F("omka", [64, NHT])
        w2s = F("w2s", [64, RB])
        a2s = F("a2s", [64, RB])
        v2s = F("v2s", [32, RB])
        twd = F("twd", [64, N])
        adm = F("adm", [64, N])
        vdm = F("vdm", [32, N])
        Xr2 = [F("Xr%d" % i, [64, G, N]) for i in range(2)]
        Xk2 = [F("Xk%d" % i, [64, G, N]) for i in range(2)]
        Xv2 = [F("Xv%d" % i, [64, G, N]) for i in range(2)]
        Xf2 = [F("Xf%d" % i, [64, G, N]) for i in range(2)]
        gate2 = [F("gate%d" % i, [64, G, N], BF16) for i in range(2)]
        gctr = [0]
        sg = F("sg", [64, G, N])
        aa = F("aa", [64, G, N])
        kk = F("kk", [64, G, N])
        kp = F("kp", [64, G, N])
        cs = F("cs", [64, G, N])
        pp = F("pp", [64, G, N])
        pinv = F("pinv", [64, G, N])
        pprev = F("pprev", [64, G, N])
        tmp = F("tmp", [64, G, N])
        AR = F("AR", [64, G, 8, 2, 64])
        BT = F("BT", [64, G, N])
        KT = F("KT", [64, G, N])
        RK = F("RK", [64, G, N])
        TK = F("TK", [64, G, 8, 5, 64])
        GM = F("GM", [64, GC, 320])
        TT = F("TT", [64, GC, 64])
        TTb = F("TTb", [64, GC, 64], BF16)
        XY = [F("XY%d" % i, [64, GC, 128], BF16) for i in range(2)]
        tmpb = F("tmpb", [64, G, N], BF16)
        RKb = F("RKb", [64, G, N], BF16)
        ones64b = F("ones64b", [64, 64], BF16)
        onesmb = F("onesmb", [64, 64], BF16)
        GA = F("GA", [64, GC, 128])
        OL = F("OL", [64, GC, 64])
        RP = F("RP", [64, GC, 64])
        PH = F("PH", [64, GC, 64])
        GP = F("GP", [64, GC, 64])
        OT = F("OT", [64, G, N])
        H = F("H", [64, NH, 64])
        yst = F("yst", [64, G, N], BF16)
        m320 = F("m320", [64, 320])
        rmask = F("rmask", [64, G * N])
        ones64 = F("ones64", [64, 64])
        onesm = F("onesm", [64, 64])
        ident = make_ident(P, ph, F32, "identf")
        idf = ident[0:64, 0:64]
        pbs = [ph.ps("pb%d" % i, [64, 512]) for i in range(8)]
        bctr = [0]

        def nb():
            b = bctr[0] % 8
            bctr[0] += 1
            return pbs[b], "pb%d" % b

        P.op("pool", lambda e: e.memset(ones64[:], 1.0), w=["ones64"])
        P.op("pool", lambda e: e.memset(onesm[:], 1.0 / 64), w=["onesm"])
        P.op("pool", lambda e: e.memset(ones64b[:], 1.0), w=["ones64b"])
        P.op("pool", lambda e: e.memset(onesmb[:], 1.0 / 64), w=["onesmb"])
        P.op("pool", lambda e: e.memset(rmask[:], 1.0), w=["rmask"])
        P.op("pool", lambda e: e.memset(rmask[:].rearrange("p (c k) -> p c k", k=64)[:, :, 0:1], 0.0), w=["rmask"])
        P.op("pool", lambda e: e.memset(H[:], 0.0), w=["H"])
        P.op("pool", lambda e: e.memset(m320[:], 1.0), w=["m320"])
        for blk, op_, cm, pat in ((0, ALU.is_gt, -1, 1), (1, ALU.is_ge, -1, 1), (2, ALU.is_gt, -1, 1), (3, ALU.is_ge, -1, 1),
                                  (4, ALU.is_gt, 1, -1)):
            P.op("pool", lambda e: e.affine_select(out=m320[:, blk * 64:(blk + 1) * 64], in_=m320[:, blk * 64:(blk + 1) * 64],
                                                   pattern=[[pat, 64]], compare_op=op_, fill=0.0, base=0, channel_multiplier=cm),
                 r=["m320"], w=["m320"])
        P.dma("sp", cpar[:, :, :], io["c_par"][l], w=["cpar"], sem="d_cpar")
        P.dma("sp", w2s[:, :], io["rw_w2"][l], w=["w2s"], sem="d_w2s")
        P.dma("sp", a2s[:, :], io["rw_a2"][l], w=["a2s"], sem="d_a2s")
        if l > 0:
            P.dma("sp", v2s[:, :], io["rw_v2"][l - 1], w=["v2s"], sem="d_v2s")
        P.op("dve", lambda e: e.tensor_scalar(out=omka[:], in0=cpar[:, :, 3], scalar1=-1.0, scalar2=1.0, op0=ALU.mult, op1=ALU.add),
             r=["cpar"], w=["omka"])

        def bc(ap2, g0):
            return ap2[:, g0:g0 + G].unsqueeze(2).to_broadcast([64, G, N])

        CM = d["CmT"]
        for st in range(NST):
            t0 = st * N
            P.dma("sp", twd[:, :], CM[3 * RB:3 * RB + 64, t0:t0 + N], w=["twd"], sem="d_twd")
            P.dma("sp", adm[:, :], CM[3 * RB + 64:3 * RB + 128, t0:t0 + N], w=["adm"], sem="d_adm")
            P.op("act", lambda e: e.activation(out=twd[:], in_=twd[:], func=AF.Tanh), r=["twd"], w=["twd"])
            if l > 0:
                P.dma("sp", vdm[:, :], d["CvdT"][:, t0:t0 + N], w=["vdm"], sem="d_vdm")
            for hg in range(NH // G):
                h0 = hg * G
                par = gctr[0] % 2
                gctr[0] += 1
                Xr, Xk, Xv, Xf, gate = Xr2[par], Xk2[par], Xv2[par], Xf2[par], gate2[par]
                kXr, kXk, kXv, kXf, kgate = "Xr%d" % par, "Xk%d" % par, "Xv%d" % par, "Xf%d" % par, "gate%d" % par
                rows = lambda base: CM[base + h0 * 64: base + (h0 + G) * 64, t0:t0 + N].rearrange("(g k) t -> k g t", k=64)
                P.dma("sp", Xr[:, :, :], rows(0), w=[kXr], sem="d_" + kXr)
                P.dma("sp", Xk[:, :, :], rows(RB), w=[kXk], sem="d_" + kXk)
                P.dma("sp", Xv[:, :, :], rows(2 * RB), w=[kXv], sem="d_" + kXv)
                P.dma("sp", gate[:, :, :], d["CgT"][h0 * 64:(h0 + G) * 64, t0:t0 + N].rearrange("(g k) t -> k g t", k=64),
                      w=[kgate], sem="d_" + kgate)
                if l > 0:
                    P.dma("sp", Xf[:, :, :], d["Cvf"][h0 * 64:(h0 + G) * 64, t0:t0 + N].rearrange("(g k) t -> k g t", k=64),
                          w=[kXf], sem="d_" + kXf)
                for g in range(G):
                    h = h0 + g
                    pb, pk = nb()
                    P.op("pe", lambda e: e.matmul(pb[:, :], lhsT=w2s[:, h * 64:(h + 1) * 64], rhs=twd[:, :], start=True, stop=True),
                         r=["w2s", "twd"], w=[pk])
                    P.op("act", lambda e: e.activation(out=sg[:, g, :], in_=pb[:, :], func=AF.Sigmoid, bias=cpar[:, h, 0:1]),
                         r=[pk, "cpar"], w=["sg"])
                    pb, pk = nb()
                    P.op("pe", lambda e: e.matmul(pb[:, :], lhsT=a2s[:, h * 64:(h + 1) * 64], rhs=adm[:, :], start=True, stop=True),
                         r=["a2s", "adm"], w=[pk])
                    P.op("act", lambda e: e.activation(out=aa[:, g, :], in_=pb[:, :], func=AF.Sigmoid, bias=cpar[:, h, 1:2]),
                         r=[pk, "cpar"], w=["aa"])
                    if l > 0:
                        pb, pk = nb()
                        P.op("pe", lambda e: e.matmul(pb[:, :], lhsT=v2s[:, h * 64:(h + 1) * 64], rhs=vdm[:, :], start=True, stop=True),
                             r=["v2s", "vdm"], w=[pk])
                        P.op("act", lambda e: e.activation(out=tmp[:, g, :], in_=pb[:, :], func=AF.Sigmoid, bias=cpar[:, h, 7:8]),
                             r=[pk, "cpar"], w=["tmp"])
                if l > 0:
                    P.op("dve", lambda e: e.tensor_tensor(out=Xf[:], in0=Xf[:], in1=Xv[:], op=ALU.subtract), r=[kXf, kXv], w=[kXf])
                    P.op("dve", lambda e: e.tensor_tensor(out=Xf[:], in0=Xf[:], in1=tmp[:], op=ALU.mult), r=[kXf, "tmp"], w=[kXf])
                    P.op("dve", lambda e: e.tensor_tensor(out=Xv[:], in0=Xv[:], in1=Xf[:], op=ALU.add), r=[kXf, kXv], w=[kXv])
                else:
                    P.dma("pool", d["Cvf"][h0 * 64:(h0 + G) * 64, t0:t0 + N].rearrange("(g k) t -> k g t", k=64), Xv[:, :, :],
                          r=[kXv], sem="d_vfo")
                P.op("dve", lambda e: e.tensor_tensor(out=kk[:], in0=Xk[:], in1=bc(cpar[:, :, 2], h0), op=ALU.mult),
                     r=[kXk, "cpar"], w=["kk"])
                P.op("act", lambda e: e.activation(out=tmpb[:], in_=kk[:], func=AF.Square), r=["kk"], w=["tmpb"])
                for g in range(G):
                    pb, pk = nb()
                    P.op("pe", lambda e: e.matmul(pb[:, :], lhsT=ones64b[:, :], rhs=tmpb[:, g, :], start=True, stop=True),
                         r=["ones64b", "tmpb"], w=[pk])
                    P.op("dve", lambda e: e.tensor_scalar(out=kp[:, g, :], in0=pb[:, :], scalar1=1e-24, scalar2=None, op0=ALU.max),
                         r=[pk], w=["kp"])
                P.op("act", lambda e: e.activation(out=kp[:], in_=kp[:], func=AF.Ln), r=["kp"], w=["kp"])
                P.op("act", lambda e: e.activation(out=kp[:], in_=kp[:], func=AF.Exp, scale=-0.5), r=["kp"], w=["kp"])
                P.op("dve", lambda e: e.tensor_tensor(out=kk[:], in0=kk[:], in1=kp[:], op=ALU.mult), r=["kk", "kp"], w=["kk"])
                P.op("dve", lambda e: e.tensor_tensor(out=kp[:], in0=aa[:], in1=bc(cpar[:, :, 3], h0), op=ALU.mult),
                     r=["aa", "cpar"], w=["kp"])
                P.op("dve", lambda e: e.tensor_tensor(out=kp[:], in0=kp[:], in1=bc(omka, h0), op=ALU.add), r=["kp", "omka"], w=["kp"])
                P.op("dve", lambda e: e.tensor_tensor(out=kp[:], in0=kp[:], in1=Xk[:], op=ALU.mult), r=["kp", kXk], w=["kp"])
                P.op("dve", lambda e: e.tensor_tensor_scan(out=cs[:].rearrange("p g n -> p (g n)"), data0=rmask[:],
                                                           data1=sg[:].rearrange("p g n -> p (g n)"), initial=0.0,
                                                           op0=ALU.mult, op1=ALU.add), r=["rmask", "sg"], w=["cs"])
                P.op("act", lambda e: e.activation(out=pp[:], in_=cs[:], func=AF.Exp, scale=-WSC), r=["cs"], w=["pp"])
                P.op("act", lambda e: e.activation(out=pinv[:], in_=cs[:], func=AF.Exp, scale=WSC), r=["cs"], w=["pinv"])
                P.op("dve", lambda e: e.tensor_tensor(out=tmp[:], in0=cs[:], in1=sg[:], op=ALU.subtract), r=["cs", "sg"], w=["tmp"])
                P.op("act", lambda e: e.activation(out=pprev[:], in_=tmp[:], func=AF.Exp, scale=-WSC), r=["tmp"], w=["pprev"])
                v4 = lambda t_: t_[:].rearrange("p g (c k) -> p g c k", k=64)
                P.op("dve", lambda e: e.scalar_tensor_tensor(out=AR[:, :, :, 0, :], in0=v4(kk), scalar=-1.0, in1=v4(pprev),
                                                             op0=ALU.mult, op1=ALU.mult), r=["kk", "pprev"], w=["AR"])
                P.op("dve", lambda e: e.tensor_tensor(out=AR[:, :, :, 1, :], in0=v4(Xr), in1=v4(pp), op=ALU.mult),
                     r=[kXr, "pp"], w=["AR"])
                P.op("dve", lambda e: e.tensor_tensor(out=BT[:], in0=kk[:], in1=aa[:], op=ALU.mult), r=["kk", "aa"], w=["BT"])
                P.op("dve", lambda e: e.tensor_tensor(out=BT[:], in0=BT[:], in1=pinv[:], op=ALU.mult), r=["BT", "pinv"], w=["BT"])
                P.op("dve", lambda e: e.tensor_tensor(out=KT[:], in0=kp[:], in1=pinv[:], op=ALU.mult), r=["kp", "pinv"], w=["KT"])
                P.op("dve", lambda e: e.tensor_tensor(out=RK[:], in0=Xr[:], in1=kp[:], op=ALU.mult), r=[kXr, "kp"], w=["RK"])
                P.op("dve", lambda e: e.tensor_tensor(out=RKb[:], in0=RK[:], in1=bc(cpar[:, :, 4], h0), op=ALU.mult),
                     r=["RK", "cpar"], w=["RKb"])
                for g in range(G):
                    for c2 in range(4):
                        pb, pk = nb()
                        pv = pb[:, :].rearrange("p (c j k) -> p c j k", c=2, j=4)
                        for cc in range(2):
                            c = c2 * 2 + cc
                            csl = slice(c * 64, (c + 1) * 64)
                            srcs = [(BT[:, g, csl], "BT"), (KT[:, g, csl], "KT"), (Xv[:, g, csl], kXv), (AR[:, g, c, 0, :], "AR")]
                            for j, (sap, skey) in enumerate(srcs):
                                P.op("pe", lambda e: e.transpose(out=pv[:, cc, j, :], in_=sap, identity=idf),
                                     r=[skey, "identf"], w=[pk], sig=(cc == 1 and j == 3))
                        P.op("act", lambda e: e.activation(out=TK[:, g, c2 * 2:c2 * 2 + 2, 0:3, :], in_=pv[:, :, 0:3, :], func=AF.Copy),
                             r=[pk], w=["TK"])
                        P.op("act", lambda e: e.activation(out=TK[:, g, c2 * 2:c2 * 2 + 2, 4, :], in_=pv[:, :, 3, :], func=AF.Copy),
                             r=[pk], w=["TK"])
                for g in range(G):
                    for c in range(8):
                        gc = g * 8 + c
                        csl = slice(c * 64, (c + 1) * 64)
                        pb, pk = nb()
                        arv = AR[:, g, c, :, :].rearrange("p a k -> p (a k)")
                        P.op("pe", lambda e: e.matmul(pb[:, 0:128], lhsT=BT[:, g, csl], rhs=arv, start=True, stop=True,
                                                      skip_group_check=True), r=["BT", "AR"], w=[pk], sig=False)
                        P.op("pe", lambda e: e.matmul(pb[:, 128:256], lhsT=KT[:, g, csl], rhs=arv, start=True, stop=True,
                                                      skip_group_check=True), r=["KT", "AR"], w=[pk], sig=False)
                        P.op("pe", lambda e: e.matmul(pb[:, 256:320], lhsT=AR[:, g, c, 0, :], rhs=BT[:, g, csl], start=True, stop=True,
                                                      skip_group_check=True), r=["BT", "AR"], w=[pk])
                        P.op("dve", lambda e: e.tensor_tensor(out=GM[:, gc, :], in0=pb[:, 0:320], in1=m320[:], op=ALU.mult),
                             r=[pk, "m320"], w=["GM"])
                P.op("dve", lambda e: e.tensor_tensor(out=TTb[:], in0=GM[:, :, 0:64], in1=idf.unsqueeze(1).to_broadcast([64, GC, 64]),
                                                      op=ALU.add), r=["GM", "identf"], w=["TTb"])
                for lev in range(5):
                    last = lev == 4
                    src = XY[(lev + 1) % 2]
                    dst = XY[lev % 2]
                    skey, dkey = "XY%d" % ((lev + 1) % 2), "XY%d" % (lev % 2)
                    for b4 in range(GC // 4):
                        pb, pk = nb()
                        pv = pb[:, :].rearrange("p (q k) -> p q k", q=4)
                        for q in range(4):
                            gc = b4 * 4 + q
                            if lev == 0:
                                Xa, Ya, rk_ = GM[:, gc, 256:320], GM[:, gc, 0:64], ["GM"]
                            else:
                                Xa, Ya, rk_ = src[:, gc, 0:64], src[:, gc, 64:128], [skey]
                            P.op("pe", lambda e: e.matmul(pv[:, q, 0:64], lhsT=Ya, rhs=Xa, start=True, stop=True, skip_group_check=True),
                                 r=rk_, w=[pk], sig=(last and q == 3))
                            if not last:
                                P.op("pe", lambda e: e.matmul(pv[:, q, 64:128], lhsT=Xa, rhs=Ya, start=True, stop=True,
                                                              skip_group_check=True), r=rk_, w=[pk], sig=(q == 3))
                        if last:
                            P.op("act", lambda e: e.activation(out=dst[:, b4 * 4:b4 * 4 + 4, 0:64], in_=pv[:, :, 0:64], func=AF.Copy),
                                 r=[pk], w=[dkey])
                        else:
                            P.op("act", lambda e: e.activation(out=dst[:, b4 * 4:b4 * 4 + 4, :], in_=pv[:, :, :], func=AF.Copy),
                                 r=[pk], w=[dkey])
                    for b4 in range(GC // 4):
                        pb2, pk2 = nb()
                        pv2 = pb2[:, 0:256].rearrange("p (q k) -> p q k", q=4)
                        for q in range(4):
                            gc = b4 * 4 + q
                            P.op("pe", lambda e: e.matmul(pv2[:, q, :], lhsT=dst[:, gc, 0:64], rhs=TTb[:, gc, :], start=True, stop=True,
                                                          skip_group_check=True), r=[dkey, "TTb"], w=[pk2], sig=(q == 3))
                        if last:
                            P.op("dve", lambda e: e.tensor_tensor(out=TT[:, b4 * 4:b4 * 4 + 4, :], in0=TTb[:, b4 * 4:b4 * 4 + 4, :],
                                                                  in1=pv2[:, :, :], op=ALU.add), r=["TTb", pk2], w=["TT"])
                        else:
                            P.op("dve", lambda e: e.tensor_tensor(out=TTb[:, b4 * 4:b4 * 4 + 4, :], in0=TTb[:, b4 * 4:b4 * 4 + 4, :],
                                                                  in1=pv2[:, :, :], op=ALU.add), r=["TTb", pk2], w=["TTb"])
                for g in range(G):
                    for c4 in range(2):
                        pb, pk = nb()
                        pv = pb[:, 0:256].rearrange("p (q k) -> p q k", q=4)
                        for q in range(4):
                            c = c4 * 4 + q
                            gc = g * 8 + c
                            P.op("pe", lambda e: e.matmul(pv[:, q, :], lhsT=GM[:, gc, 128:192], rhs=TK[:, g, c, 2, :], start=True, stop=True,
                                                          skip_group_check=True), r=["GM", "TK"], w=[pk], sig=(q == 3))
                        P.op("act", lambda e: e.activation(out=TK[:, g, c4 * 4:c4 * 4 + 4, 3, :], in_=pv[:, :, :], func=AF.Copy),
                             r=[pk], w=["TK"])
                for g in range(G):
                    for c4 in range(2):
                        pb, pk = nb()
                        pv = pb[:, :].rearrange("p (q k) -> p q k", q=4)
                        for q in range(4):
                            c = c4 * 4 + q
                            gc = g * 8 + c
                            P.op("pe", lambda e: e.matmul(pv[:, q, :], lhsT=TT[:, gc, :], rhs=TK[:, g, c, 3:5, :].rearrange("p a k -> p (a k)"),
                                                          start=True, stop=True, skip_group_check=True), r=["TT", "TK"], w=[pk], sig=(q == 3))
                        P.op("act", lambda e: e.activation(out=GA[:, g * 8 + c4 * 4:g * 8 + c4 * 4 + 4, :], in_=pv[:, :, :], func=AF.Copy),
                             r=[pk], w=["GA"])
                pC = pp[:].rearrange("p g (c k) -> p g c k", k=64)[:, :, :, 63:64]
                for g in range(G):
                    for c4 in range(2):
                        pbO, pkO = nb()
                        pbR, pkR = nb()
                        pbG, pkG = nb()
                        pvO = pbO[:, 0:256].rearrange("p (q k) -> p q k", q=4)
                        pvR = pbR[:, :].rearrange("p (q a k) -> p q a k", q=4, a=2)
                        pvG = pbG[:, 0:256].rearrange("p (q k) -> p q k", q=4)
                        for q in range(4):
                            c = c4 * 4 + q
                            gc = g * 8 + c
                            P.op("pe", lambda e: e.matmul(pvO[:, q, :], lhsT=GA[:, gc, 0:64], rhs=GM[:, gc, 64:128], start=True, stop=False,
                                                          skip_group_check=True), r=["GA", "GM"], w=[pkO], sig=False)
                            P.op("pe", lambda e: e.matmul(pvO[:, q, :], lhsT=TK[:, g, c, 2, :], rhs=GM[:, gc, 192:256], start=False, stop=True,
                                                          skip_group_check=True), r=["TK", "GM"], w=[pkO], sig=(q == 3))
                            P.op("pe", lambda e: e.matmul(pvR[:, q, 0, :], lhsT=GA[:, gc, 64:128], rhs=GM[:, gc, 64:128], start=True, stop=True,
                                                          skip_group_check=True), r=["GA", "GM"], w=[pkR], sig=False)
                            P.op("pe", lambda e: e.matmul(pvR[:, q, 1, :], lhsT=GA[:, gc, 64:128], rhs=TK[:, g, c, 0, :], start=True, stop=True,
                                                          skip_group_check=True), r=["GA", "TK"], w=[pkR], sig=(q == 3))
                            P.op("pe", lambda e: e.matmul(pvG[:, q, :], lhsT=TK[:, g, c, 0, :], rhs=GA[:, gc, 0:64], start=True, stop=False,
                                                          skip_group_check=True), r=["GA", "TK"], w=[pkG], sig=False)
                            P.op("pe", lambda e: e.matmul(pvG[:, q, :], lhsT=TK[:, g, c, 1, :], rhs=TK[:, g, c, 2, :], start=False, stop=True,
                                                          skip_group_check=True), r=["TK"], w=[pkG], sig=(q == 3))
                        gs = slice(g * 8 + c4 * 4, g * 8 + c4 * 4 + 4)
                        cs4 = slice(c4 * 4, c4 * 4 + 4)
                        P.op("act", lambda e: e.activation(out=OL[:, gs, :], in_=pvO[:, :, :], func=AF.Copy), r=[pkO], w=["OL"])
                        P.op("dve", lambda e: e.tensor_tensor(out=RP[:, gs, :], in0=pvR[:, :, 0, :], in1=AR[:, g, cs4, 1, :], op=ALU.add),
                             r=[pkR, "AR"], w=["RP"])
                        P.op("dve", lambda e: e.tensor_tensor(out=PH[:, gs, :], in0=pvR[:, :, 1, :],
                                                              in1=idf.unsqueeze(1).to_broadcast([64, 4, 64]), op=ALU.add),
                             r=[pkR, "identf"], w=["PH"])
                        P.op("dve", lambda e: e.tensor_tensor(out=GP[:, gs, :], in0=pvG[:, :, :],
                                                              in1=pC[:, g, cs4, :].to_broadcast([64, 4, 64]), op=ALU.mult),
                             r=[pkG, "pp"], w=["GP"])
                for c in range(8):
                    pbH, pkH = nb()
                    pbO, pkO = nb()
                    pvH = pbH[:, 0:G * 64].rearrange("p (g k) -> p g k", g=G)
                    pvO = pbO[:, 0:G * 64].rearrange("p (g k) -> p g k", g=G)
                    for g in range(G):
                        gc = g * 8 + c
                        P.op("pe", lambda e: e.matmul(pvH[:, g, :], lhsT=PH[:, gc, :], rhs=H[:, h0 + g, :], start=True, stop=True,
                                                      skip_group_check=True), r=["PH", "H"], w=[pkH], sig=(g == G - 1))
                    for g in range(G):
                        gc = g * 8 + c
                        P.op("pe", lambda e: e.matmul(pvO[:, g, :], lhsT=H[:, h0 + g, :], rhs=RP[:, gc, :], start=True, stop=True,
                                                      skip_group_check=True), r=["RP", "H"], w=[pkO], sig=(g == G - 1))
                    gcs = GP[:].rearrange("p (g c) k -> p g c k", g=G)[:, :, c, :]
                    ols = OL[:].rearrange("p (g c) k -> p g c k", g=G)[:, :, c, :]
                    P.op("dve", lambda e: e.tensor_tensor(out=OT[:, :, c * 64:(c + 1) * 64], in0=pvO[:, :, :], in1=ols, op=ALU.add),
                         r=[pkO, "OL"], w=["OT"])
                    P.op("dve", lambda e: e.tensor_tensor(out=H[:, h0:h0 + G, :], in0=pvH[:, :, :],
                                                          in1=pC[:, :, c, :].to_broadcast([64, G, 64]), op=ALU.mult),
                         r=[pkH, "pp", "H"], w=["H"])
                    P.op("dve", lambda e: e.tensor_tensor(out=H[:, h0:h0 + G, :], in0=H[:, h0:h0 + G, :], in1=gcs, op=ALU.add),
                         r=["H", "GP"], w=["H"])
                P.op("act", lambda e: e.activation(out=tmpb[:], in_=OT[:], func=AF.Copy), r=["OT"], w=["tmpb"])
                for g in range(G):
                    pb, pk = nb()
                    P.op("pe", lambda e: e.matmul(pb[:, :], lhsT=onesmb[:, :], rhs=tmpb[:, g, :], start=True, stop=True),
                         r=["onesmb", "tmpb"], w=[pk])
                    P.op("dve", lambda e: e.tensor_tensor(out=OT[:, g, :], in0=OT[:, g, :], in1=pb[:, :], op=ALU.subtract),
                         r=[pk, "OT"], w=["OT"])
                P.op("act", lambda e: e.activation(out=tmpb[:], in_=OT[:], func=AF.Square), r=["OT"], w=["tmpb"])
                for g in range(G):
                    pb, pk = nb()
                    P.op("pe", lambda e: e.matmul(pb[:, :], lhsT=onesmb[:, :], rhs=tmpb[:, g, :], start=True, stop=True),
                         r=["onesmb", "tmpb"], w=[pk])
                    P.op("dve", lambda e: e.tensor_scalar(out=cs[:, g, :], in0=pb[:, :], scalar1=RW_LN_EPS, scalar2=None, op0=ALU.add),
                         r=[pk], w=["cs"])
                P.op("act", lambda e: e.activation(out=cs[:], in_=cs[:], func=AF.Ln), r=["cs"], w=["cs"])
                P.op("act", lambda e: e.activation(out=cs[:], in_=cs[:], func=AF.Exp, scale=-0.5), r=["cs"], w=["cs"])
                P.op("dve", lambda e: e.tensor_tensor(out=OT[:], in0=OT[:], in1=cs[:], op=ALU.mult), r=["OT", "cs"], w=["OT"])
                P.op("dve", lambda e: e.tensor_tensor(out=OT[:], in0=OT[:], in1=bc(cpar[:, :, 5], h0), op=ALU.mult),
                     r=["OT", "cpar"], w=["OT"])
                P.op("dve", lambda e: e.tensor_tensor(out=OT[:], in0=OT[:], in1=bc(cpar[:, :, 6], h0), op=ALU.add),
                     r=["OT", "cpar"], w=["OT"])
                for g in range(G):
                    pb, pk = nb()
                    P.op("pe", lambda e: e.matmul(pb[:, :], lhsT=ones64b[:, :], rhs=RKb[:, g, :], start=True, stop=True),
                         r=["ones64b", "RKb"], w=[pk])
                    P.op("dve", lambda e: e.tensor_tensor(out=tmp[:, g, :], in0=pb[:, :], in1=Xv[:, g, :], op=ALU.mult),
                         r=[pk, kXv], w=["tmp"])
                P.op("dve", lambda e: e.tensor_tensor(out=OT[:], in0=OT[:], in1=tmp[:], op=ALU.add), r=["OT", "tmp"], w=["OT"])
                P.op("dve", lambda e: e.tensor_tensor(out=yst[:], in0=OT[:], in1=gate[:], op=ALU.mult), r=["OT", kgate], w=["yst"])
                P.dma("pool", d["yT"][2 * CFG["YB"] + h0 * 64:2 * CFG["YB"] + (h0 + G) * 64, t0:t0 + N].rearrange("(g k) t -> k g t", k=64), yst[:, :, :],
                      r=["yst"], sem="d_ystc")


def phase_mergeA(P, T, l, io, d):
    NT = T // 512
    SP = CFG["SPLIT"]
    KB = CFG["YB"] // 128
    TO = T // SP
    ODT = BF16 if SP == 1 else F32
    with P.phase("mergeA") as ph:
        W = [ph.sb("W%d" % n, [128, KB, 2048], BF16) for n in range(4)]
        yT = [ph.sb("yT%d" % i, [128, 4, KB, 512], BF16) for i in range(SP)]
        gts = [ph.sb("gts%d" % i, [128, 4, 512], BF16) for i in range(2)]
        tq = [[ph.sb("tq%d%d" % (i, n), [128, 512], F32) for n in range(4)] for i in range(2)]
        s01 = [ph.sb("s01%d" % i, [128, 512], F32) for i in range(2)]
        mst = [ph.sb("mst%d" % i, [128, 512], ODT) for i in range(2)]
        banks = [ph.ps("bk%d" % i, [128, 512]) for i in range(8)]
        for n in range(4):
            P.dma("pool", W[n][:, :, :], io["w_branch"][l, n].rearrange("(kc p) c -> p kc c", p=128), w=["W%d" % n], sem="d_W%d" % n)
        mg = d["MgT"].rearrange("(n c p) t -> p n c t", n=4, p=128)
        yv = d["yT"].rearrange("(n kc p) t -> p n kc t", n=4, p=128)
        bi = 0
        gi = 0
        for tt in range(NT):
            ts = slice(tt * 512, (tt + 1) * 512)
            yi = tt % len(yT)
            yk = "yT%d" % yi
            for n in range(4):
                P.dma("sp", yT[yi][:, n, :, :], yv[:, n, :, ts], w=[yk], sem="d_" + yk)
            for cc in range(16):
                g_ = gi % 2
                gi += 1
                P.dma("sp", gts[g_][:, :, :], mg[:, :, cc, ts], w=["gts%d" % g_], sem="d_gts%d" % g_)
                for n in range(4):
                    b = bi % 8
                    bi += 1
                    for kc in range(KB):
                        P.op("pe", lambda e: e.matmul(banks[b][:, :], lhsT=W[n][:, kc, cc * 128:(cc + 1) * 128], rhs=yT[yi][:, n, kc, :],
                                                      start=(kc == 0), stop=(kc == KB - 1)), r=["W%d" % n, yk], w=["bk%d" % b], sig=(kc == KB - 1))
                    P.op("dve", lambda e: e.tensor_tensor(out=tq[g_][n][:], in0=banks[b][:, :], in1=gts[g_][:, n, :], op=ALU.mult),
                         r=["bk%d" % b, "gts%d" % g_], w=["tq%d%d" % (g_, n)])
                    if n == 1:
                        P.op("pool", lambda e: e.tensor_tensor(out=s01[g_][:], in0=tq[g_][0][:], in1=tq[g_][1][:], op=ALU.add),
                             r=["tq%d0" % g_, "tq%d1" % g_], w=["s01%d" % g_])
                P.op("dve", lambda e: e.tensor_tensor(out=tq[g_][2][:], in0=tq[g_][2][:], in1=tq[g_][3][:], op=ALU.add),
                     r=["tq%d2" % g_, "tq%d3" % g_], w=["tq%d2" % g_])
                P.op("pool", lambda e: e.tensor_tensor(out=mst[g_][:], in0=s01[g_][:], in1=tq[g_][2][:], op=ALU.add),
                     r=["s01%d" % g_, "tq%d2" % g_], w=["mst%d" % g_])
                if SP == 1:
                    dst = d["mT"][cc * 128:(cc + 1) * 128, ts]
                else:
                    hf, tl = (tt * 512) // TO, (tt * 512) % TO
                    dst = d["mTp"][hf, cc * 128:(cc + 1) * 128, tl:tl + 512]
                P.dma("act", dst, mst[g_][:, :], r=["mst%d" % g_], sem="d_mst%d" % g_)


def phase_gather_hT(P, T, d):
    NCH = d["NCH"]
    rows = D_MODEL // NCH
    P.barrier()
    for i in range(NCH):
        P.coll("AllGather", d["hT"][i * rows:(i + 1) * rows, :], d["hTg"][i].rearrange("r f t -> (r f) t"))
    P.barrier()


def phase_scatter_mT(P, T, d):
    P.barrier()
    P.coll("ReduceScatter", d["mTp"].rearrange("r c t -> (r c) t"), d["mTs"][:, :])
    P.barrier()


def phase_mergeB(P, T, l, io, d, final_out=None):
    SP = CFG["SPLIT"]
    T = T // SP
    NT = T // 512
    with P.phase("mergeB") as ph:
        wo = ph.sb("wo", [128, KC, 2048], BF16)
        mT = [ph.sb("mT%d" % i, [128, KC, 512], BF16) for i in range(2)]
        ht = [ph.sb("ht%d" % i, [128, D_MODEL], F32) for i in range(2)]
        xx = [ph.sb("xx%d" % i, [128, D_MODEL], F32) for i in range(2)]
        junk = ph.sb("junk", [128, D_MODEL], F32)
        lg = ph.sb("lg", [128, D_MODEL], F32)
        lbt = ph.sb("lbt", [128, D_MODEL], F32)
        stt = [ph.sb("stt%d" % i, [128, 8], F32) for i in range(2)]
        hTs = [ph.sb("hTs%d" % i, [128, KC, 512], BF16) for i in range(2)]
        ident = make_ident(P, ph)
        banks = [ph.ps("bk%d" % i, [128, 512]) for i in range(4)]
        tbanks = [ph.ps("tb%d" % i, [128, 512]) for i in range(4)]
        ctr = [0]
        P.dma("pool", wo[:, :, :], io["w_out"][l].rearrange("(kc p) c -> p kc c", p=128), w=["wo"], sem="d_wo")
        P.dma("sp", lg[:, :], io["ln_g_b"][l], w=["lg"], sem="d_lg")
        P.dma("sp", lbt[:, :], io["ln_b_b"][l], w=["lbt"], sem="d_lbt")
        mv = (d["mT"] if SP == 1 else d["mTs"]).rearrange("(kc p) t -> p kc t", p=128)
        for tt in range(NT):
            mi = tt % 2
            P.dma("sp" if SP == 1 else "pool", mT[mi][:, :, :], mv[:, :, tt * 512:(tt + 1) * 512], w=["mT%d" % mi], sem="d_mT%d" % mi)
            for q in range(4):
                tk = tt * 4 + q
                xi = tk % 2
                rows = slice(tk * 128, (tk + 1) * 128)
                P.dma("sp", ht[xi][:, :], d["h"][rows, :], w=["ht%d" % xi], sem="d_ht%d" % xi)
                for nb_ in range(4):
                    for cc in range(KC):
                        P.op("pe", lambda e: e.matmul(banks[nb_][:, :], lhsT=mT[mi][:, cc, q * 128:(q + 1) * 128],
                                                      rhs=wo[:, cc, nb_ * 512:(nb_ + 1) * 512], start=(cc == 0), stop=(cc == KC - 1)),
                             r=["mT%d" % mi, "wo"], w=["bk%d" % nb_], sig=(cc == KC - 1))
                    P.op("dve", lambda e: e.scalar_tensor_tensor(out=xx[xi][:, nb_ * 512:(nb_ + 1) * 512],
                                                                 in0=ht[xi][:, nb_ * 512:(nb_ + 1) * 512], scalar=ALPHA,
                                                                 in1=banks[nb_][:, :], op0=ALU.mult, op1=ALU.add),
                         r=["ht%d" % xi, "bk%d" % nb_], w=["xx%d" % xi])
                s_ = stt[xi]
                sk = "stt%d" % xi
                P.op("act", lambda e: e.activation(out=junk[:], in_=xx[xi][:], func=AF.Copy, accum_out=s_[:, 0:1]), r=["xx%d" % xi], w=["junk", sk])
                P.op("act", lambda e: e.activation(out=junk[:], in_=xx[xi][:], func=AF.Square, accum_out=s_[:, 1:2]), r=["xx%d" % xi], w=["junk", sk])
                P.op("dve", lambda e: e.tensor_scalar(out=s_[:, 2:4], in0=s_[:, 0:2], scalar1=1.0 / D_MODEL, scalar2=None, op0=ALU.mult),
                     r=[sk], w=[sk])
                P.op("dve", lambda e: e.tensor_tensor(out=s_[:, 4:5], in0=s_[:, 2:3], in1=s_[:, 2:3], op=ALU.mult), r=[sk], w=[sk])
                P.op("dve", lambda e: e.tensor_tensor(out=s_[:, 5:6], in0=s_[:, 3:4], in1=s_[:, 4:5], op=ALU.subtract), r=[sk], w=[sk])
                P.op("dve", lambda e: e.tensor_scalar(out=s_[:, 5:6], in0=s_[:, 5:6], scalar1=LN_EPS, scalar2=None, op0=ALU.add), r=[sk], w=[sk])
                P.op("act", lambda e: e.activation(out=s_[:, 6:7], in_=s_[:, 5:6], func=AF.Ln), r=[sk], w=[sk])
                P.op("act", lambda e: e.activation(out=s_[:, 7:8], in_=s_[:, 6:7], func=AF.Exp, scale=-0.5), r=[sk], w=[sk])
                P.op("dve", lambda e: e.tensor_scalar(out=xx[xi][:], in0=xx[xi][:], scalar1=s_[:, 2:3], scalar2=s_[:, 7:8],
                                                      op0=ALU.subtract, op1=ALU.mult), r=["xx%d" % xi, sk], w=["xx%d" % xi])
                P.op("pool", lambda e: e.tensor_tensor(out=xx[xi][:], in0=xx[xi][:], in1=lg[:], op=ALU.mult), r=["xx%d" % xi, "lg"], w=["xx%d" % xi])
                P.op("dve", lambda e: e.tensor_tensor(out=xx[xi][:], in0=xx[xi][:], in1=lbt[:], op=ALU.add), r=["xx%d" % xi, "lbt"], w=["xx%d" % xi])
                dst = final_out if final_out is not None else d["h"]
                P.dma("pool", dst[rows, :], xx[xi][:, :], r=["xx%d" % xi], w=[], sem="d_xo%d" % xi)
                if final_out is None:
                    emit_transpose_tile(P, xx[xi], "xx%d" % xi, ident, tbanks, ctr, hTs[mi], "hTs%d" % mi, q * 128)
            if final_out is None:
                P.dma("sp", d["hT"].rearrange("(kc p) t -> p kc t", p=128)[:, :, tt * 512:(tt + 1) * 512], hTs[mi][:, :, :],
                      r=["hTs%d" % mi], sem="d_hTs%d" % mi)


def IO_SPECS(T):
    SP = CFG["SPLIT"]
    YB, NHA, NHC = CFG["YB"], CFG["NHA"], CFG["NHC"]
    ncols = IN_COLS if SP == 1 else (4 * YB) * 3 + (3 * YB + 128 + YB) + 8192
    nmu = (3 * YB + 128) // 128
    return {
        "x": ([T // SP, D_MODEL], F32),
        "w_in": ([DEPTH, D_MODEL, ncols], F32),
        "rw_mu_t": ([DEPTH, 128, nmu], F32),
        "rw_vmu_t": ([DEPTH - 1, 128, 1], F32),
        "rw_v1": ([DEPTH - 1, D_MODEL, 32], F32),
        "a_strip": ([NHA, 128, 640], F32),
        "a_c31": ([128, NHA], F32),
        "da_lambda_b": ([DEPTH, 128, 256], F32),
        "da_subln_t": ([DEPTH, 128, 1], F32),
        "hg_lower_t": ([128, NHA, 4], F32),
        "hg_lower_r": ([1, 4, 128 * NHA], F32),
        "hg_norm_t": ([DEPTH, 128, 1], F32),
        "c_par": ([DEPTH, 64, NHC, 8], F32),
        "rw_w2": ([DEPTH, 64, 64 * NHC], F32),
        "rw_a2": ([DEPTH, 64, 64 * NHC], F32),
        "rw_v2": ([DEPTH - 1, 32, 64 * NHC], F32),
        "w_branch": ([DEPTH, 4, YB, D_MODEL], F32),
        "w_out": ([DEPTH, D_MODEL, D_MODEL], F32),
        "ln_g_b": ([DEPTH, 128, D_MODEL], F32),
        "ln_b_b": ([DEPTH, 128, D_MODEL], F32),
    }


def build_program(T, depth=DEPTH, debug_outs=()):
    SP = CFG["SPLIT"]
    nc = bass.Bass("TRN2", target_bir_lowering=False)
    io = {k: nc.dram_tensor(k, list(s), dt, kind="ExternalInput").ap() for k, (s, dt) in IO_SPECS(T).items()}
    out = nc.dram_tensor("out", [T // SP, D_MODEL], F32, kind="ExternalOutput").ap()
    d = declare_dram(nc, T, debug_outs=debug_outs)
    P = Prog(nc)
    phase_prep0(P, T // SP, io, d)
    for l in range(depth):
        if SP > 1:
            phase_gather_hT(P, T, d)
        phase_gemm(P, T, l, io, d)
        phase_mixA(P, T, l, io, d)
        phase_b0(P, T, l, io, d)
        phase_mixB(P, T, l, io, d)
        phase_mixC(P, T, l, io, d)
        phase_mixD(P, T, l, io, d)
        phase_mergeA(P, T, l, io, d)
        if SP > 1:
            phase_scatter_mT(P, T, d)
        phase_mergeB(P, T, l, io, d, final_out=(out if l == depth - 1 else None))
    P.barrier()
    return nc, P


def host_shared_inputs(inp):
    f = lambda a: np.ascontiguousarray(np.asarray(a, dtype=np.float32))
    L = DEPTH
    sh = {}
    sh["w_in"] = f(inp["w_in"])
    sh["rw_mu_t"] = f(np.asarray(inp["rw_mu"]).reshape(L, 25, 128).transpose(0, 2, 1))
    vmu = np.zeros((L - 1, 128, 1), np.float32)
    vmu[:, :32, 0] = np.asarray(inp["rw_v_mu"])
    sh["rw_vmu_t"] = vmu
    sh["rw_v1"] = f(inp["rw_v1"])
    rel = np.asarray(inp["rel_bias"], dtype=np.float32)
    sh["a_strip"] = host_a_strip(rel)
    sh["a_c31"] = f(np.broadcast_to(rel[31][None, :], (128, 8)))
    sh["da_lambda_b"] = f(np.broadcast_to(np.asarray(inp["da_lambda"]).reshape(L, 1, 256), (L, 128, 256)))
    sh["da_subln_t"] = f(np.asarray(inp["da_subln"]).reshape(L, 128, 1))
    sh["hg_lower_t"] = f(np.asarray(inp["hg_lower"]).reshape(L, 8, 128).transpose(2, 1, 0))
    sh["hg_lower_r"] = f(np.asarray(inp["hg_lower"]).reshape(1, L, 1024))
    sh["hg_norm_t"] = f(np.asarray(inp["hg_norm"]).reshape(L, 128, 1))
    hk = lambda a: np.asarray(a).reshape(L, 16, 64).transpose(0, 2, 1)
    v0 = np.zeros((L, 1024), np.float32)
    v0[1:] = np.asarray(inp["rw_v0"])
    cp = np.stack([hk(inp["rw_w0"]), hk(inp["rw_a0"]), hk(inp["rw_kk"]), hk(inp["rw_ka"]),
                   np.asarray(inp["rw_rk"]).transpose(0, 2, 1), hk(inp["rw_lnx_g"]), hk(inp["rw_lnx_b"]), hk(v0)], axis=-1)
    sh["c_par"] = f(cp)
    sh["rw_w2"] = f(inp["rw_w2"])
    sh["rw_a2"] = f(inp["rw_a2"])
    sh["rw_v2"] = f(inp["rw_v2"])
    sh["w_branch"] = f(inp["w_branch"])
    sh["w_out"] = f(inp["w_out"])
    sh["ln_g_b"] = f(np.broadcast_to(np.asarray(inp["ln_g"])[:, None, :], (L, 128, D_MODEL)))
    sh["ln_b_b"] = f(np.broadcast_to(np.asarray(inp["ln_b"])[:, None, :], (L, 128, D_MODEL)))
    return sh


def host_core_inputs(sh, hh):
    SP = CFG["SPLIT"]
    if SP == 1:
        return dict(sh)
    YB, NHA, NHC = CFG["YB"], CFG["NHA"], CFG["NHC"]
    a = slice(hh * YB, (hh + 1) * YB)
    segs = []
    for base in (O_AQ, O_AK, O_AV, O_AG, O_HQ, O_HF, O_HI, O_HG):
        segs.append(np.arange(base + hh * YB, base + (hh + 1) * YB))
    for j in range(3):
        segs.append(np.arange(O_RM + j * 1024 + hh * YB, O_RM + j * 1024 + (hh + 1) * YB))
    segs.append(np.arange(O_RM + 3072, O_RM + 3200))
    segs.append(np.arange(O_RG + hh * YB, O_RG + (hh + 1) * YB))
    for base in (O_SQ, O_SK, O_SV, O_SG):
        segs.append(np.arange(base + hh * YB, base + (hh + 1) * YB))
    segs.append(np.arange(O_MG, O_MG + 8192))
    cols = np.concatenate(segs)
    m = {}
    m["w_in"] = np.ascontiguousarray(sh["w_in"][:, :, cols])
    mu_full = sh["rw_mu_t"]
    blk = []
    for j in range(3):
        blk += list(range(j * 8 + hh * (YB // 128), j * 8 + (hh + 1) * (YB // 128)))
    blk.append(24)
    m["rw_mu_t"] = np.ascontiguousarray(mu_full[:, :, blk])
    m["rw_vmu_t"] = sh["rw_vmu_t"]
    m["rw_v1"] = sh["rw_v1"]
    ha = slice(hh * NHA, (hh + 1) * NHA)
    hc = slice(hh * NHC, (hh + 1) * NHC)
    m["a_strip"] = np.ascontiguousarray(sh["a_strip"][ha])
    m["a_c31"] = np.ascontiguousarray(sh["a_c31"][:, ha])
    m["da_lambda_b"] = sh["da_lambda_b"]
    m["da_subln_t"] = sh["da_subln_t"]
    m["hg_lower_t"] = np.ascontiguousarray(sh["hg_lower_t"][:, ha, :])
    m["hg_lower_r"] = np.ascontiguousarray(sh["hg_lower_r"][:, :, a])
    m["hg_norm_t"] = sh["hg_norm_t"]
    m["c_par"] = np.ascontiguousarray(sh["c_par"][:, :, hc, :])
    m["rw_w2"] = np.ascontiguousarray(sh["rw_w2"][:, :, a])
    m["rw_a2"] = np.ascontiguousarray(sh["rw_a2"][:, :, a])
    m["rw_v2"] = np.ascontiguousarray(sh["rw_v2"][:, :, a])
    m["w_branch"] = np.ascontiguousarray(sh["w_branch"][:, :, a, :])
    m["w_out"] = sh["w_out"]
    m["ln_g_b"] = sh["ln_g_b"]
    m["ln_b_b"] = sh["ln_b_b"]
    return m


_CACHE = {}
SPLIT = 2


def kernel(**inputs):
    x = np.asarray(inputs["x"], dtype=np.float32)
    B, T, _ = x.shape
    ncore = B * SPLIT
    configure(SPLIT, [[2 * i, 2 * i + 1] for i in range(ncore // 2)] if SPLIT == 2 else None)
    if T not in _CACHE:
        _CACHE[T] = build_program(T)
    nc, _ = _CACHE[T]
    sh = host_shared_inputs(inputs)
    per_half = [host_core_inputs(sh, hh) for hh in range(SPLIT)]
    TO = T // SPLIT
    in_maps = []
    for b in range(B):
        for hh in range(SPLIT):
            m = dict(per_half[hh])
            m["x"] = np.ascontiguousarray(x[b, hh * TO:(hh + 1) * TO])
            in_maps.append(m)
    res = run_bass_kernel_spmd(nc, in_maps, core_ids=list(range(ncore)))
    out = np.empty((B, T, D_MODEL), np.float32)
    for b in range(B):
        for hh in range(SPLIT):
            out[b, hh * TO:(hh + 1) * TO] = np.asarray(res.results[b * SPLIT + hh]["out"], dtype=np.float32)
    return out
```

```python
import contextlib
import math
import numpy as np
import concourse.bass as bass
import concourse.mybir as mybir
from concourse.bass_utils import run_bass_kernel_spmd

F32 = mybir.dt.float32
BF16 = mybir.dt.bfloat16
AF = mybir.ActivationFunctionType
ALU = mybir.AluOpType
AX = mybir.AxisListType

D_MODEL = 2048
DEPTH = 4
MIXW = 1024
IN_COLS = 24704
KC = D_MODEL // 128
ALPHA = (2 * DEPTH) ** 0.25
LN_EPS = 1e-5
RMS_EPS = 1e-6
RW_LN_EPS = 64e-5
NEG = -30000.0

CFG = dict(SPLIT=1, YB=1024, NHA=8, NHC=16, GROUPS=None)


def configure(split, groups=None):
    CFG.update(SPLIT=split, YB=1024 // split, NHA=8 // split, NHC=16 // split, GROUPS=groups)


O_AQ, O_AK, O_AV, O_AG = 0, 1024, 2048, 3072
O_HQ, O_HF, O_HI, O_HG = 4096, 5120, 6144, 7168
O_RM, O_RG = 8192, 11392
O_SQ, O_SK, O_SV, O_SG = 12416, 13440, 14464, 15488
O_MG = 16512


class Prog:
    ENG = ("pe", "act", "dve", "pool", "sp")

    def __init__(self, nc):
        self.nc = nc
        self.E = dict(pe=nc.tensor, act=nc.scalar, dve=nc.vector, pool=nc.gpsimd, sp=nc.sync)
        self.sems = {}
        self.semval = {}
        self.known = {e: {} for e in self.ENG}
        self.res = {}
        self.pend = {e: ([], []) for e in self.ENG}
        self.stack = contextlib.ExitStack()
        self.n_inst = 0
        self.uid = 0
        self.ecount = {e: 0 for e in self.ENG}
        self.marks = []

    def sem(self, key):
        if key not in self.sems:
            self.sems[key] = self.stack.enter_context(self.nc.semaphore("s_" + key))
            self.semval[key] = 0
        return self.sems[key]

    def _res(self, k):
        r = self.res.get(k)
        if r is None:
            r = [None, {}]
            self.res[k] = r
        return r

    def _wait(self, eng, tok):
        if tok is None:
            return
        sk, v = tok
        if eng == "pe" and sk == "pe":
            return
        if self.known[eng].get(sk, 0) >= v:
            return
        self.E[eng].wait_ge(self.sems[sk], v)
        self.known[eng][sk] = v
        self.n_inst += 1

    def _deps(self, eng, r, w):
        for k in r:
            self._wait(eng, self._res(k)[0])
        for k in w:
            rr = self._res(k)
            self._wait(eng, rr[0])
            for sk, v in list(rr[1].items()):
                self._wait(eng, (sk, v))

    def op(self, eng, fn, r=(), w=(), sig=True):
        self._deps(eng, r, w)
        inst = fn(self.E[eng])
        self.n_inst += 1
        self.ecount[eng] += 1
        pr, pw = self.pend[eng]
        pr.extend(r)
        pw.extend(w)
        if not sig:
            return inst
        self.sem(eng)
        self.semval[eng] += 1
        inst.then_inc(self.sems[eng], 1)
        tok = (eng, self.semval[eng])
        for k in pr:
            rr = self._res(k)
            rr[1][eng] = tok[1]
        for k in pw:
            rr = self._res(k)
            rr[0] = tok
            rr[1] = {}
        pr.clear()
        pw.clear()
        return inst

    def dma(self, q, out, in_, r=(), w=(), sem=None):
        assert sem is not None
        self._deps(q, r, w)
        self.sem(sem)
        inst = self.E[q].dma_start(out=out, in_=in_)
        self.n_inst += 1
        self.semval[sem] += 16
        inst.then_inc(self.sems[sem], 16)
        tok = (sem, self.semval[sem])
        for k in r:
            self._res(k)[1][sem] = tok[1]
        for k in w:
            rr = self._res(k)
            rr[0] = tok
            rr[1] = {}
        return inst

    def coll(self, kind, in_ap, out_ap):
        self.sem("cc")
        alu = ALU.add if kind == "ReduceScatter" else ALU.bypass
        inst = self.E["pool"].collective_compute(kind, alu, replica_groups=CFG["GROUPS"], ins=[in_ap.opt()], outs=[out_ap.opt()])
        self.n_inst += 1
        self.semval["cc"] += 1
        inst.then_inc(self.sems["cc"], 1)
        self._wait("pool", ("cc", self.semval["cc"]))

    def barrier(self):
        for e in self.ENG:
            assert not self.pend[e][0] and not self.pend[e][1], "pending unsignalled ops at barrier"
        for e in self.ENG:
            for sk, v in self.semval.items():
                if v > 0:
                    self._wait(e, (sk, v))
        self.res = {}

    @contextlib.contextmanager
    def phase(self, name=""):
        self.barrier()
        self.marks.append((name, dict(self.ecount)))
        st = contextlib.ExitStack()
        ph = Phase(self, st)
        try:
            yield ph
        finally:
            self.barrier()
            st.close()


class Phase:
    def __init__(self, prog, st):
        self.p = prog
        self.st = st

    def sb(self, name, shape, dt):
        self.p.uid += 1
        return self.st.enter_context(self.p.nc.sbuf_tensor("%s_%d" % (name, self.p.uid), list(shape), dt))

    def ps(self, name, shape, dt=F32):
        self.p.uid += 1
        return self.st.enter_context(self.p.nc.psum_tensor("%s_%d" % (name, self.p.uid), list(shape), dt))


def declare_dram(nc, T, debug_outs=(), debug_ins=()):
    d = {}

    def t(name, shape, dt, kind="Internal"):
        if name in debug_outs:
            kind = "ExternalOutput"
        if name in debug_ins:
            kind = "ExternalInput"
        d[name] = nc.dram_tensor(name, list(shape), dt, kind=kind).ap()

    YB = CFG["YB"]
    SP = CFG["SPLIT"]
    TO = T // SP
    t("hT", [D_MODEL, TO], BF16)
    t("h", [TO, D_MODEL], F32)
    if SP > 1:
        NCH = max(1, (D_MODEL * TO * 2) // (2 << 20))
        d["NCH"] = NCH
        t("hTg", [NCH, SP, D_MODEL // NCH, TO], BF16)
        t("mTp", [SP, D_MODEL, TO], F32)
        t("mTs", [D_MODEL, TO], F32)
    t("h0g", [SP, D_MODEL], F32)
    t("Bd0", [1, 8], F32)
    t("AqT", [YB, T], BF16)
    t("AkT", [YB, T], BF16)
    t("Av", [T, YB], BF16)
    t("AgT", [YB, T], BF16)
    t("BqT", [YB, T], BF16)
    t("BfT", [YB, T], F32)
    t("Bi", [T, YB], BF16)
    t("BgT", [YB, T], BF16)
    t("CmT", [3 * YB + 128, T], F32)
    t("CgT", [YB, T], BF16)
    t("CvdT", [32, T], F32)
    t("Cvf", [YB, T], F32)
    t("DqT", [YB, T], BF16)
    t("DkT", [YB, T], BF16)
    t("Dv", [T, YB], BF16)
    t("DgT", [YB, T], BF16)
    t("MgT", [8192, T], BF16)
    t("yT", [4 * YB, T], BF16)
    t("mT", [D_MODEL, T], BF16)
    return d


def phase_gemm(P, T, l, io, d):
    nc = P.nc
    NT = T // 512
    with P.phase("gemm") as ph:
        hT = ph.sb("hT", [128, KC, T + 1], BF16)
        wsb = [ph.sb("wsb%d" % i, [128, KC, 512], BF16) for i in range(2)]
        SC = min(T, 2048)
        stg = [ph.sb("stg%d" % i, [128, SC], F32) for i in range(2)]
        stb = [ph.sb("stb%d" % i, [128, max(SC, 1024)], BF16) for i in range(2)]
        tmp = [ph.sb("tmp%d" % i, [128, 512], F32) for i in range(2)]
        zs = [ph.sb("zs%d" % i, [128, 513], F32) for i in range(2)]
        mu = ph.sb("mu", [128, 26], F32)
        om = ph.sb("om", [128, 26], F32)
        banks = [ph.ps("bk%d" % i, [128, 512]) for i in range(8)]

        P.op("pool", lambda e: e.memset(hT[:, :, 0:1], 0.0), w=["hT"])
        if CFG["SPLIT"] == 1:
            P.dma("sp", hT[:, :, 1:T + 1], d["hT"].rearrange("(kc p) t -> p kc t", p=128), w=["hT"], sem="d_hT")
        else:
            NCH = d["NCH"]
            TO = T // CFG["SPLIT"]
            JJ = KC // NCH
            for r_ in range(CFG["SPLIT"]):
                for i_ in range(NCH):
                    P.dma("sp", hT[:, i_ * JJ:(i_ + 1) * JJ, 1 + r_ * TO:1 + (r_ + 1) * TO],
                          d["hTg"][i_, r_].rearrange("(j p) t -> p j t", p=128), w=["hT"], sem="d_hT")
        NMU = io["rw_mu_t"].shape[2]
        P.op("pool", lambda e: e.memset(mu[:], 0.0), w=["mu"])
        P.dma("sp", mu[:, 0:NMU], io["rw_mu_t"][l], w=["mu"], sem="d_mu")
        if l > 0:
            P.dma("sp", mu[:, 25:26], io["rw_vmu_t"][l - 1], w=["mu"], sem="d_mu")
        else:
            P.op("pool", lambda e: e.memset(mu[:, 25:26], 0.0), w=["mu"])
        P.op("dve", lambda e: e.tensor_scalar(out=om[:], in0=mu[:], scalar1=-1.0, scalar2=1.0,
                                              op0=ALU.mult, op1=ALU.add), r=["mu"], w=["om"])

        w_l = io["w_in"][l].rearrange("(kc p) c -> p kc c", p=128)
        st = dict(wi=0, bi=0, si=0, ti=0, zi=0)

        def load_w(src_ap, ncols):
            i = st["wi"] % 2
            st["wi"] += 1
            P.dma("pool", wsb[i][:, :, 0:ncols], src_ap, w=["wsb%d" % i], sem="d_wsb%d" % i)
            return i

        def bank():
            b = st["bi"] % 8
            st["bi"] += 1
            return b

        def mm_F(wi, c0, tt, shift, b):
            off = 0 if shift else 1
            for kc in range(KC):
                P.op("pe", lambda e, kc=kc: e.matmul(banks[b][:, :], lhsT=wsb[wi][:, kc, c0:c0 + 128],
                                                     rhs=hT[:, kc, off + tt * 512: off + tt * 512 + 512],
                                                     start=(kc == 0), stop=(kc == KC - 1)),
                     r=["wsb%d" % wi, "hT"], w=["bk%d" % b], sig=(kc == KC - 1))

        def job_F(col0, ncols, dest, drow0, kind, scale=1.0, mucol0=None, out_dt=BF16, nrows=128):
            for s0 in range(0, ncols, 512):
                sw = min(512, ncols - s0)
                wi = load_w(w_l[:, :, col0 + s0: col0 + s0 + sw], sw)
                for c0 in range(0, sw, 128):
                    for tt in range(NT):
                        if (tt * 512) % SC == 0:
                            si = st["si"] % 2
                            st["si"] += 1
                            so = stb[si] if out_dt == BF16 else stg[si]
                            skey = ("stb%d" if out_dt == BF16 else "stg%d") % si
                        b = bank()
                        mm_F(wi, c0, tt, False, b)
                        lo = (tt * 512) % SC
                        osl = so[:, lo:lo + 512]
                        if kind == "copy":
                            P.op("dve", lambda e: e.tensor_copy(out=osl, in_=banks[b][:, :]),
                                 r=["bk%d" % b], w=[skey])
                        elif kind == "scale":
                            P.op("act", lambda e: e.activation(out=osl, in_=banks[b][:, :], func=AF.Copy, scale=scale),
                                 r=["bk%d" % b], w=[skey])
                        elif kind == "silu":
                            P.op("act", lambda e: e.activation(out=osl, in_=banks[b][:, :], func=AF.Silu),
                                 r=["bk%d" % b], w=[skey])
                        elif kind == "sigmoid":
                            P.op("act", lambda e: e.activation(out=osl, in_=banks[b][:, :], func=AF.Sigmoid),
                                 r=["bk%d" % b], w=[skey])
                        elif kind == "shift":
                            mc = mucol0 + (s0 + c0) // 128
                            zi = st["zi"] % 2
                            st["zi"] += 1
                            Zs, zk = zs[zi], "zs%d" % zi
                            Zp, zpk = zs[1 - zi], "zs%d" % (1 - zi)
                            if tt == 0:
                                P.op("pool", lambda e: e.memset(Zs[:, 0:1], 0.0), w=[zk])
                            else:
                                P.op("pool", lambda e: e.tensor_copy(out=Zs[:, 0:1], in_=Zp[:, 512:513]), r=[zpk], w=[zk])
                            P.op("act", lambda e: e.activation(out=Zs[:, 1:513], in_=banks[b][:, :], func=AF.Copy), r=["bk%d" % b], w=[zk])
                            ti = st["ti"] % 2
                            st["ti"] += 1
                            P.op("dve", lambda e: e.tensor_scalar(out=tmp[ti][:], in0=Zs[:, 0:512], scalar1=mu[:, mc:mc + 1],
                                                                  scalar2=None, op0=ALU.mult), r=[zk, "mu"], w=["tmp%d" % ti])
                            P.op("dve", lambda e: e.scalar_tensor_tensor(out=osl, in0=Zs[:, 1:513], scalar=om[:, mc:mc + 1],
                                                                         in1=tmp[ti][:], op0=ALU.mult, op1=ALU.add),
                                 r=[zk, "om", "tmp%d" % ti], w=[skey])
                        else:
                            raise ValueError(kind)
                        if (tt * 512 + 512) % SC == 0:
                            r0 = drow0 + s0 + c0
                            t_lo = tt * 512 + 512 - SC
                            P.dma("sp", dest[r0:r0 + nrows, t_lo:t_lo + SC], so[0:nrows, 0:SC], r=[skey], w=[], sem="d_" + skey)

        def job_T(col0, dest):
            YBl = CFG["YB"]
            wis = []
            for s0 in range(0, YBl, 512):
                wis.append(load_w(w_l[:, :, col0 + s0: col0 + s0 + 512], 512))
            for tk in range(T // 128):
                si = st["si"] % 2
                st["si"] += 1
                for h2 in range(YBl // 512):
                    b = bank()
                    for kc in range(KC):
                        P.op("pe", lambda e, kc=kc: e.matmul(banks[b][:, :], lhsT=hT[:, kc, 1 + tk * 128: 1 + tk * 128 + 128],
                                                             rhs=wsb[wis[h2]][:, kc, :], start=(kc == 0), stop=(kc == KC - 1)),
                             r=["wsb%d" % wis[h2], "hT"], w=["bk%d" % b], sig=(kc == KC - 1))
                    P.op("dve", lambda e: e.tensor_copy(out=stb[si][:, h2 * 512:(h2 + 1) * 512], in_=banks[b][:, :]),
                         r=["bk%d" % b], w=["stb%d" % si])
                P.dma("sp", dest[tk * 128:(tk + 1) * 128, :], stb[si][:, 0:YBl], r=["stb%d" % si], sem="d_stb%d" % si)

        YB = CFG["YB"]
        if CFG["SPLIT"] == 1:
            o = dict(AQ=O_AQ, AK=O_AK, AV=O_AV, AG=O_AG, HQ=O_HQ, HF=O_HF, HI=O_HI, HG=O_HG, RM=O_RM, RG=O_RG,
                     SQ=O_SQ, SK=O_SK, SV=O_SV, SG=O_SG, MG=O_MG)
        else:
            cb = 4 * YB
            cc_ = 2 * cb
            cd = cc_ + 3 * YB + 128 + YB
            o = dict(AQ=0, AK=YB, AV=2 * YB, AG=3 * YB, HQ=cb, HF=cb + YB, HI=cb + 2 * YB, HG=cb + 3 * YB,
                     RM=cc_, RG=cc_ + 3 * YB + 128, SQ=cd, SK=cd + YB, SV=cd + 2 * YB, SG=cd + 3 * YB, MG=cd + 4 * YB)
        job_F(o["AQ"], YB, d["AqT"], 0, "scale", scale=0.125)
        job_F(o["AK"], YB, d["AkT"], 0, "copy")
        job_T(o["AV"], d["Av"])
        job_F(o["AG"], YB, d["AgT"], 0, "silu")
        job_F(o["HQ"], YB, d["BqT"], 0, "copy")
        job_F(o["HF"], YB, d["BfT"], 0, "copy", out_dt=F32)
        job_T(o["HI"], d["Bi"])
        job_F(o["HG"], YB, d["BgT"], 0, "silu")
        job_F(o["RM"], 3 * YB + 128, d["CmT"], 0, "shift", mucol0=0, out_dt=F32)
        job_F(o["RG"], YB, d["CgT"], 0, "silu")
        job_F(o["SQ"], YB, d["DqT"], 0, "scale", scale=128 ** -0.5)
        job_F(o["SK"], YB, d["DkT"], 0, "copy")
        job_T(o["SV"], d["Dv"])
        job_F(o["SG"], YB, d["DgT"], 0, "silu")
        job_F(o["MG"], 8192, d["MgT"], 0, "sigmoid")
        if l > 0:
            i = st["wi"] % 2
            st["wi"] += 1
            P.op("pool", lambda e: e.memset(wsb[i][:, :, 0:128], 0.0), w=["wsb%d" % i])
            P.dma("pool", wsb[i][:, :, 0:32], io["rw_v1"][l - 1].rearrange("(kc p) c -> p kc c", p=128),
                  w=["wsb%d" % i], sem="d_wsb%d" % i)
            for tt in range(NT):
                if (tt * 512) % SC == 0:
                    si = st["si"] % 2
                    st["si"] += 1
                b = bank()
                mm_F(i, 0, tt, False, b)
                b2 = bank()
                mm_F(i, 0, tt, True, b2)
                ti = st["ti"] % 2
                st["ti"] += 1
                lo = (tt * 512) % SC
                osl = stg[si][:, lo:lo + 512]
                P.op("dve", lambda e: e.tensor_scalar(out=tmp[ti][:], in0=banks[b2][:, :], scalar1=mu[:, 25:26],
                                                      scalar2=None, op0=ALU.mult), r=["bk%d" % b2, "mu"], w=["tmp%d" % ti])
                P.op("dve", lambda e: e.scalar_tensor_tensor(out=osl, in0=banks[b][:, :], scalar=om[:, 25:26], in1=tmp[ti][:],
                                                             op0=ALU.mult, op1=ALU.add),
                     r=["bk%d" % b, "om", "tmp%d" % ti], w=["stg%d" % si])
                if (tt * 512 + 512) % SC == 0:
                    t_lo = tt * 512 + 512 - SC
                    P.dma("sp", d["CvdT"][:, t_lo:t_lo + SC], stg[si][0:32, 0:SC], r=["stg%d" % si], sem="d_stg%d" % si)


def emit_transpose_tile(P, src, src_key, ident, banks, bank_ctr, hTs, hTs_key, col0):
    for g in range(4):
        b = bank_ctr[0] % len(banks)
        bank_ctr[0] += 1
        for j in range(4):
            kc = g * 4 + j
            P.op("pe", lambda e: e.transpose(out=banks[b][:, j * 128:(j + 1) * 128], in_=src[:, kc * 128:(kc + 1) * 128],
                                             identity=ident[:, :]),
                 r=[src_key, "ident"], w=["tb%d" % b], sig=(j == 3))
        eng = "act" if g % 2 else "dve"
        if eng == "act":
            P.op("act", lambda e: e.activation(out=hTs[:, g * 4:(g + 1) * 4, col0:col0 + 128],
                                               in_=banks[b][:, :].rearrange("p (j t) -> p j t", j=4), func=AF.Copy),
                 r=["tb%d" % b], w=[hTs_key])
        else:
            P.op("dve", lambda e: e.tensor_copy(out=hTs[:, g * 4:(g + 1) * 4, col0:col0 + 128],
                                                in_=banks[b][:, :].rearrange("p (j t) -> p j t", j=4)),
                 r=["tb%d" % b], w=[hTs_key])


def make_ident(P, ph, dt=F32, name="ident"):
    ident = ph.sb(name, [128, 128], dt)
    if dt == F32:
        P.op("pool", lambda e: e.memset(ident[:], 1.0), w=[name])
        P.op("pool", lambda e: e.affine_select(out=ident[:], in_=ident[:], pattern=[[-1, 128]], compare_op=ALU.is_equal,
                                               fill=0.0, base=0, channel_multiplier=1), r=[name], w=[name])
    else:
        tmpi = ph.sb(name + "_f", [128, 128], F32)
        P.op("pool", lambda e: e.memset(tmpi[:], 1.0), w=[name + "_f"])
        P.op("pool", lambda e: e.affine_select(out=tmpi[:], in_=tmpi[:], pattern=[[-1, 128]], compare_op=ALU.is_equal,
                                               fill=0.0, base=0, channel_multiplier=1), r=[name + "_f"], w=[name + "_f"])
        P.op("dve", lambda e: e.tensor_copy(out=ident[:], in_=tmpi[:]), r=[name + "_f"], w=[name])
    return ident


def phase_prep0(P, T, io, d):
    with P.phase("prep0") as ph:
        ident = make_ident(P, ph)
        banks = [ph.ps("tb%d" % i, [128, 512]) for i in range(4)]
        xt = [ph.sb("xt%d" % i, [128, D_MODEL], F32) for i in range(2)]
        hTs = [ph.sb("hTs%d" % i, [128, KC, 512], BF16) for i in range(2)]
        ctr = [0]
        for tt in range(T // 512):
            hi = tt % 2
            for q in range(4):
                tk = tt * 4 + q
                xi = tk % 2
                P.dma("sp", xt[xi][:, :], io["x"][tk * 128:(tk + 1) * 128, :], w=["xt%d" % xi], sem="d_xt%d" % xi)
                emit_transpose_tile(P, xt[xi], "xt%d" % xi, ident, banks, ctr, hTs[hi], "hTs%d" % hi, q * 128)
                P.dma("pool", d["h"][tk * 128:(tk + 1) * 128, :], xt[xi][:, :], r=["xt%d" % xi], sem="d_xo%d" % xi)
            P.dma("sp", d["hT"].rearrange("(kc p) t -> p kc t", p=128)[:, :, tt * 512:(tt + 1) * 512], hTs[hi][:, :, :],
                  r=["hTs%d" % hi], sem="d_hTs%d" % hi)


def t5_bucket_np(dist):
    n = np.maximum(dist, 0)
    nf = np.maximum(n, 1).astype(np.float32)
    large = 16 + (np.log(nf / np.float32(16)) / np.float32(math.log(128 / 16)) * np.float32(16)).astype(np.int32)
    large = np.minimum(large, 31)
    return np.where(n < 16, n, large)


def phase_mixA(P, T, l, io, d, NH=None):
    NH = NH or CFG["NHA"]
    lam_init = 0.8 - 0.6 * math.exp(-0.3 * l)
    NG = T // 512
    NKB = T // 128
    with P.phase("mixA") as ph:
        qT = [ph.sb("qT%d" % i, [128, T], BF16) for i in range(2)]
        kT = [ph.sb("kT%d" % i, [128, T], BF16) for i in range(2)]
        V = [ph.sb("V%d" % i, [128, NKB, 128], BF16) for i in range(2)]
        gT = [ph.sb("gT%d" % i, [128, T], BF16) for i in range(2)]
        stf = [ph.sb("stf%d" % i, [128, 640], F32) for i in range(2)]
        shi = [ph.sb("shi%d" % i, [128, 640], BF16) for i in range(2)]
        slo = [ph.sb("slo%d" % i, [128, 640], BF16) for i in range(2)]
        yst = [ph.sb("yst%d" % i, [128, T], BF16) for i in range(2)]
        pT = [ph.sb("pT%d" % i, [128, 512], BF16) for i in range(3)]
        wk = {n: ph.sb(n, [128, 512], F32) for n in ("rl1", "rl2", "a1", "a2", "sq", "t1")}
        identb = make_ident(P, ph, BF16, "identb")
        onesf = ph.sb("onesf", [128, 128], F32)
        onesb = ph.sb("onesb", [128, 128], BF16)
        c31 = ph.sb("c31", [128, CFG["NHA"]], F32)
        lam = ph.sb("lam", [128, 256], F32)
        lw = ph.sb("lamw", [128, 128], F32)
        sc = ph.sb("lamsc", [128, 8], F32)
        sub = ph.sb("subln", [128, 1], F32)
        sbk = [ph.ps("sbk%d" % i, [128, 512]) for i in range(3)]
        obk = [ph.ps("obk%d" % i, [128, 512]) for i in range(2)]
        lbk = [ph.ps("lbk%d" % i, [128, 512]) for i in range(2)]

        P.op("pool", lambda e: e.memset(onesf[:], 1.0), w=["onesf"])
        P.op("pool", lambda e: e.memset(onesb[:], 1.0), w=["onesb"])
        P.dma("sp", c31[:, :], io["a_c31"], w=["c31"], sem="d_c31")
        P.dma("sp", lam[:, :], io["da_lambda_b"][l], w=["lam"], sem="d_lam")
        P.dma("sp", sub[:, :], io["da_subln_t"][l], w=["subln"], sem="d_sub")
        P.op("dve", lambda e: e.tensor_tensor(out=lw[:, 0:64], in0=lam[:, 0:64], in1=lam[:, 64:128], op=ALU.mult), r=["lam"], w=["lamw"])
        P.op("dve", lambda e: e.tensor_tensor(out=lw[:, 64:128], in0=lam[:, 128:192], in1=lam[:, 192:256], op=ALU.mult), r=["lam"], w=["lamw"])
        P.op("dve", lambda e: e.reduce_sum(out=sc[:, 0:2], in_=lw[:].rearrange("p (a b) -> p a b", a=2), axis=AX.X), r=["lamw"], w=["lamsc"])
        P.op("act", lambda e: e.activation(out=sc[:, 2:4], in_=sc[:, 0:2], func=AF.Exp), r=["lamsc"], w=["lamsc"])
        P.op("dve", lambda e: e.tensor_tensor(out=sc[:, 4:5], in0=sc[:, 3:4], in1=sc[:, 2:3], op=ALU.subtract), r=["lamsc"], w=["lamsc"])
        P.op("dve", lambda e: e.tensor_scalar(out=sc[:, 5:6], in0=sc[:, 4:5], scalar1=-lam_init, scalar2=None, op0=ALU.add), r=["lamsc"], w=["lamsc"])
        P.op("dve", lambda e: e.tensor_scalar(out=sc[:, 6:7], in0=sub[:, 0:1], scalar1=1.0 - lam_init, scalar2=None, op0=ALU.mult), r=["subln", "lamsc"], w=["lamsc"])
        negl = sc[:, 5:6]
        gsc = sc[:, 6:7]

        def load_head(h):
            i = h % 2
            P.dma("sp", qT[i][:, :], d["AqT"][h * 128:(h + 1) * 128, :], w=["qT%d" % i], sem="d_qT%d" % i)
            P.dma("sp", kT[i][:, :], d["AkT"][h * 128:(h + 1) * 128, :], w=["kT%d" % i], sem="d_kT%d" % i)
            P.dma("sp", V[i][:, :, :], d["Av"].rearrange("(kb p) c -> p kb c", p=128)[:, :, h * 128:(h + 1) * 128],
                  w=["V%d" % i], sem="d_V%d" % i)
            P.dma("sp", gT[i][:, :], d["AgT"][h * 128:(h + 1) * 128, :], w=["gT%d" % i], sem="d_gT%d" % i)
            P.dma("sp", stf[i][:, :], io["a_strip"][h], w=["stf%d" % i], sem="d_stf%d" % i)
            P.op("dve", lambda e: e.tensor_copy(out=shi[i][:], in_=stf[i][:]), r=["stf%d" % i], w=["shi%d" % i])
            P.op("dve", lambda e: e.tensor_tensor(out=stf[i][:], in0=stf[i][:], in1=shi[i][:], op=ALU.subtract),
                 r=["stf%d" % i, "shi%d" % i], w=["stf%d" % i])
            P.op("dve", lambda e: e.tensor_copy(out=slo[i][:], in_=stf[i][:]), r=["stf%d" % i], w=["slo%d" % i])

        cnt = dict(s=0, p=0)
        load_head(0)
        for h in range(NH):
            i = h % 2
            if h + 1 < NH:
                load_head(h + 1)
            for g in range(NG):
                q0 = g * 512
                blocks = [(m, ki) for m in range(2) for ki in range(4 * g + 4)]
                nblk = 4 * g + 4
                info = {}

                def stage1(bd):
                    m, ki = bd
                    pb = slice(m * 64, m * 64 + 64)
                    j = ki - 4 * g
                    near = j >= -1
                    c0 = 128 * j if j >= 1 else 0
                    n = 512 - c0
                    sb_i = cnt["s"] % 3
                    cnt["s"] += 1
                    S = sbk[sb_i]
                    skey = "sbk%d" % sb_i
                    P.op("pe", lambda e: e.matmul(S[:, c0:512], lhsT=kT[i][pb, ki * 128:(ki + 1) * 128],
                                                  rhs=qT[i][pb, q0 + c0:q0 + 512], start=True, stop=not near),
                         r=["kT%d" % i, "qT%d" % i], w=[skey], sig=not near)
                    if near:
                        so = 128 if j == -1 else 0
                        P.op("pe", lambda e: e.matmul(S[:, c0:512], lhsT=identb[:, :], rhs=shi[i][:, so:so + n],
                                                      start=False, stop=False), r=["identb", "shi%d" % i], w=[skey], sig=False)
                        P.op("pe", lambda e: e.matmul(S[:, c0:512], lhsT=identb[:, :], rhs=slo[i][:, so:so + n],
                                                      start=False, stop=True), r=["identb", "slo%d" % i], w=[skey])
                    info[bd] = (S, skey, near, c0)

                def stage2(bd):
                    m, ki = bd
                    S, skey, near, c0 = info[bd]
                    p_i = cnt["p"] % 3
                    cnt["p"] += 1
                    pk = "pT%d" % p_i
                    if near:
                        P.op("act", lambda e: e.activation(out=pT[p_i][:, c0:512], in_=S[:, c0:512], func=AF.Exp),
                             r=[skey], w=[pk])
                    else:
                        P.op("act", lambda e: e.activation(out=pT[p_i][:, c0:512], in_=S[:, c0:512], func=AF.Exp,
                                                           bias=c31[:, h:h + 1]), r=[skey, "c31"], w=[pk])
                    info[bd] = (p_i, pk, c0)

                def stage3(bd):
                    m, ki = bd
                    p_i, pk, c0 = info.pop(bd)
                    P.op("pe", lambda e: e.matmul(obk[m][:, c0:512], lhsT=V[i][:, ki, :], rhs=pT[p_i][:, c0:512],
                                                  start=(ki == 0), stop=(ki == nblk - 1), skip_group_check=True),
                         r=["V%d" % i, pk], w=["obk%d" % m], sig=False)
                    P.op("pe", lambda e: e.matmul(lbk[m][:, c0:512], lhsT=onesb[:, :], rhs=pT[p_i][:, c0:512],
                                                  start=(ki == 0), stop=(ki == nblk - 1), skip_group_check=True),
                         r=["onesb", pk], w=["lbk%d" % m])

                nb_ = len(blocks)
                for step in range(nb_ + 2):
                    if step < nb_:
                        stage1(blocks[step])
                    if 0 <= step - 1 < nb_:
                        stage2(blocks[step - 1])
                    if 0 <= step - 2 < nb_:
                        stage3(blocks[step - 2])
                P.op("dve", lambda e: e.reciprocal(out=wk["rl1"][:], in_=lbk[0][:, :]), r=["lbk0"], w=["rl1"])
                P.op("dve", lambda e: e.reciprocal(out=wk["rl2"][:], in_=lbk[1][:, :]), r=["lbk1"], w=["rl2"])
                P.op("dve", lambda e: e.tensor_tensor(out=wk["a1"][:], in0=obk[0][:, :], in1=wk["rl1"][:], op=ALU.mult),
                     r=["obk0", "rl1"], w=["a1"])
                P.op("dve", lambda e: e.tensor_tensor(out=wk["a2"][:], in0=obk[1][:, :], in1=wk["rl2"][:], op=ALU.mult),
                     r=["obk1", "rl2"], w=["a2"])
                P.op("dve", lambda e: e.scalar_tensor_tensor(out=wk["a1"][:], in0=wk["a2"][:], scalar=negl, in1=wk["a1"][:],
                                                             op0=ALU.mult, op1=ALU.add), r=["a2", "a1", "lamsc"], w=["a1"])
                P.op("act", lambda e: e.activation(out=wk["sq"][:], in_=wk["a1"][:], func=AF.Square), r=["a1"], w=["sq"])
                sb_i = cnt["s"] % 3
                cnt["s"] += 1
                S = sbk[sb_i]
                skey = "sbk%d" % sb_i
                P.op("pe", lambda e: e.matmul(S[:, :], lhsT=onesf[:, :], rhs=wk["sq"][:], start=True, stop=True),
                     r=["onesf", "sq"], w=[skey])
                P.op("dve", lambda e: e.tensor_scalar(out=wk["t1"][:], in0=S[:, :], scalar1=1.0 / 128, scalar2=RMS_EPS,
                                                      op0=ALU.mult, op1=ALU.add), r=[skey], w=["t1"])
                P.op("act", lambda e: e.activation(out=wk["t1"][:], in_=wk["t1"][:], func=AF.Ln), r=["t1"], w=["t1"])
                P.op("act", lambda e: e.activation(out=wk["t1"][:], in_=wk["t1"][:], func=AF.Exp, scale=-0.5), r=["t1"], w=["t1"])
                P.op("dve", lambda e: e.tensor_tensor(out=wk["a1"][:], in0=wk["a1"][:], in1=wk["t1"][:], op=ALU.mult),
                     r=["a1", "t1"], w=["a1"])
                P.op("dve", lambda e: e.scalar_tensor_tensor(out=yst[i][:, q0:q0 + 512], in0=wk["a1"][:], scalar=gsc,
                                                             in1=gT[i][:, q0:q0 + 512], op0=ALU.mult, op1=ALU.mult),
                     r=["a1", "lamsc", "gT%d" % i], w=["yst%d" % i])
            P.dma("pool", d["yT"][h * 128:(h + 1) * 128, :], yst[i][:, :], r=["yst%d" % i], sem="d_yst%d" % i)


def host_a_strip(rel_bias):
    v = np.arange(640)[None, :]
    s = np.arange(128)[:, None]
    dist = v - s
    bk = t5_bucket_np(dist)
    out = np.empty((8, 128, 640), np.float32)
    for h in range(8):
        out[h] = np.where(dist >= 0, rel_bias[bk, h], np.float32(NEG))
    return out


def phase_mixD(P, T, l, io, d, NH=None):
    NH = NH or CFG["NHA"]
    NG = T // 512
    NKB = T // 128
    with P.phase("mixD") as ph:
        qT = [ph.sb("qT%d" % i, [128, T], BF16) for i in range(2)]
        kT = [ph.sb("kT%d" % i, [128, T], BF16) for i in range(2)]
        V = [ph.sb("V%d" % i, [128, NKB, 128], BF16) for i in range(2)]
        gT = [ph.sb("gT%d" % i, [128, T], BF16) for i in range(2)]
        yst = [ph.sb("yst%d" % i, [128, T], BF16) for i in range(2)]
        ee = [ph.sb("ee%d" % i, [128, 512], F32) for i in range(3)]
        sp = [ph.sb("sp%d" % i, [128, 512], F32) for i in range(3)]
        lk = [ph.sb("lk%d" % i, [128, 512], F32) for i in range(3)]
        aT = [ph.sb("aT%d" % i, [128, 512], BF16) for i in range(3)]
        rsum = [ph.sb("rsum%d" % i, [128, 512], F32) for i in range(2)]
        onesf = ph.sb("onesf", [128, 128], F32)
        lstr = ph.sb("lstr", [128, 128], F32)
        m01 = ph.sb("m01", [128, 640], F32)
        zbk = [ph.ps("zbk%d" % i, [128, 512]) for i in range(3)]
        bbk = [ph.ps("bbk%d" % i, [128, 512]) for i in range(2)]
        obk = [ph.ps("obk%d" % i, [128, 512]) for i in range(2)]

        P.op("pool", lambda e: e.memset(onesf[:], 1.0), w=["onesf"])
        P.op("pool", lambda e: e.memset(lstr[:], 1.0), w=["lstr"])
        P.op("pool", lambda e: e.affine_select(out=lstr[:], in_=lstr[:], pattern=[[-1, 128]], compare_op=ALU.is_gt,
                                               fill=0.0, base=0, channel_multiplier=1), r=["lstr"], w=["lstr"])
        P.op("pool", lambda e: e.memset(m01[:], 1.0), w=["m01"])
        P.op("pool", lambda e: e.affine_select(out=m01[:], in_=m01[:], pattern=[[1, 640]], compare_op=ALU.is_gt,
                                               fill=0.0, base=0, channel_multiplier=-1), r=["m01"], w=["m01"])

        def load_head(h):
            i = h % 2
            P.dma("sp", qT[i][:, :], d["DqT"][h * 128:(h + 1) * 128, :], w=["qT%d" % i], sem="d_qT%d" % i)
            P.dma("sp", kT[i][:, :], d["DkT"][h * 128:(h + 1) * 128, :], w=["kT%d" % i], sem="d_kT%d" % i)
            P.dma("sp", V[i][:, :, :], d["Dv"].rearrange("(kb p) c -> p kb c", p=128)[:, :, h * 128:(h + 1) * 128],
                  w=["V%d" % i], sem="d_V%d" % i)
            P.dma("sp", gT[i][:, :], d["DgT"][h * 128:(h + 1) * 128, :], w=["gT%d" % i], sem="d_gT%d" % i)

        blocks = []
        gidx = 0
        for h in range(NH):
            for g in range(NG):
                kis = list(range(4 * g + 3, -1, -1))
                for ki in kis:
                    j = ki - 4 * g
                    blocks.append(dict(h=h, g=g, ki=ki, j=j, c0=(128 * j if j >= 1 else 0), first=(ki == kis[0]), last=(ki == 0),
                                       gi=gidx, n=len(blocks)))
                gidx += 1

        def S0(b):
            i = b["h"] % 2
            if b["first"] and b["g"] == 0:
                load_head(b["h"])
            zi = b["n"] % 3
            c0, ki, q0 = b["c0"], b["ki"], b["g"] * 512
            P.op("pe", lambda e: e.matmul(zbk[zi][:, c0:512], lhsT=kT[i][:, ki * 128:(ki + 1) * 128],
                                          rhs=qT[i][:, q0 + c0:q0 + 512], start=True, stop=True),
                 r=["kT%d" % i, "qT%d" % i], w=["zbk%d" % zi])

        def S1(b):
            k3 = b["n"] % 3
            cs = slice(b["c0"], 512)
            n = 512 - b["c0"]
            Z, zk = zbk[k3], "zbk%d" % k3
            if b["first"]:
                r_ = b["gi"] % 2
                P.op("pool", lambda e: e.memset(rsum[r_][:], 0.0), w=["rsum%d" % r_])
            P.op("act", lambda e: e.activation(out=ee[k3][:, cs], in_=Z[:, cs], func=AF.Exp, scale=-1.0), r=[zk], w=["ee%d" % k3])
            P.op("act", lambda e: e.activation(out=sp[k3][:, cs], in_=ee[k3][:, cs], func=AF.Ln, bias=1.0), r=["ee%d" % k3], w=["sp%d" % k3])
            P.op("dve", lambda e: e.scalar_tensor_tensor(out=lk[k3][:, cs], in0=sp[k3][:, cs], scalar=-1.0, in1=Z[:, cs],
                                                         op0=ALU.mult, op1=ALU.subtract), r=["sp%d" % k3, zk], w=["lk%d" % k3])
            if b["j"] >= -1:
                so = 128 if b["j"] == -1 else 0
                P.op("dve", lambda e: e.tensor_tensor(out=lk[k3][:, cs], in0=lk[k3][:, cs], in1=m01[:, so:so + n], op=ALU.mult),
                     r=["lk%d" % k3, "m01"], w=["lk%d" % k3])

        def S2(b):
            k3 = b["n"] % 3
            cs = slice(b["c0"], 512)
            r_ = b["gi"] % 2
            bi = b["n"] % 2
            B, bk = bbk[bi], "bbk%d" % bi
            P.op("pe", lambda e: e.matmul(B[:, cs], lhsT=lstr[:, :], rhs=lk[k3][:, cs], start=True, stop=b["first"]),
                 r=["lstr", "lk%d" % k3], w=[bk], sig=b["first"])
            if not b["first"]:
                P.op("pe", lambda e: e.matmul(B[:, cs], lhsT=onesf[:, :], rhs=rsum[r_][:, cs], start=False, stop=True),
                     r=["onesf", "rsum%d" % r_], w=[bk])
            P.op("dve", lambda e: e.tensor_tensor(out=ee[k3][:, cs], in0=B[:, cs], in1=sp[k3][:, cs], op=ALU.subtract),
                 r=[bk, "sp%d" % k3], w=["ee%d" % k3])
            if not b["last"]:
                P.op("pool", lambda e: e.tensor_tensor(out=rsum[r_][:, cs], in0=rsum[r_][:, cs], in1=lk[k3][:, cs], op=ALU.add),
                     r=["rsum%d" % r_, "lk%d" % k3], w=["rsum%d" % r_])

        def S3(b):
            k3 = b["n"] % 3
            cs = slice(b["c0"], 512)
            n = 512 - b["c0"]
            i = b["h"] % 2
            o_ = b["gi"] % 2
            P.op("act", lambda e: e.activation(out=aT[k3][:, cs], in_=ee[k3][:, cs], func=AF.Exp), r=["ee%d" % k3], w=["aT%d" % k3])
            if b["j"] >= -1:
                so = 128 if b["j"] == -1 else 0
                P.op("dve", lambda e: e.tensor_tensor(out=aT[k3][:, cs], in0=aT[k3][:, cs], in1=m01[:, so:so + n], op=ALU.mult),
                     r=["aT%d" % k3, "m01"], w=["aT%d" % k3])
            P.op("pe", lambda e: e.matmul(obk[o_][:, cs], lhsT=V[i][:, b["ki"], :], rhs=aT[k3][:, cs], start=b["first"], stop=b["last"],
                                          skip_group_check=True), r=["V%d" % i, "aT%d" % k3], w=["obk%d" % o_])
            if b["last"]:
                q0 = b["g"] * 512
                P.op("dve", lambda e: e.tensor_tensor(out=yst[i][:, q0:q0 + 512], in0=obk[o_][:, :], in1=gT[i][:, q0:q0 + 512],
                                                      op=ALU.mult), r=["obk%d" % o_, "gT%d" % i], w=["yst%d" % i])
                if b["g"] == NG - 1:
                    h = b["h"]
                    P.dma("pool", d["yT"][3 * CFG["YB"] + h * 128:3 * CFG["YB"] + (h + 1) * 128, :], yst[i][:, :], r=["yst%d" % i],
                          sem="d_yst%d" % i)

        nb_ = len(blocks)
        for step in range(nb_ + 4):
            if step < nb_:
                S0(blocks[step])
            if 0 <= step - 1 < nb_:
                S1(blocks[step - 1])
            if 0 <= step - 2 < nb_:
                S2(blocks[step - 2])
            if 0 <= step - 3 < nb_:
                S3(blocks[step - 3])


def phase_mixB(P, T, l, io, d, NH=None):
    NH = NH or CFG["NHA"]
    NHT = CFG["NHA"]
    NCH = T // 64
    NG = T // 512
    with P.phase("mixB") as ph:
        zf = [ph.sb("zf0", [128, T], F32)] * 2
        qb = [ph.sb("qb%d" % i, [128, T], BF16) for i in range(2)]
        gT = [ph.sb("gT%d" % i, [128, T], BF16) for i in range(2)]
        itok = [ph.sb("itok%d" % i, [64, NCH, 128], BF16) for i in range(2)]
        a1 = ph.sb("a1", [128, T], F32)
        a2 = ph.sb("a2", [128, T], F32)
        a3 = ph.sb("a3", [128, T], F32)
        a4 = ph.sb("a4", [128, T], F32)
        qt = ph.sb("qt", [128, T], BF16)
        kh = ph.sb("kh", [128, T], BF16)
        khtok = ph.sb("khtok", [64, NCH, 128], BF16)
        yst = [ph.sb("yst%d" % i, [128, 512], BF16) for i in range(2)]
        rmask = ph.sb("rmask", [128, 512], F32)
        cmask = ph.sb("cmask", [64, 512], F32)
        scb = [ph.sb("scb%d" % i, [64, 512], BF16) for i in range(2)]
        S = ph.sb("S", [128, 128], F32)
        Sb = [ph.sb("Sb%d" % i, [128, 128], BF16) for i in range(2)]
        sq = ph.sb("sq", [128, 512], F32)
        t1 = ph.sb("t1", [128, 512], F32)
        yy = ph.sb("yy", [128, 512], F32)
        identb = make_ident(P, ph, BF16, "identb")
        onesf = ph.sb("onesf", [128, 128], F32)
        hl = ph.sb("hl", [128, NHT, 4], F32)
        he = ph.sb("he", [128, NHT, 4], F32)
        hs = ph.sb("hs", [128, NHT], F32)
        lb = ph.sb("lb", [128, NHT], F32)
        oml = ph.sb("oml", [128, NHT], F32)
        hgn = ph.sb("hgn", [128, 1], F32)
        sbk = [ph.ps("sbk%d" % i, [128, 512]) for i in range(1)]
        obk = [ph.ps("obk%d" % i, [128, 512]) for i in range(2)]
        spk = [ph.ps("spk%d" % i, [128, 512]) for i in range(2)]
        ssb = ph.ps("ssb", [128, 512])
        tpk = [ph.ps("tpk%d" % i, [64, 8, 128], BF16) for i in range(2)]

        P.op("pool", lambda e: e.memset(onesf[:], 1.0), w=["onesf"])
        P.op("pool", lambda e: e.memset(rmask[:], 1.0), w=["rmask"])
        P.op("pool", lambda e: e.memset(rmask[:].rearrange("p (c k) -> p c k", k=64)[:, :, 0:1], 0.0), w=["rmask"])
        P.op("pool", lambda e: e.memset(cmask[:], 1.0), w=["cmask"])
        P.op("pool", lambda e: e.affine_select(out=cmask[:].rearrange("p (c t) -> p c t", t=64),
                                               in_=cmask[:].rearrange("p (c t) -> p c t", t=64),
                                               pattern=[[0, 8], [1, 64]], compare_op=ALU.is_ge, fill=0.0, base=0,
                                               channel_multiplier=-1), r=["cmask"], w=["cmask"])
        d0 = ph.sb("d0", [1, 8], F32)
        P.dma("sp", d0[:, :], d["Bd0"][:, :], w=["d0"], sem="d_d0")
        P.dma("sp", hl[:, :, :], io["hg_lower_t"], w=["hl"], sem="d_hl")
        P.dma("sp", hgn[:, :], io["hg_norm_t"][l], w=["hgn"], sem="d_hgn")
        P.op("act", lambda e: e.activation(out=he[:], in_=hl[:], func=AF.Exp), r=["hl"], w=["he"])
        P.op("dve", lambda e: e.reduce_sum(out=hs[:], in_=he[:], axis=AX.X), r=["he"], w=["hs"])
        P.op("dve", lambda e: e.reciprocal(out=hs[:], in_=hs[:]), r=["hs"], w=["hs"])
        P.op("pool", lambda e: e.memset(lb[:], 0.0), w=["lb"])
        for j in range(1, l + 1):
            P.op("dve", lambda e: e.tensor_tensor(out=lb[:], in0=lb[:], in1=he[:, :, j], op=ALU.add), r=["lb", "he"], w=["lb"])
        P.op("dve", lambda e: e.tensor_tensor(out=lb[:], in0=lb[:], in1=hs[:], op=ALU.mult), r=["lb", "hs"], w=["lb"])
        P.op("dve", lambda e: e.tensor_scalar(out=oml[:], in0=lb[:], scalar1=-1.0, scalar2=1.0, op0=ALU.mult, op1=ALU.add),
             r=["lb"], w=["oml"])

        def load_head(h):
            i = h % 2
            P.dma("sp", zf[0][:, :], d["BfT"][h * 128:(h + 1) * 128, :], w=["zf0"], sem="d_zf0")
            P.dma("sp", qb[i][:, :], d["BqT"][h * 128:(h + 1) * 128, :], w=["qb%d" % i], sem="d_qb%d" % i)
            P.dma("sp", gT[i][:, :], d["BgT"][h * 128:(h + 1) * 128, :], w=["gT%d" % i], sem="d_gT%d" % i)
            P.dma("sp", itok[i][:, :, :], d["Bi"].rearrange("(c p) e -> p c e", p=64)[:, :, h * 128:(h + 1) * 128],
                  w=["itok%d" % i], sem="d_itok%d" % i)

        cnt = dict(sb=0)
        load_head(0)
        for h in range(NH):
            i = h % 2
            zk = "zf0"
            P.op("act", lambda e: e.activation(out=a1[:], in_=zf[i][:], func=AF.Sigmoid), r=[zk], w=["a1"])
            P.op("dve", lambda e: e.tensor_scalar(out=a1[:], in0=a1[:], scalar1=oml[:, h:h + 1], scalar2=lb[:, h:h + 1],
                                                  op0=ALU.mult, op1=ALU.add), r=["a1", "oml", "lb"], w=["a1"])
            P.op("act", lambda e: e.activation(out=a2[:], in_=a1[:], func=AF.Ln), r=["a1"], w=["a2"])
            P.op("dve", lambda e: e.tensor_scalar(out=a1[:], in0=a1[:], scalar1=-1.0, scalar2=1.0, op0=ALU.mult, op1=ALU.add),
                 r=["a1"], w=["a1"])
            for g8 in range(NG):
                P.op("dve", lambda e: e.tensor_tensor_scan(out=a3[:, g8 * 512:(g8 + 1) * 512], data0=rmask[:],
                                                           data1=a2[:, g8 * 512:(g8 + 1) * 512], initial=0.0,
                                                           op0=ALU.mult, op1=ALU.add), r=["rmask", "a2"], w=["a3"])
            P.op("act", lambda e: e.activation(out=a4[:], in_=a3[:], func=AF.Exp), r=["a3"], w=["a4"])
            P.op("act", lambda e: e.activation(out=a2[:], in_=a3[:], func=AF.Exp, scale=-1.0), r=["a3"], w=["a2"])
            P.op("dve", lambda e: e.tensor_tensor(out=a1[:], in0=a1[:], in1=a2[:], op=ALU.mult), r=["a1", "a2"], w=["a1"])
            P.op("dve", lambda e: e.tensor_tensor(out=a2[:], in0=qb[i][:], in1=a4[:], op=ALU.mult), r=["qb%d" % i, "a4", "a2"], w=["a2"])
            P.op("act", lambda e: e.activation(out=qt[:], in_=a2[:], func=AF.Copy), r=["a2"], w=["qt"])
            ebl = a4[:].rearrange("p (c k) -> p c k", k=64)[:, :, 63:64]
            P.op("dve", lambda e: e.tensor_tensor(out=kh[:].rearrange("p (c k) -> p c k", k=64),
                                                  in0=a1[:].rearrange("p (c k) -> p c k", k=64),
                                                  in1=ebl.to_broadcast([128, NCH, 64]), op=ALU.mult), r=["a1", "a4"], w=["kh"])
            if h + 1 < NH:
                load_head(h + 1)
            for c8 in range(NCH // 8):
                tp = tpk[c8 % 2]
                tk_ = "tpk%d" % (c8 % 2)
                for cc in range(8):
                    c = c8 * 8 + cc
                    P.op("pe", lambda e: e.transpose(out=tp[:, cc, :], in_=kh[:, c * 64:(c + 1) * 64], identity=identb[:, :]),
                         r=["kh", "identb"], w=[tk_], sig=(cc == 7))
                P.op("act", lambda e: e.activation(out=khtok[:, c8 * 8:(c8 + 1) * 8, :], in_=tp[:, :, :], func=AF.Copy),
                     r=[tk_], w=["khtok"])
            P.op("pool", lambda e: e.memset(S[:], 0.0), w=["S"])
            P.op("pool", lambda e: e.memset(Sb[0][:], 0.0), w=["Sb0"])
            sbi = 0
            for g in range(NG):
                q0 = g * 512
                for cc in range(8):
                    c = g * 8 + cc
                    P.op("pe", lambda e: e.matmul(sbk[0][0:64, cc * 64:(cc + 1) * 64], lhsT=a1[:, c * 64:(c + 1) * 64],
                                                  rhs=a2[:, c * 64:(c + 1) * 64], start=True, stop=True, skip_group_check=True),
                         r=["a1", "a2"], w=["sbk0"], sig=(cc == 7))
                si = g % 2
                P.op("dve", lambda e: e.tensor_tensor(out=scb[si][:], in0=sbk[0][0:64, :], in1=cmask[:], op=ALU.mult),
                     r=["sbk0", "cmask"], w=["scb%d" % si])
                if g == 0:
                    P.op("dve", lambda e: e.tensor_copy(out=scb[si][0:1, 0:1], in_=d0[0:1, h:h + 1]), r=["d0", "scb%d" % si],
                         w=["scb%d" % si])
                for half in range(2):
                    for c4 in range(4):
                        c = g * 8 + half * 4 + c4
                        P.op("pe", lambda e: e.matmul(spk[half][:, c4 * 128:(c4 + 1) * 128], lhsT=khtok[:, c, :],
                                                      rhs=itok[i][:, c, :], start=True, stop=True, skip_group_check=True),
                             r=["khtok", "itok%d" % i], w=["spk%d" % half], sig=(c4 == 3))
                ob = obk[g % 2]
                ok_ = "obk%d" % (g % 2)
                for cc in range(8):
                    c = g * 8 + cc
                    P.op("pe", lambda e: e.matmul(ob[:, cc * 64:(cc + 1) * 64], lhsT=itok[i][:, c, :],
                                                  rhs=scb[si][:, cc * 64:(cc + 1) * 64], start=True, stop=False,
                                                  skip_group_check=True), r=["itok%d" % i, "scb%d" % si], w=[ok_], sig=False)
                    P.op("pe", lambda e: e.matmul(ob[:, cc * 64:(cc + 1) * 64], lhsT=Sb[sbi][:, :],
                                                  rhs=qt[:, c * 64:(c + 1) * 64], start=False, stop=True,
                                                  skip_group_check=True), r=["Sb%d" % sbi, "qt"], w=[ok_])
                    half, c4 = cc // 4, cc % 4
                    P.op("dve", lambda e: e.scalar_tensor_tensor(out=S[:], in0=S[:], scalar=ebl[:, c, :],
                                                                 in1=spk[half][:, c4 * 128:(c4 + 1) * 128],
                                                                 op0=ALU.mult, op1=ALU.add), r=["S", "a4", "spk%d" % half], w=["S"])
                    sbi = 1 - sbi
                    P.op("act", lambda e: e.activation(out=Sb[sbi][:], in_=S[:], func=AF.Copy), r=["S"], w=["Sb%d" % sbi])
                P.op("act", lambda e: e.activation(out=sq[:], in_=ob[:, :], func=AF.Square), r=[ok_], w=["sq"])
                P.op("pe", lambda e: e.matmul(ssb[:, :], lhsT=onesf[:, :], rhs=sq[:], start=True, stop=True), r=["onesf", "sq"], w=["ssb"])
                P.op("dve", lambda e: e.tensor_scalar(out=t1[:], in0=ssb[:, :], scalar1=1.0 / 128, scalar2=RMS_EPS,
                                                      op0=ALU.mult, op1=ALU.add), r=["ssb"], w=["t1"])
                P.op("act", lambda e: e.activation(out=t1[:], in_=t1[:], func=AF.Ln), r=["t1"], w=["t1"])
                P.op("act", lambda e: e.activation(out=t1[:], in_=t1[:], func=AF.Exp, scale=-0.5), r=["t1"], w=["t1"])
                P.op("dve", lambda e: e.tensor_tensor(out=yy[:], in0=ob[:, :], in1=t1[:], op=ALU.mult), r=[ok_, "t1"], w=["yy"])
                yi = g % 2
                P.op("dve", lambda e: e.scalar_tensor_tensor(out=yst[yi][:, :], in0=yy[:], scalar=hgn[:, 0:1],
                                                             in1=gT[i][:, q0:q0 + 512], op0=ALU.mult, op1=ALU.mult),
                     r=["yy", "hgn", "gT%d" % i], w=["yst%d" % yi])
                P.dma("pool", d["yT"][CFG["YB"] + h * 128:CFG["YB"] + (h + 1) * 128, q0:q0 + 512], yst[yi][:, :], r=["yst%d" % yi],
                      sem="d_yst%d" % yi)


def phase_b0(P, T, l, io, d):
    SP = CFG["SPLIT"]
    YB = CFG["YB"]
    NHB = CFG["NHA"]
    qcol = O_HQ if SP == 1 else 4 * YB
    fcol = O_HF if SP == 1 else 4 * YB + YB
    P.barrier()
    if SP > 1:
        P.coll("AllGather", d["h"][0:1, :], d["h0g"][:, :])
        P.barrier()
    with P.phase("b0") as ph:
        h0 = ph.sb("h0", [128, KC], F32)
        wb = [ph.sb("wb%d" % i, [128, KC, 256], F32) for i in range(2)]
        row = ph.sb("row", [1, 2 * YB], F32)
        hlr = ph.sb("hlr", [1, 4, YB], F32)
        er = ph.sb("er", [1, 4, YB], F32)
        srow = ph.sb("srow", [1, YB], F32)
        lbr = ph.sb("lbr", [1, YB], F32)
        fr = ph.sb("fr", [1, YB], F32)
        dots = ph.sb("dots", [1, 8], F32)
        pbk = [ph.ps("pbk%d" % i, [1, 512]) for i in range(2)]
        src_h0 = d["h"][0:1, :] if SP == 1 else d["h0g"][0:1, :]
        h0r = ph.sb("h0r", [KC, 128], F32)
        ident = make_ident(P, ph, F32, "identf")
        tps = ph.ps("tps", [128, KC])
        P.dma("sp", h0r[:, :], src_h0.rearrange("o (kc p) -> (o kc) p", p=128), w=["h0r"], sem="d_h0")
        P.op("pe", lambda e: e.transpose(out=tps[:, :], in_=h0r[:, :], identity=ident[0:KC, 0:KC]), r=["h0r", "identf"], w=["tps"])
        P.op("dve", lambda e: e.tensor_copy(out=h0[:, :], in_=tps[:, :]), r=["tps"], w=["h0"])
        P.dma("sp", hlr[:, :, :], io["hg_lower_r"], w=["hlr"], sem="d_hlr")
        w_l = io["w_in"][l].rearrange("(kc p) c -> p kc c", p=128)
        nblk = 2 * YB // 256
        for bi in range(nblk):
            c0 = (qcol if bi < nblk // 2 else fcol) + (bi % (nblk // 2)) * 256
            wi = bi % 2
            P.dma("sp", wb[wi][:, :, :], w_l[:, :, c0:c0 + 256], w=["wb%d" % wi], sem="d_wbf%d" % wi)
            for kc in range(KC):
                P.op("pe", lambda e: e.matmul(pbk[wi][0:1, 0:256], lhsT=h0[:, kc:kc + 1], rhs=wb[wi][:, kc, :],
                                              start=(kc == 0), stop=(kc == KC - 1)), r=["h0", "wb%d" % wi], w=["pbk%d" % wi],
                     sig=(kc == KC - 1))
            P.op("dve", lambda e: e.tensor_copy(out=row[0:1, bi * 256:(bi + 1) * 256], in_=pbk[wi][0:1, 0:256]),
                 r=["pbk%d" % wi], w=["row"])
        P.op("act", lambda e: e.activation(out=er[:], in_=hlr[:], func=AF.Exp), r=["hlr"], w=["er"])
        P.op("dve", lambda e: e.tensor_tensor(out=srow[:], in0=er[:, 0, :], in1=er[:, 1, :], op=ALU.add), r=["er"], w=["srow"])
        P.op("dve", lambda e: e.tensor_tensor(out=srow[:], in0=srow[:], in1=er[:, 2, :], op=ALU.add), r=["er", "srow"], w=["srow"])
        P.op("dve", lambda e: e.tensor_tensor(out=srow[:], in0=srow[:], in1=er[:, 3, :], op=ALU.add), r=["er", "srow"], w=["srow"])
        P.op("dve", lambda e: e.reciprocal(out=srow[:], in_=srow[:]), r=["srow"], w=["srow"])
        P.op("pool", lambda e: e.memset(lbr[:], 0.0), w=["lbr"])
        for j in range(1, l + 1):
            P.op("dve", lambda e: e.tensor_tensor(out=lbr[:], in0=lbr[:], in1=er[:, j, :], op=ALU.add), r=["lbr", "er"], w=["lbr"])
        P.op("dve", lambda e: e.tensor_tensor(out=lbr[:], in0=lbr[:], in1=srow[:], op=ALU.mult), r=["lbr", "srow"], w=["lbr"])
        P.op("act", lambda e: e.activation(out=fr[:], in_=row[0:1, YB:2 * YB], func=AF.Sigmoid), r=["row"], w=["fr"])
        P.op("dve", lambda e: e.tensor_scalar(out=fr[:], in0=fr[:], scalar1=-1.0, scalar2=1.0, op0=ALU.mult, op1=ALU.add),
             r=["fr"], w=["fr"])
        P.op("dve", lambda e: e.tensor_scalar(out=lbr[:], in0=lbr[:], scalar1=-1.0, scalar2=1.0, op0=ALU.mult, op1=ALU.add),
             r=["lbr"], w=["lbr"])
        P.op("dve", lambda e: e.tensor_tensor(out=fr[:], in0=fr[:], in1=lbr[:], op=ALU.mult), r=["fr", "lbr"], w=["fr"])
        P.op("dve", lambda e: e.tensor_tensor(out=fr[:], in0=fr[:], in1=row[0:1, 0:YB], op=ALU.mult), r=["fr", "row"], w=["fr"])
        P.op("pool", lambda e: e.memset(dots[:], 0.0), w=["dots"])
        P.op("dve", lambda e: e.reduce_sum(out=dots[0:1, 0:NHB], in_=fr[:].rearrange("o (h d) -> o h d", d=128), axis=AX.X),
             r=["fr", "dots"], w=["dots"])
        P.dma("sp", d["Bd0"][:, :], dots[:, :], r=["dots"], sem="d_dots")


WSC = math.exp(-0.5)


def phase_mixC(P, T, l, io, d, NH=None, G=2):
    NH = NH or CFG["NHC"]
    NHT = CFG["NHC"]
    RB = 64 * NHT
    N = 512
    NST = T // N
    GC = G * 8
    with P.phase("mixC") as ph:
        F = lambda name, shape, dt=F32: ph.sb(name, shape, dt)
        cpar = F("cpar", [64, NHT, 8])
        omka = F("omka", [64, NHT])
        w2s = F("w2s", [64, RB])
        a2s = F("a2s", [64, RB])
        v2s = F("v2s", [32, RB])
        twd = F("twd", [64, N])
        adm = F("adm", [64, N])
        vdm = F("vdm", [32, N])
        Xr2 = [F("Xr%d" % i, [64, G, N]) for i in range(2)]
        Xk2 = [F("Xk%d" % i, [64, G, N]) for i in range(2)]
        Xv2 = [F("Xv%d" % i, [64, G, N]) for i in range(2)]
        Xf2 = [F("Xf%d" % i, [64, G, N]) for i in range(2)]
        gate2 = [F("gate%d" % i, [64, G, N], BF16) for i in range(2)]
        gctr = [0]
        sg = F("sg", [64, G, N])
        aa = F("aa", [64, G, N])
        kk = F("kk", [64, G, N])
        kp = F("kp", [64, G, N])
        cs = F("cs", [64, G, N])
        pp = F("pp", [64, G, N])
        pinv = F("pinv", [64, G, N])
        pprev = F("pprev", [64, G, N])
        tmp = F("tmp", [64, G, N])
        AR = F("AR", [64, G, 8, 2, 64])
        BT = F("BT", [64, G, N])
        KT = F("KT", [64, G, N])
        RK = F("RK", [64, G, N])
        TK = F("TK", [64, G, 8, 5, 64])
        GM = F("GM", [64, GC, 320])
        TT = F("TT", [64, GC, 64])
        TTb = F("TTb", [64, GC, 64], BF16)
        XY = [F("XY%d" % i, [64, GC, 128], BF16) for i in range(2)]
        tmpb = F("tmpb", [64, G, N], BF16)
        RKb = F("RKb", [64, G, N], BF16)
        ones64b = F("ones64b", [64, 64], BF16)
        onesmb = F("onesmb", [64, 64], BF16)
        GA = F("GA", [64, GC, 128])
        OL = F("OL", [64, GC, 64])
        RP = F("RP", [64, GC, 64])
        PH = F("PH", [64, GC, 64])
        GP = F("GP", [64, GC, 64])
        OT = F("OT", [64, G, N])
        H = F("H", [64, NH, 64])
        yst = F("yst", [64, G, N], BF16)
        m320 = F("m320", [64, 320])
        rmask = F("rmask", [64, G * N])
        ones64 = F("ones64", [64, 64])
        onesm = F("onesm", [64, 64])
        ident = make_ident(P, ph, F32, "identf")
        idf = ident[0:64, 0:64]
        pbs = [ph.ps("pb%d" % i, [64, 512]) for i in range(8)]
        bctr = [0]

        def nb():
            b = bctr[0] % 8
            bctr[0] += 1
            return pbs[b], "pb%d" % b

        P.op("pool", lambda e: e.memset(ones64[:], 1.0), w=["ones64"])
        P.op("pool", lambda e: e.memset(onesm[:], 1.0 / 64), w=["onesm"])
        P.op("pool", lambda e: e.memset(ones64b[:], 1.0), w=["ones64b"])
        P.op("pool", lambda e: e.memset(onesmb[:], 1.0 / 64), w=["onesmb"])
        P.op("pool", lambda e: e.memset(rmask[:], 1.0), w=["rmask"])
        P.op("pool", lambda e: e.memset(rmask[:].rearrange("p (c k) -> p c k", k=64)[:, :, 0:1], 0.0), w=["rmask"])
        P.op("pool", lambda e: e.memset(H[:], 0.0), w=["H"])
        P.op("pool", lambda e: e.memset(m320[:], 1.0), w=["m320"])
        for blk, op_, cm, pat in ((0, ALU.is_gt, -1, 1), (1, ALU.is_ge, -1, 1), (2, ALU.is_gt, -1, 1), (3, ALU.is_ge, -1, 1),
                                  (4, ALU.is_gt, 1, -1)):
            P.op("pool", lambda e: e.affine_select(out=m320[:, blk * 64:(blk + 1) * 64], in_=m320[:, blk * 64:(blk + 1) * 64],
                                                   pattern=[[pat, 64]], compare_op=op_, fill=0.0, base=0, channel_multiplier=cm),
                 r=["m320"], w=["m320"])
        P.dma("sp", cpar[:, :, :], io["c_par"][l], w=["cpar"], sem="d_cpar")
        P.dma("sp", w2s[:, :], io["rw_w2"][l], w=["w2s"], sem="d_w2s")
        P.dma("sp", a2s[:, :], io["rw_a2"][l], w=["a2s"], sem="d_a2s")
        if l > 0:
            P.dma("sp", v2s[:, :], io["rw_v2"][l - 1], w=["v2s"], sem="d_v2s")
        P.op("dve", lambda e: e.tensor_scalar(out=omka[:], in0=cpar[:, :, 3], scalar1=-1.0, scalar2=1.0, op0=ALU.mult, op1=ALU.add),
             r=["cpar"], w=["omka"])

        def bc(ap2, g0):
            return ap2[:, g0:g0 + G].unsqueeze(2).to_broadcast([64, G, N])

        CM = d["CmT"]
        for st in range(NST):
            t0 = st * N
            P.dma("sp", twd[:, :], CM[3 * RB:3 * RB + 64, t0:t0 + N], w=["twd"], sem="d_twd")
            P.dma("sp", adm[:, :], CM[3 * RB + 64:3 * RB + 128, t0:t0 + N], w=["adm"], sem="d_adm")
            P.op("act", lambda e: e.activation(out=twd[:], in_=twd[:], func=AF.Tanh), r=["twd"], w=["twd"])
            if l > 0:
                P.dma("sp", vdm[:, :], d["CvdT"][:, t0:t0 + N], w=["vdm"], sem="d_vdm")
            for hg in range(NH // G):
                h0 = hg * G
                par = gctr[0] % 2
                gctr[0] += 1
                Xr, Xk, Xv, Xf, gate = Xr2[par], Xk2[par], Xv2[par], Xf2[par], gate2[par]
                kXr, kXk, kXv, kXf, kgate = "Xr%d" % par, "Xk%d" % par, "Xv%d" % par, "Xf%d" % par, "gate%d" % par
                rows = lambda base: CM[base + h0 * 64: base + (h0 + G) * 64, t0:t0 + N].rearrange("(g k) t -> k g t", k=64)
                P.dma("sp", Xr[:, :, :], rows(0), w=[kXr], sem="d_" + kXr)
                P.dma("sp", Xk[:, :, :], rows(RB), w=[kXk], sem="d_" + kXk)
                P.dma("sp", Xv[:, :, :], rows(2 * RB), w=[kXv], sem="d_" + kXv)
                P.dma("sp", gate[:, :, :], d["CgT"][h0 * 64:(h0 + G) * 64, t0:t0 + N].rearrange("(g k) t -> k g t", k=64),
                      w=[kgate], sem="d_" + kgate)
                if l > 0:
                    P.dma("sp", Xf[:, :, :], d["Cvf"][h0 * 64:(h0 + G) * 64, t0:t0 + N].rearrange("(g k) t -> k g t", k=64),
                          w=[kXf], sem="d_" + kXf)
                for g in range(G):
                    h = h0 + g
                    pb, pk = nb()
                    P.op("pe", lambda e: e.matmul(pb[:, :], lhsT=w2s[:, h * 64:(h + 1) * 64], rhs=twd[:, :], start=True, stop=True),
                         r=["w2s", "twd"], w=[pk])
                    P.op("act", lambda e: e.activation(out=sg[:, g, :], in_=pb[:, :], func=AF.Sigmoid, bias=cpar[:, h, 0:1]),
                         r=[pk, "cpar"], w=["sg"])
                    pb, pk = nb()
                    P.op("pe", lambda e: e.matmul(pb[:, :], lhsT=a2s[:, h * 64:(h + 1) * 64], rhs=adm[:, :], start=True, stop=True),
                         r=["a2s", "adm"], w=[pk])
                    P.op("act", lambda e: e.activation(out=aa[:, g, :], in_=pb[:, :], func=AF.Sigmoid, bias=cpar[:, h, 1:2]),
                         r=[pk, "cpar"], w=["aa"])
                    if l > 0:
                        pb, pk = nb()
                        P.op("pe", lambda e: e.matmul(pb[:, :], lhsT=v2s[:, h * 64:(h + 1) * 64], rhs=vdm[:, :], start=True, stop=True),
                             r=["v2s", "vdm"], w=[pk])
                        P.op("act", lambda e: e.activation(out=tmp[:, g, :], in_=pb[:, :], func=AF.Sigmoid, bias=cpar[:, h, 7:8]),
                             r=[pk, "cpar"], w=["tmp"])
                if l > 0:
                    P.op("dve", lambda e: e.tensor_tensor(out=Xf[:], in0=Xf[:], in1=Xv[:], op=ALU.subtract), r=[kXf, kXv], w=[kXf])
                    P.op("dve", lambda e: e.tensor_tensor(out=Xf[:], in0=Xf[:], in1=tmp[:], op=ALU.mult), r=[kXf, "tmp"], w=[kXf])
                    P.op("dve", lambda e: e.tensor_tensor(out=Xv[:], in0=Xv[:], in1=Xf[:], op=ALU.add), r=[kXf, kXv], w=[kXv])
                else:
                    P.dma("pool", d["Cvf"][h0 * 64:(h0 + G) * 64, t0:t0 + N].rearrange("(g k) t -> k g t", k=64), Xv[:, :, :],
                          r=[kXv], sem="d_vfo")
                P.op("dve", lambda e: e.tensor_tensor(out=kk[:], in0=Xk[:], in1=bc(cpar[:, :, 2], h0), op=ALU.mult),
                     r=[kXk, "cpar"], w=["kk"])
                P.op("act", lambda e: e.activation(out=tmpb[:], in_=kk[:], func=AF.Square), r=["kk"], w=["tmpb"])
                for g in range(G):
                    pb, pk = nb()
                    P.op("pe", lambda e: e.matmul(pb[:, :], lhsT=ones64b[:, :], rhs=tmpb[:, g, :], start=True, stop=True),
                         r=["ones64b", "tmpb"], w=[pk])
                    P.op("dve", lambda e: e.tensor_scalar(out=kp[:, g, :], in0=pb[:, :], scalar1=1e-24, scalar2=None, op0=ALU.max),
                         r=[pk], w=["kp"])
                P.op("act", lambda e: e.activation(out=kp[:], in_=kp[:], func=AF.Ln), r=["kp"], w=["kp"])
                P.op("act", lambda e: e.activation(out=kp[:], in_=kp[:], func=AF.Exp, scale=-0.5), r=["kp"], w=["kp"])
                P.op("dve", lambda e: e.tensor_tensor(out=kk[:], in0=kk[:], in1=kp[:], op=ALU.mult), r=["kk", "kp"], w=["kk"])
                P.op("dve", lambda e: e.tensor_tensor(out=kp[:], in0=aa[:], in1=bc(cpar[:, :, 3], h0), op=ALU.mult),
                     r=["aa", "cpar"], w=["kp"])
                P.op("dve", lambda e: e.tensor_tensor(out=kp[:], in0=kp[:], in1=bc(omka, h0), op=ALU.add), r=["kp", "omka"], w=["kp"])
                P.op("dve", lambda e: e.tensor_tensor(out=kp[:], in0=kp[:], in1=Xk[:], op=ALU.mult), r=["kp", kXk], w=["kp"])
                P.op("dve", lambda e: e.tensor_tensor_scan(out=cs[:].rearrange("p g n -> p (g n)"), data0=rmask[:],
                                                           data1=sg[:].rearrange("p g n -> p (g n)"), initial=0.0,
                                                           op0=ALU.mult, op1=ALU.add), r=["rmask", "sg"], w=["cs"])
                P.op("act", lambda e: e.activation(out=pp[:], in_=cs[:], func=AF.Exp, scale=-WSC), r=["cs"], w=["pp"])
                P.op("act", lambda e: e.activation(out=pinv[:], in_=cs[:], func=AF.Exp, scale=WSC), r=["cs"], w=["pinv"])
                P.op("dve", lambda e: e.tensor_tensor(out=tmp[:], in0=cs[:], in1=sg[:], op=ALU.subtract), r=["cs", "sg"], w=["tmp"])
                P.op("act", lambda e: e.activation(out=pprev[:], in_=tmp[:], func=AF.Exp, scale=-WSC), r=["tmp"], w=["pprev"])
                v4 = lambda t_: t_[:].rearrange("p g (c k) -> p g c k", k=64)
                P.op("dve", lambda e: e.scalar_tensor_tensor(out=AR[:, :, :, 0, :], in0=v4(kk), scalar=-1.0, in1=v4(pprev),
                                                             op0=ALU.mult, op1=ALU.mult), r=["kk", "pprev"], w=["AR"])
                P.op("dve", lambda e: e.tensor_tensor(out=AR[:, :, :, 1, :], in0=v4(Xr), in1=v4(pp), op=ALU.mult),
                     r=[kXr, "pp"], w=["AR"])
                P.op("dve", lambda e: e.tensor_tensor(out=BT[:], in0=kk[:], in1=aa[:], op=ALU.mult), r=["kk", "aa"], w=["BT"])
                P.op("dve", lambda e: e.tensor_tensor(out=BT[:], in0=BT[:], in1=pinv[:], op=ALU.mult), r=["BT", "pinv"], w=["BT"])
                P.op("dve", lambda e: e.tensor_tensor(out=KT[:], in0=kp[:], in1=pinv[:], op=ALU.mult), r=["kp", "pinv"], w=["KT"])
                P.op("dve", lambda e: e.tensor_tensor(out=RK[:], in0=Xr[:], in1=kp[:], op=ALU.mult), r=[kXr, "kp"], w=["RK"])
                P.op("dve", lambda e: e.tensor_tensor(out=RKb[:], in0=RK[:], in1=bc(cpar[:, :, 4], h0), op=ALU.mult),
                     r=["RK", "cpar"], w=["RKb"])
                for g in range(G):
                    for c2 in range(4):
                        pb, pk = nb()
                        pv = pb[:, :].rearrange("p (c j k) -> p c j k", c=2, j=4)
                        for cc in range(2):
                            c = c2 * 2 + cc
                            csl = slice(c * 64, (c + 1) * 64)
                            srcs = [(BT[:, g, csl], "BT"), (KT[:, g, csl], "KT"), (Xv[:, g, csl], kXv), (AR[:, g, c, 0, :], "AR")]
                            for j, (sap, skey) in enumerate(srcs):
                                P.op("pe", lambda e: e.transpose(out=pv[:, cc, j, :], in_=sap, identity=idf),
                                     r=[skey, "identf"], w=[pk], sig=(cc == 1 and j == 3))
                        P.op("act", lambda e: e.activation(out=TK[:, g, c2 * 2:c2 * 2 + 2, 0:3, :], in_=pv[:, :, 0:3, :], func=AF.Copy),
                             r=[pk], w=["TK"])
                        P.op("act", lambda e: e.activation(out=TK[:, g, c2 * 2:c2 * 2 + 2, 4, :], in_=pv[:, :, 3, :], func=AF.Copy),
                             r=[pk], w=["TK"])
                for g in range(G):
                    for c in range(8):
                        gc = g * 8 + c
                        csl = slice(c * 64, (c + 1) * 64)
                        pb, pk = nb()
                        arv = AR[:, g, c, :, :].rearrange("p a k -> p (a k)")
                        P.op("pe", lambda e: e.matmul(pb[:, 0:128], lhsT=BT[:, g, csl], rhs=arv, start=True, stop=True,
                                                      skip_group_check=True), r=["BT", "AR"], w=[pk], sig=False)
                        P.op("pe", lambda e: e.matmul(pb[:, 128:256], lhsT=KT[:, g, csl], rhs=arv, start=True, stop=True,
                                                      skip_group_check=True), r=["KT", "AR"], w=[pk], sig=False)
                        P.op("pe", lambda e: e.matmul(pb[:, 256:320], lhsT=AR[:, g, c, 0, :], rhs=BT[:, g, csl], start=True, stop=True,
                                                      skip_group_check=True), r=["BT", "AR"], w=[pk])
                        P.op("dve", lambda e: e.tensor_tensor(out=GM[:, gc, :], in0=pb[:, 0:320], in1=m320[:], op=ALU.mult),
                             r=[pk, "m320"], w=["GM"])
                P.op("dve", lambda e: e.tensor_tensor(out=TTb[:], in0=GM[:, :, 0:64], in1=idf.unsqueeze(1).to_broadcast([64, GC, 64]),
                                                      op=ALU.add), r=["GM", "identf"], w=["TTb"])
                for lev in range(5):
                    last = lev == 4
                    src = XY[(lev + 1) % 2]
                    dst = XY[lev % 2]
                    skey, dkey = "XY%d" % ((lev + 1) % 2), "XY%d" % (lev % 2)
                    for b4 in range(GC // 4):
                        pb, pk = nb()
                        pv = pb[:, :].rearrange("p (q k) -> p q k", q=4)
                        for q in range(4):
                            gc = b4 * 4 + q
                            if lev == 0:
                                Xa, Ya, rk_ = GM[:, gc, 256:320], GM[:, gc, 0:64], ["GM"]
                            else:
                                Xa, Ya, rk_ = src[:, gc, 0:64], src[:, gc, 64:128], [skey]
                            P.op("pe", lambda e: e.matmul(pv[:, q, 0:64], lhsT=Ya, rhs=Xa, start=True, stop=True, skip_group_check=True),
                                 r=rk_, w=[pk], sig=(last and q == 3))
                            if not last:
                                P.op("pe", lambda e: e.matmul(pv[:, q, 64:128], lhsT=Xa, rhs=Ya, start=True, stop=True,
                                                              skip_group_check=True), r=rk_, w=[pk], sig=(q == 3))
                        if last:
                            P.op("act", lambda e: e.activation(out=dst[:, b4 * 4:b4 * 4 + 4, 0:64], in_=pv[:, :, 0:64], func=AF.Copy),
                                 r=[pk], w=[dkey])
                        else:
                            P.op("act", lambda e: e.activation(out=dst[:, b4 * 4:b4 * 4 + 4, :], in_=pv[:, :, :], func=AF.Copy),
                                 r=[pk], w=[dkey])
                    for b4 in range(GC // 4):
                        pb2, pk2 = nb()
                        pv2 = pb2[:, 0:256].rearrange("p (q k) -> p q k", q=4)
                        for q in range(4):
                            gc = b4 * 4 + q
                            P.op("pe", lambda e: e.matmul(pv2[:, q, :], lhsT=dst[:, gc, 0:64], rhs=TTb[:, gc, :], start=True, stop=True,
                                                          skip_group_check=True), r=[dkey, "TTb"], w=[pk2], sig=(q == 3))
                        if last:
                            P.op("dve", lambda e: e.tensor_tensor(out=TT[:, b4 * 4:b4 * 4 + 4, :], in0=TTb[:, b4 * 4:b4 * 4 + 4, :],
                                                                  in1=pv2[:, :, :], op=ALU.add), r=["TTb", pk2], w=["TT"])
                        else:
                            P.op("dve", lambda e: e.tensor_tensor(out=TTb[:, b4 * 4:b4 * 4 + 4, :], in0=TTb[:, b4 * 4:b4 * 4 + 4, :],
                                                                  in1=pv2[:, :, :], op=ALU.add), r=["TTb", pk2], w=["TTb"])
                for g in range(G):
                    for c4 in range(2):
                        pb, pk = nb()
                        pv = pb[:, 0:256].rearrange("p (q k) -> p q k", q=4)
                        for q in range(4):
                            c = c4 * 4 + q
                            gc = g * 8 + c
                            P.op("pe", lambda e: e.matmul(pv[:, q, :], lhsT=GM[:, gc, 128:192], rhs=TK[:, g, c, 2, :], start=True, stop=True,
                                                          skip_group_check=True), r=["GM", "TK"], w=[pk], sig=(q == 3))
                        P.op("act", lambda e: e.activation(out=TK[:, g, c4 * 4:c4 * 4 + 4, 3, :], in_=pv[:, :, :], func=AF.Copy),
                             r=[pk], w=["TK"])
                for g in range(G):
                    for c4 in range(2):
                        pb, pk = nb()
                        pv = pb[:, :].rearrange("p (q k) -> p q k", q=4)
                        for q in range(4):
                            c = c4 * 4 + q
                            gc = g * 8 + c
                            P.op("pe", lambda e: e.matmul(pv[:, q, :], lhsT=TT[:, gc, :], rhs=TK[:, g, c, 3:5, :].rearrange("p a k -> p (a k)"),
                                                          start=True, stop=True, skip_group_check=True), r=["TT", "TK"], w=[pk], sig=(q == 3))
                        P.op("act", lambda e: e.activation(out=GA[:, g * 8 + c4 * 4:g * 8 + c4 * 4 + 4, :], in_=pv[:, :, :], func=AF.Copy),
                             r=[pk], w=["GA"])
                pC = pp[:].rearrange("p g (c k) -> p g c k", k=64)[:, :, :, 63:64]
                for g in range(G):
                    for c4 in range(2):
                        pbO, pkO = nb()
                        pbR, pkR = nb()
                        pbG, pkG = nb()
                        pvO = pbO[:, 0:256].rearrange("p (q k) -> p q k", q=4)
                        pvR = pbR[:, :].rearrange("p (q a k) -> p q a k", q=4, a=2)
                        pvG = pbG[:, 0:256].rearrange("p (q k) -> p q k", q=4)
                        for q in range(4):
                            c = c4 * 4 + q
                            gc = g * 8 + c
                            P.op("pe", lambda e: e.matmul(pvO[:, q, :], lhsT=GA[:, gc, 0:64], rhs=GM[:, gc, 64:128], start=True, stop=False,
                                                          skip_group_check=True), r=["GA", "GM"], w=[pkO], sig=False)
                            P.op("pe", lambda e: e.matmul(pvO[:, q, :], lhsT=TK[:, g, c, 2, :], rhs=GM[:, gc, 192:256], start=False, stop=True,
                                                          skip_group_check=True), r=["TK", "GM"], w=[pkO], sig=(q == 3))
                            P.op("pe", lambda e: e.matmul(pvR[:, q, 0, :], lhsT=GA[:, gc, 64:128], rhs=GM[:, gc, 64:128], start=True, stop=True,
                                                          skip_group_check=True), r=["GA", "GM"], w=[pkR], sig=False)
                            P.op("pe", lambda e: e.matmul(pvR[:, q, 1, :], lhsT=GA[:, gc, 64:128], rhs=TK[:, g, c, 0, :], start=True, stop=True,
                                                          skip_group_check=True), r=["GA", "TK"], w=[pkR], sig=(q == 3))
                            P.op("pe", lambda e: e.matmul(pvG[:, q, :], lhsT=TK[:, g, c, 0, :], rhs=GA[:, gc, 0:64], start=True, stop=False,
                                                          skip_group_check=True), r=["GA", "TK"], w=[pkG], sig=False)
                            P.op("pe", lambda e: e.matmul(pvG[:, q, :], lhsT=TK[:, g, c, 1, :], rhs=TK[:, g, c, 2, :], start=False, stop=True,
                                                          skip_group_check=True), r=["TK"], w=[pkG], sig=(q == 3))
                        gs = slice(g * 8 + c4 * 4, g * 8 + c4 * 4 + 4)
                        cs4 = slice(c4 * 4, c4 * 4 + 4)
                        P.op("act", lambda e: e.activation(out=OL[:, gs, :], in_=pvO[:, :, :], func=AF.Copy), r=[pkO], w=["OL"])
                        P.op("dve", lambda e: e.tensor_tensor(out=RP[:, gs, :], in0=pvR[:, :, 0, :], in1=AR[:, g, cs4, 1, :], op=ALU.add),
                             r=[pkR, "AR"], w=["RP"])
                        P.op("dve", lambda e: e.tensor_tensor(out=PH[:, gs, :], in0=pvR[:, :, 1, :],
                                                              in1=idf.unsqueeze(1).to_broadcast([64, 4, 64]), op=ALU.add),
                             r=[pkR, "identf"], w=["PH"])
                        P.op("dve", lambda e: e.tensor_tensor(out=GP[:, gs, :], in0=pvG[:, :, :],
                                                              in1=pC[:, g, cs4, :].to_broadcast([64, 4, 64]), op=ALU.mult),
                             r=[pkG, "pp"], w=["GP"])
                for c in range(8):
                    pbH, pkH = nb()
                    pbO, pkO = nb()
                    pvH = pbH[:, 0:G * 64].rearrange("p (g k) -> p g k", g=G)
                    pvO = pbO[:, 0:G * 64].rearrange("p (g k) -> p g k", g=G)
                    for g in range(G):
                        gc = g * 8 + c
                        P.op("pe", lambda e: e.matmul(pvH[:, g, :], lhsT=PH[:, gc, :], rhs=H[:, h0 + g, :], start=True, stop=True,
                                                      skip_group_check=True), r=["PH", "H"], w=[pkH], sig=(g == G - 1))
                    for g in range(G):
                        gc = g * 8 + c
                        P.op("pe", lambda e: e.matmul(pvO[:, g, :], lhsT=H[:, h0 + g, :], rhs=RP[:, gc, :], start=True, stop=True,
                                                      skip_group_check=True), r=["RP", "H"], w=[pkO], sig=(g == G - 1))
                    gcs = GP[:].rearrange("p (g c) k -> p g c k", g=G)[:, :, c, :]
                    ols = OL[:].rearrange("p (g c) k -> p g c k", g=G)[:, :, c, :]
                    P.op("dve", lambda e: e.tensor_tensor(out=OT[:, :, c * 64:(c + 1) * 64], in0=pvO[:, :, :], in1=ols, op=ALU.add),
                         r=[pkO, "OL"], w=["OT"])
                    P.op("dve", lambda e: e.tensor_tensor(out=H[:, h0:h0 + G, :], in0=pvH[:, :, :],
                                                          in1=pC[:, :, c, :].to_broadcast([64, G, 64]), op=ALU.mult),
                         r=[pkH, "pp", "H"], w=["H"])
                    P.op("dve", lambda e: e.tensor_tensor(out=H[:, h0:h0 + G, :], in0=H[:, h0:h0 + G, :], in1=gcs, op=ALU.add),
                         r=["H", "GP"], w=["H"])
                P.op("act", lambda e: e.activation(out=tmpb[:], in_=OT[:], func=AF.Copy), r=["OT"], w=["tmpb"])
                for g in range(G):
                    pb, pk = nb()
                    P.op("pe", lambda e: e.matmul(pb[:, :], lhsT=onesmb[:, :], rhs=tmpb[:, g, :], start=True, stop=True),
                         r=["onesmb", "tmpb"], w=[pk])
                    P.op("dve", lambda e: e.tensor_tensor(out=OT[:, g, :], in0=OT[:, g, :], in1=pb[:, :], op=ALU.subtract),
                         r=[pk, "OT"], w=["OT"])
                P.op("act", lambda e: e.activation(out=tmpb[:], in_=OT[:], func=AF.Square), r=["OT"], w=["tmpb"])
                for g in range(G):
                    pb, pk = nb()
                    P.op("pe", lambda e: e.matmul(pb[:, :], lhsT=onesmb[:, :], rhs=tmpb[:, g, :], start=True, stop=True),
                         r=["onesmb", "tmpb"], w=[pk])
                    P.op("dve", lambda e: e.tensor_scalar(out=cs[:, g, :], in0=pb[:, :], scalar1=RW_LN_EPS, scalar2=None, op0=ALU.add),
                         r=[pk], w=["cs"])
                P.op("act", lambda e: e.activation(out=cs[:], in_=cs[:], func=AF.Ln), r=["cs"], w=["cs"])
                P.op("act", lambda e: e.activation(out=cs[:], in_=cs[:], func=AF.Exp, scale=-0.5), r=["cs"], w=["cs"])
                P.op("dve", lambda e: e.tensor_tensor(out=OT[:], in0=OT[:], in1=cs[:], op=ALU.mult), r=["OT", "cs"], w=["OT"])
                P.op("dve", lambda e: e.tensor_tensor(out=OT[:], in0=OT[:], in1=bc(cpar[:, :, 5], h0), op=ALU.mult),
                     r=["OT", "cpar"], w=["OT"])
                P.op("dve", lambda e: e.tensor_tensor(out=OT[:], in0=OT[:], in1=bc(cpar[:, :, 6], h0), op=ALU.add),
                     r=["OT", "cpar"], w=["OT"])
                for g in range(G):
                    pb, pk = nb()
                    P.op("pe", lambda e: e.matmul(pb[:, :], lhsT=ones64b[:, :], rhs=RKb[:, g, :], start=True, stop=True),
                         r=["ones64b", "RKb"], w=[pk])
                    P.op("dve", lambda e: e.tensor_tensor(out=tmp[:, g, :], in0=pb[:, :], in1=Xv[:, g, :], op=ALU.mult),
                         r=[pk, kXv], w=["tmp"])
                P.op("dve", lambda e: e.tensor_tensor(out=OT[:], in0=OT[:], in1=tmp[:], op=ALU.add), r=["OT", "tmp"], w=["OT"])
                P.op("dve", lambda e: e.tensor_tensor(out=yst[:], in0=OT[:], in1=gate[:], op=ALU.mult), r=["OT", kgate], w=["yst"])
                P.dma("pool", d["yT"][2 * CFG["YB"] + h0 * 64:2 * CFG["YB"] + (h0 + G) * 64, t0:t0 + N].rearrange("(g k) t -> k g t", k=64), yst[:, :, :],
                      r=["yst"], sem="d_ystc")


def phase_mergeA(P, T, l, io, d):
    NT = T // 512
    SP = CFG["SPLIT"]
    KB = CFG["YB"] // 128
    TO = T // SP
    ODT = BF16 if SP == 1 else F32
    with P.phase("mergeA") as ph:
        W = [ph.sb("W%d" % n, [128, KB, 2048], BF16) for n in range(4)]
        yT = [ph.sb("yT%d" % i, [128, 4, KB, 512], BF16) for i in range(SP)]
        gts = [ph.sb("gts%d" % i, [128, 4, 512], BF16) for i in range(2)]
        tq = [[ph.sb("tq%d%d" % (i, n), [128, 512], F32) for n in range(4)] for i in range(2)]
        s01 = [ph.sb("s01%d" % i, [128, 512], F32) for i in range(2)]
        mst = [ph.sb("mst%d" % i, [128, 512], ODT) for i in range(2)]
        banks = [ph.ps("bk%d" % i, [128, 512]) for i in range(8)]
        for n in range(4):
            P.dma("pool", W[n][:, :, :], io["w_branch"][l, n].rearrange("(kc p) c -> p kc c", p=128), w=["W%d" % n], sem="d_W%d" % n)
        mg = d["MgT"].rearrange("(n c p) t -> p n c t", n=4, p=128)
        yv = d["yT"].rearrange("(n kc p) t -> p n kc t", n=4, p=128)
        bi = 0
        gi = 0
        for tt in range(NT):
            ts = slice(tt * 512, (tt + 1) * 512)
            yi = tt % len(yT)
            yk = "yT%d" % yi
            for n in range(4):
                P.dma("sp", yT[yi][:, n, :, :], yv[:, n, :, ts], w=[yk], sem="d_" + yk)
            for cc in range(16):
                g_ = gi % 2
                gi += 1
                P.dma("sp", gts[g_][:, :, :], mg[:, :, cc, ts], w=["gts%d" % g_], sem="d_gts%d" % g_)
                for n in range(4):
                    b = bi % 8
                    bi += 1
                    for kc in range(KB):
                        P.op("pe", lambda e: e.matmul(banks[b][:, :], lhsT=W[n][:, kc, cc * 128:(cc + 1) * 128], rhs=yT[yi][:, n, kc, :],
                                                      start=(kc == 0), stop=(kc == KB - 1)), r=["W%d" % n, yk], w=["bk%d" % b], sig=(kc == KB - 1))
                    P.op("dve", lambda e: e.tensor_tensor(out=tq[g_][n][:], in0=banks[b][:, :], in1=gts[g_][:, n, :], op=ALU.mult),
                         r=["bk%d" % b, "gts%d" % g_], w=["tq%d%d" % (g_, n)])
                    if n == 1:
                        P.op("pool", lambda e: e.tensor_tensor(out=s01[g_][:], in0=tq[g_][0][:], in1=tq[g_][1][:], op=ALU.add),
                             r=["tq%d0" % g_, "tq%d1" % g_], w=["s01%d" % g_])
                P.op("dve", lambda e: e.tensor_tensor(out=tq[g_][2][:], in0=tq[g_][2][:], in1=tq[g_][3][:], op=ALU.add),
                     r=["tq%d2" % g_, "tq%d3" % g_], w=["tq%d2" % g_])
                P.op("pool", lambda e: e.tensor_tensor(out=mst[g_][:], in0=s01[g_][:], in1=tq[g_][2][:], op=ALU.add),
                     r=["s01%d" % g_, "tq%d2" % g_], w=["mst%d" % g_])
                if SP == 1:
                    dst = d["mT"][cc * 128:(cc + 1) * 128, ts]
                else:
                    hf, tl = (tt * 512) // TO, (tt * 512) % TO
                    dst = d["mTp"][hf, cc * 128:(cc + 1) * 128, tl:tl + 512]
                P.dma("act", dst, mst[g_][:, :], r=["mst%d" % g_], sem="d_mst%d" % g_)


def phase_gather_hT(P, T, d):
    NCH = d["NCH"]
    rows = D_MODEL // NCH
    P.barrier()
    for i in range(NCH):
        P.coll("AllGather", d["hT"][i * rows:(i + 1) * rows, :], d["hTg"][i].rearrange("r f t -> (r f) t"))
    P.barrier()


def phase_scatter_mT(P, T, d):
    P.barrier()
    P.coll("ReduceScatter", d["mTp"].rearrange("r c t -> (r c) t"), d["mTs"][:, :])
    P.barrier()


def phase_mergeB(P, T, l, io, d, final_out=None):
    SP = CFG["SPLIT"]
    T = T // SP
    NT = T // 512
    with P.phase("mergeB") as ph:
        wo = ph.sb("wo", [128, KC, 2048], BF16)
        mT = [ph.sb("mT%d" % i, [128, KC, 512], BF16) for i in range(2)]
        ht = [ph.sb("ht%d" % i, [128, D_MODEL], F32) for i in range(2)]
        xx = [ph.sb("xx%d" % i, [128, D_MODEL], F32) for i in range(2)]
        junk = ph.sb("junk", [128, D_MODEL], F32)
        lg = ph.sb("lg", [128, D_MODEL], F32)
        lbt = ph.sb("lbt", [128, D_MODEL], F32)
        stt = [ph.sb("stt%d" % i, [128, 8], F32) for i in range(2)]
        hTs = [ph.sb("hTs%d" % i, [128, KC, 512], BF16) for i in range(2)]
        ident = make_ident(P, ph)
        banks = [ph.ps("bk%d" % i, [128, 512]) for i in range(4)]
        tbanks = [ph.ps("tb%d" % i, [128, 512]) for i in range(4)]
        ctr = [0]
        P.dma("pool", wo[:, :, :], io["w_out"][l].rearrange("(kc p) c -> p kc c", p=128), w=["wo"], sem="d_wo")
        P.dma("sp", lg[:, :], io["ln_g_b"][l], w=["lg"], sem="d_lg")
        P.dma("sp", lbt[:, :], io["ln_b_b"][l], w=["lbt"], sem="d_lbt")
        mv = (d["mT"] if SP == 1 else d["mTs"]).rearrange("(kc p) t -> p kc t", p=128)
        for tt in range(NT):
            mi = tt % 2
            P.dma("sp" if SP == 1 else "pool", mT[mi][:, :, :], mv[:, :, tt * 512:(tt + 1) * 512], w=["mT%d" % mi], sem="d_mT%d" % mi)
            for q in range(4):
                tk = tt * 4 + q
                xi = tk % 2
                rows = slice(tk * 128, (tk + 1) * 128)
                P.dma("sp", ht[xi][:, :], d["h"][rows, :], w=["ht%d" % xi], sem="d_ht%d" % xi)
                for nb_ in range(4):
                    for cc in range(KC):
                        P.op("pe", lambda e: e.matmul(banks[nb_][:, :], lhsT=mT[mi][:, cc, q * 128:(q + 1) * 128],
                                                      rhs=wo[:, cc, nb_ * 512:(nb_ + 1) * 512], start=(cc == 0), stop=(cc == KC - 1)),
                             r=["mT%d" % mi, "wo"], w=["bk%d" % nb_], sig=(cc == KC - 1))
                    P.op("dve", lambda e: e.scalar_tensor_tensor(out=xx[xi][:, nb_ * 512:(nb_ + 1) * 512],
                                                                 in0=ht[xi][:, nb_ * 512:(nb_ + 1) * 512], scalar=ALPHA,
                                                                 in1=banks[nb_][:, :], op0=ALU.mult, op1=ALU.add),
                         r=["ht%d" % xi, "bk%d" % nb_], w=["xx%d" % xi])
                s_ = stt[xi]
                sk = "stt%d" % xi
                P.op("act", lambda e: e.activation(out=junk[:], in_=xx[xi][:], func=AF.Copy, accum_out=s_[:, 0:1]), r=["xx%d" % xi], w=["junk", sk])
                P.op("act", lambda e: e.activation(out=junk[:], in_=xx[xi][:], func=AF.Square, accum_out=s_[:, 1:2]), r=["xx%d" % xi], w=["junk", sk])
                P.op("dve", lambda e: e.tensor_scalar(out=s_[:, 2:4], in0=s_[:, 0:2], scalar1=1.0 / D_MODEL, scalar2=None, op0=ALU.mult),
                     r=[sk], w=[sk])
                P.op("dve", lambda e: e.tensor_tensor(out=s_[:, 4:5], in0=s_[:, 2:3], in1=s_[:, 2:3], op=ALU.mult), r=[sk], w=[sk])
                P.op("dve", lambda e: e.tensor_tensor(out=s_[:, 5:6], in0=s_[:, 3:4], in1=s_[:, 4:5], op=ALU.subtract), r=[sk], w=[sk])
                P.op("dve", lambda e: e.tensor_scalar(out=s_[:, 5:6], in0=s_[:, 5:6], scalar1=LN_EPS, scalar2=None, op0=ALU.add), r=[sk], w=[sk])
                P.op("act", lambda e: e.activation(out=s_[:, 6:7], in_=s_[:, 5:6], func=AF.Ln), r=[sk], w=[sk])
                P.op("act", lambda e: e.activation(out=s_[:, 7:8], in_=s_[:, 6:7], func=AF.Exp, scale=-0.5), r=[sk], w=[sk])
                P.op("dve", lambda e: e.tensor_scalar(out=xx[xi][:], in0=xx[xi][:], scalar1=s_[:, 2:3], scalar2=s_[:, 7:8],
                                                      op0=ALU.subtract, op1=ALU.mult), r=["xx%d" % xi, sk], w=["xx%d" % xi])
                P.op("pool", lambda e: e.tensor_tensor(out=xx[xi][:], in0=xx[xi][:], in1=lg[:], op=ALU.mult), r=["xx%d" % xi, "lg"], w=["xx%d" % xi])
                P.op("dve", lambda e: e.tensor_tensor(out=xx[xi][:], in0=xx[xi][:], in1=lbt[:], op=ALU.add), r=["xx%d" % xi, "lbt"], w=["xx%d" % xi])
                dst = final_out if final_out is not None else d["h"]
                P.dma("pool", dst[rows, :], xx[xi][:, :], r=["xx%d" % xi], w=[], sem="d_xo%d" % xi)
                if final_out is None:
                    emit_transpose_tile(P, xx[xi], "xx%d" % xi, ident, tbanks, ctr, hTs[mi], "hTs%d" % mi, q * 128)
            if final_out is None:
                P.dma("sp", d["hT"].rearrange("(kc p) t -> p kc t", p=128)[:, :, tt * 512:(tt + 1) * 512], hTs[mi][:, :, :],
                      r=["hTs%d" % mi], sem="d_hTs%d" % mi)


def IO_SPECS(T):
    SP = CFG["SPLIT"]
    YB, NHA, NHC = CFG["YB"], CFG["NHA"], CFG["NHC"]
    ncols = IN_COLS if SP == 1 else (4 * YB) * 3 + (3 * YB + 128 + YB) + 8192
    nmu = (3 * YB + 128) // 128
    return {
        "x": ([T // SP, D_MODEL], F32),
        "w_in": ([DEPTH, D_MODEL, ncols], F32),
        "rw_mu_t": ([DEPTH, 128, nmu], F32),
        "rw_vmu_t": ([DEPTH - 1, 128, 1], F32),
        "rw_v1": ([DEPTH - 1, D_MODEL, 32], F32),
        "a_strip": ([NHA, 128, 640], F32),
        "a_c31": ([128, NHA], F32),
        "da_lambda_b": ([DEPTH, 128, 256], F32),
        "da_subln_t": ([DEPTH, 128, 1], F32),
        "hg_lower_t": ([128, NHA, 4], F32),
        "hg_lower_r": ([1, 4, 128 * NHA], F32),
        "hg_norm_t": ([DEPTH, 128, 1], F32),
        "c_par": ([DEPTH, 64, NHC, 8], F32),
        "rw_w2": ([DEPTH, 64, 64 * NHC], F32),
        "rw_a2": ([DEPTH, 64, 64 * NHC], F32),
        "rw_v2": ([DEPTH - 1, 32, 64 * NHC], F32),
        "w_branch": ([DEPTH, 4, YB, D_MODEL], F32),
        "w_out": ([DEPTH, D_MODEL, D_MODEL], F32),
        "ln_g_b": ([DEPTH, 128, D_MODEL], F32),
        "ln_b_b": ([DEPTH, 128, D_MODEL], F32),
    }


def build_program(T, depth=DEPTH, debug_outs=()):
    SP = CFG["SPLIT"]
    nc = bass.Bass("TRN2", target_bir_lowering=False)
    io = {k: nc.dram_tensor(k, list(s), dt, kind="ExternalInput").ap() for k, (s, dt) in IO_SPECS(T).items()}
    out = nc.dram_tensor("out", [T // SP, D_MODEL], F32, kind="ExternalOutput").ap()
    d = declare_dram(nc, T, debug_outs=debug_outs)
    P = Prog(nc)
    phase_prep0(P, T // SP, io, d)
    for l in range(depth):
        if SP > 1:
            phase_gather_hT(P, T, d)
        phase_gemm(P, T, l, io, d)
        phase_mixA(P, T, l, io, d)
        phase_b0(P, T, l, io, d)
        phase_mixB(P, T, l, io, d)
        phase_mixC(P, T, l, io, d)
        phase_mixD(P, T, l, io, d)
        phase_mergeA(P, T, l, io, d)
        if SP > 1:
            phase_scatter_mT(P, T, d)
        phase_mergeB(P, T, l, io, d, final_out=(out if l == depth - 1 else None))
    P.barrier()
    return nc, P


def host_shared_inputs(inp):
    f = lambda a: np.ascontiguousarray(np.asarray(a, dtype=np.float32))
    L = DEPTH
    sh = {}
    sh["w_in"] = f(inp["w_in"])
    sh["rw_mu_t"] = f(np.asarray(inp["rw_mu"]).reshape(L, 25, 128).transpose(0, 2, 1))
    vmu = np.zeros((L - 1, 128, 1), np.float32)
    vmu[:, :32, 0] = np.asarray(inp["rw_v_mu"])
    sh["rw_vmu_t"] = vmu
    sh["rw_v1"] = f(inp["rw_v1"])
    rel = np.asarray(inp["rel_bias"], dtype=np.float32)
    sh["a_strip"] = host_a_strip(rel)
    sh["a_c31"] = f(np.broadcast_to(rel[31][None, :], (128, 8)))
    sh["da_lambda_b"] = f(np.broadcast_to(np.asarray(inp["da_lambda"]).reshape(L, 1, 256), (L, 128, 256)))
    sh["da_subln_t"] = f(np.asarray(inp["da_subln"]).reshape(L, 128, 1))
    sh["hg_lower_t"] = f(np.asarray(inp["hg_lower"]).reshape(L, 8, 128).transpose(2, 1, 0))
    sh["hg_lower_r"] = f(np.asarray(inp["hg_lower"]).reshape(1, L, 1024))
    sh["hg_norm_t"] = f(np.asarray(inp["hg_norm"]).reshape(L, 128, 1))
    hk = lambda a: np.asarray(a).reshape(L, 16, 64).transpose(0, 2, 1)
    v0 = np.zeros((L, 1024), np.float32)
    v0[1:] = np.asarray(inp["rw_v0"])
    cp = np.stack([hk(inp["rw_w0"]), hk(inp["rw_a0"]), hk(inp["rw_kk"]), hk(inp["rw_ka"]),
                   np.asarray(inp["rw_rk"]).transpose(0, 2, 1), hk(inp["rw_lnx_g"]), hk(inp["rw_lnx_b"]), hk(v0)], axis=-1)
    sh["c_par"] = f(cp)
    sh["rw_w2"] = f(inp["rw_w2"])
    sh["rw_a2"] = f(inp["rw_a2"])
    sh["rw_v2"] = f(inp["rw_v2"])
    sh["w_branch"] = f(inp["w_branch"])
    sh["w_out"] = f(inp["w_out"])
    sh["ln_g_b"] = f(np.broadcast_to(np.asarray(inp["ln_g"])[:, None, :], (L, 128, D_MODEL)))
    sh["ln_b_b"] = f(np.broadcast_to(np.asarray(inp["ln_b"])[:, None, :], (L, 128, D_MODEL)))
    return sh


def host_core_inputs(sh, hh):
    SP = CFG["SPLIT"]
    if SP == 1:
        return dict(sh)
    YB, NHA, NHC = CFG["YB"], CFG["NHA"], CFG["NHC"]
    a = slice(hh * YB, (hh + 1) * YB)
    segs = []
    for base in (O_AQ, O_AK, O_AV, O_AG, O_HQ, O_HF, O_HI, O_HG):
        segs.append(np.arange(base + hh * YB, base + (hh + 1) * YB))
    for j in range(3):
        segs.append(np.arange(O_RM + j * 1024 + hh * YB, O_RM + j * 1024 + (hh + 1) * YB))
    segs.append(np.arange(O_RM + 3072, O_RM + 3200))
    segs.append(np.arange(O_RG + hh * YB, O_RG + (hh + 1) * YB))
    for base in (O_SQ, O_SK, O_SV, O_SG):
        segs.append(np.arange(base + hh * YB, base + (hh + 1) * YB))
    segs.append(np.arange(O_MG, O_MG + 8192))
    cols = np.concatenate(segs)
    m = {}
    m["w_in"] = np.ascontiguousarray(sh["w_in"][:, :, cols])
    mu_full = sh["rw_mu_t"]
    blk = []
    for j in range(3):
        blk += list(range(j * 8 + hh * (YB // 128), j * 8 + (hh + 1) * (YB // 128)))
    blk.append(24)
    m["rw_mu_t"] = np.ascontiguousarray(mu_full[:, :, blk])
    m["rw_vmu_t"] = sh["rw_vmu_t"]
    m["rw_v1"] = sh["rw_v1"]
    ha = slice(hh * NHA, (hh + 1) * NHA)
    hc = slice(hh * NHC, (hh + 1) * NHC)
    m["a_strip"] = np.ascontiguousarray(sh["a_strip"][ha])
    m["a_c31"] = np.ascontiguousarray(sh["a_c31"][:, ha])
    m["da_lambda_b"] = sh["da_lambda_b"]
    m["da_subln_t"] = sh["da_subln_t"]
    m["hg_lower_t"] = np.ascontiguousarray(sh["hg_lower_t"][:, ha, :])
    m["hg_lower_r"] = np.ascontiguousarray(sh["hg_lower_r"][:, :, a])
    m["hg_norm_t"] = sh["hg_norm_t"]
    m["c_par"] = np.ascontiguousarray(sh["c_par"][:, :, hc, :])
    m["rw_w2"] = np.ascontiguousarray(sh["rw_w2"][:, :, a])
    m["rw_a2"] = np.ascontiguousarray(sh["rw_a2"][:, :, a])
    m["rw_v2"] = np.ascontiguousarray(sh["rw_v2"][:, :, a])
    m["w_branch"] = np.ascontiguousarray(sh["w_branch"][:, :, a, :])
    m["w_out"] = sh["w_out"]
    m["ln_g_b"] = sh["ln_g_b"]
    m["ln_b_b"] = sh["ln_b_b"]
    return m


_CACHE = {}
SPLIT = 2


def kernel(**inputs):
    x = np.asarray(inputs["x"], dtype=np.float32)
    B, T, _ = x.shape
    ncore = B * SPLIT
    configure(SPLIT, [[2 * i, 2 * i + 1] for i in range(ncore // 2)] if SPLIT == 2 else None)
    if T not in _CACHE:
        _CACHE[T] = build_program(T)
    nc, _ = _CACHE[T]
    sh = host_shared_inputs(inputs)
    per_half = [host_core_inputs(sh, hh) for hh in range(SPLIT)]
    TO = T // SPLIT
    in_maps = []
    for b in range(B):
        for hh in range(SPLIT):
            m = dict(per_half[hh])
            m["x"] = np.ascontiguousarray(x[b, hh * TO:(hh + 1) * TO])
            in_maps.append(m)
    res = run_bass_kernel_spmd(nc, in_maps, core_ids=list(range(ncore)))
    out = np.empty((B, T, D_MODEL), np.float32)
    for b in range(B):
        for hh in range(SPLIT):
            out[b, hh * TO:(hh + 1) * TO] = np.asarray(res.results[b * SPLIT + hh]["out"], dtype=np.float32)
    return out
```
